# Optimizing a Trainium2 kernel written in Bass

```python
import math
import jax, jax.numpy as jnp
from jax import lax
import numpy as np

D_MODEL = 2048
BATCH = 8
SEQ = 2048
DEPTH = 2
DEC_BATCH = 32
DEC_SEQ = 32
PAST_LEN = 4096

CHUNK = 64
WINDOW = 128
WINDOW_CHUNKS = WINDOW // CHUNK
N_MIXERS = 2
N_ATT_LAYERS = (DEPTH + 1) // 2
N_RWKV_LAYERS = DEPTH // 2
HEAD_DIM = 64
N_HEADS = D_MODEL // HEAD_DIM
N_KV_HEADS = 4
GROUP = N_HEADS // N_KV_HEADS
Q_DIM = N_HEADS * HEAD_DIM
KV_DIM = N_KV_HEADS * HEAD_DIM
QKV_DIM = Q_DIM + 2 * KV_DIM
N_BUCKETS = 32
MAX_DISTANCE = 128
RWKV_HEAD = 64
RWKV_HEADS = D_MODEL // RWKV_HEAD
DECAY_LORA = 96
AAA_LORA = 96
GATE_LORA = 256
D_FF = 5632
RMS_EPS = 1e-6
GN_EPS = RWKV_HEAD * 1e-5
NEG_INF = -1e30

kernel_name = "hybrid_swa_sink_rwkv7_macaron_stream_step"


def rms_norm(x, g):
    xf = x.astype(jnp.float32)
    y = xf * lax.rsqrt(jnp.mean(jnp.square(xf), axis=-1, keepdims=True) + RMS_EPS)
    return (y * g.astype(jnp.float32)).astype(x.dtype)


def swiglu(h, w_gate, w_up, w_down):
    return (jax.nn.silu(h @ w_gate) * (h @ w_up)) @ w_down


def t5_bucket(rel):
    nb = N_BUCKETS // 2
    max_exact = nb // 2
    offset = jnp.where(rel > 0, nb, 0)
    n = jnp.abs(rel)
    nf = jnp.maximum(n, 1).astype(jnp.float32)
    large = max_exact + (jnp.log(nf / max_exact) / math.log(MAX_DISTANCE / max_exact)
                         * (nb - max_exact)).astype(jnp.int32)
    large = jnp.minimum(large, nb - 1)
    return offset + jnp.where(n < max_exact, n, large)


def rel_bias(table, n_q, n_k):
    rel = (jnp.arange(n_k, dtype=jnp.int32)[None, :] - WINDOW
           - jnp.arange(n_q, dtype=jnp.int32)[:, None])
    b = table[t5_bucket(rel)]
    return jnp.transpose(b, (2, 0, 1)).reshape(N_KV_HEADS, GROUP, n_q, n_k)


def band_attention(q, k, v, bias, mask, sinks):
    s = jnp.einsum('bcqhgd,bcshd->bchgqs', q, k).astype(jnp.float32) * (HEAD_DIM ** -0.5)
    s = s + bias.astype(jnp.float32)[None, None]
    s = jnp.where(mask[None, :, None, None, None, :], s, NEG_INF)
    sink = sinks.astype(jnp.float32).reshape(1, 1, N_KV_HEADS, GROUP, 1, 1)
    m = jnp.maximum(jnp.max(s, axis=-1, keepdims=True), sink)
    p = jnp.exp(s - m)
    denom = jnp.sum(p, axis=-1, keepdims=True) + jnp.exp(sink - m)
    return jnp.einsum('bchgqs,bcshd->bcqhgd', (p / denom).astype(v.dtype), v)


def attention_mixer(h, w_qkv, b_qkv, w_o, sinks, table, cache_k, cache_v):
    B, T, _ = h.shape
    qkv = h @ w_qkv + b_qkv
    q = qkv[..., :Q_DIM].reshape(B, T, N_KV_HEADS, GROUP, HEAD_DIM)
    k = qkv[..., Q_DIM:Q_DIM + KV_DIM].reshape(B, T, N_KV_HEADS, HEAD_DIM)
    v = qkv[..., Q_DIM + KV_DIM:].reshape(B, T, N_KV_HEADS, HEAD_DIM)
    if cache_k is None:
        n_c = T // CHUNK
        n_k = (WINDOW_CHUNKS + 1) * CHUNK
        qc = q.reshape(B, n_c, CHUNK, N_KV_HEADS, GROUP, HEAD_DIM)

        def band(t):
            tc = t.reshape(B, n_c, CHUNK, N_KV_HEADS, HEAD_DIM)
            tp = jnp.pad(tc, ((0, 0), (WINDOW_CHUNKS, 0), (0, 0), (0, 0), (0, 0)))
            return jnp.concatenate([tp[:, j:j + n_c] for j in range(WINDOW_CHUNKS + 1)], axis=2)

        key_chunk = (jnp.arange(n_c)[:, None] + jnp.arange(n_k)[None, :] // CHUNK - WINDOW_CHUNKS)
        mask = key_chunk >= 0
        bias = rel_bias(table, CHUNK, n_k)
        o = band_attention(qc, band(k), band(v), bias, mask, sinks).reshape(B, T, Q_DIM)
        new_k, new_v = k[:, T - WINDOW:], v[:, T - WINDOW:]
    else:
        n_k = WINDOW + T
        kb = jnp.concatenate([cache_k.astype(k.dtype), k], axis=1)[:, None]
        vb = jnp.concatenate([cache_v.astype(v.dtype), v], axis=1)[:, None]
        mask = jnp.ones((1, n_k), dtype=bool)
        bias = rel_bias(table, T, n_k)
        o = band_attention(q[:, None], kb, vb, bias, mask, sinks).reshape(B, T, Q_DIM)
        new_k, new_v = k, v
    return o @ w_o, new_k, new_v


def rwkv7_mixer(h, shift_prev, wkv_prev, mu, w_r, w_k, w_v, w_o, w0, w1, w2,
                a0, a1, a2, g1, g2, k_k, k_a, r_k, ln_w, ln_b):
    B, T, D = h.shape
    f32 = jnp.float32
    dx = jnp.concatenate([shift_prev.astype(h.dtype), h[:, :-1]], axis=1) - h
    xr, xw, xk, xv, xa, xg = [h + dx * mu[n] for n in range(6)]
    r = xr @ w_r
    k = xk @ w_k
    v = xv @ w_v
    w_log = -jax.nn.softplus(-(w0 + jnp.tanh(xw @ w1) @ w2)) - 0.5
    decay = jnp.exp(-jnp.exp(w_log.astype(f32)))
    a = jax.nn.sigmoid(a0 + (xa @ a1) @ a2)
    g = jax.nn.sigmoid(xg @ g1) @ g2

    def heads(t):
        return t.reshape(B, T, RWKV_HEADS, RWKV_HEAD).astype(f32)

    kk = heads(k * k_k)
    kk = kk / jnp.maximum(jnp.sqrt(jnp.sum(kk * kk, axis=-1, keepdims=True)), 1e-12)
    k = k * (1 + (a - 1) * k_a)
    rh, kh, vh, ah, wh = heads(r), heads(k), heads(v), heads(a), heads(decay)

    def step(S, inp):
        r_t, w_t, k_t, v_t, kk_t, a_t = inp
        sa = jnp.einsum('bhvk,bhk->bhv', S, -kk_t)
        S = (S * w_t[:, :, None, :] + sa[..., None] * (kk_t * a_t)[:, :, None, :]
             + v_t[..., None] * k_t[:, :, None, :])
        return S, jnp.einsum('bhvk,bhk->bhv', S, r_t)

    tm = lambda t: jnp.swapaxes(t, 0, 1)
    S_final, y = lax.scan(step, wkv_prev.astype(f32),
                          (tm(rh), tm(wh), tm(kh), tm(vh), tm(kk), tm(ah)))
    y = jnp.swapaxes(y, 0, 1)
    mean = jnp.mean(y, axis=-1, keepdims=True)
    var = jnp.mean(jnp.square(y - mean), axis=-1, keepdims=True)
    yn = ((y - mean) * lax.rsqrt(var + GN_EPS)).reshape(B, T, D)
    yn = yn * ln_w.astype(f32) + ln_b.astype(f32)
    bonus = jnp.sum(rh * kh * r_k.astype(f32), axis=-1, keepdims=True) * vh
    out = ((yn + bonus.reshape(B, T, D)) * g.astype(f32)).astype(h.dtype) @ w_o
    return out, h[:, T - 1:], S_final.astype(h.dtype)


def setup_inputs(seed: int = 0) -> dict:
    key = jax.random.key(seed)
    ks = iter(jax.random.split(key, 48))
    f32 = jnp.float32
    D = D_MODEL

    def nrm(shape, scale):
        return jax.random.normal(next(ks), shape, f32) * scale

    return {
        "x_prompt": nrm((BATCH, SEQ, D), 1.0),
        "x_sample": nrm((DEC_BATCH, DEC_SEQ, D), 1.0),
        "cache_k": nrm((N_ATT_LAYERS, DEC_BATCH, WINDOW, N_KV_HEADS, HEAD_DIM), 1.0),
        "cache_v": nrm((N_ATT_LAYERS, DEC_BATCH, WINDOW, N_KV_HEADS, HEAD_DIM), 1.0),
        "state_shift": nrm((N_RWKV_LAYERS, DEC_BATCH, 1, D), 1.0),
        "state_wkv": nrm((N_RWKV_LAYERS, DEC_BATCH, RWKV_HEADS, RWKV_HEAD, RWKV_HEAD), 1.0),
        "norm_g": 1.0 + nrm((DEPTH, 6, D), 0.05),
        "ffn_w_gate": nrm((DEPTH, 2, D, D_FF), D ** -0.5),
        "ffn_w_up": nrm((DEPTH, 2, D, D_FF), D ** -0.5),
        "ffn_w_down": nrm((DEPTH, 2, D_FF, D), D_FF ** -0.5),
        "rel_table": nrm((N_BUCKETS, N_HEADS), 0.5),
        "att_w_qkv": nrm((N_ATT_LAYERS, D, QKV_DIM), D ** -0.5),
        "att_b_qkv": nrm((N_ATT_LAYERS, QKV_DIM), 0.02),
        "att_w_o": nrm((N_ATT_LAYERS, Q_DIM, D), Q_DIM ** -0.5),
        "att_sinks": nrm((N_ATT_LAYERS, N_HEADS), 1.0),
        "rwkv_mu": jax.random.uniform(next(ks), (N_RWKV_LAYERS, 6, D), f32),
        "rwkv_w_r": nrm((N_RWKV_LAYERS, D, D), D ** -0.5),
        "rwkv_w_k": nrm((N_RWKV_LAYERS, D, D), D ** -0.5),
        "rwkv_w_v": nrm((N_RWKV_LAYERS, D, D), D ** -0.5),
        "rwkv_w_o": nrm((N_RWKV_LAYERS, D, D), D ** -0.5),
        "rwkv_w0": jnp.linspace(-6.5, -1.5, D, dtype=f32)[None, :] + nrm((N_RWKV_LAYERS, D), 0.1),
        "rwkv_w1": nrm((N_RWKV_LAYERS, D, DECAY_LORA), D ** -0.5),
        "rwkv_w2": nrm((N_RWKV_LAYERS, DECAY_LORA, D), 0.5 * DECAY_LORA ** -0.5),
        "rwkv_a0": nrm((N_RWKV_LAYERS, D), 0.1),
        "rwkv_a1": nrm((N_RWKV_LAYERS, D, AAA_LORA), D ** -0.5),
        "rwkv_a2": nrm((N_RWKV_LAYERS, AAA_LORA, D), AAA_LORA ** -0.5),
        "rwkv_g1": nrm((N_RWKV_LAYERS, D, GATE_LORA), D ** -0.5),
        "rwkv_g2": nrm((N_RWKV_LAYERS, GATE_LORA, D), GATE_LORA ** -0.5),
        "rwkv_k_k": 0.85 + nrm((N_RWKV_LAYERS, D), 0.05),
        "rwkv_k_a": 1.0 + nrm((N_RWKV_LAYERS, D), 0.05),
        "rwkv_r_k": nrm((N_RWKV_LAYERS, RWKV_HEADS, RWKV_HEAD), 0.1),
        "rwkv_ln_w": 1.0 + nrm((N_RWKV_LAYERS, D), 0.05),
        "rwkv_ln_b": nrm((N_RWKV_LAYERS, D), 0.02),
    }


def reference(x_prompt, x_sample, cache_k, cache_v, state_shift, state_wkv,
              norm_g, ffn_w_gate, ffn_w_up, ffn_w_down, rel_table,
              att_w_qkv, att_b_qkv, att_w_o, att_sinks,
              rwkv_mu, rwkv_w_r, rwkv_w_k, rwkv_w_v, rwkv_w_o,
              rwkv_w0, rwkv_w1, rwkv_w2, rwkv_a0, rwkv_a1, rwkv_a2,
              rwkv_g1, rwkv_g2, rwkv_k_k, rwkv_k_a, rwkv_r_k, rwkv_ln_w, rwkv_ln_b):

    def trunk(x, kv_cache, rwkv_state):
        B = x.shape[0]
        new_k, new_v, new_shift, new_wkv = [], [], [], []
        for i in range(DEPTH):
            g = norm_g[i]
            x = x + 0.5 * rms_norm(swiglu(rms_norm(x, g[0]), ffn_w_gate[i, 0],
                                          ffn_w_up[i, 0], ffn_w_down[i, 0]), g[1])
            h = rms_norm(x, g[2])
            j = i // N_MIXERS
            if i % N_MIXERS == 0:
                ck = None if kv_cache is None else kv_cache[0][j]
                cv = None if kv_cache is None else kv_cache[1][j]
                out, k_rows, v_rows = attention_mixer(h, att_w_qkv[j], att_b_qkv[j], att_w_o[j],
                                                      att_sinks[j], rel_table, ck, cv)
                new_k.append(k_rows)
                new_v.append(v_rows)
            else:
                if rwkv_state is None:
                    sp = jnp.zeros((B, 1, D_MODEL), h.dtype)
                    s0 = jnp.zeros((B, RWKV_HEADS, RWKV_HEAD, RWKV_HEAD), jnp.float32)
                else:
                    sp, s0 = rwkv_state[0][j], rwkv_state[1][j]
                out, sh, st = rwkv7_mixer(h, sp, s0, rwkv_mu[j], rwkv_w_r[j], rwkv_w_k[j],
                                          rwkv_w_v[j], rwkv_w_o[j], rwkv_w0[j], rwkv_w1[j],
                                          rwkv_w2[j], rwkv_a0[j], rwkv_a1[j], rwkv_a2[j],
                                          rwkv_g1[j], rwkv_g2[j], rwkv_k_k[j], rwkv_k_a[j],
                                          rwkv_r_k[j], rwkv_ln_w[j], rwkv_ln_b[j])
                new_shift.append(sh)
                new_wkv.append(st)
            x = x + rms_norm(out, g[3])
            x = x + 0.5 * rms_norm(swiglu(rms_norm(x, g[4]), ffn_w_gate[i, 1],
                                          ffn_w_up[i, 1], ffn_w_down[i, 1]), g[5])
        return x, jnp.stack(new_k), jnp.stack(new_v), jnp.stack(new_shift), jnp.stack(new_wkv)

    y_prompt, k_prompt, v_prompt, shift_prompt, wkv_prompt = trunk(x_prompt, None, None)
    y_sample, k_sample, v_sample, shift_sample, wkv_sample = trunk(
        x_sample, (cache_k, cache_v), (state_shift, state_wkv))
    return (y_prompt, y_sample, k_prompt, v_prompt, k_sample, v_sample,
            shift_prompt, wkv_prompt, shift_sample, wkv_sample)
```

```python
import contextlib
import math
import numpy as np
import concourse.bass as bass
import concourse.mybir as mybir
from concourse.bass_utils import run_bass_kernel_spmd

F32 = mybir.dt.float32
BF16 = mybir.dt.bfloat16
AF = mybir.ActivationFunctionType
ALU = mybir.AluOpType
AX = mybir.AxisListType

D = 2048
DC = 16
FFD = 5632
FC = 44
NCORE = 8
SEQ = 2048
NH = 32
HD = 64
WINDOW = 128
N_BUCKETS = 32
MAX_DISTANCE = 128
RMS_EPS = 1e-6
GN_EPS = 64 * 1e-5
NEG = -1.0e30
GROUPS = [(0, 768, False), (768, 768, False), (1536, 512, True)]
NMAX = 768


class T:
    __slots__ = ("name", "w", "r", "dsem", "dtot", "bank")

    def __init__(self, name):
        self.name = name
        self.w = {}
        self.r = {}
        self.dsem = None
        self.dtot = 0
        self.bank = None


def TL(name, n):
    return [T(f"{name}{i}") for i in range(n)]


class Prog:
    COMPUTE = ("pe", "act", "dve", "pool")

    def __init__(self, nc, strict_same=True):
        self.nc = nc
        self.es = contextlib.ExitStack()
        self.eng = {"pe": nc.tensor, "act": nc.scalar, "dve": nc.vector,
                    "pool": nc.gpsimd, "sp": nc.sync}
        self.sem = {}
        self.cnt = {}
        for e in self.COMPUTE:
            self.sem[e] = self.es.enter_context(nc.semaphore("s_" + e))
            self.cnt[e] = 0
        self.seen = {e: {} for e in self.eng}
        self.strict_same = strict_same
        self.nsem = 0
        self.n_ops = 0
        self.n_waits = 0
        self.out_events = []
        self.uid = 0

    def sb(self, name, shape, dt):
        return self.es.enter_context(self.nc.sbuf_tensor("sb_" + name, list(shape), dt))

    def ps(self, name, shape, dt=F32):
        return self.es.enter_context(self.nc.psum_tensor("ps_" + name, list(shape), dt))

    def newsem(self, name):
        self.nsem += 1
        self.uid += 1
        return self.es.enter_context(self.nc.semaphore(f"{name}_{self.uid}"))

    def _wait(self, e, ev):
        sem, val = ev
        k = id(sem)
        if self.seen[e].get(k, 0) >= val:
            return
        self.seen[e][k] = val
        self.eng[e].wait_ge(sem, val)
        self.n_waits += 1

    def _deps(self, e, reads, writes):
        own = id(self.sem[e]) if e in self.sem else None
        skip_own = (e == "pe") or (not self.strict_same)
        for t in reads:
            for k, ev in t.w.items():
                if k == own and skip_own:
                    continue
                self._wait(e, ev)
        for t in writes:
            for k, ev in t.w.items():
                if k == own and skip_own:
                    continue
                self._wait(e, ev)
            for k, ev in t.r.items():
                if k == own and skip_own:
                    continue
                self._wait(e, ev)
        for t in list(reads) + list(writes):
            if t.bank is not None:
                for k, ev in t.bank.w.items():
                    if k != own:
                        self._wait(e, ev)

    def _record(self, ev, reads, writes):
        k = id(ev[0])
        for t in reads:
            t.r[k] = ev
            if t.bank is not None:
                t.bank.w = {k: ev}
        for t in writes:
            t.w = {k: ev}
            t.r = {}
            if t.bank is not None:
                t.bank.w = {k: ev}

    def op(self, e, fn, reads=(), writes=(), inc=True):
        self._deps(e, reads, writes)
        ins = fn(self.eng[e])
        self.n_ops += 1
        if inc:
            self.cnt[e] += 1
            ins.then_inc(self.sem[e], 1)
            ev = (self.sem[e], self.cnt[e])
        else:
            ev = (self.sem[e], self.cnt[e] + 1)
        self._record(ev, reads, writes)
        return ins

    def dma(self, q, out_ap, in_ap, reads=(), writes=(), semt=None, is_output=False, concurrent=False, **kw):
        if semt is None:
            semt = writes[0] if writes else reads[0]
        if semt.dsem is None:
            semt.dsem = self.newsem("d")
        if concurrent:
            k = id(semt.dsem)
            saved = [(t, t.w.pop(k)) for t in writes if k in t.w]
            self._deps(q, reads, writes)
            for t, ev in saved:
                t.w[k] = ev
        else:
            self._deps(q, reads, writes)
        semt.dtot += 16
        ins = self.eng[q].dma_start(out=out_ap, in_=in_ap, **kw)
        ins.then_inc(semt.dsem, 16)
        self.n_ops += 1
        ev = (semt.dsem, semt.dtot)
        self._record(ev, reads, writes)
        if is_output:
            self.out_events.append(ev)
        return ins

    def finish(self):
        last = {}
        for sem, val in self.out_events:
            k = id(sem)
            if k not in last or last[k][1] < val:
                last[k] = (sem, val)
        for ev in last.values():
            self._wait("sp", ev)
        for e in self.COMPUTE:
            if self.cnt[e] > 0:
                self._wait("sp", (self.sem[e], self.cnt[e]))

    def close(self):
        self.es.close()


def w_chunks(w, cw=128):
    K, M = w.shape
    return np.ascontiguousarray(w.reshape(K // 128, 128, M // cw, cw).transpose(2, 1, 0, 3))


def fcol(v):
    s = v.shape[:-1]
    a = v.reshape(*s, DC, 128)
    a = np.moveaxis(a, -1, 0)
    return np.ascontiguousarray(a)


def t5_bucket_np(rel):
    nb = N_BUCKETS // 2
    max_exact = nb // 2
    offset = np.where(rel > 0, nb, 0)
    n = np.abs(rel)
    nf = np.maximum(n, 1).astype(np.float32)
    large = max_exact + (np.log(nf / np.float32(max_exact)) / np.float32(math.log(MAX_DISTANCE / max_exact))
                         * np.float32(nb - max_exact)).astype(np.int32)
    large = np.minimum(large, nb - 1)
    return offset + np.where(n < max_exact, n, large)


def static_consts():
    c = {}
    c["ident"] = np.eye(128, dtype=np.float32)
    i = np.arange(128)
    for L in (64, 32):
        same = (i[:, None] // L) == (i[None, :] // L)
        c[f"tri{L}"] = (same & (i[:, None] <= i[None, :])).astype(np.float32)
        ii = i % L
        c[f"mstrict{L}"] = (ii[:, None] < ii[None, :]).astype(np.float32)
        c[f"mincl{L}"] = (ii[:, None] <= ii[None, :]).astype(np.float32)
        b = 1
        lv = 0
        while b < L:
            c[f"lv{L}_{lv}"] = ((ii[:, None] // (2 * b) == ii[None, :] // (2 * b)) & ((ii[None, :] // b) % 2 == 1)
                               & ((ii[:, None] // b) % 2 == 0)).astype(np.float32)
            b *= 2
            lv += 1
    c["bdones"] = ((i[:, None] // 64) == (i[None, :] // 64)).astype(np.float32)
    r = np.arange(255)
    bk = t5_bucket_np((r - 191).astype(np.int32))
    oh = np.zeros((32, 255), np.float32)
    oh[bk, r] = 1.0
    c["onehot"] = oh
    return c


def build(ngroups=3, nlayers=2, dbg=None):
    nc = bass.Bass("TRN2", target_bir_lowering=False)
    P = Prog(nc)

    def din(name, shape):
        return nc.dram_tensor(name, list(shape), F32, kind="ExternalInput").ap()

    def dout(name, shape):
        return nc.dram_tensor(name, list(shape), F32, kind="ExternalOutput").ap()

    xp = din("xp", [128, DC, SEQ])
    xs = din("xs", [128, DC, 128])
    ck = din("ck", [4, 128, 256])
    cv = din("cv", [4, 128, 256])
    sshift = din("sshift", [128, 4, DC])
    swkv = din("swkv", [4, 16, 128, 64])
    NCONST = 12 * 16 + 20 + 6 * 16 + 7 * 16
    consts_d = din("consts", [128, NCONST])
    wg_d = din("wg", [2, 2, FC, 128, DC, 128])
    wu_d = din("wu", [2, 2, FC, 128, DC, 128])
    wd_d = din("wd", [2, 2, DC, 2, 128, 22, 128])
    wqkv_d = din("wqkv", [20, 128, DC, 128])
    wkvt_d = din("wkvt", [4, 128, DC, 128])
    bkv_d = nc.dram_tensor("bkv", [1, 512], F32, kind="ExternalInput")
    sinks_d = nc.dram_tensor("sinks", [1, 32], F32, kind="ExternalInput")
    table_d = din("table", [32, 32])
    wao_d = din("wao", [16, 128, DC, 128])
    wr_d = din("wr", [16, 128, DC, 128])
    wk_d = din("wk", [16, 128, DC, 128])
    wv_d = din("wv", [16, 128, DC, 128])
    wro_d = din("wro", [16, 128, DC, 128])
    w1_d = din("w1", [128, DC, 96])
    a1_d = din("a1", [128, DC, 96])
    g1_d = din("g1", [2, 128, DC, 128])
    w2_d = din("w2", [96, D])
    a2_d = din("a2", [96, D])
    g2_d = din("g2", [128, 2, D])
    cst = {k: din("c_" + k, v.shape) for k, v in static_consts().items()}
    fscr = nc.dram_tensor("fscr", [32, 255], F32, kind="Internal")

    yT_d = dout("yT", [128, DC, SEQ + 128])
    kp_d = dout("kp", [128, 256])
    vp_d = dout("vp", [128, 256])
    ks_d = dout("ks", [128, 256])
    vs_d = dout("vs", [128, 256])
    shp_d = dout("shp", [128, DC])
    shs_d = dout("shs", [128, 4, DC])
    wkvp_d = dout("wkvp", [16, 128, 64])
    wkvs_d = dout("wkvs", [4, 16, 128, 64])
    dbg_d = dout("dbg", [128, DC, NMAX]) if dbg else None

    CO = {}
    o = 0
    CO["g"] = o; o += 12 * 16
    CO["bq"] = o; o += 20
    CO["mu"] = o; o += 6 * 16
    for nm in ("w0", "a0", "kk", "ka", "rk", "lnw", "lnb"):
        CO[nm] = o; o += 16
    assert o == NCONST

    xT = P.sb("xT", [128, DC, NMAX], F32); xT_T = TL("xT", DC)
    hT = P.sb("hT", [128, DC, NMAX + 1], BF16); hT_T = TL("hT", DC)
    BIGB = 66 * 1024
    big = P.sb("big", [128, BIGB // 2], BF16)
    big_T = TL("big", FC)
    SL = 768

    def bigv(off_b, shape, dt):
        n = int(np.prod(shape[1:]))
        esz = 4 if dt == F32 else 2
        assert off_b % 4 == 0 and off_b + n * esz <= BIGB, (off_b, shape)
        if dt == F32:
            ap = big[:, off_b // 2: off_b // 2 + n * 2].bitcast(F32)
        else:
            ap = big[:, off_b // 2: off_b // 2 + n]
        if len(shape) == 3:
            ap = ap.rearrange("p (a b) -> p a b", b=shape[2])
        elif len(shape) == 4:
            ap = ap.rearrange("p (a b c) -> p a b c", b=shape[2], c=shape[3])
        t0 = off_b // (SL * 2)
        t1 = (off_b + n * esz - 1) // (SL * 2)
        return ap, big_T[t0:t1 + 1]

    wslot = [P.sb(f"ws{i}", [128, DC, 128], BF16) for i in range(4)]
    wslot_T = TL("ws", 4)
    dslot = [P.sb(f"wds{i}", [128, 22, 128], BF16) for i in range(2)]
    dslot_T = TL("wds", 2)
    wctr = [0, 0]

    consts = P.sb("consts", [128, NCONST], F32); consts_T = T("consts")
    identf = P.sb("identf", [128, 128], F32)
    identb = P.sb("identb", [128, 128], BF16)
    onesb = P.sb("onesb", [128, 128], BF16)
    bdones = P.sb("bdones", [128, 128], BF16)
    tri = {L: P.sb(f"tri{L}", [128, 128], F32) for L in (64, 32)}
    mstrict = {L: P.sb(f"mstrict{L}", [128, 128], BF16) for L in (64, 32)}
    mincl = {L: P.sb(f"mincl{L}", [128, 128], BF16) for L in (64, 32)}
    lvm = {L: [P.sb(f"lv{L}_{i}", [128, 128], BF16) for i in range(6 if L == 64 else 5)] for L in (64, 32)}
    cT = T("cst")
    bkv = P.sb("bkv", [128, 512], F32)
    sinks = P.sb("sinks", [128, 32], F32)
    bias2 = P.sb("bias2", [128, 32, 192], BF16); bias2_T = T("bias2")
    ktc = P.sb("ktc", [128, 4, 128], BF16); ktc_T = T("ktc")
    vbc = P.sb("vbc", [128, 256], BF16); vbc_T = T("vbc")
    Smast = P.sb("Smast", [128, 16, 128], F32); Smast_T = TL("Sm", 16)
    rstd = P.sb("rstd", [128, NMAX], F32); rstd_T = T("rstd")
    sq = [P.sb(f"sq{i}", [128, NMAX], BF16) for i in range(2)]; sq_T = TL("sq", 2)
    tmpf = [P.sb(f"tmpf{i}", [128, 512], F32) for i in range(2)]; tmpf_T = TL("tmpf", 2)
    small = P.sb("small", [128, 64], F32); small_T = TL("small", 4)
    ctr = {"sq": 0, "tmpf": 0, "bank": 0, "q": 0, "ev": 0, "nb": 2}
    SSB = [6, 7]

    psb = [P.ps(f"psb{i}", [128, 512]) for i in range(8)]
    psT = [TL(f"ps{i}_", 4) for i in range(8)]
    for i in range(8):
        bx = T(f"bank{i}")
        for t_ in psT[i]:
            t_.bank = bx

    def set_pool(nb):
        ctr["nb"] = nb

    def bank():
        b = ctr["bank"] % ctr["nb"]
        ctr["bank"] += 1
        return b

    def quarter():
        nbk = 6 - ctr["nb"]
        q = ctr["q"] % (nbk * 4)
        ctr["q"] += 1
        return ctr["nb"] + q % nbk, q // nbk

    def qf(bq):
        b, q = bq
        return psb[b][:, 128 * q:128 * (q + 1)]

    def qb(bq):
        b, q = bq
        return psb[b][:, 128 * q:128 * (q + 1)].bitcast(BF16)

    def qT(bq):
        return [psT[bq[0]][bq[1]]]

    def ev_eng():
        ctr["ev"] += 1
        return "act" if ctr["ev"] % 2 else "dve"

    def copy(e, out, in_, reads, writes):
        if e == "act":
            P.op("act", lambda x: x.copy(out=out, in_=in_), reads=reads, writes=writes)
        else:
            P.op(e, lambda x: x.tensor_copy(out=out, in_=in_), reads=reads, writes=writes)

    def cc(nm, j=None):
        if j is None:
            return consts[:, CO[nm]:CO[nm] + 16]
        return consts[:, CO[nm] + j:CO[nm] + j + 1]

    def gcol(l, n, c=None):
        o0 = CO["g"] + (l * 6 + n) * 16
        if c is None:
            return consts[:, o0:o0 + 16]
        return consts[:, o0 + c:o0 + c + 1]

    def blocks(N):
        out = []
        c0 = 0
        while c0 < N:
            cn = min(512, N - c0)
            out.append((c0, cn))
            c0 += cn
        return out

    P.dma("sp", consts[:], consts_d, writes=[consts_T])
    P.dma("sp", identf[:], cst["ident"], writes=[cT])
    for L in (64, 32):
        P.dma("sp", tri[L][:], cst[f"tri{L}"], writes=[cT])
        P.dma("pool", mstrict[L][:], cst[f"mstrict{L}"], writes=[cT])
        P.dma("pool", mincl[L][:], cst[f"mincl{L}"], writes=[cT])
        for i_, m_ in enumerate(lvm[L]):
            P.dma("pool", m_[:], cst[f"lv{L}_{i_}"], writes=[cT])
    P.dma("pool", identb[:], cst["ident"], writes=[cT])
    P.dma("pool", bdones[:], cst["bdones"], writes=[cT])
    P.dma("sp", bkv[:], bkv_d.ap().partition_broadcast(128), writes=[cT])
    P.dma("sp", sinks[:], sinks_d.ap().partition_broadcast(128), writes=[cT])
    P.op("dve", lambda e: e.memset(onesb[:], 1.0), writes=[cT])
    P.op("dve", lambda e: e.memset(hT[:, :, 0:1], 0.0), writes=hT_T)
    P.op("dve", lambda e: e.memset(Smast[:], 0.0), writes=Smast_T)
    P.op("dve", lambda e: e.memset(ktc[:], 0.0), writes=[ktc_T])
    P.op("dve", lambda e: e.memset(vbc[:], 0.0), writes=[vbc_T])

    def build_bias():
        tb = tmpf[0]; oh = tmpf[1]
        P.dma("sp", tb[0:32, 0:32], table_d, writes=[tmpf_T[0]])
        P.dma("sp", oh[0:32, 0:255], cst["onehot"], writes=[tmpf_T[1]])
        bq = quarter()
        pso = psb[bq[0]][0:32, 0:255]
        P.op("pe", lambda e: e.matmul(pso, tb[0:32, 0:32], oh[0:32, 0:255], start=True, stop=True),
             reads=[tmpf_T[0], tmpf_T[1]], writes=psT[bq[0]])
        fs, fs_T = bigv(0, [128, 256], F32)
        P.op("act", lambda e: e.copy(out=fs[0:32, 0:255], in_=pso), reads=psT[bq[0]], writes=fs_T)
        fT = T("fscr")
        P.dma("sp", fscr.ap(), fs[0:32, 0:255], reads=fs_T, writes=[fT])
        stg, stg_T = bigv(1024, [128, 32, 192], F32)
        for i in range(64):
            src = bass.AP(fscr, 63 - i, [[0, 1], [255, 32], [1, 192]])
            for half in range(2):
                p = half * 64 + i
                P.dma("sp", stg[p:p + 1, :, :], src, reads=[fT], writes=stg_T, semt=stg_T[0], concurrent=True)
        P.op("act", lambda e: e.copy(out=bias2[:, 0:16, :], in_=stg[:, 0:16, :]), reads=stg_T, writes=[bias2_T])
        P.op("dve", lambda e: e.tensor_copy(out=bias2[:, 16:32, :], in_=stg[:, 16:32, :]), reads=stg_T + [bias2_T], writes=[bias2_T])

    build_bias()

    def sumsq_accumulate(src_ap_fn, src_tiles_fn, N, nch, ssb):
        for c in range(nch):
            s = ctr["sq"] % 2; ctr["sq"] += 1
            P.op("act", lambda e: e.activation(out=sq[s][:, 0:N], in_=src_ap_fn(c), func=AF.Square),
                 reads=src_tiles_fn(c), writes=[sq_T[s]])
            for bi, (c0, cn) in enumerate(blocks(N)):
                P.op("pe", lambda e: e.matmul(psb[ssb[bi]][:, 0:cn], onesb[:], sq[s][:, c0:c0 + cn],
                                              start=(c == 0), stop=(c == nch - 1)),
                     reads=[sq_T[s], cT], writes=psT[ssb[bi]], inc=True)

    def rstd_from_ss(N, ssb, eps):
        for bi, (c0, cn) in enumerate(blocks(N)):
            P.op("dve", lambda e: e.tensor_scalar(out=rstd[:, c0:c0 + cn], in0=psb[ssb[bi]][:, 0:cn],
                                                  scalar1=1.0 / D, scalar2=eps, op0=ALU.mult, op1=ALU.add),
                 reads=psT[ssb[bi]], writes=[rstd_T])
        P.op("act", lambda e: e.activation(out=rstd[:, 0:N], in_=rstd[:, 0:N], func=AF.Ln),
             reads=[rstd_T], writes=[rstd_T])
        P.op("act", lambda e: e.activation(out=rstd[:, 0:N], in_=rstd[:, 0:N], func=AF.Exp, scale=-0.5),
             reads=[rstd_T], writes=[rstd_T])

    def prenorm(l, n, N):
        ssb = SSB
        sumsq_accumulate(lambda c: xT[:, c, 0:N], lambda c: [xT_T[c]], N, DC, ssb)
        rstd_from_ss(N, ssb, RMS_EPS)
        for c in range(DC):
            P.op("dve", lambda e: e.scalar_tensor_tensor(out=hT[:, c, 1:1 + N], in0=xT[:, c, 0:N],
                                                         scalar=gcol(l, n, c), in1=rstd[:, 0:N],
                                                         op0=ALU.mult, op1=ALU.mult),
                 reads=[xT_T[c], rstd_T, consts_T], writes=[hT_T[c]])

    def postnorm_add(l, n, N, ssb, weight):
        rstd_from_ss(N, ssb, RMS_EPS)
        for c in range(DC):
            for (c0, cn) in blocks(N):
                s = ctr["tmpf"] % 2; ctr["tmpf"] += 1
                P.op("dve", lambda e: e.scalar_tensor_tensor(out=tmpf[s][:, 0:cn], in0=hT[:, c, 1 + c0:1 + c0 + cn],
                                                             scalar=gcol(l, n, c), in1=rstd[:, c0:c0 + cn],
                                                             op0=ALU.mult, op1=ALU.mult),
                     reads=[hT_T[c], rstd_T, consts_T], writes=[tmpf_T[s]])
                P.op("dve", lambda e: e.scalar_tensor_tensor(out=xT[:, c, c0:c0 + cn], in0=tmpf[s][:, 0:cn],
                                                             scalar=float(weight), in1=xT[:, c, c0:c0 + cn],
                                                             op0=ALU.mult, op1=ALU.add),
                     reads=[tmpf_T[s]], writes=[xT_T[c]])

    def load_w(dram_ap):
        s = wctr[0] % 4; wctr[0] += 1
        P.dma("pool", wslot[s][:], dram_ap, writes=[wslot_T[s]])
        return wslot[s], wslot_T[s]

    def out_evac_ss(c, N, pbanks, ssb, first, last, bias=None):
        s = ctr["sq"] % 2; ctr["sq"] += 1
        for bi, (c0, cn) in enumerate(blocks(N)):
            b = pbanks[bi]
            P.op("act", lambda e: e.copy(out=hT[:, c, 1 + c0:1 + c0 + cn], in_=psb[b][:, 0:cn]),
                 reads=psT[b], writes=[hT_T[c]])
            P.op("act", lambda e: e.activation(out=sq[s][:, c0:c0 + cn], in_=psb[b][:, 0:cn], func=AF.Square),
                 reads=psT[b], writes=[sq_T[s]])
        def pe_part():
            for bi, (c0, cn) in enumerate(blocks(N)):
                P.op("pe", lambda e: e.matmul(psb[ssb[bi]][:, 0:cn], onesb[:], sq[s][:, c0:c0 + cn],
                                              start=first, stop=last),
                     reads=[sq_T[s], cT], writes=psT[ssb[bi]], inc=True)
        return pe_part

    def ffn(l, s, N):
        n_in, n_out = (0, 1) if s == 0 else (4, 5)
        set_pool(6)
        prenorm(l, n_in, N)
        actT = big[:, 0:FC * SL].rearrange("p (f t) -> p f t", t=SL)
        blks = blocks(N)
        for f in range(FC):
            wgs, wgT = load_w(wg_d[l, s, f])
            wus, wuT = load_w(wu_d[l, s, f])
            for (c0, cn) in blks:
                bg = bank(); bu = bank()
                for kt in range(DC):
                    P.op("pe", lambda e: e.matmul(psb[bg][:, 0:cn], wgs[:, kt, :], hT[:, kt, 1 + c0:1 + c0 + cn],
                                                  start=(kt == 0), stop=(kt == DC - 1)),
                         reads=[wgT, hT_T[kt]], writes=psT[bg], inc=(kt == DC - 1))
                for kt in range(DC):
                    P.op("pe", lambda e: e.matmul(psb[bu][:, 0:cn], wus[:, kt, :], hT[:, kt, 1 + c0:1 + c0 + cn],
                                                  start=(kt == 0), stop=(kt == DC - 1)),
                         reads=[wuT, hT_T[kt]], writes=psT[bu], inc=(kt == DC - 1))
                ts = ctr["tmpf"] % 2; ctr["tmpf"] += 1
                P.op("act", lambda e: e.activation(out=tmpf[ts][:, 0:cn], in_=psb[bg][:, 0:cn], func=AF.Silu),
                     reads=psT[bg], writes=[tmpf_T[ts]])
                P.op("dve", lambda e: e.tensor_tensor(out=actT[:, f, c0:c0 + cn], in0=tmpf[ts][:, 0:cn],
                                                      in1=psb[bu][:, 0:cn], op=ALU.mult),
                     reads=[tmpf_T[ts]] + psT[bu], writes=[big_T[f]])
        ssb = SSB
        pend = None
        for d in range(DC):
            pb = [bank() for _ in blks]
            for half in range(2):
                sl = wctr[1] % 2; wctr[1] += 1
                P.dma("pool", dslot[sl][:], wd_d[l, s, d, half], writes=[dslot_T[sl]])
                for bi, (c0, cn) in enumerate(blks):
                    for k in range(22):
                        f = half * 22 + k
                        P.op("pe", lambda e: e.matmul(psb[pb[bi]][:, 0:cn], dslot[sl][:, k, :], actT[:, f, c0:c0 + cn],
                                                      start=(f == 0), stop=(f == FC - 1)),
                             reads=[dslot_T[sl], big_T[f]], writes=psT[pb[bi]], inc=(f == FC - 1))
            if pend is not None:
                pend()
            pend = out_evac_ss(d, N, pb, ssb, d == 0, d == DC - 1)
        pend()
        postnorm_add(l, n_out, N, ssb, 0.5)

    def attention(gi, N, npr, has_sample):
        l = 0
        ntile = N // 128
        nptile = npr // 128
        set_pool(2)
        prenorm(l, 2, N)
        off = 0
        qTb, qT_T = bigv(off, [128, DC, NMAX], BF16); off += DC * NMAX * 2
        KT, KT_T = bigv(off, [128, 4, 128 + NMAX], BF16); off += 4 * (128 + NMAX) * 2
        Vb, Vb_T = bigv(off, [128, 7, 256], BF16); off += 7 * 256 * 2
        sbufs = []
        for i in range(4):
            a, t = bigv(off, [128, 256], F32); off += 1024
            sbufs.append((a, t))
        pbufs = []
        for i in range(3):
            a, t = bigv(off, [128, 256], BF16); off += 512
            pbufs.append((a, t))
        ptbufs = []
        for i in range(3):
            a, t = bigv(off, [128, 2, 128], BF16); off += 512
            ptbufs.append((a, t))
        stage, stage_T = bigv(off, [128, 512], F32); off += 2048
        if has_sample:
            KTs, KTs_T = bigv(off, [128, 4, 4, 256], BF16); off += 4 * 4 * 256 * 2
            Vc, Vc_T = bigv(off, [128, 4, 256], BF16); off += 4 * 256 * 2
            ssb_s = []
            for i in range(4):
                a, t = bigv(off, [128, 256], F32); off += 1024
                ssb_s.append((a, t))
            ckf, ckf_T = bigv(off, [128, 256], F32); off += 1024
        assert off <= BIGB, off

        P.op("pool", lambda e: e.tensor_copy(out=KT[:, :, 0:128], in_=ktc[:]), reads=[ktc_T], writes=KT_T)
        P.op("pool", lambda e: e.tensor_copy(out=Vb[:, 0, :], in_=vbc[:]), reads=[vbc_T], writes=Vb_T)

        blks = blocks(N)
        for j in range(20):
            ws, wT = load_w(wqkv_d[j])
            for (c0, cn) in blks:
                b = bank()
                for kt in range(DC):
                    P.op("pe", lambda e: e.matmul(psb[b][:, 0:cn], ws[:, kt, :], hT[:, kt, 1 + c0:1 + c0 + cn],
                                                  start=(kt == 0), stop=(kt == DC - 1)),
                         reads=[wT, hT_T[kt]], writes=psT[b], inc=(kt == DC - 1))
                bcol = consts[:, CO["bq"] + j:CO["bq"] + j + 1]
                if j < 16:
                    P.op("act", lambda e: e.activation(out=qTb[:, j, c0:c0 + cn], in_=psb[b][:, 0:cn],
                                                       func=AF.Identity, bias=bcol, scale=1.0),
                         reads=psT[b] + [consts_T], writes=qT_T)
                else:
                    P.op("act", lambda e: e.activation(out=KT[:, j - 16, 128 + c0:128 + c0 + cn], in_=psb[b][:, 0:cn],
                                                       func=AF.Identity, bias=bcol, scale=1.0),
                         reads=psT[b] + [consts_T], writes=KT_T)
        for cchunk in range(4):
            is_k = cchunk < 2
            ws, wT = load_w(wkvt_d[cchunk])
            for t in range(ntile):
                out_tile = (gi == 2) and (t >= nptile - 1)
                if is_k and not out_tile:
                    continue
                bq = quarter()
                for kt in range(DC):
                    P.op("pe", lambda e: e.matmul(qf(bq), hT[:, kt, 1 + 128 * t:1 + 128 * (t + 1)], ws[:, kt, :],
                                                  start=(kt == 0), stop=(kt == DC - 1)),
                         reads=[wT, hT_T[kt]], writes=qT(bq), inc=(kt == DC - 1))
                bsl = bkv[:, 128 * cchunk:128 * (cchunk + 1)]
                if not is_k:
                    P.op("dve", lambda e: e.tensor_tensor(out=Vb[:, 1 + t, 128 * (cchunk - 2):128 * (cchunk - 1)],
                                                          in0=qf(bq), in1=bsl, op=ALU.add),
                         reads=qT(bq) + [cT], writes=Vb_T)
                if out_tile:
                    P.op("dve", lambda e: e.tensor_tensor(out=stage[:, 128 * cchunk:128 * (cchunk + 1)],
                                                          in0=qf(bq), in1=bsl, op=ALU.add),
                         reads=qT(bq) + [cT], writes=stage_T)
                    is_s = (t == nptile)
                    dst = (ks_d if is_s else kp_d) if is_k else (vs_d if is_s else vp_d)
                    co = 128 * (cchunk % 2)
                    P.dma("sp", dst[:, co:co + 128], stage[:, 128 * cchunk:128 * (cchunk + 1)],
                          reads=stage_T, is_output=True)

        def preset(buf, tiles):
            P.op("pool", lambda e: e.memset(buf[:], NEG), writes=tiles)

        for (a, t_) in sbufs:
            preset(a, t_)

        if has_sample:
            for s in range(4):
                preset(ssb_s[s][0], ssb_s[s][1])
                P.dma("pool", Vc[:, s, :], cv[s], writes=Vc_T, semt=Vc_T[0])
                P.dma("sp", ckf[:], ck[s], writes=ckf_T)
                for kvh in range(4):
                    di = ctr["tmpf"] % 2; ctr["tmpf"] += 1
                    dsrc, dsrc_T = tmpf[di], tmpf_T[di]
                    for dup in range(2):
                        P.op("dve", lambda e: e.tensor_copy(out=dsrc[:, 64 * dup:64 * (dup + 1)],
                                                            in_=ckf[:, 64 * kvh:64 * (kvh + 1)]),
                             reads=ckf_T, writes=[dsrc_T])
                    bq = quarter()
                    P.op("pe", lambda e: e.transpose(qf(bq), dsrc[:, 0:128], identf[:]),
                         reads=[dsrc_T, cT], writes=qT(bq))
                    copy("act", KTs[:, s, kvh, 0:128], qf(bq), qT(bq), KTs_T)
                P.op("pool", lambda e: e.tensor_copy(out=KTs[:, s, :, 128:256], in_=KT[:, :, 128 + npr:128 + npr + 128]),
                     reads=KT_T, writes=KTs_T)

        jobs = []

        def stage_a(jb):
            i = jb["i"]; h = jb["h"]; nq = jb["nq"]
            sb, sb_T = jb["sb"]
            b = bank()
            P.op("pe", lambda e: e.matmul(psb[b][0:nq, 0:256], jb["q"], jb["k"], start=True, stop=True),
                 reads=qT_T + jb["k_T"], writes=psT[b][0:2])
            for (r0, r1, oc0, oc1, bc0) in jb["bops"]:
                P.op("dve", lambda e: e.scalar_tensor_tensor(out=sb[r0:r1, oc0:oc1], in0=psb[b][r0:r1, oc0:oc1], scalar=0.125,
                                                             in1=bias2[r0:r1, h, bc0:bc0 + (oc1 - oc0)],
                                                             op0=ALU.mult, op1=ALU.add),
                     reads=psT[b][0:2] + [bias2_T], writes=sb_T)
            g = i % 4
            sm = small[:, 16 * g:16 * g + 16]; sT = [small_T[g]]
            P.op("dve", lambda e: e.reduce_max(out=sm[0:nq, 0:1], in_=sb[0:nq, :], axis=AX.X), reads=sb_T, writes=sT)
            P.op("dve", lambda e: e.tensor_tensor(out=sm[0:nq, 1:2], in0=sm[0:nq, 0:1], in1=sinks[0:nq, h:h + 1], op=ALU.max),
                 reads=sT + [cT], writes=sT)
            P.op("dve", lambda e: e.tensor_scalar(out=sm[0:nq, 2:3], in0=sm[0:nq, 1:2], scalar1=-1.0, scalar2=None, op0=ALU.mult),
                 reads=sT, writes=sT)
            P.op("dve", lambda e: e.memset(sm[0:nq, 3:4], 0.0), writes=sT)
            pb_, pb_T = pbufs[i % 3]
            P.op("act", lambda e: e.activation(out=pb_[0:nq, :], in_=sb[0:nq, :], func=AF.Exp, bias=sm[0:nq, 2:3], scale=1.0,
                                               accum_out=sm[0:nq, 3:4]),
                 reads=sb_T + sT, writes=pb_T + sT)
            P.op("act", lambda e: e.activation(out=sm[0:nq, 4:5], in_=sinks[0:nq, h:h + 1], func=AF.Exp, bias=sm[0:nq, 2:3], scale=1.0),
                 reads=sT + [cT], writes=sT)
            P.op("dve", lambda e: e.tensor_tensor(out=sm[0:nq, 5:6], in0=sm[0:nq, 3:4], in1=sm[0:nq, 4:5], op=ALU.add),
                 reads=sT, writes=sT)
            P.op("dve", lambda e: e.reciprocal(out=sm[0:nq, 6:7], in_=sm[0:nq, 5:6]), reads=sT, writes=sT)
            P.op("dve", lambda e: e.tensor_scalar(out=pb_[0:nq, :], in0=pb_[0:nq, :], scalar1=sm[0:nq, 6:7], scalar2=None, op0=ALU.mult),
                 reads=pb_T + sT, writes=pb_T)

        def stage_b(jb):
            i = jb["i"]; nq = jb["nq"]
            pb_, pb_T = pbufs[i % 3]
            pt_, pt_T = ptbufs[i % 3]
            for kt in range(2):
                bq = quarter()
                P.op("pe", lambda e: e.transpose(qb(bq)[:, 0:nq], pb_[0:nq, 128 * kt:128 * (kt + 1)], identb[0:nq, 0:nq]),
                     reads=pb_T + [cT], writes=qT(bq))
                copy(ev_eng(), pt_[:, kt, 0:nq], qb(bq)[:, 0:nq], qT(bq), pt_T)

        def stage_c(jb):
            i = jb["i"]; nq = jb["nq"]
            pt_, pt_T = ptbufs[i % 3]
            if jb["hh"] == 0:
                jb["pair"]["bq"] = quarter()
            bq = jb["pair"]["bq"]
            r0 = 64 * jb["hh"]
            for kt in range(2):
                P.op("pe", lambda e: e.matmul(qf(bq)[r0:r0 + 64, 0:nq], jb["v"][kt], pt_[:, kt, 0:nq],
                                              start=(kt == 0), stop=(kt == 1)),
                     reads=pt_T + jb["v_T"], writes=qT(bq), inc=(kt == 1))
            if jb["hh"] == 1:
                copy("act", jb["o_dst"], qf(bq)[:, 0:nq], qT(bq), qT_T)

        def add_pair(j, nq, qcols, kfn, k_T, v, v_T, sbpair, bops):
            pair = {}
            for hh in range(2):
                i = len(jobs)
                jobs.append(dict(i=i, h=2 * j + hh, hh=hh, nq=nq, pair=pair,
                                 q=qTb[64 * hh:64 * (hh + 1), j, qcols[0]:qcols[0] + nq], k=kfn(hh), k_T=k_T,
                                 v=v, v_T=v_T, sb=sbpair[i % 2], bops=bops,
                                 o_dst=qTb[:, j, qcols[0]:qcols[0] + nq]))

        for t in range(nptile):
            first_tile = (gi == 0) and t == 0
            if first_tile:
                bops = [(0, 64, 128, 192, 128), (64, 128, 128, 256, 64)]
                sbp = sbufs[0:2]
            else:
                bops = [(0, 64, 0, 192, 0), (64, 128, 64, 256, 0)]
                sbp = sbufs[2:4]
            for j in range(DC):
                kvh = (2 * j) // 8
                add_pair(j, 128, (128 * t,),
                         lambda hh, kvh=kvh, t=t: KT[64 * hh:64 * (hh + 1), kvh, 128 * t:128 * t + 256], KT_T,
                         [Vb[:, t, 64 * kvh:64 * (kvh + 1)], Vb[:, t + 1, 64 * kvh:64 * (kvh + 1)]], Vb_T, sbp, bops)
        if has_sample:
            for j in range(DC):
                kvh = (2 * j) // 8
                for s in range(4):
                    bops = [(0, 32, 0, 128, 0), (0, 32, 128 + 32 * s, 160 + 32 * s, 128)]
                    add_pair(j, 32, (npr + 32 * s,),
                             lambda hh, kvh=kvh, s=s: KTs[64 * hh:64 * (hh + 1), s, kvh, :], KTs_T,
                             [Vc[:, s, 64 * kvh:64 * (kvh + 1)], Vb[:, 1 + nptile, 64 * kvh:64 * (kvh + 1)]],
                             Vc_T + Vb_T, [ssb_s[s], ssb_s[s]], bops)
        nj = len(jobs)
        for step in range(nj + 2):
            if step < nj:
                stage_a(jobs[step])
            if 0 <= step - 1 < nj:
                stage_b(jobs[step - 1])
            if 0 <= step - 2 < nj:
                stage_c(jobs[step - 2])
        if gi < 2:
            P.op("pool", lambda e: e.tensor_copy(out=ktc[:], in_=KT[:, :, npr:npr + 128]), reads=KT_T, writes=[ktc_T])
            P.op("pool", lambda e: e.tensor_copy(out=vbc[:], in_=Vb[:, nptile, :]), reads=Vb_T, writes=[vbc_T])
        set_pool(6)
        ssb = SSB
        pend = None
        for dch in range(DC):
            ws, wT = load_w(wao_d[dch])
            pb = [bank() for _ in blks]
            for bi, (c0, cn) in enumerate(blks):
                for kt in range(DC):
                    P.op("pe", lambda e: e.matmul(psb[pb[bi]][:, 0:cn], ws[:, kt, :], qTb[:, kt, c0:c0 + cn],
                                                  start=(kt == 0), stop=(kt == DC - 1)),
                         reads=[wT] + qT_T, writes=psT[pb[bi]], inc=(kt == DC - 1))
            if pend is not None:
                pend()
            pend = out_evac_ss(dch, N, pb, ssb, dch == 0, dch == DC - 1)
        pend()
        postnorm_add(l, 3, N, ssb, 1.0)

    def rwkv(gi, N, npr, has_sample):
        l = 1
        set_pool(6)
        prenorm(l, 2, N)
        blks = blocks(N)
        st = {"off": 0}

        def h_last(col, dst_ap):
            di = ctr["tmpf"] % 2; ctr["tmpf"] += 1
            so, so_T = tmpf[di], [tmpf_T[di]]
            P.op("dve", lambda e: e.scalar_tensor_tensor(out=so[:, 0:DC], in0=xT[:, :, col], scalar=rstd[:, col:col + 1], in1=gcol(l, 2),
                                                         op0=ALU.mult, op1=ALU.mult),
                 reads=xT_T + [rstd_T, consts_T], writes=so_T)
            P.dma("sp", dst_ap, so[:, 0:DC], reads=so_T, is_output=True)
        def buf(shape, dt):
            esz = 4 if dt == F32 else 2
            a_, t_ = bigv(st["off"], list(shape), dt)
            st["off"] = (st["off"] + int(np.prod(shape[1:])) * esz + 3) // 4 * 4
            return a_, t_

        if gi == 2:
            h_last(npr - 1, shp_d)
            for s_ in range(4):
                h_last(npr + 32 * s_ + 31, shs_d[:, s_, :])
        U = 4
        NB = N
        ygT, yg_T = buf([128, DC, NB], BF16)
        if has_sample:
            hsh, hsh_T = buf([128, DC, 128], BF16)
        lora, lora_T = buf([128, 4, NB], BF16)
        w2s, w2s_T = buf([128, 128], BF16); a2s, a2s_T = buf([128, 128], BF16); g2s, g2s_T = buf([128, 2, 128], BF16)
        rT_, rT_T = buf([128, NB], BF16); kT_, kT_T = buf([128, NB], BF16); vT_, vT_T = buf([128, NB], BF16)
        aT_, aT_T = buf([128, NB], BF16); kkn, kkn_T = buf([128, NB], BF16); ynT, ynT_T = buf([128, NB], BF16)
        lwT, lwT_T = buf([128, NB], F32)
        (wa, wa_T), (wb, wb_T) = buf([128, DC, 128], BF16), buf([128, DC, 128], BF16)

        def aux_views(flat_bf16, region_T, shapes):
            outs = []
            o_ = 0
            for shp in shapes:
                n_ = int(np.prod(shp[1:]))
                ap_ = flat_bf16[:, o_:o_ + n_]
                if len(shp) == 3:
                    ap_ = ap_.rearrange("p (a b) -> p a b", b=shp[2])
                t_ = T("aux")
                t_.w = dict(region_T.w); t_.r = dict(region_T.r)
                outs.append((ap_, [t_]))
                o_ += n_
            return outs

        def aux_release(region_T, subs):
            for _, tl in subs:
                for t_ in tl:
                    for k_, ev_ in list(t_.w.items()) + list(t_.r.items()):
                        if k_ not in region_T.r or region_T.r[k_][1] < ev_[1]:
                            region_T.r[k_] = ev_

        d0 = dslot[0][:].rearrange("p a b -> p (a b)")
        d1 = dslot[1][:].rearrange("p a b -> p (a b)")
        r0 = rstd[:].bitcast(BF16)
        aux0 = aux_views(d0, dslot_T[0], [[128, U, 128]] * 5)
        aux1 = aux_views(d1, dslot_T[1], [[128, U, 128]] * 3 + [[128, U, 256]])
        aux2 = aux_views(r0, rstd_T, [[128, U, 128]] * 3)
        Db = [aux0[0], aux0[1]]; Eb = [aux0[2], aux0[3]]; Mfull, Mfull_T = aux0[4]
        (tokV, tokV_T), (tokK, tokK_T), (tokB, tokB_T), (bdR, bdR_T) = aux1
        (bdB, bdB_T), (bdK, bdK_T), (bdV, bdV_T) = aux2
        Qm, Qm_T = buf([128, U, 128], BF16); P1b, P1b_T = buf([128, U, 128], BF16)
        ArbT, ArbT_T = buf([128, U, 128], BF16); ArkT, ArkT_T = buf([128, U, 128], BF16)
        Xb, Xb_T = buf([128, U, 256], BF16); Vbar, Vbar_T = buf([128, U, 128], BF16)
        AbT, AbT_T = buf([128, U, 128], BF16); Ub, Ub_T = buf([128, U, 128], BF16); Sb, Sb_T = buf([128, 128], BF16)
        e1, e1_T = buf([128, 256], BF16); e2, e2_T = buf([128, 256], BF16); e3, e3_T = buf([128, 256], BF16)
        gam, gam_T = buf([128, 8], F32)
        if has_sample:
            Ss, Ss_T = buf([128, U, 128], F32)
        assert st["off"] <= BIGB, st["off"]
        zero_list = [(bdR, bdR_T), (bdB, bdB_T), (bdK, bdK_T), (bdV, bdV_T)]

        def zero_bd():
            for a_, t_ in zero_list:
                P.op("pool", lambda e: e.memset(a_, 0.0), writes=t_)

        if has_sample:
            di = ctr["tmpf"] % 2; ctr["tmpf"] += 1
            sh32, sh32_T = tmpf[di], tmpf_T[di]
            P.dma("sp", sh32[:, 0:4 * DC], sshift.rearrange("p s c -> p (s c)"), writes=[sh32_T])
            for s_ in range(4):
                P.op("dve", lambda e: e.tensor_copy(out=hsh[:, :, 32 * s_:32 * s_ + 1],
                                                    in_=sh32[:, s_ * DC:(s_ + 1) * DC].unsqueeze(2)),
                     reads=[sh32_T], writes=hsh_T)
                P.op("dve", lambda e: e.tensor_copy(out=hsh[:, :, 32 * s_ + 1:32 * s_ + 32],
                                                    in_=hT[:, :, 1 + npr + 32 * s_:1 + npr + 32 * s_ + 31]),
                     reads=hT_T, writes=hsh_T)

        def rhs_pairs(c0, cn):
            if has_sample and c0 >= npr:
                return (lambda kt: hT[:, kt, 1 + c0:1 + c0 + cn]), (lambda kt: hsh[:, kt, c0 - npr:c0 - npr + cn]), hsh_T
            return (lambda kt: hT[:, kt, 1 + c0:1 + c0 + cn]), (lambda kt: hT[:, kt, c0:c0 + cn]), []

        def mixed_linear(wdram_ap, n, M, evac):
            si = wctr[0] % 4; wctr[0] += 1
            ws, wT = wslot[si], wslot_T[si]
            P.dma("pool", ws[:, :, 0:M], wdram_ap, writes=[wT])
            mu = consts[:, CO["mu"] + n * 16:CO["mu"] + (n + 1) * 16].unsqueeze(2).broadcast_to([128, DC, M])
            P.op("dve", lambda e: e.tensor_tensor(out=wb[:, :, 0:M], in0=ws[:, :, 0:M], in1=mu, op=ALU.mult),
                 reads=[wT, consts_T], writes=wb_T)
            P.op("dve", lambda e: e.tensor_tensor(out=wa[:, :, 0:M], in0=ws[:, :, 0:M], in1=wb[:, :, 0:M], op=ALU.subtract),
                 reads=[wT] + wb_T, writes=wa_T)
            for (c0, cn) in blks:
                cur, shf, extra = rhs_pairs(c0, cn)
                b = bank()
                for kt in range(DC):
                    P.op("pe", lambda e: e.matmul(psb[b][0:M, 0:cn], wa[:, kt, 0:M], cur(kt), start=(kt == 0), stop=False),
                         reads=wa_T + [hT_T[kt]], writes=psT[b], inc=False)
                    P.op("pe", lambda e: e.matmul(psb[b][0:M, 0:cn], wb[:, kt, 0:M], shf(kt), start=False, stop=(kt == DC - 1)),
                         reads=wb_T + [hT_T[kt]] + extra, writes=psT[b], inc=(kt == DC - 1))
                evac(b, c0, cn)

        mixed_linear(w1_d, 1, 96, lambda b, c0, cn: P.op(
            "act", lambda e: e.activation(out=lora[0:96, 0, c0:c0 + cn], in_=psb[b][0:96, 0:cn], func=AF.Tanh),
            reads=psT[b], writes=lora_T))
        mixed_linear(a1_d, 4, 96, lambda b, c0, cn: P.op(
            "act", lambda e: e.copy(out=lora[0:96, 1, c0:c0 + cn], in_=psb[b][0:96, 0:cn]),
            reads=psT[b], writes=lora_T))
        for gc in range(2):
            mixed_linear(g1_d[gc], 5, 128, lambda b, c0, cn: P.op(
                "act", lambda e: e.activation(out=lora[:, 2 + gc, c0:c0 + cn], in_=psb[b][:, 0:cn], func=AF.Sigmoid),
                reads=psT[b], writes=lora_T))

        bcount = [0]

        def v3(ap_, TP, w0_, w1_):
            return ap_[0:TP, :, w0_:w1_]

        def batch(j, cb, L, nu, states):
            TP = 2 * L
            ncol = nu * L
            ntl = ncol // 128
            bcl = bank()
            for tl in range(ntl):
                c_ = cb + 128 * tl
                b_ = bank()
                P.op("pe", lambda e: e.transpose(psb[b_][:, 0:128], lwT[:, c_:c_ + 128], identf[:]), reads=lwT_T + [cT], writes=psT[b_][0:1])
                di = ctr["tmpf"] % 2; ctr["tmpf"] += 1
                P.op("act", lambda e: e.copy(out=tmpf[di][:, 0:128], in_=psb[b_][:, 0:128]), reads=psT[b_][0:1], writes=[tmpf_T[di]])
                P.op("pe", lambda e: e.matmul(psb[bcl][:, 128 * tl:128 * (tl + 1)], tmpf[di][:, 0:128], tri[L][:], start=True, stop=True),
                     reads=[tmpf_T[di], cT], writes=psT[bcl])
            cl = psb[bcl][:, 0:ncol]
            P.op("act", lambda e: e.activation(out=e1[:, 0:ncol], in_=cl, func=AF.Exp), reads=psT[bcl], writes=e1_T)
            P.op("act", lambda e: e.activation(out=e2[:, 0:ncol], in_=cl, func=AF.Exp, scale=-1.0), reads=psT[bcl], writes=e2_T)
            P.op("act", lambda e: e.activation(out=gam[:, 0:nu], in_=psb[bcl][:, 0:ncol].rearrange("p (u l) -> p u l", l=L)[:, :, L - 1],
                                               func=AF.Exp), reads=psT[bcl], writes=gam_T)
            di = ctr["tmpf"] % 2; ctr["tmpf"] += 1
            P.op("dve", lambda e: e.tensor_tensor(out=tmpf[di][:, 0:ncol], in0=cl, in1=lwT[:, cb:cb + ncol], op=ALU.subtract),
                 reads=psT[bcl] + lwT_T, writes=[tmpf_T[di]])
            P.op("act", lambda e: e.activation(out=e3[:, 0:ncol], in_=tmpf[di][:, 0:ncol], func=AF.Exp), reads=[tmpf_T[di]], writes=e3_T)
            def src(ap_, ps_):
                return ap_[ps_, cb:cb + ncol].rearrange("p (u l) -> p u l", l=L)

            def esrc(ap_, ps_):
                return ap_[ps_, 0:ncol].rearrange("p (u l) -> p u l", l=L)
            for hh in range(2):
                ps_ = slice(64 * hh, 64 * hh + 64)
                cs0, cs1 = L * hh, L * hh + L
                eng = "dve" if hh == 0 else "pool"
                P.op("dve", lambda e: e.scalar_tensor_tensor(out=bdR[ps_, 0:nu, cs0:cs1], in0=src(kkn, ps_), scalar=-1.0, in1=esrc(e3, ps_),
                                                           op0=ALU.mult, op1=ALU.mult),
                     reads=kkn_T + e3_T, writes=bdR_T)
                P.op(eng, lambda e: e.tensor_tensor(out=bdR[ps_, 0:nu, TP + cs0:TP + cs1], in0=src(rT_, ps_), in1=esrc(e1, ps_), op=ALU.mult),
                     reads=rT_T + e1_T, writes=bdR_T)
                P.op(eng, lambda e: e.tensor_tensor(out=bdK[ps_, 0:nu, cs0:cs1], in0=src(kT_, ps_), in1=esrc(e2, ps_), op=ALU.mult),
                     reads=kT_T + e2_T, writes=bdK_T)
                P.op(eng, lambda e: e.tensor_tensor(out=bdB[ps_, 0:nu, cs0:cs1], in0=src(kkn, ps_), in1=src(aT_, ps_), op=ALU.mult),
                     reads=kkn_T + aT_T, writes=bdB_T)
                P.op(eng, lambda e: e.tensor_tensor(out=bdB[ps_, 0:nu, cs0:cs1], in0=bdB[ps_, 0:nu, cs0:cs1], in1=esrc(e2, ps_), op=ALU.mult),
                     reads=bdB_T + e2_T, writes=bdB_T)
                P.op("act", lambda e: e.copy(out=bdV[ps_, 0:nu, cs0:cs1], in_=src(vT_, ps_)), reads=vT_T, writes=bdV_T)
            def tr_all(dst, dst_T, srcfn, src_T, rows_in, cols_in, eng):
                b_ = bank()
                pv = psb[b_][:, :].bitcast(BF16)
                for u in range(nu):
                    P.op("pe", lambda e: e.transpose(pv[0:cols_in, rows_in * u:rows_in * (u + 1)], srcfn(u), identb[0:rows_in, 0:rows_in]),
                         reads=src_T + [cT], writes=psT[b_], inc=(u == nu - 1))
                copy(eng, dst, pv[0:cols_in, 0:rows_in * nu].rearrange("p (u r) -> p u r", r=rows_in), psT[b_], dst_T)
            tr_all(tokV[0:TP, 0:nu, :], tokV_T, lambda u: bdV[:, u, 0:TP], bdV_T, 128, TP, "act")
            tr_all(tokK[0:TP, 0:nu, :], tokK_T, lambda u: bdK[:, u, 0:TP], bdK_T, 128, TP, "dve")
            tr_all(tokB[0:TP, 0:nu, :], tokB_T, lambda u: bdB[:, u, 0:TP], bdB_T, 128, TP, "act")
            tr_all(Xb[0:TP, 0:nu, 0:128], Xb_T, lambda u: bdR[:, u, 0:TP], bdR_T, 128, TP, "dve")
            upb = 512 // (2 * TP)
            for lhs, lhs_T, outs in ((bdB, bdB_T, ((Mfull, Mfull_T, None), (ArbT, ArbT_T, mincl[L]))),
                                     (bdK, bdK_T, ((P1b, P1b_T, mstrict[L]), (ArkT, ArkT_T, mincl[L])))):
                for u0 in range(0, nu, upb):
                    b_ = bank()
                    n_ = min(upb, nu - u0)
                    for u in range(u0, u0 + n_):
                        o_ = (u - u0) * 2 * TP
                        P.op("pe", lambda e: e.matmul(psb[b_][0:TP, o_:o_ + 2 * TP], lhs[:, u, 0:TP], bdR[:, u, 0:2 * TP], start=True, stop=True),
                             reads=lhs_T + bdR_T, writes=psT[b_], inc=(u == u0 + n_ - 1))
                    pv = psb[b_][0:TP, 0:n_ * 2 * TP].rearrange("p (u c) -> p u c", c=2 * TP)
                    for part, (dst, dst_T, msk) in enumerate(outs):
                        sv = pv[:, :, part * TP:(part + 1) * TP]
                        dv = dst[0:TP, u0:u0 + n_, 0:TP]
                        if msk is None:
                            copy("act", dv, sv, psT[b_], dst_T)
                        else:
                            P.op("dve", lambda e: e.tensor_tensor(out=dv, in0=sv, in1=msk[0:TP, 0:TP].unsqueeze(1).broadcast_to([TP, n_, TP]), op=ALU.mult),
                                 reads=psT[b_] + [cT], writes=dst_T)
            b_ = bank()
            for u in range(nu):
                P.op("pe", lambda e: e.matmul(psb[b_][0:TP, 128 * u:128 * (u + 1)], P1b[0:TP, u, 0:TP], tokV[0:TP, u, :], start=True, stop=True),
                     reads=P1b_T + tokV_T, writes=psT[b_], inc=(u == nu - 1))
            copy("act", Xb[0:TP, 0:nu, 128:256], psb[b_][0:TP, 0:128 * nu].rearrange("p (u c) -> p u c", c=128), psT[b_], Xb_T)
            idb = identb[0:TP, 0:TP].unsqueeze(1).broadcast_to([TP, nu, TP])
            cur = 0
            P.op("act", lambda e: e.copy(out=Db[0][0][0:TP, 0:nu, 0:TP], in_=idb), reads=[cT], writes=Db[0][1])
            P.op("pool", lambda e: e.tensor_copy(out=Eb[0][0][0:TP, 0:nu, 0:TP], in_=idb), reads=[cT], writes=Eb[0][1])
            for lv in range(len(lvm[L])):
                (Dc, Dc_T), (Ec, Ec_T) = Db[cur], Eb[cur]
                (Dn, Dn_T), (En, En_T) = Db[1 - cur], Eb[1 - cur]
                P.op("pool", lambda e: e.tensor_tensor(out=Qm[0:TP, 0:nu, 0:TP], in0=Mfull[0:TP, 0:nu, 0:TP],
                                                       in1=lvm[L][lv][0:TP, 0:TP].unsqueeze(1).broadcast_to([TP, nu, TP]), op=ALU.mult),
                     reads=Mfull_T + [cT], writes=Qm_T)
                b1_ = bank()
                for u in range(nu):
                    P.op("pe", lambda e: e.matmul(psb[b1_][0:TP, 128 * u:128 * u + TP], Qm[0:TP, u, 0:TP], Dc[0:TP, u, 0:TP], start=True, stop=True),
                         reads=Qm_T + Dc_T, writes=psT[b1_], inc=(u == nu - 1))
                copy("act", P1b[0:TP, 0:nu, 0:TP], psb[b1_][0:TP, 0:128 * nu].rearrange("p (u c) -> p u c", c=128)[:, :, 0:TP], psT[b1_], P1b_T)
                b2_ = bank()
                for u in range(nu):
                    P.op("pe", lambda e: e.matmul(psb[b2_][0:TP, 128 * u:128 * u + TP], Ec[0:TP, u, 0:TP], P1b[0:TP, u, 0:TP], start=True, stop=True),
                         reads=Ec_T + P1b_T, writes=psT[b2_], inc=(u == nu - 1))
                P.op("dve", lambda e: e.tensor_tensor(out=Dn[0:TP, 0:nu, 0:TP], in0=Dc[0:TP, 0:nu, 0:TP],
                                                      in1=psb[b2_][0:TP, 0:128 * nu].rearrange("p (u c) -> p u c", c=128)[:, :, 0:TP], op=ALU.add),
                     reads=Dc_T + psT[b2_], writes=Dn_T)
                tr_all(En[0:TP, 0:nu, 0:TP], En_T, lambda u: Dn[0:TP, u, 0:TP], Dn_T, TP, TP, "act")
                cur = 1 - cur
            (Dc, Dc_T), (Ec, Ec_T) = Db[cur], Eb[cur]
            for u0 in range(0, nu, 2):
                b_ = bank()
                n_ = min(2, nu - u0)
                for u in range(u0, u0 + n_):
                    P.op("pe", lambda e: e.matmul(psb[b_][0:TP, 256 * (u - u0):256 * (u - u0 + 1)], Ec[0:TP, u, 0:TP], Xb[0:TP, u, :], start=True, stop=True),
                         reads=Ec_T + Xb_T, writes=psT[b_], inc=(u == u0 + n_ - 1))
                pv = psb[b_][0:TP, 0:256 * n_].rearrange("p (u c) -> p u c", c=256)
                copy("act", Vbar[0:TP, u0:u0 + n_, :], pv[:, :, 128:256], psT[b_], Vbar_T)
                copy("dve", Xb[0:TP, u0:u0 + n_, 0:128], pv[:, :, 0:128], psT[b_], Xb_T)
            tr_all(AbT[:, 0:nu, 0:TP], AbT_T, lambda u: Xb[0:TP, u, 0:128], Xb_T, TP, 128, "act")
            ynbd = Xb[:, :, 0:128]
            P.op("pool", lambda e: e.memset(ynbd[0:TP, 0:nu, :], 0.0), writes=Xb_T)
            by = SSB[bcount[0] % 2]
            for u in range(nu):
                Sm_ap, Sm_T = states[u]
                P.op("act", lambda e: e.copy(out=Sb[:, :], in_=Sm_ap), reads=Sm_T, writes=Sb_T)
                bu = bank()
                P.op("pe", lambda e: e.matmul(psb[bu][0:TP, 0:128], AbT[:, u, 0:TP], Sb[:, :], start=True, stop=True),
                     reads=AbT_T + Sb_T, writes=psT[bu])
                P.op("dve", lambda e: e.tensor_tensor(out=Ub[0:TP, u, :], in0=psb[bu][0:TP, 0:128], in1=Vbar[0:TP, u, :], op=ALU.add),
                     reads=psT[bu] + Vbar_T, writes=Ub_T)
                yo = psb[by][0:TP, 128 * u:128 * (u + 1)]
                P.op("pe", lambda e: e.matmul(yo, bdR[:, u, TP:2 * TP], Sb[:, :], start=True, stop=False),
                     reads=bdR_T + Sb_T, writes=psT[by], inc=False)
                P.op("pe", lambda e: e.matmul(yo, ArkT[0:TP, u, 0:TP], tokV[0:TP, u, :], start=False, stop=False),
                     reads=ArkT_T + tokV_T, writes=psT[by], inc=False)
                P.op("pe", lambda e: e.matmul(yo, ArbT[0:TP, u, 0:TP], Ub[0:TP, u, :], start=False, stop=True),
                     reads=ArbT_T + Ub_T, writes=psT[by])
                bs = bank()
                P.op("pe", lambda e: e.matmul(psb[bs][:, 0:128], tokK[0:TP, u, :], tokV[0:TP, u, :], start=True, stop=False),
                     reads=tokK_T + tokV_T, writes=psT[bs], inc=False)
                P.op("pe", lambda e: e.matmul(psb[bs][:, 0:128], tokB[0:TP, u, :], Ub[0:TP, u, :], start=False, stop=True),
                     reads=tokB_T + Ub_T, writes=psT[bs])
                di = ctr["tmpf"] % 2; ctr["tmpf"] += 1
                P.op("act", lambda e: e.activation(out=tmpf[di][:, 0:128], in_=psb[bs][:, 0:128], func=AF.Copy, scale=gam[:, u:u + 1]),
                     reads=psT[bs] + gam_T, writes=[tmpf_T[di]])
                P.op("dve", lambda e: e.scalar_tensor_tensor(out=Sm_ap, in0=Sm_ap, scalar=gam[:, u:u + 1], in1=tmpf[di][:, 0:128],
                                                             op0=ALU.mult, op1=ALU.add),
                     reads=Sm_T + gam_T + [tmpf_T[di]], writes=Sm_T)
            g = bcount[0] % 4; bcount[0] += 1
            sm = small[:, 16 * g:16 * g + 16]; sT = [small_T[g]]
            Y3 = psb[by][0:TP, 0:128 * nu].rearrange("p (u c) -> p u c", c=128)
            P.op("dve", lambda e: e.reduce_sum(out=sm[0:TP, 0:nu], in_=Y3, axis=AX.X), reads=psT[by], writes=sT)
            di = ctr["tmpf"] % 2; ctr["tmpf"] += 1
            sqv = tmpf[di][0:TP, 0:128 * nu].rearrange("p (u c) -> p u c", c=128)
            P.op("act", lambda e: e.activation(out=sqv, in_=Y3, func=AF.Square), reads=psT[by], writes=[tmpf_T[di]])
            P.op("dve", lambda e: e.reduce_sum(out=sm[0:TP, 4:4 + nu], in_=sqv, axis=AX.X), reads=[tmpf_T[di]], writes=sT)
            P.op("dve", lambda e: e.tensor_scalar(out=sm[0:TP, 0:nu], in0=sm[0:TP, 0:nu], scalar1=1.0 / 64, scalar2=None, op0=ALU.mult),
                 reads=sT, writes=sT)
            P.op("dve", lambda e: e.tensor_tensor(out=sm[0:TP, 8:8 + nu], in0=sm[0:TP, 0:nu], in1=sm[0:TP, 0:nu], op=ALU.mult),
                 reads=sT, writes=sT)
            P.op("dve", lambda e: e.scalar_tensor_tensor(out=sm[0:TP, 4:4 + nu], in0=sm[0:TP, 4:4 + nu], scalar=1.0 / 64, in1=sm[0:TP, 8:8 + nu],
                                                         op0=ALU.mult, op1=ALU.subtract),
                 reads=sT, writes=sT)
            P.op("dve", lambda e: e.tensor_scalar(out=sm[0:TP, 4:4 + nu], in0=sm[0:TP, 4:4 + nu], scalar1=GN_EPS, scalar2=None, op0=ALU.add),
                 reads=sT, writes=sT)
            P.op("act", lambda e: e.activation(out=sm[0:TP, 4:4 + nu], in_=sm[0:TP, 4:4 + nu], func=AF.Ln), reads=sT, writes=sT)
            P.op("act", lambda e: e.activation(out=sm[0:TP, 4:4 + nu], in_=sm[0:TP, 4:4 + nu], func=AF.Exp, scale=-0.5), reads=sT, writes=sT)
            for hh in range(2):
                rs = slice(L * hh, L * hh + L)
                cs = slice(64 * hh, 64 * hh + 64)
                P.op("dve", lambda e: e.tensor_tensor(out=ynbd[rs, 0:nu, cs], in0=Y3[rs, :, cs],
                                                      in1=sm[rs, 0:nu].unsqueeze(2).broadcast_to([L, nu, 64]), op=ALU.subtract),
                     reads=psT[by] + sT, writes=Xb_T)
                P.op("dve", lambda e: e.tensor_tensor(out=ynbd[rs, 0:nu, cs], in0=ynbd[rs, 0:nu, cs],
                                                      in1=sm[rs, 4:4 + nu].unsqueeze(2).broadcast_to([L, nu, 64]), op=ALU.mult),
                     reads=Xb_T + sT, writes=Xb_T)
            b_ = bank()
            pv = psb[b_][:, :].bitcast(BF16)
            for u in range(nu):
                P.op("pe", lambda e: e.transpose(pv[:, TP * u:TP * (u + 1)], ynbd[0:TP, u, :], identb[0:TP, 0:TP]),
                     reads=Xb_T + [cT], writes=psT[b_], inc=(u == nu - 1))
            for hh in range(2):
                ps_ = slice(64 * hh, 64 * hh + 64)
                P.op("act", lambda e: e.copy(out=ynT[ps_, cb:cb + ncol].rearrange("p (u l) -> p u l", l=L),
                                             in_=pv[ps_, 0:TP * nu].rearrange("p (u c) -> p u c", c=TP)[:, :, L * hh:L * hh + L]),
                     reads=psT[b_], writes=ynT_T)

        def state_out(src_ap, src_T, dst):
            b_ = bank()
            P.op("pe", lambda e: e.transpose(psb[b_][:, 0:128], src_ap, identf[:]), reads=src_T + [cT], writes=psT[b_])
            di = ctr["tmpf"] % 2; ctr["tmpf"] += 1
            so, so_T = tmpf[di], [tmpf_T[di]]
            for hh in range(2):
                ps_ = slice(64 * hh, 64 * hh + 64)
                P.op("act", lambda e: e.copy(out=so[ps_, 0:64], in_=psb[b_][ps_, 64 * hh:64 * hh + 64]), reads=psT[b_], writes=so_T)
            P.dma("sp", dst, so[:, 0:64], reads=so_T, is_output=True)

        for j in range(DC):
            P.dma("pool", w2s[0:96, :], w2_d[:, 128 * j:128 * (j + 1)], writes=w2s_T, semt=w2s_T[0])
            P.dma("pool", a2s[0:96, :], a2_d[:, 128 * j:128 * (j + 1)], writes=a2s_T, semt=a2s_T[0])
            P.dma("pool", g2s[:, :, :], g2_d[:, :, 128 * j:128 * (j + 1)], writes=g2s_T, semt=g2s_T[0])
            for (wdram, n, dst, dst_T) in ((wr_d, 0, rT_, rT_T), (wk_d, 2, kT_, kT_T), (wv_d, 3, vT_, vT_T)):
                mixed_linear(wdram[j], n, 128, lambda b, c0, cn, dst=dst, dst_T=dst_T: copy(
                    ev_eng(), dst[:, c0:c0 + cn], psb[b][:, 0:cn], psT[b], dst_T))
            for (c0, cn) in blks:
                b = bank()
                P.op("pe", lambda e: e.matmul(psb[b][:, 0:cn], w2s[0:96, :], lora[0:96, 0, c0:c0 + cn], start=True, stop=True),
                     reads=w2s_T + lora_T, writes=psT[b])
                P.op("act", lambda e: e.activation(out=lwT[:, c0:c0 + cn], in_=psb[b][:, 0:cn], func=AF.Exp, bias=cc("w0", j), scale=1.0),
                     reads=psT[b] + [consts_T], writes=lwT_T)
                ts = ctr["tmpf"] % 2; ctr["tmpf"] += 1
                tf, tf_T = tmpf[ts], [tmpf_T[ts]]
                P.op("dve", lambda e: e.tensor_scalar(out=tf[:, 0:cn], in0=lwT[:, c0:c0 + cn], scalar1=1.0, scalar2=None, op0=ALU.add),
                     reads=lwT_T, writes=tf_T)
                P.op("dve", lambda e: e.reciprocal(out=tf[:, 0:cn], in_=tf[:, 0:cn]), reads=tf_T, writes=tf_T)
                P.op("dve", lambda e: e.scalar_tensor_tensor(out=lwT[:, c0:c0 + cn], in0=lwT[:, c0:c0 + cn], scalar=-math.exp(-0.5), in1=tf[:, 0:cn],
                                                             op0=ALU.mult, op1=ALU.mult),
                     reads=lwT_T + tf_T, writes=lwT_T)
                b = bank()
                P.op("pe", lambda e: e.matmul(psb[b][:, 0:cn], a2s[0:96, :], lora[0:96, 1, c0:c0 + cn], start=True, stop=True),
                     reads=a2s_T + lora_T, writes=psT[b])
                P.op("act", lambda e: e.activation(out=aT_[:, c0:c0 + cn], in_=psb[b][:, 0:cn], func=AF.Sigmoid, bias=cc("a0", j), scale=1.0),
                     reads=psT[b] + [consts_T], writes=aT_T)
            P.op("dve", lambda e: e.tensor_scalar(out=kkn[:, 0:N], in0=kT_[:, 0:N], scalar1=cc("kk", j), scalar2=None, op0=ALU.mult),
                 reads=kT_T + [consts_T], writes=kkn_T)
            s = ctr["sq"] % 2; ctr["sq"] += 1
            P.op("act", lambda e: e.activation(out=sq[s][:, 0:N], in_=kkn[:, 0:N], func=AF.Square), reads=kkn_T, writes=[sq_T[s]])
            for (c0, cn) in blks:
                b = bank()
                P.op("pe", lambda e: e.matmul(psb[b][:, 0:cn], bdones[:], sq[s][:, c0:c0 + cn], start=True, stop=True),
                     reads=[sq_T[s], cT], writes=psT[b])
                ts = ctr["tmpf"] % 2; ctr["tmpf"] += 1
                tf, tf_T = tmpf[ts], [tmpf_T[ts]]
                P.op("dve", lambda e: e.tensor_scalar(out=tf[:, 0:cn], in0=psb[b][:, 0:cn], scalar1=1e-30, scalar2=None, op0=ALU.add),
                     reads=psT[b], writes=tf_T)
                P.op("act", lambda e: e.activation(out=tf[:, 0:cn], in_=tf[:, 0:cn], func=AF.Ln), reads=tf_T, writes=tf_T)
                P.op("act", lambda e: e.activation(out=tf[:, 0:cn], in_=tf[:, 0:cn], func=AF.Exp, scale=-0.5), reads=tf_T, writes=tf_T)
                P.op("dve", lambda e: e.tensor_tensor(out=kkn[:, c0:c0 + cn], in0=kkn[:, c0:c0 + cn], in1=tf[:, 0:cn], op=ALU.mult),
                     reads=kkn_T + tf_T, writes=kkn_T)
                P.op("dve", lambda e: e.tensor_scalar(out=tf[:, 0:cn], in0=aT_[:, c0:c0 + cn], scalar1=-1.0, scalar2=cc("ka", j), op0=ALU.add, op1=ALU.mult),
                     reads=aT_T + [consts_T], writes=tf_T)
                P.op("dve", lambda e: e.tensor_scalar(out=tf[:, 0:cn], in0=tf[:, 0:cn], scalar1=1.0, scalar2=None, op0=ALU.add),
                     reads=tf_T, writes=tf_T)
                P.op("dve", lambda e: e.tensor_tensor(out=kT_[:, c0:c0 + cn], in0=kT_[:, c0:c0 + cn], in1=tf[:, 0:cn], op=ALU.mult),
                     reads=kT_T + tf_T, writes=kT_T)
            zero_bd()
            nb_prompt = npr // 256
            for bi_ in range(nb_prompt):
                batch(j, 256 * bi_, 64, 4, [(Smast[:, j, :], [Smast_T[j]])] * 4)
            if has_sample:
                zero_bd()
                for u in range(4):
                    di = ctr["tmpf"] % 2; ctr["tmpf"] += 1
                    si_, si_T = tmpf[di], [tmpf_T[di]]
                    P.dma("sp", si_[:, 256:320], swkv[u, j], writes=si_T)
                    P.op("dve", lambda e: e.memset(si_[:, 0:128], 0.0), writes=si_T)
                    for hh in range(2):
                        ps_ = slice(64 * hh, 64 * hh + 64)
                        P.op("dve", lambda e: e.tensor_copy(out=si_[ps_, 64 * hh:64 * hh + 64], in_=si_[ps_, 256:320]), reads=si_T, writes=si_T)
                    b_ = bank()
                    P.op("pe", lambda e: e.transpose(psb[b_][:, 0:128], si_[:, 0:128], identf[:]), reads=si_T + [cT], writes=psT[b_])
                    P.op("act", lambda e: e.copy(out=Ss[:, u, :], in_=psb[b_][:, 0:128]), reads=psT[b_], writes=Ss_T)
                batch(j, npr, 32, 4, [(Ss[:, u, :], Ss_T) for u in range(4)])
                for u in range(4):
                    state_out(Ss[:, u, :], Ss_T, wkvs_d[u, j])
            if gi == 2:
                state_out(Smast[:, j, :], [Smast_T[j]], wkvp_d[j])
            s = ctr["sq"] % 2; ctr["sq"] += 1
            P.op("dve", lambda e: e.scalar_tensor_tensor(out=sq[s][:, 0:N], in0=rT_[:, 0:N], scalar=cc("rk", j), in1=kT_[:, 0:N],
                                                         op0=ALU.mult, op1=ALU.mult),
                 reads=rT_T + kT_T + [consts_T], writes=[sq_T[s]])
            for (c0, cn) in blks:
                b = bank()
                P.op("pe", lambda e: e.matmul(psb[b][:, 0:cn], bdones[:], sq[s][:, c0:c0 + cn], start=True, stop=True),
                     reads=[sq_T[s], cT], writes=psT[b])
                ts = ctr["tmpf"] % 2; ctr["tmpf"] += 1
                tf, tf_T = tmpf[ts], [tmpf_T[ts]]
                P.op("dve", lambda e: e.tensor_tensor(out=tf[:, 0:cn], in0=psb[b][:, 0:cn], in1=vT_[:, c0:c0 + cn], op=ALU.mult),
                     reads=psT[b] + vT_T, writes=tf_T)
                ts2 = ctr["tmpf"] % 2; ctr["tmpf"] += 1
                tg, tg_T = tmpf[ts2], [tmpf_T[ts2]]
                P.op("dve", lambda e: e.tensor_scalar(out=tg[:, 0:cn], in0=ynT[:, c0:c0 + cn], scalar1=cc("lnw", j), scalar2=cc("lnb", j),
                                                      op0=ALU.mult, op1=ALU.add),
                     reads=ynT_T + [consts_T], writes=tg_T)
                P.op("dve", lambda e: e.tensor_tensor(out=tf[:, 0:cn], in0=tf[:, 0:cn], in1=tg[:, 0:cn], op=ALU.add),
                     reads=tf_T + tg_T, writes=tf_T)
                bg = bank()
                for kt in range(2):
                    P.op("pe", lambda e: e.matmul(psb[bg][:, 0:cn], g2s[:, kt, :], lora[:, 2 + kt, c0:c0 + cn],
                                                  start=(kt == 0), stop=(kt == 1)),
                         reads=g2s_T + lora_T, writes=psT[bg], inc=(kt == 1))
                P.op("dve", lambda e: e.tensor_tensor(out=ygT[:, j, c0:c0 + cn], in0=tf[:, 0:cn], in1=psb[bg][:, 0:cn], op=ALU.mult),
                     reads=tf_T + psT[bg], writes=yg_T)

        aux_release(dslot_T[0], aux0); aux_release(dslot_T[1], aux1); aux_release(rstd_T, aux2)
        return ygT, yg_T

    def rwkv_full(gi, N, npr, has_sample):
        ygT, yg_T = rwkv(gi, N, npr, has_sample)
        if dbg == "yg":
            for c in range(DC):
                P.op("act", lambda e: e.copy(out=xT[:, c, 0:N], in_=ygT[:, c, 0:N]), reads=yg_T, writes=[xT_T[c]])
            return
        P.op("pool", lambda e: e.tensor_copy(out=hT[:, :, 0:1], in_=hT[:, :, npr:npr + 1]), reads=hT_T, writes=hT_T)
        blks = blocks(N)
        set_pool(6)
        ssb = SSB
        pend = None
        for dch in range(DC):
            ws, wT = load_w(wro_d[dch])
            pb = [bank() for _ in blks]
            for bi, (c0, cn) in enumerate(blks):
                for kt in range(DC):
                    P.op("pe", lambda e: e.matmul(psb[pb[bi]][:, 0:cn], ws[:, kt, :], ygT[:, kt, c0:c0 + cn],
                                                  start=(kt == 0), stop=(kt == DC - 1)),
                         reads=[wT] + yg_T, writes=psT[pb[bi]], inc=(kt == DC - 1))
            if pend is not None:
                pend()
            pend = out_evac_ss(dch, N, pb, ssb, dch == 0, dch == DC - 1)
        pend()
        postnorm_add(1, 3, N, ssb, 1.0)

    for gi, (p0, npr, has_s) in enumerate(GROUPS[:ngroups]):
        N = npr + (128 if has_s else 0)
        for c in range(DC):
            P.dma("sp", xT[:, c, 0:npr], xp[:, c, p0:p0 + npr], writes=[xT_T[c]])
            if has_s:
                P.dma("sp", xT[:, c, npr:npr + 128], xs[:, c, :], writes=[xT_T[c]])
        for l in range(nlayers):
            ffn(l, 0, N)
            if dbg == f"ffn{l}0" and gi == 0:
                break
            if l == 0:
                attention(gi, N, npr, has_s)
            else:
                rwkv_full(gi, N, npr, has_s)
            if dbg in (f"mix{l}", "yg") and gi == 0 and (dbg != "yg" or l == 1):
                break
            ffn(l, 1, N)
        for c in range(DC):
            P.dma("sp", yT_d[:, c, p0:p0 + npr], xT[:, c, 0:npr], reads=[xT_T[c]], is_output=True)
            if has_s:
                P.dma("sp", yT_d[:, c, SEQ:SEQ + 128], xT[:, c, npr:npr + 128], reads=[xT_T[c]], is_output=True)
    P.finish()
    stats = dict(ops=P.n_ops, waits=P.n_waits, sems=P.nsem, cnt=dict(P.cnt))
    P.close()
    return nc, stats


def prep_shared(inp):
    f = lambda a: np.ascontiguousarray(np.asarray(a, dtype=np.float32))
    sh = {}
    sh["wg"] = np.stack([np.stack([w_chunks(f(inp["ffn_w_gate"][l, s])) for s in range(2)]) for l in range(2)])
    sh["wu"] = np.stack([np.stack([w_chunks(f(inp["ffn_w_up"][l, s])) for s in range(2)]) for l in range(2)])
    wd = np.stack([np.stack([w_chunks(f(inp["ffn_w_down"][l, s])) for s in range(2)]) for l in range(2)])
    sh["wd"] = np.ascontiguousarray(wd.reshape(2, 2, DC, 128, 2, 22, 128).transpose(0, 1, 2, 4, 3, 5, 6))
    wqkv = f(inp["att_w_qkv"][0])
    bqkv = f(inp["att_b_qkv"][0])
    kcols = [np.concatenate([wqkv[:, 2048 + 64 * h:2048 + 64 * (h + 1)]] * 2, axis=1) for h in range(4)]
    wext = np.concatenate([wqkv[:, :2048]] + kcols, axis=1)
    sh["wqkv"] = w_chunks(wext)
    bext = np.concatenate([bqkv[:2048]] + [np.concatenate([bqkv[2048 + 64 * h:2048 + 64 * (h + 1)]] * 2) for h in range(4)])
    sh["wkvt"] = w_chunks(wqkv[:, 2048:2560])
    sh["bkv"] = bqkv[2048:2560].reshape(1, 512).copy()
    sh["sinks"] = f(inp["att_sinks"]).reshape(1, 32).copy()
    sh["table"] = f(inp["rel_table"])
    sh["wao"] = w_chunks(f(inp["att_w_o"][0]))
    sh["wr"] = w_chunks(f(inp["rwkv_w_r"][0]))
    sh["wk"] = w_chunks(f(inp["rwkv_w_k"][0]))
    sh["wv"] = w_chunks(f(inp["rwkv_w_v"][0]))
    sh["wro"] = w_chunks(f(inp["rwkv_w_o"][0]))
    sh["w1"] = w_chunks(f(inp["rwkv_w1"][0]), 96)[0]
    sh["a1"] = w_chunks(f(inp["rwkv_a1"][0]), 96)[0]
    sh["g1"] = w_chunks(f(inp["rwkv_g1"][0]))
    sh["w2"] = f(inp["rwkv_w2"][0])
    sh["a2"] = f(inp["rwkv_a2"][0])
    sh["g2"] = np.ascontiguousarray(f(inp["rwkv_g2"][0]).reshape(2, 128, D).transpose(1, 0, 2))
    cols = [fcol(f(inp["norm_g"])).reshape(128, 12 * 16),
            bext.reshape(20, 128).T,
            fcol(f(inp["rwkv_mu"][0])).reshape(128, 6 * 16)]
    for nm in ("rwkv_w0", "rwkv_a0", "rwkv_k_k", "rwkv_k_a"):
        cols.append(fcol(f(inp[nm][0])))
    cols.append(fcol(f(inp["rwkv_r_k"][0]).reshape(D)))
    for nm in ("rwkv_ln_w", "rwkv_ln_b"):
        cols.append(fcol(f(inp[nm][0])))
    sh["consts"] = np.ascontiguousarray(np.concatenate(cols, axis=1))
    for k, v in static_consts().items():
        sh["c_" + k] = v
    return sh


def prep_core(inp, c):
    f = lambda a: np.ascontiguousarray(np.asarray(a, dtype=np.float32))
    m = {}
    m["xp"] = fcol(f(inp["x_prompt"][c]).T.copy()) if False else np.ascontiguousarray(
        f(inp["x_prompt"][c]).T.reshape(DC, 128, SEQ).transpose(1, 0, 2))
    xs = f(inp["x_sample"][4 * c:4 * c + 4]).reshape(128, D)
    m["xs"] = np.ascontiguousarray(xs.T.reshape(DC, 128, 128).transpose(1, 0, 2))
    m["ck"] = f(inp["cache_k"][0, 4 * c:4 * c + 4]).reshape(4, 128, 256)
    m["cv"] = f(inp["cache_v"][0, 4 * c:4 * c + 4]).reshape(4, 128, 256)
    ss = f(inp["state_shift"][0, 4 * c:4 * c + 4, 0])
    m["sshift"] = np.ascontiguousarray(ss.reshape(4, DC, 128).transpose(2, 0, 1))
    m["swkv"] = f(inp["state_wkv"][0, 4 * c:4 * c + 4]).reshape(4, 16, 128, 64)
    return m


_CACHE = {}


def kernel(**inputs):
    if "nc" not in _CACHE:
        _CACHE["nc"] = build()[0]
    nc = _CACHE["nc"]
    sh = prep_shared(inputs)
    in_maps = []
    for c in range(NCORE):
        m = dict(sh)
        m.update(prep_core(inputs, c))
        in_maps.append(m)
    res = run_bass_kernel_spmd(nc, in_maps, core_ids=list(range(NCORE)))
    R = res.results
    y_prompt = np.zeros((8, SEQ, D), np.float32)
    y_sample = np.zeros((32, 32, D), np.float32)
    k_prompt = np.zeros((1, 8, 128, 4, 64), np.float32)
    v_prompt = np.zeros((1, 8, 128, 4, 64), np.float32)
    k_sample = np.zeros((1, 32, 32, 4, 64), np.float32)
    v_sample = np.zeros((1, 32, 32, 4, 64), np.float32)
    shift_prompt = np.zeros((1, 8, 1, D), np.float32)
    wkv_prompt = np.zeros((1, 8, 32, 64, 64), np.float32)
    shift_sample = np.zeros((1, 32, 1, D), np.float32)
    wkv_sample = np.zeros((1, 32, 32, 64, 64), np.float32)
    for c in range(NCORE):
        r = R[c]
        yT = np.asarray(r["yT"])
        y = yT.transpose(2, 1, 0).reshape(SEQ + 128, D)
        y_prompt[c] = y[:SEQ]
        y_sample[4 * c:4 * c + 4] = y[SEQ:].reshape(4, 32, D)
        k_prompt[0, c] = np.asarray(r["kp"]).reshape(128, 4, 64)
        v_prompt[0, c] = np.asarray(r["vp"]).reshape(128, 4, 64)
        k_sample[0, 4 * c:4 * c + 4] = np.asarray(r["ks"]).reshape(4, 32, 4, 64)
        v_sample[0, 4 * c:4 * c + 4] = np.asarray(r["vs"]).reshape(4, 32, 4, 64)
        shift_prompt[0, c, 0] = np.asarray(r["shp"]).T.reshape(D)
        shift_sample[0, 4 * c:4 * c + 4, 0] = np.asarray(r["shs"]).transpose(1, 2, 0).reshape(4, D)
        wkv_prompt[0, c] = np.asarray(r["wkvp"]).reshape(32, 64, 64)
        wkv_sample[0, 4 * c:4 * c + 4] = np.asarray(r["wkvs"]).reshape(4, 32, 64, 64)
    return (y_prompt, y_sample, k_prompt, v_prompt, k_sample, v_sample,
            shift_prompt, wkv_prompt, shift_sample, wkv_sample)
```

```python
import contextlib
import math
import numpy as np
import concourse.bass as bass
import concourse.mybir as mybir
from concourse.bass_utils import run_bass_kernel_spmd

F32 = mybir.dt.float32
BF16 = mybir.dt.bfloat16
AF = mybir.ActivationFunctionType
ALU = mybir.AluOpType
AX = mybir.AxisListType

D = 2048
DC = 16
FFD = 5632
FC = 44
NCORE = 8
SEQ = 2048
NH = 32
HD = 64
WINDOW = 128
N_BUCKETS = 32
MAX_DISTANCE = 128
RMS_EPS = 1e-6
GN_EPS = 64 * 1e-5
NEG = -1.0e30
GROUPS = [(0, 768, False), (768, 768, False), (1536, 512, True)]
NMAX = 768


class T:
    __slots__ = ("name", "w", "r", "dsem", "dtot", "bank")

    def __init__(self, name):
        self.name = name
        self.w = {}
        self.r = {}
        self.dsem = None
        self.dtot = 0
        self.bank = None


def TL(name, n):
    return [T(f"{name}{i}") for i in range(n)]


class Prog:
    COMPUTE = ("pe", "act", "dve", "pool")

    def __init__(self, nc, strict_same=True):
        self.nc = nc
        self.es = contextlib.ExitStack()
        self.eng = {"pe": nc.tensor, "act": nc.scalar, "dve": nc.vector,
                    "pool": nc.gpsimd, "sp": nc.sync}
        self.sem = {}
        self.cnt = {}
        for e in self.COMPUTE:
            self.sem[e] = self.es.enter_context(nc.semaphore("s_" + e))
            self.cnt[e] = 0
        self.seen = {e: {} for e in self.eng}
        self.strict_same = strict_same
        self.nsem = 0
        self.n_ops = 0
        self.n_waits = 0
        self.out_events = []
        self.uid = 0

    def sb(self, name, shape, dt):
        return self.es.enter_context(self.nc.sbuf_tensor("sb_" + name, list(shape), dt))

    def ps(self, name, shape, dt=F32):
        return self.es.enter_context(self.nc.psum_tensor("ps_" + name, list(shape), dt))

    def newsem(self, name):
        self.nsem += 1
        self.uid += 1
        return self.es.enter_context(self.nc.semaphore(f"{name}_{self.uid}"))

    def _wait(self, e, ev):
        sem, val = ev
        k = id(sem)
        if self.seen[e].get(k, 0) >= val:
            return
        self.seen[e][k] = val
        self.eng[e].wait_ge(sem, val)
        self.n_waits += 1

    def _deps(self, e, reads, writes):
        own = id(self.sem[e]) if e in self.sem else None
        skip_own = (e == "pe") or (not self.strict_same)
        for t in reads:
            for k, ev in t.w.items():
                if k == own and skip_own:
                    continue
                self._wait(e, ev)
        for t in writes:
            for k, ev in t.w.items():
                if k == own and skip_own:
                    continue
                self._wait(e, ev)
            for k, ev in t.r.items():
                if k == own and skip_own:
                    continue
                self._wait(e, ev)
        for t in list(reads) + list(writes):
            if t.bank is not None:
                for k, ev in t.bank.w.items():
                    if k != own:
                        self._wait(e, ev)

    def _record(self, ev, reads, writes):
        k = id(ev[0])
        for t in reads:
            t.r[k] = ev
            if t.bank is not None:
                t.bank.w = {k: ev}
        for t in writes:
            t.w = {k: ev}
            t.r = {}
            if t.bank is not None:
                t.bank.w = {k: ev}

    def op(self, e, fn, reads=(), writes=(), inc=True):
        self._deps(e, reads, writes)
        ins = fn(self.eng[e])
        self.n_ops += 1
        if inc:
            self.cnt[e] += 1
            ins.then_inc(self.sem[e], 1)
            ev = (self.sem[e], self.cnt[e])
        else:
            ev = (self.sem[e], self.cnt[e] + 1)
        self._record(ev, reads, writes)
        return ins

    def dma(self, q, out_ap, in_ap, reads=(), writes=(), semt=None, is_output=False, concurrent=False, **kw):
        if semt is None:
            semt = writes[0] if writes else reads[0]
        if semt.dsem is None:
            semt.dsem = self.newsem("d")
        if concurrent:
            k = id(semt.dsem)
            saved = [(t, t.w.pop(k)) for t in writes if k in t.w]
            self._deps(q, reads, writes)
            for t, ev in saved:
                t.w[k] = ev
        else:
            self._deps(q, reads, writes)
        semt.dtot += 16
        ins = self.eng[q].dma_start(out=out_ap, in_=in_ap, **kw)
        ins.then_inc(semt.dsem, 16)
        self.n_ops += 1
        ev = (semt.dsem, semt.dtot)
        self._record(ev, reads, writes)
        if is_output:
            self.out_events.append(ev)
        return ins

    def finish(self):
        last = {}
        for sem, val in self.out_events:
            k = id(sem)
            if k not in last or last[k][1] < val:
                last[k] = (sem, val)
        for ev in last.values():
            self._wait("sp", ev)
        for e in self.COMPUTE:
            if self.cnt[e] > 0:
                self._wait("sp", (self.sem[e], self.cnt[e]))

    def close(self):
        self.es.close()


def w_chunks(w, cw=128):
    K, M = w.shape
    return np.ascontiguousarray(w.reshape(K // 128, 128, M // cw, cw).transpose(2, 1, 0, 3))


def fcol(v):
    s = v.shape[:-1]
    a = v.reshape(*s, DC, 128)
    a = np.moveaxis(a, -1, 0)
    return np.ascontiguousarray(a)


def t5_bucket_np(rel):
    nb = N_BUCKETS // 2
    max_exact = nb // 2
    offset = np.where(rel > 0, nb, 0)
    n = np.abs(rel)
    nf = np.maximum(n, 1).astype(np.float32)
    large = max_exact + (np.log(nf / np.float32(max_exact)) / np.float32(math.log(MAX_DISTANCE / max_exact))
                         * np.float32(nb - max_exact)).astype(np.int32)
    large = np.minimum(large, nb - 1)
    return offset + np.where(n < max_exact, n, large)


def static_consts():
    c = {}
    c["ident"] = np.eye(128, dtype=np.float32)
    i = np.arange(128)
    for L in (64, 32):
        same = (i[:, None] // L) == (i[None, :] // L)
        c[f"tri{L}"] = (same & (i[:, None] <= i[None, :])).astype(np.float32)
        ii = i % L
        c[f"mstrict{L}"] = (ii[:, None] < ii[None, :]).astype(np.float32)
        c[f"mincl{L}"] = (ii[:, None] <= ii[None, :]).astype(np.float32)
        b = 1
        lv = 0
        while b < L:
            c[f"lv{L}_{lv}"] = ((ii[:, None] // (2 * b) == ii[None, :] // (2 * b)) & ((ii[None, :] // b) % 2 == 1)
                               & ((ii[:, None] // b) % 2 == 0)).astype(np.float32)
            b *= 2
            lv += 1
    c["bdones"] = ((i[:, None] // 64) == (i[None, :] // 64)).astype(np.float32)
    r = np.arange(255)
    bk = t5_bucket_np((r - 191).astype(np.int32))
    oh = np.zeros((32, 255), np.float32)
    oh[bk, r] = 1.0
    c["onehot"] = oh
    return c


def build(ngroups=3, nlayers=2, dbg=None):
    nc = bass.Bass("TRN2", target_bir_lowering=False)
    P = Prog(nc)

    def din(name, shape):
        return nc.dram_tensor(name, list(shape), F32, kind="ExternalInput").ap()

    def dout(name, shape):
        return nc.dram_tensor(name, list(shape), F32, kind="ExternalOutput").ap()

    xp = din("xp", [128, DC, SEQ])
    xs = din("xs", [128, DC, 128])
    ck = din("ck", [4, 128, 256])
    cv = din("cv", [4, 128, 256])
    sshift = din("sshift", [128, 4, DC])
    swkv = din("swkv", [4, 16, 128, 64])
    NCONST = 12 * 16 + 20 + 6 * 16 + 7 * 16
    consts_d = din("consts", [128, NCONST])
    wg_d = din("wg", [2, 2, FC, 128, DC, 128])
    wu_d = din("wu", [2, 2, FC, 128, DC, 128])
    wd_d = din("wd", [2, 2, DC, 4, 128, 11, 128])
    wqkv_d = din("wqkv", [20, 128, DC, 128])
    wkvt_d = din("wkvt", [4, 128, DC, 128])
    bkv_d = nc.dram_tensor("bkv", [1, 512], F32, kind="ExternalInput")
    sinks_d = nc.dram_tensor("sinks", [1, 32], F32, kind="ExternalInput")
    table_d = din("table", [32, 32])
    wao_d = din("wao", [16, 128, DC, 128])
    wr_d = din("wr", [16, 128, DC, 128])
    wk_d = din("wk", [16, 128, DC, 128])
    wv_d = din("wv", [16, 128, DC, 128])
    wro_d = din("wro", [16, 128, DC, 128])
    w1_d = din("w1", [128, DC, 96])
    a1_d = din("a1", [128, DC, 96])
    g1_d = din("g1", [2, 128, DC, 128])
    w2_d = din("w2", [96, D])
    a2_d = din("a2", [96, D])
    g2_d = din("g2", [128, 2, D])
    cst = {k: din("c_" + k, v.shape) for k, v in static_consts().items()}
    fscr = nc.dram_tensor("fscr", [32, 255], F32, kind="Internal")

    yT_d = dout("yT", [128, DC, SEQ + 128])
    kp_d = dout("kp", [128, 256])
    vp_d = dout("vp", [128, 256])
    ks_d = dout("ks", [128, 256])
    vs_d = dout("vs", [128, 256])
    shp_d = dout("shp", [128, DC])
    shs_d = dout("shs", [128, 4, DC])
    wkvp_d = dout("wkvp", [16, 128, 64])
    wkvs_d = dout("wkvs", [4, 16, 128, 64])
    dbg_d = dout("dbg", [128, DC, NMAX]) if dbg else None

    CO = {}
    o = 0
    CO["g"] = o; o += 12 * 16
    CO["bq"] = o; o += 20
    CO["mu"] = o; o += 6 * 16
    for nm in ("w0", "a0", "kk", "ka", "rk", "lnw", "lnb"):
        CO[nm] = o; o += 16
    assert o == NCONST

    xT = P.sb("xT", [128, DC, NMAX], F32); xT_T = TL("xT", DC)
    hT = P.sb("hT", [128, DC, NMAX + 1], BF16); hT_T = TL("hT", DC)
    BIGB = 66 * 1024
    big = P.sb("big", [128, BIGB // 2], BF16)
    big_T = TL("big", FC)
    SL = 768

    def bigv(off_b, shape, dt):
        n = int(np.prod(shape[1:]))
        esz = 4 if dt == F32 else 2
        assert off_b % 4 == 0 and off_b + n * esz <= BIGB, (off_b, shape)
        if dt == F32:
            ap = big[:, off_b // 2: off_b // 2 + n * 2].bitcast(F32)
        else:
            ap = big[:, off_b // 2: off_b // 2 + n]
        if len(shape) == 3:
            ap = ap.rearrange("p (a b) -> p a b", b=shape[2])
        elif len(shape) == 4:
            ap = ap.rearrange("p (a b c) -> p a b c", b=shape[2], c=shape[3])
        t0 = off_b // (SL * 2)
        t1 = (off_b + n * esz - 1) // (SL * 2)
        return ap, big_T[t0:t1 + 1]

    wslot = [P.sb(f"ws{i}", [128, DC, 128], BF16) for i in range(4)]
    wslot_T = TL("ws", 4)
    dsl = P.sb("wds", [128, 4, 11, 128], BF16)
    dslot = [dsl[:, i] for i in range(4)]
    dslot_T = TL("wds", 4)
    wctr = [0, 0]

    consts = P.sb("consts", [128, NCONST], F32); consts_T = T("consts")
    identf = P.sb("identf", [128, 128], F32)
    identb = P.sb("identb", [128, 128], BF16)
    onesb = P.sb("onesb", [128, 128], BF16)
    bdones = P.sb("bdones", [128, 128], BF16)
    tri = {L: P.sb(f"tri{L}", [128, 128], F32) for L in (64, 32)}
    mstrict = {L: P.sb(f"mstrict{L}", [128, 128], BF16) for L in (64, 32)}
    mincl = {L: P.sb(f"mincl{L}", [128, 128], BF16) for L in (64, 32)}
    lvm = {L: [P.sb(f"lv{L}_{i}", [128, 128], BF16) for i in range(6 if L == 64 else 5)] for L in (64, 32)}
    cT = T("cst")
    bkv = P.sb("bkv", [128, 512], F32)
    sinks = P.sb("sinks", [128, 32], F32)
    bias2 = P.sb("bias2", [128, 32, 192], BF16); bias2_T = T("bias2")
    ktc = P.sb("ktc", [128, 4, 128], BF16); ktc_T = T("ktc")
    vbc = P.sb("vbc", [128, 256], BF16); vbc_T = T("vbc")
    Smast = P.sb("Smast", [128, 16, 128], F32); Smast_T = TL("Sm", 16)
    rstd = P.sb("rstd", [128, NMAX], F32); rstd_T = T("rstd")
    sq = [P.sb(f"sq{i}", [128, NMAX], BF16) for i in range(2)]; sq_T = TL("sq", 2)
    tmpf = [P.sb(f"tmpf{i}", [128, 512], F32) for i in range(2)]; tmpf_T = TL("tmpf", 2)
    small = P.sb("small", [128, 64], F32); small_T = TL("small", 4)
    ctr = {"sq": 0, "tmpf": 0, "bank": 0, "q": 0, "ev": 0, "nb": 2}
    SSB = [6, 7]

    psb = [P.ps(f"psb{i}", [128, 512]) for i in range(8)]
    psT = [TL(f"ps{i}_", 4) for i in range(8)]
    for i in range(8):
        bx = T(f"bank{i}")
        for t_ in psT[i]:
            t_.bank = bx

    def set_pool(nb):
        ctr["nb"] = nb

    def bank():
        b = ctr["bank"] % ctr["nb"]
        ctr["bank"] += 1
        return b

    def quarter():
        nbk = 6 - ctr["nb"]
        q = ctr["q"] % (nbk * 4)
        ctr["q"] += 1
        return ctr["nb"] + q % nbk, q // nbk

    def qf(bq):
        b, q = bq
        return psb[b][:, 128 * q:128 * (q + 1)]

    def qb(bq):
        b, q = bq
        return psb[b][:, 128 * q:128 * (q + 1)].bitcast(BF16)

    def qT(bq):
        return [psT[bq[0]][bq[1]]]

    def ev_eng():
        ctr["ev"] += 1
        return "act" if ctr["ev"] % 2 else "dve"

    def copy(e, out, in_, reads, writes):
        if e == "act":
            P.op("act", lambda x: x.copy(out=out, in_=in_), reads=reads, writes=writes)
        else:
            P.op(e, lambda x: x.tensor_copy(out=out, in_=in_), reads=reads, writes=writes)

    def cc(nm, j=None):
        if j is None:
            return consts[:, CO[nm]:CO[nm] + 16]
        return consts[:, CO[nm] + j:CO[nm] + j + 1]

    def gcol(l, n, c=None):
        o0 = CO["g"] + (l * 6 + n) * 16
        if c is None:
            return consts[:, o0:o0 + 16]
        return consts[:, o0 + c:o0 + c + 1]

    def blocks(N):
        out = []
        c0 = 0
        while c0 < N:
            cn = min(512, N - c0)
            out.append((c0, cn))
            c0 += cn
        return out

    P.dma("sp", consts[:], consts_d, writes=[consts_T])
    P.dma("sp", identf[:], cst["ident"], writes=[cT])
    for L in (64, 32):
        P.dma("sp", tri[L][:], cst[f"tri{L}"], writes=[cT])
        P.dma("pool", mstrict[L][:], cst[f"mstrict{L}"], writes=[cT])
        P.dma("pool", mincl[L][:], cst[f"mincl{L}"], writes=[cT])
        for i_, m_ in enumerate(lvm[L]):
            P.dma("pool", m_[:], cst[f"lv{L}_{i_}"], writes=[cT])
    P.dma("pool", identb[:], cst["ident"], writes=[cT])
    P.dma("pool", bdones[:], cst["bdones"], writes=[cT])
    P.dma("sp", bkv[:], bkv_d.ap().partition_broadcast(128), writes=[cT])
    P.dma("sp", sinks[:], sinks_d.ap().partition_broadcast(128), writes=[cT])
    P.op("dve", lambda e: e.memset(onesb[:], 1.0), writes=[cT])
    P.op("dve", lambda e: e.memset(hT[:, :, 0:1], 0.0), writes=hT_T)
    P.op("dve", lambda e: e.memset(Smast[:], 0.0), writes=Smast_T)
    P.op("dve", lambda e: e.memset(ktc[:], 0.0), writes=[ktc_T])
    P.op("dve", lambda e: e.memset(vbc[:], 0.0), writes=[vbc_T])

    def build_bias():
        tb = tmpf[0]; oh = tmpf[1]
        P.dma("sp", tb[0:32, 0:32], table_d, writes=[tmpf_T[0]])
        P.dma("sp", oh[0:32, 0:255], cst["onehot"], writes=[tmpf_T[1]])
        bq = quarter()
        pso = psb[bq[0]][0:32, 0:255]
        P.op("pe", lambda e: e.matmul(pso, tb[0:32, 0:32], oh[0:32, 0:255], start=True, stop=True),
             reads=[tmpf_T[0], tmpf_T[1]], writes=psT[bq[0]])
        fs, fs_T = bigv(0, [128, 256], F32)
        P.op("act", lambda e: e.copy(out=fs[0:32, 0:255], in_=pso), reads=psT[bq[0]], writes=fs_T)
        fT = T("fscr")
        P.dma("sp", fscr.ap(), fs[0:32, 0:255], reads=fs_T, writes=[fT])
        stg, stg_T = bigv(1024, [128, 32, 192], F32)
        for i in range(64):
            src = bass.AP(fscr, 63 - i, [[0, 1], [255, 32], [1, 192]])
            for half in range(2):
                p = half * 64 + i
                P.dma("sp", stg[p:p + 1, :, :], src, reads=[fT], writes=stg_T, semt=stg_T[0], concurrent=True)
        P.op("act", lambda e: e.copy(out=bias2[:, 0:16, :], in_=stg[:, 0:16, :]), reads=stg_T, writes=[bias2_T])
        P.op("dve", lambda e: e.tensor_copy(out=bias2[:, 16:32, :], in_=stg[:, 16:32, :]), reads=stg_T + [bias2_T], writes=[bias2_T])

    build_bias()

    def sumsq_accumulate(src_ap_fn, src_tiles_fn, N, nch, ssb):
        for c in range(nch):
            s = ctr["sq"] % 2; ctr["sq"] += 1
            P.op("act", lambda e: e.activation(out=sq[s][:, 0:N], in_=src_ap_fn(c), func=AF.Square),
                 reads=src_tiles_fn(c), writes=[sq_T[s]])
            for bi, (c0, cn) in enumerate(blocks(N)):
                P.op("pe", lambda e: e.matmul(psb[ssb[bi]][:, 0:cn], onesb[:], sq[s][:, c0:c0 + cn],
                                              start=(c == 0), stop=(c == nch - 1)),
                     reads=[sq_T[s], cT], writes=psT[ssb[bi]], inc=True)

    def rstd_from_ss(N, ssb, eps):
        for bi, (c0, cn) in enumerate(blocks(N)):
            P.op("dve", lambda e: e.tensor_scalar(out=rstd[:, c0:c0 + cn], in0=psb[ssb[bi]][:, 0:cn],
                                                  scalar1=1.0 / D, scalar2=eps, op0=ALU.mult, op1=ALU.add),
                 reads=psT[ssb[bi]], writes=[rstd_T])
        P.op("act", lambda e: e.activation(out=rstd[:, 0:N], in_=rstd[:, 0:N], func=AF.Ln),
             reads=[rstd_T], writes=[rstd_T])
        P.op("act", lambda e: e.activation(out=rstd[:, 0:N], in_=rstd[:, 0:N], func=AF.Exp, scale=-0.5),
             reads=[rstd_T], writes=[rstd_T])

    def prenorm(l, n, N):
        ssb = SSB
        sumsq_accumulate(lambda c: xT[:, c, 0:N], lambda c: [xT_T[c]], N, DC, ssb)
        rstd_from_ss(N, ssb, RMS_EPS)
        for c in range(DC):
            P.op("dve", lambda e: e.scalar_tensor_tensor(out=hT[:, c, 1:1 + N], in0=xT[:, c, 0:N],
                                                         scalar=gcol(l, n, c), in1=rstd[:, 0:N],
                                                         op0=ALU.mult, op1=ALU.mult),
                 reads=[xT_T[c], rstd_T, consts_T], writes=[hT_T[c]])

    def postnorm_add(l, n, N, ssb, weight):
        rstd_from_ss(N, ssb, RMS_EPS)
        for c in range(DC):
            for (c0, cn) in blocks(N):
                s = ctr["tmpf"] % 2; ctr["tmpf"] += 1
                P.op("dve", lambda e: e.scalar_tensor_tensor(out=tmpf[s][:, 0:cn], in0=hT[:, c, 1 + c0:1 + c0 + cn],
                                                             scalar=gcol(l, n, c), in1=rstd[:, c0:c0 + cn],
                                                             op0=ALU.mult, op1=ALU.mult),
                     reads=[hT_T[c], rstd_T, consts_T], writes=[tmpf_T[s]])
                P.op("dve", lambda e: e.scalar_tensor_tensor(out=xT[:, c, c0:c0 + cn], in0=tmpf[s][:, 0:cn],
                                                             scalar=float(weight), in1=xT[:, c, c0:c0 + cn],
                                                             op0=ALU.mult, op1=ALU.add),
                     reads=[tmpf_T[s]], writes=[xT_T[c]])

    def load_w(dram_ap):
        s = wctr[0] % 4; wctr[0] += 1
        P.dma("pool", wslot[s][:], dram_ap, writes=[wslot_T[s]])
        return wslot[s], wslot_T[s]

    def out_evac_ss(c, N, pbanks, ssb, first, last, bias=None):
        s = ctr["sq"] % 2; ctr["sq"] += 1
        for bi, (c0, cn) in enumerate(blocks(N)):
            b = pbanks[bi]
            P.op("act", lambda e: e.copy(out=hT[:, c, 1 + c0:1 + c0 + cn], in_=psb[b][:, 0:cn]),
                 reads=psT[b], writes=[hT_T[c]])
            P.op("act", lambda e: e.activation(out=sq[s][:, c0:c0 + cn], in_=psb[b][:, 0:cn], func=AF.Square),
                 reads=psT[b], writes=[sq_T[s]])
        def pe_part():
            for bi, (c0, cn) in enumerate(blocks(N)):
                P.op("pe", lambda e: e.matmul(psb[ssb[bi]][:, 0:cn], onesb[:], sq[s][:, c0:c0 + cn],
                                              start=first, stop=last),
                     reads=[sq_T[s], cT], writes=psT[ssb[bi]], inc=True)
        return pe_part

    def ffn(l, s, N):
        n_in, n_out = (0, 1) if s == 0 else (4, 5)
        set_pool(6)
        prenorm(l, n_in, N)
        actT = big[:, 0:FC * SL].rearrange("p (f t) -> p f t", t=SL)
        blks = blocks(N)
        for f in range(FC):
            wgs, wgT = load_w(wg_d[l, s, f])
            wus, wuT = load_w(wu_d[l, s, f])
            for (c0, cn) in blks:
                bg = bank(); bu = bank()
                for kt in range(DC):
                    P.op("pe", lambda e: e.matmul(psb[bg][:, 0:cn], wgs[:, kt, :], hT[:, kt, 1 + c0:1 + c0 + cn],
                                                  start=(kt == 0), stop=(kt == DC - 1)),
                         reads=[wgT, hT_T[kt]], writes=psT[bg], inc=(kt == DC - 1))
                for kt in range(DC):
                    P.op("pe", lambda e: e.matmul(psb[bu][:, 0:cn], wus[:, kt, :], hT[:, kt, 1 + c0:1 + c0 + cn],
                                                  start=(kt == 0), stop=(kt == DC - 1)),
                         reads=[wuT, hT_T[kt]], writes=psT[bu], inc=(kt == DC - 1))
                ts = ctr["tmpf"] % 2; ctr["tmpf"] += 1
                P.op("act", lambda e: e.activation(out=tmpf[ts][:, 0:cn], in_=psb[bg][:, 0:cn], func=AF.Silu),
                     reads=psT[bg], writes=[tmpf_T[ts]])
                P.op("dve", lambda e: e.tensor_tensor(out=actT[:, f, c0:c0 + cn], in0=tmpf[ts][:, 0:cn],
                                                      in1=psb[bu][:, 0:cn], op=ALU.mult),
                     reads=[tmpf_T[ts]] + psT[bu], writes=[big_T[f]])
        ssb = SSB
        pend = None
        for d in range(DC):
            pb = [bank() for _ in blks]
            for qr in range(4):
                sl = wctr[1] % 4; wctr[1] += 1
                P.dma("pool", dslot[sl], wd_d[l, s, d, qr], writes=[dslot_T[sl]])
                for bi, (c0, cn) in enumerate(blks):
                    for k in range(11):
                        f = qr * 11 + k
                        P.op("pe", lambda e: e.matmul(psb[pb[bi]][:, 0:cn], dslot[sl][:, k, :], actT[:, f, c0:c0 + cn],
                                                      start=(f == 0), stop=(f == FC - 1)),
                             reads=[dslot_T[sl], big_T[f]], writes=psT[pb[bi]], inc=(f == FC - 1))
            if pend is not None:
                pend()
            pend = out_evac_ss(d, N, pb, ssb, d == 0, d == DC - 1)
        pend()
        postnorm_add(l, n_out, N, ssb, 0.5)

    def attention(gi, N, npr, has_sample):
        l = 0
        ntile = N // 128
        nptile = npr // 128
        set_pool(2)
        prenorm(l, 2, N)
        off = 0
        qTb, qT_T = bigv(off, [128, DC, NMAX], BF16); off += DC * NMAX * 2
        KT, KT_T = bigv(off, [128, 4, 128 + NMAX], BF16); off += 4 * (128 + NMAX) * 2
        Vb, Vb_T = bigv(off, [128, 7, 256], BF16); off += 7 * 256 * 2
        sbufs = []
        for i in range(4):
            a, t = bigv(off, [128, 256], F32); off += 1024
            sbufs.append((a, t))
        pbufs = []
        for i in range(4):
            a, t = bigv(off, [128, 256], BF16); off += 512
            pbufs.append((a, t))
        ptbufs = []
        for i in range(3):
            a, t = bigv(off, [128, 2, 128], BF16); off += 512
            ptbufs.append((a, t))
        stage, stage_T = bigv(off, [128, 512], F32); off += 2048
        if has_sample:
            KTs, KTs_T = bigv(off, [128, 4, 4, 256], BF16); off += 4 * 4 * 256 * 2
            Vc, Vc_T = bigv(off, [128, 4, 256], BF16); off += 4 * 256 * 2
            ssb_s = []
            for i in range(4):
                a, t = bigv(off, [128, 256], F32); off += 1024
                ssb_s.append((a, t))
            ckf, ckf_T = bigv(off, [128, 256], F32); off += 1024
        assert off <= BIGB, off

        P.op("pool", lambda e: e.tensor_copy(out=KT[:, :, 0:128], in_=ktc[:]), reads=[ktc_T], writes=KT_T)
        P.op("pool", lambda e: e.tensor_copy(out=Vb[:, 0, :], in_=vbc[:]), reads=[vbc_T], writes=Vb_T)

        blks = blocks(N)
        for j in range(20):
            ws, wT = load_w(wqkv_d[j])
            for (c0, cn) in blks:
                b = bank()
                for kt in range(DC):
                    P.op("pe", lambda e: e.matmul(psb[b][:, 0:cn], ws[:, kt, :], hT[:, kt, 1 + c0:1 + c0 + cn],
                                                  start=(kt == 0), stop=(kt == DC - 1)),
                         reads=[wT, hT_T[kt]], writes=psT[b], inc=(kt == DC - 1))
                bcol = consts[:, CO["bq"] + j:CO["bq"] + j + 1]
                if j < 16:
                    P.op("act", lambda e: e.activation(out=qTb[:, j, c0:c0 + cn], in_=psb[b][:, 0:cn],
                                                       func=AF.Identity, bias=bcol, scale=1.0),
                         reads=psT[b] + [consts_T], writes=qT_T)
                else:
                    P.op("act", lambda e: e.activation(out=KT[:, j - 16, 128 + c0:128 + c0 + cn], in_=psb[b][:, 0:cn],
                                                       func=AF.Identity, bias=bcol, scale=1.0),
                         reads=psT[b] + [consts_T], writes=KT_T)
        for cchunk in range(4):
            is_k = cchunk < 2
            ws, wT = load_w(wkvt_d[cchunk])
            for t in range(ntile):
                out_tile = (gi == 2) and (t >= nptile - 1)
                if is_k and not out_tile:
                    continue
                bq = quarter()
                for kt in range(DC):
                    P.op("pe", lambda e: e.matmul(qf(bq), hT[:, kt, 1 + 128 * t:1 + 128 * (t + 1)], ws[:, kt, :],
                                                  start=(kt == 0), stop=(kt == DC - 1)),
                         reads=[wT, hT_T[kt]], writes=qT(bq), inc=(kt == DC - 1))
                bsl = bkv[:, 128 * cchunk:128 * (cchunk + 1)]
                if not is_k:
                    P.op("dve", lambda e: e.tensor_tensor(out=Vb[:, 1 + t, 128 * (cchunk - 2):128 * (cchunk - 1)],
                                                          in0=qf(bq), in1=bsl, op=ALU.add),
                         reads=qT(bq) + [cT], writes=Vb_T)
                if out_tile:
                    P.op("dve", lambda e: e.tensor_tensor(out=stage[:, 128 * cchunk:128 * (cchunk + 1)],
                                                          in0=qf(bq), in1=bsl, op=ALU.add),
                         reads=qT(bq) + [cT], writes=stage_T)
                    is_s = (t == nptile)
                    dst = (ks_d if is_s else kp_d) if is_k else (vs_d if is_s else vp_d)
                    co = 128 * (cchunk % 2)
                    P.dma("sp", dst[:, co:co + 128], stage[:, 128 * cchunk:128 * (cchunk + 1)],
                          reads=stage_T, is_output=True)

        def preset(buf, tiles):
            P.op("pool", lambda e: e.memset(buf[:], NEG), writes=tiles)

        for (a, t_) in sbufs:
            preset(a, t_)

        if has_sample:
            for s in range(4):
                preset(ssb_s[s][0], ssb_s[s][1])
                P.dma("pool", Vc[:, s, :], cv[s], writes=Vc_T, semt=Vc_T[0])
                P.dma("sp", ckf[:], ck[s], writes=ckf_T)
                for kvh in range(4):
                    di = ctr["tmpf"] % 2; ctr["tmpf"] += 1
                    dsrc, dsrc_T = tmpf[di], tmpf_T[di]
                    for dup in range(2):
                        P.op("dve", lambda e: e.tensor_copy(out=dsrc[:, 64 * dup:64 * (dup + 1)],
                                                            in_=ckf[:, 64 * kvh:64 * (kvh + 1)]),
                             reads=ckf_T, writes=[dsrc_T])
                    bq = quarter()
                    P.op("pe", lambda e: e.transpose(qf(bq), dsrc[:, 0:128], identf[:]),
                         reads=[dsrc_T, cT], writes=qT(bq))
                    copy("act", KTs[:, s, kvh, 0:128], qf(bq), qT(bq), KTs_T)
                P.op("pool", lambda e: e.tensor_copy(out=KTs[:, s, :, 128:256], in_=KT[:, :, 128 + npr:128 + npr + 128]),
                     reads=KT_T, writes=KTs_T)

        jobs = []

        def stage_a(jb):
            i = jb["i"]; h = jb["h"]; nq = jb["nq"]
            sb, sb_T = jb["sb"]
            b = bank()
            P.op("pe", lambda e: e.matmul(psb[b][0:nq, 0:256], jb["q"], jb["k"], start=True, stop=True),
                 reads=qT_T + jb["k_T"], writes=psT[b][0:2])
            for (r0, r1, oc0, oc1, bc0) in jb["bops"]:
                P.op("dve", lambda e: e.scalar_tensor_tensor(out=sb[r0:r1, oc0:oc1], in0=psb[b][r0:r1, oc0:oc1], scalar=0.125,
                                                             in1=bias2[r0:r1, h, bc0:bc0 + (oc1 - oc0)],
                                                             op0=ALU.mult, op1=ALU.add),
                     reads=psT[b][0:2] + [bias2_T], writes=sb_T)
            g = i % 4
            sm = small[:, 16 * g:16 * g + 16]; sT = [small_T[g]]
            P.op("dve", lambda e: e.reduce_max(out=sm[0:nq, 0:1], in_=sb[0:nq, :], axis=AX.X), reads=sb_T, writes=sT)
            P.op("dve", lambda e: e.tensor_scalar(out=sm[0:nq, 2:3], in0=sm[0:nq, 0:1], scalar1=sinks[0:nq, h:h + 1], scalar2=-1.0,
                                                  op0=ALU.max, op1=ALU.mult),
                 reads=sT + [cT], writes=sT)
            pb_, pb_T = pbufs[i % 4]
            P.op("act", lambda e: e.activation(out=pb_[0:nq, :], in_=sb[0:nq, :], func=AF.Exp, bias=sm[0:nq, 2:3], scale=1.0,
                                               accum_out=sm[0:nq, 3:4]),
                 reads=sb_T + sT, writes=pb_T + sT)
            P.op("act", lambda e: e.activation(out=sm[0:nq, 4:5], in_=sinks[0:nq, h:h + 1], func=AF.Exp, bias=sm[0:nq, 2:3], scale=1.0),
                 reads=sT + [cT], writes=sT)

        def stage_a2(jb):
            i = jb["i"]; nq = jb["nq"]
            g = i % 4
            sm = small[:, 16 * g:16 * g + 16]; sT = [small_T[g]]
            pb_, pb_T = pbufs[i % 4]
            P.op("dve", lambda e: e.tensor_tensor(out=sm[0:nq, 5:6], in0=sm[0:nq, 3:4], in1=sm[0:nq, 4:5], op=ALU.add),
                 reads=sT, writes=sT)
            P.op("dve", lambda e: e.reciprocal(out=sm[0:nq, 6:7], in_=sm[0:nq, 5:6]), reads=sT, writes=sT)
            P.op("dve", lambda e: e.tensor_scalar(out=pb_[0:nq, :], in0=pb_[0:nq, :], scalar1=sm[0:nq, 6:7], scalar2=None, op0=ALU.mult),
                 reads=pb_T + sT, writes=pb_T)

        def stage_b(jb):
            i = jb["i"]; nq = jb["nq"]
            pb_, pb_T = pbufs[i % 4]
            pt_, pt_T = ptbufs[i % 3]
            for kt in range(2):
                bq = quarter()
                P.op("pe", lambda e: e.transpose(qb(bq)[:, 0:nq], pb_[0:nq, 128 * kt:128 * (kt + 1)], identb[0:nq, 0:nq]),
                     reads=pb_T + [cT], writes=qT(bq))
                copy(ev_eng(), pt_[:, kt, 0:nq], qb(bq)[:, 0:nq], qT(bq), pt_T)

        def stage_c(jb):
            i = jb["i"]; nq = jb["nq"]
            pt_, pt_T = ptbufs[i % 3]
            if jb["hh"] == 0:
                jb["pair"]["bq"] = quarter()
            bq = jb["pair"]["bq"]
            r0 = 64 * jb["hh"]
            for kt in range(2):
                P.op("pe", lambda e: e.matmul(qf(bq)[r0:r0 + 64, 0:nq], jb["v"][kt], pt_[:, kt, 0:nq],
                                              start=(kt == 0), stop=(kt == 1)),
                     reads=pt_T + jb["v_T"], writes=qT(bq), inc=(kt == 1))
            if jb["hh"] == 1:
                copy("act", jb["o_dst"], qf(bq)[:, 0:nq], qT(bq), qT_T)

        def add_pair(j, nq, qcols, kfn, k_T, v, v_T, sbpair, bops):
            pair = {}
            for hh in range(2):
                i = len(jobs)
                jobs.append(dict(i=i, h=2 * j + hh, hh=hh, nq=nq, pair=pair,
                                 q=qTb[64 * hh:64 * (hh + 1), j, qcols[0]:qcols[0] + nq], k=kfn(hh), k_T=k_T,
                                 v=v, v_T=v_T, sb=sbpair[i % 2], bops=bops,
                                 o_dst=qTb[:, j, qcols[0]:qcols[0] + nq]))

        for t in range(nptile):
            first_tile = (gi == 0) and t == 0
            if first_tile:
                bops = [(0, 64, 128, 192, 128), (64, 128, 128, 256, 64)]
                sbp = sbufs[0:2]
            else:
                bops = [(0, 64, 0, 192, 0), (64, 128, 64, 256, 0)]
                sbp = sbufs[2:4]
            for j in range(DC):
                kvh = (2 * j) // 8
                add_pair(j, 128, (128 * t,),
                         lambda hh, kvh=kvh, t=t: KT[64 * hh:64 * (hh + 1), kvh, 128 * t:128 * t + 256], KT_T,
                         [Vb[:, t, 64 * kvh:64 * (kvh + 1)], Vb[:, t + 1, 64 * kvh:64 * (kvh + 1)]], Vb_T, sbp, bops)
        if has_sample:
            for j in range(DC):
                kvh = (2 * j) // 8
                for s in range(4):
                    bops = [(0, 32, 0, 128, 0), (0, 32, 128 + 32 * s, 160 + 32 * s, 128)]
                    add_pair(j, 32, (npr + 32 * s,),
                             lambda hh, kvh=kvh, s=s: KTs[64 * hh:64 * (hh + 1), s, kvh, :], KTs_T,
                             [Vc[:, s, 64 * kvh:64 * (kvh + 1)], Vb[:, 1 + nptile, 64 * kvh:64 * (kvh + 1)]],
                             Vc_T + Vb_T, [ssb_s[s], ssb_s[s]], bops)
        nj = len(jobs)
        for step in range(nj + 3):
            if step < nj:
                stage_a(jobs[step])
            if 0 <= step - 1 < nj:
                stage_a2(jobs[step - 1])
            if 0 <= step - 2 < nj:
                stage_b(jobs[step - 2])
            if 0 <= step - 3 < nj:
                stage_c(jobs[step - 3])
        if gi < 2:
            P.op("pool", lambda e: e.tensor_copy(out=ktc[:], in_=KT[:, :, npr:npr + 128]), reads=KT_T, writes=[ktc_T])
            P.op("pool", lambda e: e.tensor_copy(out=vbc[:], in_=Vb[:, nptile, :]), reads=Vb_T, writes=[vbc_T])
        set_pool(6)
        ssb = SSB
        pend = None
        for dch in range(DC):
            ws, wT = load_w(wao_d[dch])
            pb = [bank() for _ in blks]
            for bi, (c0, cn) in enumerate(blks):
                for kt in range(DC):
                    P.op("pe", lambda e: e.matmul(psb[pb[bi]][:, 0:cn], ws[:, kt, :], qTb[:, kt, c0:c0 + cn],
                                                  start=(kt == 0), stop=(kt == DC - 1)),
                         reads=[wT] + qT_T, writes=psT[pb[bi]], inc=(kt == DC - 1))
            if pend is not None:
                pend()
            pend = out_evac_ss(dch, N, pb, ssb, dch == 0, dch == DC - 1)
        pend()
        postnorm_add(l, 3, N, ssb, 1.0)

    def rwkv(gi, N, npr, has_sample):
        l = 1
        set_pool(6)
        prenorm(l, 2, N)
        blks = blocks(N)
        st = {"off": 0}

        def h_last(col, dst_ap):
            di = ctr["tmpf"] % 2; ctr["tmpf"] += 1
            so, so_T = tmpf[di], [tmpf_T[di]]
            P.op("dve", lambda e: e.scalar_tensor_tensor(out=so[:, 0:DC], in0=xT[:, :, col], scalar=rstd[:, col:col + 1], in1=gcol(l, 2),
                                                         op0=ALU.mult, op1=ALU.mult),
                 reads=xT_T + [rstd_T, consts_T], writes=so_T)
            P.dma("sp", dst_ap, so[:, 0:DC], reads=so_T, is_output=True)
        def buf(shape, dt):
            esz = 4 if dt == F32 else 2
            a_, t_ = bigv(st["off"], list(shape), dt)
            st["off"] = (st["off"] + int(np.prod(shape[1:])) * esz + 3) // 4 * 4
            return a_, t_

        if gi == 2:
            h_last(npr - 1, shp_d)
            for s_ in range(4):
                h_last(npr + 32 * s_ + 31, shs_d[:, s_, :])
        U = 4
        NB = N
        ygT, yg_T = buf([128, DC, NB], BF16)
        if has_sample:
            hsh, hsh_T = buf([128, DC, 128], BF16)
        lora, lora_T = buf([128, 4, NB], BF16)
        w2s, w2s_T = buf([128, 128], BF16); a2s, a2s_T = buf([128, 128], BF16); g2s, g2s_T = buf([128, 2, 128], BF16)
        rT_, rT_T = buf([128, NB], BF16); kT_, kT_T = buf([128, NB], BF16); vT_, vT_T = buf([128, NB], BF16)
        aT_, aT_T = buf([128, NB], BF16); kkn, kkn_T = buf([128, NB], BF16); ynT, ynT_T = buf([128, NB], BF16)
        lwT, lwT_T = buf([128, NB], F32)
        (wa, wa_T), (wb, wb_T) = buf([128, DC, 128], BF16), buf([128, DC, 128], BF16)

        def aux_views(flat_bf16, region_T, shapes):
            outs = []
            o_ = 0
            for shp in shapes:
                n_ = int(np.prod(shp[1:]))
                ap_ = flat_bf16[:, o_:o_ + n_]
                if len(shp) == 3:
                    ap_ = ap_.rearrange("p (a b) -> p a b", b=shp[2])
                t_ = T("aux")
                for rt_ in region_T:
                    for k_, ev_ in rt_.w.items():
                        if k_ not in t_.w or t_.w[k_][1] < ev_[1]:
                            t_.w[k_] = ev_
                    for k_, ev_ in rt_.r.items():
                        if k_ not in t_.r or t_.r[k_][1] < ev_[1]:
                            t_.r[k_] = ev_
                outs.append((ap_, [t_]))
                o_ += n_
            return outs

        def aux_release(region_T, subs):
            for rt_ in region_T:
                for _, tl in subs:
                    for t_ in tl:
                        for k_, ev_ in list(t_.w.items()) + list(t_.r.items()):
                            if k_ not in rt_.r or rt_.r[k_][1] < ev_[1]:
                                rt_.r[k_] = ev_

        d0 = dsl[:, 0:2].rearrange("p a b c -> p (a b c)")
        d1 = dsl[:, 2:4].rearrange("p a b c -> p (a b c)")
        r0 = rstd[:].bitcast(BF16)
        aux0 = aux_views(d0, dslot_T[0:2], [[128, U, 128]] * 5)
        aux1 = aux_views(d1, dslot_T[2:4], [[128, U, 128]] * 3 + [[128, U, 256]])
        aux2 = aux_views(r0, [rstd_T], [[128, U, 128]] * 3)
        Db = [aux0[0], aux0[1]]; Eb = [aux0[2], aux0[3]]; Mfull, Mfull_T = aux0[4]
        (tokV, tokV_T), (tokK, tokK_T), (tokB, tokB_T), (bdR, bdR_T) = aux1
        (bdB, bdB_T), (bdK, bdK_T), (bdV, bdV_T) = aux2
        Qm, Qm_T = buf([128, U, 128], BF16); P1b, P1b_T = buf([128, U, 128], BF16)
        ArbT, ArbT_T = buf([128, U, 128], BF16); ArkT, ArkT_T = buf([128, U, 128], BF16)
        Xb, Xb_T = buf([128, U, 256], BF16); Vbar, Vbar_T = buf([128, U, 128], BF16)
        AbT, AbT_T = buf([128, U, 128], BF16); Ub, Ub_T = buf([128, U, 128], BF16); Sb, Sb_T = buf([128, 128], BF16)
        e1, e1_T = buf([128, 256], BF16); e2, e2_T = buf([128, 256], BF16); e3, e3_T = buf([128, 256], BF16)
        gam, gam_T = buf([128, 8], F32)
        if has_sample:
            Ss, Ss_T = buf([128, U, 128], F32)
        assert st["off"] <= BIGB, st["off"]
        zero_list = [(bdR, bdR_T), (bdB, bdB_T), (bdK, bdK_T), (bdV, bdV_T)]

        def zero_bd():
            for a_, t_ in zero_list:
                P.op("pool", lambda e: e.memset(a_, 0.0), writes=t_)

        if has_sample:
            di = ctr["tmpf"] % 2; ctr["tmpf"] += 1
            sh32, sh32_T = tmpf[di], tmpf_T[di]
            P.dma("sp", sh32[:, 0:4 * DC], sshift.rearrange("p s c -> p (s c)"), writes=[sh32_T])
            for s_ in range(4):
                P.op("dve", lambda e: e.tensor_copy(out=hsh[:, :, 32 * s_:32 * s_ + 1],
                                                    in_=sh32[:, s_ * DC:(s_ + 1) * DC].unsqueeze(2)),
                     reads=[sh32_T], writes=hsh_T)
                P.op("dve", lambda e: e.tensor_copy(out=hsh[:, :, 32 * s_ + 1:32 * s_ + 32],
                                                    in_=hT[:, :, 1 + npr + 32 * s_:1 + npr + 32 * s_ + 31]),
                     reads=hT_T, writes=hsh_T)

        def rhs_pairs(c0, cn):
            if has_sample and c0 >= npr:
                return (lambda kt: hT[:, kt, 1 + c0:1 + c0 + cn]), (lambda kt: hsh[:, kt, c0 - npr:c0 - npr + cn]), hsh_T
            return (lambda kt: hT[:, kt, 1 + c0:1 + c0 + cn]), (lambda kt: hT[:, kt, c0:c0 + cn]), []

        def mixed_linear(wdram_ap, n, M, evac):
            si = wctr[0] % 4; wctr[0] += 1
            ws, wT = wslot[si], wslot_T[si]
            P.dma("pool", ws[:, :, 0:M], wdram_ap, writes=[wT])
            mu = consts[:, CO["mu"] + n * 16:CO["mu"] + (n + 1) * 16].unsqueeze(2).broadcast_to([128, DC, M])
            P.op("dve", lambda e: e.tensor_tensor(out=wb[:, :, 0:M], in0=ws[:, :, 0:M], in1=mu, op=ALU.mult),
                 reads=[wT, consts_T], writes=wb_T)
            P.op("dve", lambda e: e.tensor_tensor(out=wa[:, :, 0:M], in0=ws[:, :, 0:M], in1=wb[:, :, 0:M], op=ALU.subtract),
                 reads=[wT] + wb_T, writes=wa_T)
            for (c0, cn) in blks:
                cur, shf, extra = rhs_pairs(c0, cn)
                b = bank()
                for kt in range(DC):
                    P.op("pe", lambda e: e.matmul(psb[b][0:M, 0:cn], wa[:, kt, 0:M], cur(kt), start=(kt == 0), stop=False),
                         reads=wa_T + [hT_T[kt]], writes=psT[b], inc=False)
                    P.op("pe", lambda e: e.matmul(psb[b][0:M, 0:cn], wb[:, kt, 0:M], shf(kt), start=False, stop=(kt == DC - 1)),
                         reads=wb_T + [hT_T[kt]] + extra, writes=psT[b], inc=(kt == DC - 1))
                evac(b, c0, cn)

        mixed_linear(w1_d, 1, 96, lambda b, c0, cn: P.op(
            "act", lambda e: e.activation(out=lora[0:96, 0, c0:c0 + cn], in_=psb[b][0:96, 0:cn], func=AF.Tanh),
            reads=psT[b], writes=lora_T))
        mixed_linear(a1_d, 4, 96, lambda b, c0, cn: P.op(
            "act", lambda e: e.copy(out=lora[0:96, 1, c0:c0 + cn], in_=psb[b][0:96, 0:cn]),
            reads=psT[b], writes=lora_T))
        for gc in range(2):
            mixed_linear(g1_d[gc], 5, 128, lambda b, c0, cn: P.op(
                "act", lambda e: e.activation(out=lora[:, 2 + gc, c0:c0 + cn], in_=psb[b][:, 0:cn], func=AF.Sigmoid),
                reads=psT[b], writes=lora_T))

        bcount = [0]

        def v3(ap_, TP, w0_, w1_):
            return ap_[0:TP, :, w0_:w1_]

        def batch(j, cb, L, nu, states):
            TP = 2 * L
            ncol = nu * L
            ntl = ncol // 128
            bcl = bank()
            for tl in range(ntl):
                c_ = cb + 128 * tl
                b_ = bank()
                P.op("pe", lambda e: e.transpose(psb[b_][:, 0:128], lwT[:, c_:c_ + 128], identf[:]), reads=lwT_T + [cT], writes=psT[b_][0:1])
                di = ctr["tmpf"] % 2; ctr["tmpf"] += 1
                P.op("act", lambda e: e.copy(out=tmpf[di][:, 0:128], in_=psb[b_][:, 0:128]), reads=psT[b_][0:1], writes=[tmpf_T[di]])
                P.op("pe", lambda e: e.matmul(psb[bcl][:, 128 * tl:128 * (tl + 1)], tmpf[di][:, 0:128], tri[L][:], start=True, stop=True),
                     reads=[tmpf_T[di], cT], writes=psT[bcl])
            cl = psb[bcl][:, 0:ncol]
            P.op("act", lambda e: e.activation(out=e1[:, 0:ncol], in_=cl, func=AF.Exp), reads=psT[bcl], writes=e1_T)
            P.op("act", lambda e: e.activation(out=e2[:, 0:ncol], in_=cl, func=AF.Exp, scale=-1.0), reads=psT[bcl], writes=e2_T)
            P.op("act", lambda e: e.activation(out=gam[:, 0:nu], in_=psb[bcl][:, 0:ncol].rearrange("p (u l) -> p u l", l=L)[:, :, L - 1],
                                               func=AF.Exp), reads=psT[bcl], writes=gam_T)
            di = ctr["tmpf"] % 2; ctr["tmpf"] += 1
            P.op("dve", lambda e: e.tensor_tensor(out=tmpf[di][:, 0:ncol], in0=cl, in1=lwT[:, cb:cb + ncol], op=ALU.subtract),
                 reads=psT[bcl] + lwT_T, writes=[tmpf_T[di]])
            P.op("act", lambda e: e.activation(out=e3[:, 0:ncol], in_=tmpf[di][:, 0:ncol], func=AF.Exp), reads=[tmpf_T[di]], writes=e3_T)
            def src(ap_, ps_):
                return ap_[ps_, cb:cb + ncol].rearrange("p (u l) -> p u l", l=L)

            def esrc(ap_, ps_):
                return ap_[ps_, 0:ncol].rearrange("p (u l) -> p u l", l=L)
            for hh in range(2):
                ps_ = slice(64 * hh, 64 * hh + 64)
                cs0, cs1 = L * hh, L * hh + L
                eng = "dve" if hh == 0 else "pool"
                P.op("dve", lambda e: e.scalar_tensor_tensor(out=bdR[ps_, 0:nu, cs0:cs1], in0=src(kkn, ps_), scalar=-1.0, in1=esrc(e3, ps_),
                                                           op0=ALU.mult, op1=ALU.mult),
                     reads=kkn_T + e3_T, writes=bdR_T)
                P.op(eng, lambda e: e.tensor_tensor(out=bdR[ps_, 0:nu, TP + cs0:TP + cs1], in0=src(rT_, ps_), in1=esrc(e1, ps_), op=ALU.mult),
                     reads=rT_T + e1_T, writes=bdR_T)
                P.op(eng, lambda e: e.tensor_tensor(out=bdK[ps_, 0:nu, cs0:cs1], in0=src(kT_, ps_), in1=esrc(e2, ps_), op=ALU.mult),
                     reads=kT_T + e2_T, writes=bdK_T)
                P.op(eng, lambda e: e.tensor_tensor(out=bdB[ps_, 0:nu, cs0:cs1], in0=src(kkn, ps_), in1=src(aT_, ps_), op=ALU.mult),
                     reads=kkn_T + aT_T, writes=bdB_T)
                P.op(eng, lambda e: e.tensor_tensor(out=bdB[ps_, 0:nu, cs0:cs1], in0=bdB[ps_, 0:nu, cs0:cs1], in1=esrc(e2, ps_), op=ALU.mult),
                     reads=bdB_T + e2_T, writes=bdB_T)
                P.op("act", lambda e: e.copy(out=bdV[ps_, 0:nu, cs0:cs1], in_=src(vT_, ps_)), reads=vT_T, writes=bdV_T)
            def tr_all(dst, dst_T, srcfn, src_T, rows_in, cols_in, eng):
                b_ = bank()
                pv = psb[b_][:, :].bitcast(BF16)
                for u in range(nu):
                    P.op("pe", lambda e: e.transpose(pv[0:cols_in, rows_in * u:rows_in * (u + 1)], srcfn(u), identb[0:rows_in, 0:rows_in]),
                         reads=src_T + [cT], writes=psT[b_], inc=(u == nu - 1))
                copy(eng, dst, pv[0:cols_in, 0:rows_in * nu].rearrange("p (u r) -> p u r", r=rows_in), psT[b_], dst_T)
            tr_all(tokV[0:TP, 0:nu, :], tokV_T, lambda u: bdV[:, u, 0:TP], bdV_T, 128, TP, "act")
            tr_all(tokK[0:TP, 0:nu, :], tokK_T, lambda u: bdK[:, u, 0:TP], bdK_T, 128, TP, "dve")
            tr_all(tokB[0:TP, 0:nu, :], tokB_T, lambda u: bdB[:, u, 0:TP], bdB_T, 128, TP, "act")
            tr_all(Xb[0:TP, 0:nu, 0:128], Xb_T, lambda u: bdR[:, u, 0:TP], bdR_T, 128, TP, "dve")
            upb = 512 // (2 * TP)
            for lhs, lhs_T, outs in ((bdB, bdB_T, ((Mfull, Mfull_T, None), (ArbT, ArbT_T, mincl[L]))),
                                     (bdK, bdK_T, ((P1b, P1b_T, mstrict[L]), (ArkT, ArkT_T, mincl[L])))):
                for u0 in range(0, nu, upb):
                    b_ = bank()
                    n_ = min(upb, nu - u0)
                    for u in range(u0, u0 + n_):
                        o_ = (u - u0) * 2 * TP
                        P.op("pe", lambda e: e.matmul(psb[b_][0:TP, o_:o_ + 2 * TP], lhs[:, u, 0:TP], bdR[:, u, 0:2 * TP], start=True, stop=True),
                             reads=lhs_T + bdR_T, writes=psT[b_], inc=(u == u0 + n_ - 1))
                    pv = psb[b_][0:TP, 0:n_ * 2 * TP].rearrange("p (u c) -> p u c", c=2 * TP)
                    for part, (dst, dst_T, msk) in enumerate(outs):
                        sv = pv[:, :, part * TP:(part + 1) * TP]
                        dv = dst[0:TP, u0:u0 + n_, 0:TP]
                        if msk is None:
                            copy("act", dv, sv, psT[b_], dst_T)
                        else:
                            P.op("dve", lambda e: e.tensor_tensor(out=dv, in0=sv, in1=msk[0:TP, 0:TP].unsqueeze(1).broadcast_to([TP, n_, TP]), op=ALU.mult),
                                 reads=psT[b_] + [cT], writes=dst_T)
            b_ = bank()
            for u in range(nu):
                P.op("pe", lambda e: e.matmul(psb[b_][0:TP, 128 * u:128 * (u + 1)], P1b[0:TP, u, 0:TP], tokV[0:TP, u, :], start=True, stop=True),
                     reads=P1b_T + tokV_T, writes=psT[b_], inc=(u == nu - 1))
            copy("act", Xb[0:TP, 0:nu, 128:256], psb[b_][0:TP, 0:128 * nu].rearrange("p (u c) -> p u c", c=128), psT[b_], Xb_T)
            idb = identb[0:TP, 0:TP].unsqueeze(1).broadcast_to([TP, nu, TP])
            cur = 0
            P.op("act", lambda e: e.copy(out=Db[0][0][0:TP, 0:nu, 0:TP], in_=idb), reads=[cT], writes=Db[0][1])
            P.op("pool", lambda e: e.tensor_copy(out=Eb[0][0][0:TP, 0:nu, 0:TP], in_=idb), reads=[cT], writes=Eb[0][1])
            for lv in range(len(lvm[L])):
                (Dc, Dc_T), (Ec, Ec_T) = Db[cur], Eb[cur]
                (Dn, Dn_T), (En, En_T) = Db[1 - cur], Eb[1 - cur]
                P.op("pool", lambda e: e.tensor_tensor(out=Qm[0:TP, 0:nu, 0:TP], in0=Mfull[0:TP, 0:nu, 0:TP],
                                                       in1=lvm[L][lv][0:TP, 0:TP].unsqueeze(1).broadcast_to([TP, nu, TP]), op=ALU.mult),
                     reads=Mfull_T + [cT], writes=Qm_T)
                b1_ = bank()
                for u in range(nu):
                    P.op("pe", lambda e: e.matmul(psb[b1_][0:TP, 128 * u:128 * u + TP], Qm[0:TP, u, 0:TP], Dc[0:TP, u, 0:TP], start=True, stop=True),
                         reads=Qm_T + Dc_T, writes=psT[b1_], inc=(u == nu - 1))
                copy("act", P1b[0:TP, 0:nu, 0:TP], psb[b1_][0:TP, 0:128 * nu].rearrange("p (u c) -> p u c", c=128)[:, :, 0:TP], psT[b1_], P1b_T)
                b2_ = bank()
                for u in range(nu):
                    P.op("pe", lambda e: e.matmul(psb[b2_][0:TP, 128 * u:128 * u + TP], Ec[0:TP, u, 0:TP], P1b[0:TP, u, 0:TP], start=True, stop=True),
                         reads=Ec_T + P1b_T, writes=psT[b2_], inc=(u == nu - 1))
                P.op("dve", lambda e: e.tensor_tensor(out=Dn[0:TP, 0:nu, 0:TP], in0=Dc[0:TP, 0:nu, 0:TP],
                                                      in1=psb[b2_][0:TP, 0:128 * nu].rearrange("p (u c) -> p u c", c=128)[:, :, 0:TP], op=ALU.add),
                     reads=Dc_T + psT[b2_], writes=Dn_T)
                tr_all(En[0:TP, 0:nu, 0:TP], En_T, lambda u: Dn[0:TP, u, 0:TP], Dn_T, TP, TP, "act")
                cur = 1 - cur
            (Dc, Dc_T), (Ec, Ec_T) = Db[cur], Eb[cur]
            for u0 in range(0, nu, 2):
                b_ = bank()
                n_ = min(2, nu - u0)
                for u in range(u0, u0 + n_):
                    P.op("pe", lambda e: e.matmul(psb[b_][0:TP, 256 * (u - u0):256 * (u - u0 + 1)], Ec[0:TP, u, 0:TP], Xb[0:TP, u, :], start=True, stop=True),
                         reads=Ec_T + Xb_T, writes=psT[b_], inc=(u == u0 + n_ - 1))
                pv = psb[b_][0:TP, 0:256 * n_].rearrange("p (u c) -> p u c", c=256)
                copy("act", Vbar[0:TP, u0:u0 + n_, :], pv[:, :, 128:256], psT[b_], Vbar_T)
                copy("dve", Xb[0:TP, u0:u0 + n_, 0:128], pv[:, :, 0:128], psT[b_], Xb_T)
            tr_all(AbT[:, 0:nu, 0:TP], AbT_T, lambda u: Xb[0:TP, u, 0:128], Xb_T, TP, 128, "act")
            ynbd = Xb[:, :, 0:128]
            P.op("pool", lambda e: e.memset(ynbd[0:TP, 0:nu, :], 0.0), writes=Xb_T)
            by = SSB[bcount[0] % 2]
            for u in range(nu):
                Sm_ap, Sm_T = states[u]
                P.op("act", lambda e: e.copy(out=Sb[:, :], in_=Sm_ap), reads=Sm_T, writes=Sb_T)
                bu = bank()
                P.op("pe", lambda e: e.matmul(psb[bu][0:TP, 0:128], AbT[:, u, 0:TP], Sb[:, :], start=True, stop=True),
                     reads=AbT_T + Sb_T, writes=psT[bu])
                P.op("dve", lambda e: e.tensor_tensor(out=Ub[0:TP, u, :], in0=psb[bu][0:TP, 0:128], in1=Vbar[0:TP, u, :], op=ALU.add),
                     reads=psT[bu] + Vbar_T, writes=Ub_T)
                yo = psb[by][0:TP, 128 * u:128 * (u + 1)]
                P.op("pe", lambda e: e.matmul(yo, bdR[:, u, TP:2 * TP], Sb[:, :], start=True, stop=False),
                     reads=bdR_T + Sb_T, writes=psT[by], inc=False)
                P.op("pe", lambda e: e.matmul(yo, ArkT[0:TP, u, 0:TP], tokV[0:TP, u, :], start=False, stop=False),
                     reads=ArkT_T + tokV_T, writes=psT[by], inc=False)
                P.op("pe", lambda e: e.matmul(yo, ArbT[0:TP, u, 0:TP], Ub[0:TP, u, :], start=False, stop=True),
                     reads=ArbT_T + Ub_T, writes=psT[by])
                bs = bank()
                P.op("pe", lambda e: e.matmul(psb[bs][:, 0:128], tokK[0:TP, u, :], tokV[0:TP, u, :], start=True, stop=False),
                     reads=tokK_T + tokV_T, writes=psT[bs], inc=False)
                P.op("pe", lambda e: e.matmul(psb[bs][:, 0:128], tokB[0:TP, u, :], Ub[0:TP, u, :], start=False, stop=True),
                     reads=tokB_T + Ub_T, writes=psT[bs])
                di = ctr["tmpf"] % 2; ctr["tmpf"] += 1
                P.op("act", lambda e: e.activation(out=tmpf[di][:, 0:128], in_=psb[bs][:, 0:128], func=AF.Copy, scale=gam[:, u:u + 1]),
                     reads=psT[bs] + gam_T, writes=[tmpf_T[di]])
                P.op("dve", lambda e: e.scalar_tensor_tensor(out=Sm_ap, in0=Sm_ap, scalar=gam[:, u:u + 1], in1=tmpf[di][:, 0:128],
                                                             op0=ALU.mult, op1=ALU.add),
                     reads=Sm_T + gam_T + [tmpf_T[di]], writes=Sm_T)
            g = bcount[0] % 4; bcount[0] += 1
            sm = small[:, 16 * g:16 * g + 16]; sT = [small_T[g]]
            Y3 = psb[by][0:TP, 0:128 * nu].rearrange("p (u c) -> p u c", c=128)
            P.op("dve", lambda e: e.reduce_sum(out=sm[0:TP, 0:nu], in_=Y3, axis=AX.X), reads=psT[by], writes=sT)
            di = ctr["tmpf"] % 2; ctr["tmpf"] += 1
            sqv = tmpf[di][0:TP, 0:128 * nu].rearrange("p (u c) -> p u c", c=128)
            P.op("act", lambda e: e.activation(out=sqv, in_=Y3, func=AF.Square), reads=psT[by], writes=[tmpf_T[di]])
            P.op("dve", lambda e: e.reduce_sum(out=sm[0:TP, 4:4 + nu], in_=sqv, axis=AX.X), reads=[tmpf_T[di]], writes=sT)
            P.op("dve", lambda e: e.tensor_scalar(out=sm[0:TP, 0:nu], in0=sm[0:TP, 0:nu], scalar1=1.0 / 64, scalar2=None, op0=ALU.mult),
                 reads=sT, writes=sT)
            P.op("dve", lambda e: e.tensor_tensor(out=sm[0:TP, 8:8 + nu], in0=sm[0:TP, 0:nu], in1=sm[0:TP, 0:nu], op=ALU.mult),
                 reads=sT, writes=sT)
            P.op("dve", lambda e: e.scalar_tensor_tensor(out=sm[0:TP, 4:4 + nu], in0=sm[0:TP, 4:4 + nu], scalar=1.0 / 64, in1=sm[0:TP, 8:8 + nu],
                                                         op0=ALU.mult, op1=ALU.subtract),
                 reads=sT, writes=sT)
            P.op("dve", lambda e: e.tensor_scalar(out=sm[0:TP, 4:4 + nu], in0=sm[0:TP, 4:4 + nu], scalar1=GN_EPS, scalar2=None, op0=ALU.add),
                 reads=sT, writes=sT)
            P.op("act", lambda e: e.activation(out=sm[0:TP, 4:4 + nu], in_=sm[0:TP, 4:4 + nu], func=AF.Ln), reads=sT, writes=sT)
            P.op("act", lambda e: e.activation(out=sm[0:TP, 4:4 + nu], in_=sm[0:TP, 4:4 + nu], func=AF.Exp, scale=-0.5), reads=sT, writes=sT)
            for hh in range(2):
                rs = slice(L * hh, L * hh + L)
                cs = slice(64 * hh, 64 * hh + 64)
                P.op("dve", lambda e: e.tensor_tensor(out=ynbd[rs, 0:nu, cs], in0=Y3[rs, :, cs],
                                                      in1=sm[rs, 0:nu].unsqueeze(2).broadcast_to([L, nu, 64]), op=ALU.subtract),
                     reads=psT[by] + sT, writes=Xb_T)
                P.op("dve", lambda e: e.tensor_tensor(out=ynbd[rs, 0:nu, cs], in0=ynbd[rs, 0:nu, cs],
                                                      in1=sm[rs, 4:4 + nu].unsqueeze(2).broadcast_to([L, nu, 64]), op=ALU.mult),
                     reads=Xb_T + sT, writes=Xb_T)
            b_ = bank()
            pv = psb[b_][:, :].bitcast(BF16)
            for u in range(nu):
                P.op("pe", lambda e: e.transpose(pv[:, TP * u:TP * (u + 1)], ynbd[0:TP, u, :], identb[0:TP, 0:TP]),
                     reads=Xb_T + [cT], writes=psT[b_], inc=(u == nu - 1))
            for hh in range(2):
                ps_ = slice(64 * hh, 64 * hh + 64)
                P.op("act", lambda e: e.copy(out=ynT[ps_, cb:cb + ncol].rearrange("p (u l) -> p u l", l=L),
                                             in_=pv[ps_, 0:TP * nu].rearrange("p (u c) -> p u c", c=TP)[:, :, L * hh:L * hh + L]),
                     reads=psT[b_], writes=ynT_T)

        def state_out(src_ap, src_T, dst):
            b_ = bank()
            P.op("pe", lambda e: e.transpose(psb[b_][:, 0:128], src_ap, identf[:]), reads=src_T + [cT], writes=psT[b_])
            di = ctr["tmpf"] % 2; ctr["tmpf"] += 1
            so, so_T = tmpf[di], [tmpf_T[di]]
            for hh in range(2):
                ps_ = slice(64 * hh, 64 * hh + 64)
                P.op("act", lambda e: e.copy(out=so[ps_, 0:64], in_=psb[b_][ps_, 64 * hh:64 * hh + 64]), reads=psT[b_], writes=so_T)
            P.dma("sp", dst, so[:, 0:64], reads=so_T, is_output=True)

        for j in range(DC):
            P.dma("pool", w2s[0:96, :], w2_d[:, 128 * j:128 * (j + 1)], writes=w2s_T, semt=w2s_T[0])
            P.dma("pool", a2s[0:96, :], a2_d[:, 128 * j:128 * (j + 1)], writes=a2s_T, semt=a2s_T[0])
            P.dma("pool", g2s[:, :, :], g2_d[:, :, 128 * j:128 * (j + 1)], writes=g2s_T, semt=g2s_T[0])
            for (wdram, n, dst, dst_T) in ((wr_d, 0, rT_, rT_T), (wk_d, 2, kT_, kT_T), (wv_d, 3, vT_, vT_T)):
                mixed_linear(wdram[j], n, 128, lambda b, c0, cn, dst=dst, dst_T=dst_T: copy(
                    ev_eng(), dst[:, c0:c0 + cn], psb[b][:, 0:cn], psT[b], dst_T))
            for (c0, cn) in blks:
                b = bank()
                P.op("pe", lambda e: e.matmul(psb[b][:, 0:cn], w2s[0:96, :], lora[0:96, 0, c0:c0 + cn], start=True, stop=True),
                     reads=w2s_T + lora_T, writes=psT[b])
                P.op("act", lambda e: e.activation(out=lwT[:, c0:c0 + cn], in_=psb[b][:, 0:cn], func=AF.Exp, bias=cc("w0", j), scale=1.0),
                     reads=psT[b] + [consts_T], writes=lwT_T)
                ts = ctr["tmpf"] % 2; ctr["tmpf"] += 1
                tf, tf_T = tmpf[ts], [tmpf_T[ts]]
                P.op("dve", lambda e: e.tensor_scalar(out=tf[:, 0:cn], in0=lwT[:, c0:c0 + cn], scalar1=1.0, scalar2=None, op0=ALU.add),
                     reads=lwT_T, writes=tf_T)
                P.op("dve", lambda e: e.reciprocal(out=tf[:, 0:cn], in_=tf[:, 0:cn]), reads=tf_T, writes=tf_T)
                P.op("dve", lambda e: e.scalar_tensor_tensor(out=lwT[:, c0:c0 + cn], in0=lwT[:, c0:c0 + cn], scalar=-math.exp(-0.5), in1=tf[:, 0:cn],
                                                             op0=ALU.mult, op1=ALU.mult),
                     reads=lwT_T + tf_T, writes=lwT_T)
                b = bank()
                P.op("pe", lambda e: e.matmul(psb[b][:, 0:cn], a2s[0:96, :], lora[0:96, 1, c0:c0 + cn], start=True, stop=True),
                     reads=a2s_T + lora_T, writes=psT[b])
                P.op("act", lambda e: e.activation(out=aT_[:, c0:c0 + cn], in_=psb[b][:, 0:cn], func=AF.Sigmoid, bias=cc("a0", j), scale=1.0),
                     reads=psT[b] + [consts_T], writes=aT_T)
            P.op("dve", lambda e: e.tensor_scalar(out=kkn[:, 0:N], in0=kT_[:, 0:N], scalar1=cc("kk", j), scalar2=None, op0=ALU.mult),
                 reads=kT_T + [consts_T], writes=kkn_T)
            s = ctr["sq"] % 2; ctr["sq"] += 1
            P.op("act", lambda e: e.activation(out=sq[s][:, 0:N], in_=kkn[:, 0:N], func=AF.Square), reads=kkn_T, writes=[sq_T[s]])
            for (c0, cn) in blks:
                b = bank()
                P.op("pe", lambda e: e.matmul(psb[b][:, 0:cn], bdones[:], sq[s][:, c0:c0 + cn], start=True, stop=True),
                     reads=[sq_T[s], cT], writes=psT[b])
                ts = ctr["tmpf"] % 2; ctr["tmpf"] += 1
                tf, tf_T = tmpf[ts], [tmpf_T[ts]]
                P.op("dve", lambda e: e.tensor_scalar(out=tf[:, 0:cn], in0=psb[b][:, 0:cn], scalar1=1e-30, scalar2=None, op0=ALU.add),
                     reads=psT[b], writes=tf_T)
                P.op("act", lambda e: e.activation(out=tf[:, 0:cn], in_=tf[:, 0:cn], func=AF.Ln), reads=tf_T, writes=tf_T)
                P.op("act", lambda e: e.activation(out=tf[:, 0:cn], in_=tf[:, 0:cn], func=AF.Exp, scale=-0.5), reads=tf_T, writes=tf_T)
                P.op("dve", lambda e: e.tensor_tensor(out=kkn[:, c0:c0 + cn], in0=kkn[:, c0:c0 + cn], in1=tf[:, 0:cn], op=ALU.mult),
                     reads=kkn_T + tf_T, writes=kkn_T)
                P.op("dve", lambda e: e.tensor_scalar(out=tf[:, 0:cn], in0=aT_[:, c0:c0 + cn], scalar1=-1.0, scalar2=cc("ka", j), op0=ALU.add, op1=ALU.mult),
                     reads=aT_T + [consts_T], writes=tf_T)
                P.op("dve", lambda e: e.tensor_scalar(out=tf[:, 0:cn], in0=tf[:, 0:cn], scalar1=1.0, scalar2=None, op0=ALU.add),
                     reads=tf_T, writes=tf_T)
                P.op("dve", lambda e: e.tensor_tensor(out=kT_[:, c0:c0 + cn], in0=kT_[:, c0:c0 + cn], in1=tf[:, 0:cn], op=ALU.mult),
                     reads=kT_T + tf_T, writes=kT_T)
            zero_bd()
            nb_prompt = npr // 256
            for bi_ in range(nb_prompt):
                batch(j, 256 * bi_, 64, 4, [(Smast[:, j, :], [Smast_T[j]])] * 4)
            if has_sample:
                zero_bd()
                for u in range(4):
                    di = ctr["tmpf"] % 2; ctr["tmpf"] += 1
                    si_, si_T = tmpf[di], [tmpf_T[di]]
                    P.dma("sp", si_[:, 256:320], swkv[u, j], writes=si_T)
                    P.op("dve", lambda e: e.memset(si_[:, 0:128], 0.0), writes=si_T)
                    for hh in range(2):
                        ps_ = slice(64 * hh, 64 * hh + 64)
                        P.op("dve", lambda e: e.tensor_copy(out=si_[ps_, 64 * hh:64 * hh + 64], in_=si_[ps_, 256:320]), reads=si_T, writes=si_T)
                    b_ = bank()
                    P.op("pe", lambda e: e.transpose(psb[b_][:, 0:128], si_[:, 0:128], identf[:]), reads=si_T + [cT], writes=psT[b_])
                    P.op("act", lambda e: e.copy(out=Ss[:, u, :], in_=psb[b_][:, 0:128]), reads=psT[b_], writes=Ss_T)
                batch(j, npr, 32, 4, [(Ss[:, u, :], Ss_T) for u in range(4)])
                for u in range(4):
                    state_out(Ss[:, u, :], Ss_T, wkvs_d[u, j])
            if gi == 2:
                state_out(Smast[:, j, :], [Smast_T[j]], wkvp_d[j])
            s = ctr["sq"] % 2; ctr["sq"] += 1
            P.op("dve", lambda e: e.scalar_tensor_tensor(out=sq[s][:, 0:N], in0=rT_[:, 0:N], scalar=cc("rk", j), in1=kT_[:, 0:N],
                                                         op0=ALU.mult, op1=ALU.mult),
                 reads=rT_T + kT_T + [consts_T], writes=[sq_T[s]])
            for (c0, cn) in blks:
                b = bank()
                P.op("pe", lambda e: e.matmul(psb[b][:, 0:cn], bdones[:], sq[s][:, c0:c0 + cn], start=True, stop=True),
                     reads=[sq_T[s], cT], writes=psT[b])
                ts = ctr["tmpf"] % 2; ctr["tmpf"] += 1
                tf, tf_T = tmpf[ts], [tmpf_T[ts]]
                P.op("dve", lambda e: e.tensor_tensor(out=tf[:, 0:cn], in0=psb[b][:, 0:cn], in1=vT_[:, c0:c0 + cn], op=ALU.mult),
                     reads=psT[b] + vT_T, writes=tf_T)
                ts2 = ctr["tmpf"] % 2; ctr["tmpf"] += 1
                tg, tg_T = tmpf[ts2], [tmpf_T[ts2]]
                P.op("dve", lambda e: e.tensor_scalar(out=tg[:, 0:cn], in0=ynT[:, c0:c0 + cn], scalar1=cc("lnw", j), scalar2=cc("lnb", j),
                                                      op0=ALU.mult, op1=ALU.add),
                     reads=ynT_T + [consts_T], writes=tg_T)
                P.op("dve", lambda e: e.tensor_tensor(out=tf[:, 0:cn], in0=tf[:, 0:cn], in1=tg[:, 0:cn], op=ALU.add),
                     reads=tf_T + tg_T, writes=tf_T)
                bg = bank()
                for kt in range(2):
                    P.op("pe", lambda e: e.matmul(psb[bg][:, 0:cn], g2s[:, kt, :], lora[:, 2 + kt, c0:c0 + cn],
                                                  start=(kt == 0), stop=(kt == 1)),
                         reads=g2s_T + lora_T, writes=psT[bg], inc=(kt == 1))
                P.op("dve", lambda e: e.tensor_tensor(out=ygT[:, j, c0:c0 + cn], in0=tf[:, 0:cn], in1=psb[bg][:, 0:cn], op=ALU.mult),
                     reads=tf_T + psT[bg], writes=yg_T)

        aux_release(dslot_T[0:2], aux0); aux_release(dslot_T[2:4], aux1); aux_release([rstd_T], aux2)
        return ygT, yg_T

    def rwkv_full(gi, N, npr, has_sample):
        ygT, yg_T = rwkv(gi, N, npr, has_sample)
        if dbg == "yg":
            for c in range(DC):
                P.op("act", lambda e: e.copy(out=xT[:, c, 0:N], in_=ygT[:, c, 0:N]), reads=yg_T, writes=[xT_T[c]])
            return
        P.op("pool", lambda e: e.tensor_copy(out=hT[:, :, 0:1], in_=hT[:, :, npr:npr + 1]), reads=hT_T, writes=hT_T)
        blks = blocks(N)
        set_pool(6)
        ssb = SSB
        pend = None
        for dch in range(DC):
            ws, wT = load_w(wro_d[dch])
            pb = [bank() for _ in blks]
            for bi, (c0, cn) in enumerate(blks):
                for kt in range(DC):
                    P.op("pe", lambda e: e.matmul(psb[pb[bi]][:, 0:cn], ws[:, kt, :], ygT[:, kt, c0:c0 + cn],
                                                  start=(kt == 0), stop=(kt == DC - 1)),
                         reads=[wT] + yg_T, writes=psT[pb[bi]], inc=(kt == DC - 1))
            if pend is not None:
                pend()
            pend = out_evac_ss(dch, N, pb, ssb, dch == 0, dch == DC - 1)
        pend()
        postnorm_add(1, 3, N, ssb, 1.0)

    for gi, (p0, npr, has_s) in enumerate(GROUPS[:ngroups]):
        N = npr + (128 if has_s else 0)
        for c in range(DC):
            P.dma("sp", xT[:, c, 0:npr], xp[:, c, p0:p0 + npr], writes=[xT_T[c]])
            if has_s:
                P.dma("sp", xT[:, c, npr:npr + 128], xs[:, c, :], writes=[xT_T[c]])
        for l in range(nlayers):
            ffn(l, 0, N)
            if dbg == f"ffn{l}0" and gi == 0:
                break
            if l == 0:
                attention(gi, N, npr, has_s)
            else:
                rwkv_full(gi, N, npr, has_s)
            if dbg in (f"mix{l}", "yg") and gi == 0 and (dbg != "yg" or l == 1):
                break
            ffn(l, 1, N)
        for c in range(DC):
            P.dma("sp", yT_d[:, c, p0:p0 + npr], xT[:, c, 0:npr], reads=[xT_T[c]], is_output=True)
            if has_s:
                P.dma("sp", yT_d[:, c, SEQ:SEQ + 128], xT[:, c, npr:npr + 128], reads=[xT_T[c]], is_output=True)
    P.finish()
    stats = dict(ops=P.n_ops, waits=P.n_waits, sems=P.nsem, cnt=dict(P.cnt))
    P.close()
    return nc, stats


def prep_shared(inp):
    f = lambda a: np.ascontiguousarray(np.asarray(a, dtype=np.float32))
    sh = {}
    sh["wg"] = np.stack([np.stack([w_chunks(f(inp["ffn_w_gate"][l, s])) for s in range(2)]) for l in range(2)])
    sh["wu"] = np.stack([np.stack([w_chunks(f(inp["ffn_w_up"][l, s])) for s in range(2)]) for l in range(2)])
    wd = np.stack([np.stack([w_chunks(f(inp["ffn_w_down"][l, s])) for s in range(2)]) for l in range(2)])
    sh["wd"] = np.ascontiguousarray(wd.reshape(2, 2, DC, 128, 4, 11, 128).transpose(0, 1, 2, 4, 3, 5, 6))
    wqkv = f(inp["att_w_qkv"][0])
    bqkv = f(inp["att_b_qkv"][0])
    kcols = [np.concatenate([wqkv[:, 2048 + 64 * h:2048 + 64 * (h + 1)]] * 2, axis=1) for h in range(4)]
    wext = np.concatenate([wqkv[:, :2048]] + kcols, axis=1)
    sh["wqkv"] = w_chunks(wext)
    bext = np.concatenate([bqkv[:2048]] + [np.concatenate([bqkv[2048 + 64 * h:2048 + 64 * (h + 1)]] * 2) for h in range(4)])
    sh["wkvt"] = w_chunks(wqkv[:, 2048:2560])
    sh["bkv"] = bqkv[2048:2560].reshape(1, 512).copy()
    sh["sinks"] = f(inp["att_sinks"]).reshape(1, 32).copy()
    sh["table"] = f(inp["rel_table"])
    sh["wao"] = w_chunks(f(inp["att_w_o"][0]))
    sh["wr"] = w_chunks(f(inp["rwkv_w_r"][0]))
    sh["wk"] = w_chunks(f(inp["rwkv_w_k"][0]))
    sh["wv"] = w_chunks(f(inp["rwkv_w_v"][0]))
    sh["wro"] = w_chunks(f(inp["rwkv_w_o"][0]))
    sh["w1"] = w_chunks(f(inp["rwkv_w1"][0]), 96)[0]
    sh["a1"] = w_chunks(f(inp["rwkv_a1"][0]), 96)[0]
    sh["g1"] = w_chunks(f(inp["rwkv_g1"][0]))
    sh["w2"] = f(inp["rwkv_w2"][0])
    sh["a2"] = f(inp["rwkv_a2"][0])
    sh["g2"] = np.ascontiguousarray(f(inp["rwkv_g2"][0]).reshape(2, 128, D).transpose(1, 0, 2))
    cols = [fcol(f(inp["norm_g"])).reshape(128, 12 * 16),
            bext.reshape(20, 128).T,
            fcol(f(inp["rwkv_mu"][0])).reshape(128, 6 * 16)]
    for nm in ("rwkv_w0", "rwkv_a0", "rwkv_k_k", "rwkv_k_a"):
        cols.append(fcol(f(inp[nm][0])))
    cols.append(fcol(f(inp["rwkv_r_k"][0]).reshape(D)))
    for nm in ("rwkv_ln_w", "rwkv_ln_b"):
        cols.append(fcol(f(inp[nm][0])))
    sh["consts"] = np.ascontiguousarray(np.concatenate(cols, axis=1))
    for k, v in static_consts().items():
        sh["c_" + k] = v
    return sh


def prep_core(inp, c):
    f = lambda a: np.ascontiguousarray(np.asarray(a, dtype=np.float32))
    m = {}
    m["xp"] = fcol(f(inp["x_prompt"][c]).T.copy()) if False else np.ascontiguousarray(
        f(inp["x_prompt"][c]).T.reshape(DC, 128, SEQ).transpose(1, 0, 2))
    xs = f(inp["x_sample"][4 * c:4 * c + 4]).reshape(128, D)
    m["xs"] = np.ascontiguousarray(xs.T.reshape(DC, 128, 128).transpose(1, 0, 2))
    m["ck"] = f(inp["cache_k"][0, 4 * c:4 * c + 4]).reshape(4, 128, 256)
    m["cv"] = f(inp["cache_v"][0, 4 * c:4 * c + 4]).reshape(4, 128, 256)
    ss = f(inp["state_shift"][0, 4 * c:4 * c + 4, 0])
    m["sshift"] = np.ascontiguousarray(ss.reshape(4, DC, 128).transpose(2, 0, 1))
    m["swkv"] = f(inp["state_wkv"][0, 4 * c:4 * c + 4]).reshape(4, 16, 128, 64)
    return m


_CACHE = {}


def kernel(**inputs):
    if "nc" not in _CACHE:
        _CACHE["nc"] = build()[0]
    nc = _CACHE["nc"]
    sh = prep_shared(inputs)
    in_maps = []
    for c in range(NCORE):
        m = dict(sh)
        m.update(prep_core(inputs, c))
        in_maps.append(m)
    res = run_bass_kernel_spmd(nc, in_maps, core_ids=list(range(NCORE)))
    R = res.results
    y_prompt = np.zeros((8, SEQ, D), np.float32)
    y_sample = np.zeros((32, 32, D), np.float32)
    k_prompt = np.zeros((1, 8, 128, 4, 64), np.float32)
    v_prompt = np.zeros((1, 8, 128, 4, 64), np.float32)
    k_sample = np.zeros((1, 32, 32, 4, 64), np.float32)
    v_sample = np.zeros((1, 32, 32, 4, 64), np.float32)
    shift_prompt = np.zeros((1, 8, 1, D), np.float32)
    wkv_prompt = np.zeros((1, 8, 32, 64, 64), np.float32)
    shift_sample = np.zeros((1, 32, 1, D), np.float32)
    wkv_sample = np.zeros((1, 32, 32, 64, 64), np.float32)
    for c in range(NCORE):
        r = R[c]
        yT = np.asarray(r["yT"])
        y = yT.transpose(2, 1, 0).reshape(SEQ + 128, D)
        y_prompt[c] = y[:SEQ]
        y_sample[4 * c:4 * c + 4] = y[SEQ:].reshape(4, 32, D)
        k_prompt[0, c] = np.asarray(r["kp"]).reshape(128, 4, 64)
        v_prompt[0, c] = np.asarray(r["vp"]).reshape(128, 4, 64)
        k_sample[0, 4 * c:4 * c + 4] = np.asarray(r["ks"]).reshape(4, 32, 4, 64)
        v_sample[0, 4 * c:4 * c + 4] = np.asarray(r["vs"]).reshape(4, 32, 4, 64)
        shift_prompt[0, c, 0] = np.asarray(r["shp"]).T.reshape(D)
        shift_sample[0, 4 * c:4 * c + 4, 0] = np.asarray(r["shs"]).transpose(1, 2, 0).reshape(4, D)
        wkv_prompt[0, c] = np.asarray(r["wkvp"]).reshape(32, 64, 64)
        wkv_sample[0, 4 * c:4 * c + 4] = np.asarray(r["wkvs"]).reshape(4, 32, 64, 64)
    return (y_prompt, y_sample, k_prompt, v_prompt, k_sample, v_sample,
            shift_prompt, wkv_prompt, shift_sample, wkv_sample)
```

```python
import contextlib
import math
import numpy as np
import concourse.bass as bass
import concourse.mybir as mybir
from concourse.bass_utils import run_bass_kernel_spmd

F32 = mybir.dt.float32
BF16 = mybir.dt.bfloat16
AF = mybir.ActivationFunctionType
ALU = mybir.AluOpType
AX = mybir.AxisListType

D = 2048
DC = 16
FFD = 5632
FC = 44
NCORE = 8
SEQ = 2048
NH = 32
HD = 64
WINDOW = 128
N_BUCKETS = 32
MAX_DISTANCE = 128
RMS_EPS = 1e-6
GN_EPS = 64 * 1e-5
NEG = -1.0e30
GROUPS = [(0, 768, False), (768, 768, False), (1536, 512, True)]
NMAX = 768


class T:
    __slots__ = ("name", "w", "r", "dsem", "dtot", "bank")

    def __init__(self, name):
        self.name = name
        self.w = {}
        self.r = {}
        self.dsem = None
        self.dtot = 0
        self.bank = None


def TL(name, n):
    return [T(f"{name}{i}") for i in range(n)]


class Prog:
    COMPUTE = ("pe", "act", "dve", "pool")

    def __init__(self, nc, strict_same=True):
        self.nc = nc
        self.es = contextlib.ExitStack()
        self.eng = {"pe": nc.tensor, "act": nc.scalar, "dve": nc.vector,
                    "pool": nc.gpsimd, "sp": nc.sync}
        self.sem = {}
        self.cnt = {}
        for e in self.COMPUTE:
            self.sem[e] = self.es.enter_context(nc.semaphore("s_" + e))
            self.cnt[e] = 0
        self.seen = {e: {} for e in self.eng}
        self.strict_same = strict_same
        self.nsem = 0
        self.n_ops = 0
        self.n_waits = 0
        self.out_events = []
        self.uid = 0

    def sb(self, name, shape, dt):
        return self.es.enter_context(self.nc.sbuf_tensor("sb_" + name, list(shape), dt))

    def ps(self, name, shape, dt=F32):
        return self.es.enter_context(self.nc.psum_tensor("ps_" + name, list(shape), dt))

    def newsem(self, name):
        self.nsem += 1
        self.uid += 1
        return self.es.enter_context(self.nc.semaphore(f"{name}_{self.uid}"))

    def _wait(self, e, ev):
        sem, val = ev
        k = id(sem)
        if self.seen[e].get(k, 0) >= val:
            return
        self.seen[e][k] = val
        self.eng[e].wait_ge(sem, val)
        self.n_waits += 1

    def _deps(self, e, reads, writes):
        own = id(self.sem[e]) if e in self.sem else None
        skip_own = (e == "pe") or (not self.strict_same)
        for t in reads:
            for k, ev in t.w.items():
                if k == own and skip_own:
                    continue
                self._wait(e, ev)
        for t in writes:
            for k, ev in t.w.items():
                if k == own and skip_own:
                    continue
                self._wait(e, ev)
            for k, ev in t.r.items():
                if k == own and skip_own:
                    continue
                self._wait(e, ev)
        for t in list(reads) + list(writes):
            if t.bank is not None:
                for k, ev in t.bank.w.items():
                    if k != own:
                        self._wait(e, ev)

    def _record(self, ev, reads, writes):
        k = id(ev[0])
        for t in reads:
            t.r[k] = ev
            if t.bank is not None:
                t.bank.w = {k: ev}
        for t in writes:
            t.w = {k: ev}
            t.r = {}
            if t.bank is not None:
                t.bank.w = {k: ev}

    def op(self, e, fn, reads=(), writes=(), inc=True):
        self._deps(e, reads, writes)
        ins = fn(self.eng[e])
        self.n_ops += 1
        if inc:
            self.cnt[e] += 1
            ins.then_inc(self.sem[e], 1)
            ev = (self.sem[e], self.cnt[e])
        else:
            ev = (self.sem[e], self.cnt[e] + 1)
        self._record(ev, reads, writes)
        return ins

    def dma(self, q, out_ap, in_ap, reads=(), writes=(), semt=None, is_output=False, concurrent=False, **kw):
        if semt is None:
            semt = writes[0] if writes else reads[0]
        if semt.dsem is None:
            semt.dsem = self.newsem("d")
        if concurrent:
            k = id(semt.dsem)
            saved = [(t, t.w.pop(k)) for t in writes if k in t.w]
            self._deps(q, reads, writes)
            for t, ev in saved:
                t.w[k] = ev
        else:
            self._deps(q, reads, writes)
        semt.dtot += 16
        ins = self.eng[q].dma_start(out=out_ap, in_=in_ap, **kw)
        ins.then_inc(semt.dsem, 16)
        self.n_ops += 1
        ev = (semt.dsem, semt.dtot)
        self._record(ev, reads, writes)
        if is_output:
            self.out_events.append(ev)
        return ins

    def finish(self):
        last = {}
        for sem, val in self.out_events:
            k = id(sem)
            if k not in last or last[k][1] < val:
                last[k] = (sem, val)
        for ev in last.values():
            self._wait("sp", ev)
        for e in self.COMPUTE:
            if self.cnt[e] > 0:
                self._wait("sp", (self.sem[e], self.cnt[e]))

    def close(self):
        self.es.close()


def w_chunks(w, cw=128):
    K, M = w.shape
    return np.ascontiguousarray(w.reshape(K // 128, 128, M // cw, cw).transpose(2, 1, 0, 3))


def fcol(v):
    s = v.shape[:-1]
    a = v.reshape(*s, DC, 128)
    a = np.moveaxis(a, -1, 0)
    return np.ascontiguousarray(a)


def t5_bucket_np(rel):
    nb = N_BUCKETS // 2
    max_exact = nb // 2
    offset = np.where(rel > 0, nb, 0)
    n = np.abs(rel)
    nf = np.maximum(n, 1).astype(np.float32)
    large = max_exact + (np.log(nf / np.float32(max_exact)) / np.float32(math.log(MAX_DISTANCE / max_exact))
                         * np.float32(nb - max_exact)).astype(np.int32)
    large = np.minimum(large, nb - 1)
    return offset + np.where(n < max_exact, n, large)


def static_consts():
    c = {}
    c["ident"] = np.eye(128, dtype=np.float32)
    i = np.arange(128)
    for L in (64, 32):
        same = (i[:, None] // L) == (i[None, :] // L)
        c[f"tri{L}"] = (same & (i[:, None] <= i[None, :])).astype(np.float32)
        ii = i % L
        c[f"mstrict{L}"] = (ii[:, None] < ii[None, :]).astype(np.float32)
        c[f"mincl{L}"] = (ii[:, None] <= ii[None, :]).astype(np.float32)
        b = 1
        lv = 0
        while b < L:
            c[f"lv{L}_{lv}"] = ((ii[:, None] // (2 * b) == ii[None, :] // (2 * b)) & ((ii[None, :] // b) % 2 == 1)
                               & ((ii[:, None] // b) % 2 == 0)).astype(np.float32)
            b *= 2
            lv += 1
    c["bdones"] = ((i[:, None] // 64) == (i[None, :] // 64)).astype(np.float32)
    r = np.arange(255)
    bk = t5_bucket_np((r - 191).astype(np.int32))
    oh = np.zeros((32, 255), np.float32)
    oh[bk, r] = 1.0
    c["onehot"] = oh
    return c


def build(ngroups=3, nlayers=2, dbg=None):
    nc = bass.Bass("TRN2", target_bir_lowering=False)
    import os as _os
    P = Prog(nc, strict_same=(_os.environ.get("K_STRICT", "1") == "1"))

    def din(name, shape):
        return nc.dram_tensor(name, list(shape), F32, kind="ExternalInput").ap()

    def dout(name, shape):
        return nc.dram_tensor(name, list(shape), F32, kind="ExternalOutput").ap()

    xp = din("xp", [128, DC, SEQ])
    xs = din("xs", [128, DC, 128])
    ck = din("ck", [4, 128, 256])
    cv = din("cv", [4, 128, 256])
    sshift = din("sshift", [128, 4, DC])
    swkv = din("swkv", [4, 16, 128, 64])
    NCONST = 12 * 16 + 20 + 6 * 16 + 7 * 16
    consts_d = din("consts", [128, NCONST])
    wg_d = din("wg", [2, 2, FC, 128, DC, 128])
    wu_d = din("wu", [2, 2, FC, 128, DC, 128])
    wd_d = din("wd", [2, 2, DC, 4, 128, 11, 128])
    wqkv_d = din("wqkv", [20, 128, DC, 128])
    wkvt_d = din("wkvt", [4, 128, DC, 128])
    bkv_d = nc.dram_tensor("bkv", [1, 512], F32, kind="ExternalInput")
    sinks_d = nc.dram_tensor("sinks", [1, 32], F32, kind="ExternalInput")
    table_d = din("table", [32, 32])
    wao_d = din("wao", [16, 128, DC, 128])
    wr_d = din("wr", [16, 128, DC, 128])
    wk_d = din("wk", [16, 128, DC, 128])
    wv_d = din("wv", [16, 128, DC, 128])
    wro_d = din("wro", [16, 128, DC, 128])
    w1_d = din("w1", [128, DC, 96])
    a1_d = din("a1", [128, DC, 96])
    g1_d = din("g1", [2, 128, DC, 128])
    w2_d = din("w2", [96, D])
    a2_d = din("a2", [96, D])
    g2_d = din("g2", [128, 2, D])
    cst = {k: din("c_" + k, v.shape) for k, v in static_consts().items()}
    fscr = nc.dram_tensor("fscr", [32, 255], F32, kind="Internal")

    yT_d = dout("yT", [128, DC, SEQ + 128])
    kp_d = dout("kp", [128, 256])
    vp_d = dout("vp", [128, 256])
    ks_d = dout("ks", [128, 256])
    vs_d = dout("vs", [128, 256])
    shp_d = dout("shp", [128, DC])
    shs_d = dout("shs", [128, 4, DC])
    wkvp_d = dout("wkvp", [16, 128, 64])
    wkvs_d = dout("wkvs", [4, 16, 128, 64])
    dbg_d = dout("dbg", [128, DC, NMAX]) if dbg else None

    CO = {}
    o = 0
    CO["g"] = o; o += 12 * 16
    CO["bq"] = o; o += 20
    CO["mu"] = o; o += 6 * 16
    for nm in ("w0", "a0", "kk", "ka", "rk", "lnw", "lnb"):
        CO[nm] = o; o += 16
    assert o == NCONST

    xT = P.sb("xT", [128, DC, NMAX], F32); xT_T = TL("xT", DC)
    hT = P.sb("hT", [128, DC, NMAX + 1], BF16); hT_T = TL("hT", DC)
    BIGB = 66 * 1024
    big = P.sb("big", [128, BIGB // 2], BF16)
    big_T = TL("big", FC)
    SL = 768

    def bigv(off_b, shape, dt):
        n = int(np.prod(shape[1:]))
        esz = 4 if dt == F32 else 2
        assert off_b % 4 == 0 and off_b + n * esz <= BIGB, (off_b, shape)
        if dt == F32:
            ap = big[:, off_b // 2: off_b // 2 + n * 2].bitcast(F32)
        else:
            ap = big[:, off_b // 2: off_b // 2 + n]
        if len(shape) == 3:
            ap = ap.rearrange("p (a b) -> p a b", b=shape[2])
        elif len(shape) == 4:
            ap = ap.rearrange("p (a b c) -> p a b c", b=shape[2], c=shape[3])
        t0 = off_b // (SL * 2)
        t1 = (off_b + n * esz - 1) // (SL * 2)
        return ap, big_T[t0:t1 + 1]

    wslot = [P.sb(f"ws{i}", [128, DC, 128], BF16) for i in range(4)]
    wslot_T = TL("ws", 4)
    dsl = P.sb("wds", [128, 4, 11, 128], BF16)
    dslot = [dsl[:, i] for i in range(4)]
    dslot_T = TL("wds", 4)
    wctr = [0, 0]

    consts = P.sb("consts", [128, NCONST], F32); consts_T = T("consts")
    identf = P.sb("identf", [128, 128], F32)
    identb = P.sb("identb", [128, 128], BF16)
    onesb = P.sb("onesb", [128, 128], BF16)
    bdones = P.sb("bdones", [128, 128], BF16)
    tri = {L: P.sb(f"tri{L}", [128, 128], F32) for L in (64, 32)}
    mstrict = {L: P.sb(f"mstrict{L}", [128, 128], BF16) for L in (64, 32)}
    mincl = {L: P.sb(f"mincl{L}", [128, 128], BF16) for L in (64, 32)}
    lvm = {L: [P.sb(f"lv{L}_{i}", [128, 128], BF16) for i in range(6 if L == 64 else 5)] for L in (64, 32)}
    cT = T("cst")
    bkv = P.sb("bkv", [128, 512], F32)
    sinks = P.sb("sinks", [128, 32], F32)
    bias2 = P.sb("bias2", [128, 32, 192], BF16); bias2_T = T("bias2")
    ktc = P.sb("ktc", [128, 4, 128], BF16); ktc_T = T("ktc")
    vbc = P.sb("vbc", [128, 256], BF16); vbc_T = T("vbc")
    Smast = P.sb("Smast", [128, 16, 128], F32); Smast_T = TL("Sm", 16)
    rstd = P.sb("rstd", [128, NMAX], F32); rstd_T = T("rstd")
    sq = [P.sb(f"sq{i}", [128, NMAX], BF16) for i in range(2)]; sq_T = TL("sq", 2)
    tmpf = [P.sb(f"tmpf{i}", [128, 512], F32) for i in range(2)]; tmpf_T = TL("tmpf", 2)
    small = P.sb("small", [128, 64], F32); small_T = TL("small", 4)
    ctr = {"sq": 0, "tmpf": 0, "bank": 0, "q": 0, "ev": 0, "nb": 2}
    SSB = [6, 7]

    psb = [P.ps(f"psb{i}", [128, 512]) for i in range(8)]
    psT = [TL(f"ps{i}_", 4) for i in range(8)]
    for i in range(8):
        bx = T(f"bank{i}")
        for t_ in psT[i]:
            t_.bank = bx

    def set_pool(nb):
        ctr["nb"] = nb

    def bank():
        b = ctr["bank"] % ctr["nb"]
        ctr["bank"] += 1
        return b

    def quarter():
        nbk = 6 - ctr["nb"]
        q = ctr["q"] % (nbk * 4)
        ctr["q"] += 1
        return ctr["nb"] + q % nbk, q // nbk

    def qf(bq):
        b, q = bq
        return psb[b][:, 128 * q:128 * (q + 1)]

    def qb(bq):
        b, q = bq
        return psb[b][:, 128 * q:128 * (q + 1)].bitcast(BF16)

    def qT(bq):
        return [psT[bq[0]][bq[1]]]

    def ev_eng():
        ctr["ev"] += 1
        return "act" if ctr["ev"] % 2 else "dve"

    def copy(e, out, in_, reads, writes):
        if e == "act":
            P.op("act", lambda x: x.copy(out=out, in_=in_), reads=reads, writes=writes)
        else:
            P.op(e, lambda x: x.tensor_copy(out=out, in_=in_), reads=reads, writes=writes)

    def cc(nm, j=None):
        if j is None:
            return consts[:, CO[nm]:CO[nm] + 16]
        return consts[:, CO[nm] + j:CO[nm] + j + 1]

    def gcol(l, n, c=None):
        o0 = CO["g"] + (l * 6 + n) * 16
        if c is None:
            return consts[:, o0:o0 + 16]
        return consts[:, o0 + c:o0 + c + 1]

    def blocks(N):
        out = []
        c0 = 0
        while c0 < N:
            cn = min(512, N - c0)
            out.append((c0, cn))
            c0 += cn
        return out

    P.dma("sp", consts[:], consts_d, writes=[consts_T])
    P.dma("sp", identf[:], cst["ident"], writes=[cT])
    for L in (64, 32):
        P.dma("sp", tri[L][:], cst[f"tri{L}"], writes=[cT])
        P.dma("pool", mstrict[L][:], cst[f"mstrict{L}"], writes=[cT])
        P.dma("pool", mincl[L][:], cst[f"mincl{L}"], writes=[cT])
        for i_, m_ in enumerate(lvm[L]):
            P.dma("pool", m_[:], cst[f"lv{L}_{i_}"], writes=[cT])
    P.dma("pool", identb[:], cst["ident"], writes=[cT])
    P.dma("pool", bdones[:], cst["bdones"], writes=[cT])
    P.dma("sp", bkv[:], bkv_d.ap().partition_broadcast(128), writes=[cT])
    P.dma("sp", sinks[:], sinks_d.ap().partition_broadcast(128), writes=[cT])
    P.op("dve", lambda e: e.memset(onesb[:], 1.0), writes=[cT])
    P.op("dve", lambda e: e.memset(hT[:, :, 0:1], 0.0), writes=hT_T)
    P.op("dve", lambda e: e.memset(Smast[:], 0.0), writes=Smast_T)
    P.op("dve", lambda e: e.memset(ktc[:], 0.0), writes=[ktc_T])
    P.op("dve", lambda e: e.memset(vbc[:], 0.0), writes=[vbc_T])

    def build_bias():
        tb = tmpf[0]; oh = tmpf[1]
        P.dma("sp", tb[0:32, 0:32], table_d, writes=[tmpf_T[0]])
        P.dma("sp", oh[0:32, 0:255], cst["onehot"], writes=[tmpf_T[1]])
        bq = quarter()
        pso = psb[bq[0]][0:32, 0:255]
        P.op("pe", lambda e: e.matmul(pso, tb[0:32, 0:32], oh[0:32, 0:255], start=True, stop=True),
             reads=[tmpf_T[0], tmpf_T[1]], writes=psT[bq[0]])
        fs, fs_T = bigv(0, [128, 256], F32)
        P.op("act", lambda e: e.copy(out=fs[0:32, 0:255], in_=pso), reads=psT[bq[0]], writes=fs_T)
        fT = T("fscr")
        P.dma("sp", fscr.ap(), fs[0:32, 0:255], reads=fs_T, writes=[fT])
        stg, stg_T = bigv(1024, [128, 32, 192], F32)
        for i in range(64):
            src = bass.AP(fscr, 63 - i, [[0, 1], [255, 32], [1, 192]])
            for half in range(2):
                p = half * 64 + i
                P.dma("sp", stg[p:p + 1, :, :], src, reads=[fT], writes=stg_T, semt=stg_T[0], concurrent=True)
        P.op("act", lambda e: e.copy(out=bias2[:, 0:16, :], in_=stg[:, 0:16, :]), reads=stg_T, writes=[bias2_T])
        P.op("dve", lambda e: e.tensor_copy(out=bias2[:, 16:32, :], in_=stg[:, 16:32, :]), reads=stg_T + [bias2_T], writes=[bias2_T])

    build_bias()

    def sumsq_accumulate(src_ap_fn, src_tiles_fn, N, nch, ssb):
        for c in range(nch):
            s = ctr["sq"] % 2; ctr["sq"] += 1
            P.op("act", lambda e: e.activation(out=sq[s][:, 0:N], in_=src_ap_fn(c), func=AF.Square),
                 reads=src_tiles_fn(c), writes=[sq_T[s]])
            for bi, (c0, cn) in enumerate(blocks(N)):
                P.op("pe", lambda e: e.matmul(psb[ssb[bi]][:, 0:cn], onesb[:], sq[s][:, c0:c0 + cn],
                                              start=(c == 0), stop=(c == nch - 1)),
                     reads=[sq_T[s], cT], writes=psT[ssb[bi]], inc=True)

    def rstd_from_ss(N, ssb, eps):
        for bi, (c0, cn) in enumerate(blocks(N)):
            P.op("dve", lambda e: e.tensor_scalar(out=rstd[:, c0:c0 + cn], in0=psb[ssb[bi]][:, 0:cn],
                                                  scalar1=1.0 / D, scalar2=eps, op0=ALU.mult, op1=ALU.add),
                 reads=psT[ssb[bi]], writes=[rstd_T])
        P.op("act", lambda e: e.activation(out=rstd[:, 0:N], in_=rstd[:, 0:N], func=AF.Ln),
             reads=[rstd_T], writes=[rstd_T])
        P.op("act", lambda e: e.activation(out=rstd[:, 0:N], in_=rstd[:, 0:N], func=AF.Exp, scale=-0.5),
             reads=[rstd_T], writes=[rstd_T])

    def prenorm(l, n, N):
        ssb = SSB
        sumsq_accumulate(lambda c: xT[:, c, 0:N], lambda c: [xT_T[c]], N, DC, ssb)
        rstd_from_ss(N, ssb, RMS_EPS)
        for c in range(DC):
            P.op("dve", lambda e: e.scalar_tensor_tensor(out=hT[:, c, 1:1 + N], in0=xT[:, c, 0:N],
                                                         scalar=gcol(l, n, c), in1=rstd[:, 0:N],
                                                         op0=ALU.mult, op1=ALU.mult),
                 reads=[xT_T[c], rstd_T, consts_T], writes=[hT_T[c]])

    def postnorm_add(l, n, N, ssb, weight):
        rstd_from_ss(N, ssb, RMS_EPS)
        for c in range(DC):
            for (c0, cn) in blocks(N):
                s = ctr["tmpf"] % 2; ctr["tmpf"] += 1
                P.op("dve", lambda e: e.scalar_tensor_tensor(out=tmpf[s][:, 0:cn], in0=hT[:, c, 1 + c0:1 + c0 + cn],
                                                             scalar=gcol(l, n, c), in1=rstd[:, c0:c0 + cn],
                                                             op0=ALU.mult, op1=ALU.mult),
                     reads=[hT_T[c], rstd_T, consts_T], writes=[tmpf_T[s]])
                P.op("dve", lambda e: e.scalar_tensor_tensor(out=xT[:, c, c0:c0 + cn], in0=tmpf[s][:, 0:cn],
                                                             scalar=float(weight), in1=xT[:, c, c0:c0 + cn],
                                                             op0=ALU.mult, op1=ALU.add),
                     reads=[tmpf_T[s]], writes=[xT_T[c]])

    def load_w(dram_ap):
        s = wctr[0] % 4; wctr[0] += 1
        P.dma("pool", wslot[s][:], dram_ap, writes=[wslot_T[s]])
        return wslot[s], wslot_T[s]

    def out_evac_ss(c, N, pbanks, ssb, first, last, bias=None):
        s = ctr["sq"] % 2; ctr["sq"] += 1
        for bi, (c0, cn) in enumerate(blocks(N)):
            b = pbanks[bi]
            P.op("act", lambda e: e.copy(out=hT[:, c, 1 + c0:1 + c0 + cn], in_=psb[b][:, 0:cn]),
                 reads=psT[b], writes=[hT_T[c]])
            P.op("act", lambda e: e.activation(out=sq[s][:, c0:c0 + cn], in_=psb[b][:, 0:cn], func=AF.Square),
                 reads=psT[b], writes=[sq_T[s]])
        def pe_part():
            for bi, (c0, cn) in enumerate(blocks(N)):
                P.op("pe", lambda e: e.matmul(psb[ssb[bi]][:, 0:cn], onesb[:], sq[s][:, c0:c0 + cn],
                                              start=first, stop=last),
                     reads=[sq_T[s], cT], writes=psT[ssb[bi]], inc=True)
        return pe_part

    def ffn(l, s, N):
        n_in, n_out = (0, 1) if s == 0 else (4, 5)
        set_pool(6)
        prenorm(l, n_in, N)
        actT = big[:, 0:FC * SL].rearrange("p (f t) -> p f t", t=SL)
        blks = blocks(N)
        for f in range(FC):
            wgs, wgT = load_w(wg_d[l, s, f])
            wus, wuT = load_w(wu_d[l, s, f])
            for (c0, cn) in blks:
                bg = bank(); bu = bank()
                for kt in range(DC):
                    P.op("pe", lambda e: e.matmul(psb[bg][:, 0:cn], wgs[:, kt, :], hT[:, kt, 1 + c0:1 + c0 + cn],
                                                  start=(kt == 0), stop=(kt == DC - 1)),
                         reads=[wgT, hT_T[kt]], writes=psT[bg], inc=(kt == DC - 1))
                for kt in range(DC):
                    P.op("pe", lambda e: e.matmul(psb[bu][:, 0:cn], wus[:, kt, :], hT[:, kt, 1 + c0:1 + c0 + cn],
                                                  start=(kt == 0), stop=(kt == DC - 1)),
                         reads=[wuT, hT_T[kt]], writes=psT[bu], inc=(kt == DC - 1))
                ts = ctr["tmpf"] % 2; ctr["tmpf"] += 1
                P.op("act", lambda e: e.activation(out=tmpf[ts][:, 0:cn], in_=psb[bg][:, 0:cn], func=AF.Silu),
                     reads=psT[bg], writes=[tmpf_T[ts]])
                P.op("dve", lambda e: e.tensor_tensor(out=actT[:, f, c0:c0 + cn], in0=tmpf[ts][:, 0:cn],
                                                      in1=psb[bu][:, 0:cn], op=ALU.mult),
                     reads=[tmpf_T[ts]] + psT[bu], writes=[big_T[f]])
        ssb = SSB
        pend = None
        for d in range(DC):
            pb = [bank() for _ in blks]
            for qr in range(4):
                sl = wctr[1] % 4; wctr[1] += 1
                P.dma("pool", dslot[sl], wd_d[l, s, d, qr], writes=[dslot_T[sl]])
                for bi, (c0, cn) in enumerate(blks):
                    for k in range(11):
                        f = qr * 11 + k
                        P.op("pe", lambda e: e.matmul(psb[pb[bi]][:, 0:cn], dslot[sl][:, k, :], actT[:, f, c0:c0 + cn],
                                                      start=(f == 0), stop=(f == FC - 1)),
                             reads=[dslot_T[sl], big_T[f]], writes=psT[pb[bi]], inc=(f == FC - 1))
            if pend is not None:
                pend()
            pend = out_evac_ss(d, N, pb, ssb, d == 0, d == DC - 1)
        pend()
        postnorm_add(l, n_out, N, ssb, 0.5)

    def attention(gi, N, npr, has_sample):
        l = 0
        ntile = N // 128
        nptile = npr // 128
        set_pool(2)
        prenorm(l, 2, N)
        off = 0
        qTb, qT_T = bigv(off, [128, DC, NMAX], BF16); off += DC * NMAX * 2
        KT, KT_T = bigv(off, [128, 4, 128 + NMAX], BF16); off += 4 * (128 + NMAX) * 2
        Vb, Vb_T = bigv(off, [128, 7, 256], BF16); off += 7 * 256 * 2
        sbufs = []
        for i in range(4):
            a, t = bigv(off, [128, 256], F32); off += 1024
            sbufs.append((a, t))
        pbufs = []
        for i in range(4):
            a, t = bigv(off, [128, 256], BF16); off += 512
            pbufs.append((a, t))
        ptbufs = []
        for i in range(3):
            a, t = bigv(off, [128, 2, 128], BF16); off += 512
            ptbufs.append((a, t))
        stage, stage_T = bigv(off, [128, 512], F32); off += 2048
        if has_sample:
            KTs, KTs_T = bigv(off, [128, 4, 4, 256], BF16); off += 4 * 4 * 256 * 2
            Vc, Vc_T = bigv(off, [128, 4, 256], BF16); off += 4 * 256 * 2
            ssb_s = []
            for i in range(4):
                a, t = bigv(off, [128, 256], F32); off += 1024
                ssb_s.append((a, t))
            ckf, ckf_T = bigv(off, [128, 256], F32); off += 1024
        assert off <= BIGB, off

        P.op("pool", lambda e: e.tensor_copy(out=KT[:, :, 0:128], in_=ktc[:]), reads=[ktc_T], writes=KT_T)
        P.op("pool", lambda e: e.tensor_copy(out=Vb[:, 0, :], in_=vbc[:]), reads=[vbc_T], writes=Vb_T)

        blks = blocks(N)
        for j in range(20):
            ws, wT = load_w(wqkv_d[j])
            for (c0, cn) in blks:
                b = bank()
                for kt in range(DC):
                    P.op("pe", lambda e: e.matmul(psb[b][:, 0:cn], ws[:, kt, :], hT[:, kt, 1 + c0:1 + c0 + cn],
                                                  start=(kt == 0), stop=(kt == DC - 1)),
                         reads=[wT, hT_T[kt]], writes=psT[b], inc=(kt == DC - 1))
                bcol = consts[:, CO["bq"] + j:CO["bq"] + j + 1]
                if j < 16:
                    P.op("act", lambda e: e.activation(out=qTb[:, j, c0:c0 + cn], in_=psb[b][:, 0:cn],
                                                       func=AF.Identity, bias=bcol, scale=1.0),
                         reads=psT[b] + [consts_T], writes=qT_T)
                else:
                    P.op("act", lambda e: e.activation(out=KT[:, j - 16, 128 + c0:128 + c0 + cn], in_=psb[b][:, 0:cn],
                                                       func=AF.Identity, bias=bcol, scale=1.0),
                         reads=psT[b] + [consts_T], writes=KT_T)
        for cchunk in range(4):
            is_k = cchunk < 2
            ws, wT = load_w(wkvt_d[cchunk])
            for t in range(ntile):
                out_tile = (gi == 2) and (t >= nptile - 1)
                if is_k and not out_tile:
                    continue
                bq = quarter()
                for kt in range(DC):
                    P.op("pe", lambda e: e.matmul(qf(bq), hT[:, kt, 1 + 128 * t:1 + 128 * (t + 1)], ws[:, kt, :],
                                                  start=(kt == 0), stop=(kt == DC - 1)),
                         reads=[wT, hT_T[kt]], writes=qT(bq), inc=(kt == DC - 1))
                bsl = bkv[:, 128 * cchunk:128 * (cchunk + 1)]
                if not is_k:
                    P.op("dve", lambda e: e.tensor_tensor(out=Vb[:, 1 + t, 128 * (cchunk - 2):128 * (cchunk - 1)],
                                                          in0=qf(bq), in1=bsl, op=ALU.add),
                         reads=qT(bq) + [cT], writes=Vb_T)
                if out_tile:
                    P.op("dve", lambda e: e.tensor_tensor(out=stage[:, 128 * cchunk:128 * (cchunk + 1)],
                                                          in0=qf(bq), in1=bsl, op=ALU.add),
                         reads=qT(bq) + [cT], writes=stage_T)
                    is_s = (t == nptile)
                    dst = (ks_d if is_s else kp_d) if is_k else (vs_d if is_s else vp_d)
                    co = 128 * (cchunk % 2)
                    P.dma("sp", dst[:, co:co + 128], stage[:, 128 * cchunk:128 * (cchunk + 1)],
                          reads=stage_T, is_output=True)

        def preset(buf, tiles):
            P.op("pool", lambda e: e.memset(buf[:], NEG), writes=tiles)

        for (a, t_) in sbufs:
            preset(a, t_)

        if has_sample:
            for s in range(4):
                preset(ssb_s[s][0], ssb_s[s][1])
                P.dma("pool", Vc[:, s, :], cv[s], writes=Vc_T, semt=Vc_T[0])
                P.dma("sp", ckf[:], ck[s], writes=ckf_T)
                for kvh in range(4):
                    di = ctr["tmpf"] % 2; ctr["tmpf"] += 1
                    dsrc, dsrc_T = tmpf[di], tmpf_T[di]
                    for dup in range(2):
                        P.op("dve", lambda e: e.tensor_copy(out=dsrc[:, 64 * dup:64 * (dup + 1)],
                                                            in_=ckf[:, 64 * kvh:64 * (kvh + 1)]),
                             reads=ckf_T, writes=[dsrc_T])
                    bq = quarter()
                    P.op("pe", lambda e: e.transpose(qf(bq), dsrc[:, 0:128], identf[:]),
                         reads=[dsrc_T, cT], writes=qT(bq))
                    copy("act", KTs[:, s, kvh, 0:128], qf(bq), qT(bq), KTs_T)
                P.op("pool", lambda e: e.tensor_copy(out=KTs[:, s, :, 128:256], in_=KT[:, :, 128 + npr:128 + npr + 128]),
                     reads=KT_T, writes=KTs_T)

        jobs = []

        def stage_a(jb):
            i = jb["i"]; h = jb["h"]; nq = jb["nq"]
            sb, sb_T = jb["sb"]
            b = bank()
            P.op("pe", lambda e: e.matmul(psb[b][0:nq, 0:256], jb["q"], jb["k"], start=True, stop=True),
                 reads=qT_T + jb["k_T"], writes=psT[b][0:2])
            for (r0, r1, oc0, oc1, bc0) in jb["bops"]:
                P.op("dve", lambda e: e.scalar_tensor_tensor(out=sb[r0:r1, oc0:oc1], in0=psb[b][r0:r1, oc0:oc1], scalar=0.125,
                                                             in1=bias2[r0:r1, h, bc0:bc0 + (oc1 - oc0)],
                                                             op0=ALU.mult, op1=ALU.add),
                     reads=psT[b][0:2] + [bias2_T], writes=sb_T)
            g = i % 4
            sm = small[:, 16 * g:16 * g + 16]; sT = [small_T[g]]
            P.op("dve", lambda e: e.reduce_max(out=sm[0:nq, 0:1], in_=sb[0:nq, :], axis=AX.X), reads=sb_T, writes=sT)
            P.op("dve", lambda e: e.tensor_scalar(out=sm[0:nq, 2:3], in0=sm[0:nq, 0:1], scalar1=sinks[0:nq, h:h + 1], scalar2=-1.0,
                                                  op0=ALU.max, op1=ALU.mult),
                 reads=sT + [cT], writes=sT)
            pb_, pb_T = pbufs[i % 4]
            P.op("act", lambda e: e.activation(out=pb_[0:nq, :], in_=sb[0:nq, :], func=AF.Exp, bias=sm[0:nq, 2:3], scale=1.0,
                                               accum_out=sm[0:nq, 3:4]),
                 reads=sb_T + sT, writes=pb_T + sT)
            P.op("act", lambda e: e.activation(out=sm[0:nq, 4:5], in_=sinks[0:nq, h:h + 1], func=AF.Exp, bias=sm[0:nq, 2:3], scale=1.0),
                 reads=sT + [cT], writes=sT)

        def stage_a2(jb):
            i = jb["i"]; nq = jb["nq"]
            g = i % 4
            sm = small[:, 16 * g:16 * g + 16]; sT = [small_T[g]]
            pb_, pb_T = pbufs[i % 4]
            P.op("dve", lambda e: e.tensor_tensor(out=sm[0:nq, 5:6], in0=sm[0:nq, 3:4], in1=sm[0:nq, 4:5], op=ALU.add),
                 reads=sT, writes=sT)
            P.op("dve", lambda e: e.reciprocal(out=sm[0:nq, 6:7], in_=sm[0:nq, 5:6]), reads=sT, writes=sT)
            P.op("dve", lambda e: e.tensor_scalar(out=pb_[0:nq, :], in0=pb_[0:nq, :], scalar1=sm[0:nq, 6:7], scalar2=None, op0=ALU.mult),
                 reads=pb_T + sT, writes=pb_T)

        def stage_b(jb):
            i = jb["i"]; nq = jb["nq"]
            pb_, pb_T = pbufs[i % 4]
            pt_, pt_T = ptbufs[i % 3]
            for kt in range(2):
                bq = quarter()
                P.op("pe", lambda e: e.transpose(qb(bq)[:, 0:nq], pb_[0:nq, 128 * kt:128 * (kt + 1)], identb[0:nq, 0:nq]),
                     reads=pb_T + [cT], writes=qT(bq))
                copy(ev_eng(), pt_[:, kt, 0:nq], qb(bq)[:, 0:nq], qT(bq), pt_T)

        def stage_c(jb):
            i = jb["i"]; nq = jb["nq"]
            pt_, pt_T = ptbufs[i % 3]
            if jb["hh"] == 0:
                jb["pair"]["bq"] = quarter()
            bq = jb["pair"]["bq"]
            r0 = 64 * jb["hh"]
            for kt in range(2):
                P.op("pe", lambda e: e.matmul(qf(bq)[r0:r0 + 64, 0:nq], jb["v"][kt], pt_[:, kt, 0:nq],
                                              start=(kt == 0), stop=(kt == 1)),
                     reads=pt_T + jb["v_T"], writes=qT(bq), inc=(kt == 1))
            if jb["hh"] == 1:
                copy("act", jb["o_dst"], qf(bq)[:, 0:nq], qT(bq), qT_T)

        def add_pair(j, nq, qcols, kfn, k_T, v, v_T, sbpair, bops):
            pair = {}
            for hh in range(2):
                i = len(jobs)
                jobs.append(dict(i=i, h=2 * j + hh, hh=hh, nq=nq, pair=pair,
                                 q=qTb[64 * hh:64 * (hh + 1), j, qcols[0]:qcols[0] + nq], k=kfn(hh), k_T=k_T,
                                 v=v, v_T=v_T, sb=sbpair[i % 2], bops=bops,
                                 o_dst=qTb[:, j, qcols[0]:qcols[0] + nq]))

        for t in range(nptile):
            first_tile = (gi == 0) and t == 0
            if first_tile:
                bops = [(0, 64, 128, 192, 128), (64, 128, 128, 256, 64)]
                sbp = sbufs[0:2]
            else:
                bops = [(0, 64, 0, 192, 0), (64, 128, 64, 256, 0)]
                sbp = sbufs[2:4]
            for j in range(DC):
                kvh = (2 * j) // 8
                add_pair(j, 128, (128 * t,),
                         lambda hh, kvh=kvh, t=t: KT[64 * hh:64 * (hh + 1), kvh, 128 * t:128 * t + 256], KT_T,
                         [Vb[:, t, 64 * kvh:64 * (kvh + 1)], Vb[:, t + 1, 64 * kvh:64 * (kvh + 1)]], Vb_T, sbp, bops)
        if has_sample:
            for j in range(DC):
                kvh = (2 * j) // 8
                for s in range(4):
                    bops = [(0, 32, 0, 128, 0), (0, 32, 128 + 32 * s, 160 + 32 * s, 128)]
                    add_pair(j, 32, (npr + 32 * s,),
                             lambda hh, kvh=kvh, s=s: KTs[64 * hh:64 * (hh + 1), s, kvh, :], KTs_T,
                             [Vc[:, s, 64 * kvh:64 * (kvh + 1)], Vb[:, 1 + nptile, 64 * kvh:64 * (kvh + 1)]],
                             Vc_T + Vb_T, [ssb_s[s], ssb_s[s]], bops)
        nj = len(jobs)
        for step in range(nj + 3):
            if step < nj:
                stage_a(jobs[step])
            if 0 <= step - 1 < nj:
                stage_a2(jobs[step - 1])
            if 0 <= step - 2 < nj:
                stage_b(jobs[step - 2])
            if 0 <= step - 3 < nj:
                stage_c(jobs[step - 3])
        if gi < 2:
            P.op("pool", lambda e: e.tensor_copy(out=ktc[:], in_=KT[:, :, npr:npr + 128]), reads=KT_T, writes=[ktc_T])
            P.op("pool", lambda e: e.tensor_copy(out=vbc[:], in_=Vb[:, nptile, :]), reads=Vb_T, writes=[vbc_T])
        set_pool(6)
        ssb = SSB
        pend = None
        for dch in range(DC):
            ws, wT = load_w(wao_d[dch])
            pb = [bank() for _ in blks]
            for bi, (c0, cn) in enumerate(blks):
                for kt in range(DC):
                    P.op("pe", lambda e: e.matmul(psb[pb[bi]][:, 0:cn], ws[:, kt, :], qTb[:, kt, c0:c0 + cn],
                                                  start=(kt == 0), stop=(kt == DC - 1)),
                         reads=[wT] + qT_T, writes=psT[pb[bi]], inc=(kt == DC - 1))
            if pend is not None:
                pend()
            pend = out_evac_ss(dch, N, pb, ssb, dch == 0, dch == DC - 1)
        pend()
        postnorm_add(l, 3, N, ssb, 1.0)

    def rwkv(gi, N, npr, has_sample):
        l = 1
        set_pool(6)
        prenorm(l, 2, N)
        blks = blocks(N)
        st = {"off": 0}

        def h_last(col, dst_ap):
            di = ctr["tmpf"] % 2; ctr["tmpf"] += 1
            so, so_T = tmpf[di], [tmpf_T[di]]
            P.op("dve", lambda e: e.scalar_tensor_tensor(out=so[:, 0:DC], in0=xT[:, :, col], scalar=rstd[:, col:col + 1], in1=gcol(l, 2),
                                                         op0=ALU.mult, op1=ALU.mult),
                 reads=xT_T + [rstd_T, consts_T], writes=so_T)
            P.dma("sp", dst_ap, so[:, 0:DC], reads=so_T, is_output=True)
        def buf(shape, dt):
            esz = 4 if dt == F32 else 2
            a_, t_ = bigv(st["off"], list(shape), dt)
            st["off"] = (st["off"] + int(np.prod(shape[1:])) * esz + 3) // 4 * 4
            return a_, t_

        if gi == 2:
            h_last(npr - 1, shp_d)
            for s_ in range(4):
                h_last(npr + 32 * s_ + 31, shs_d[:, s_, :])
        U = 4
        NB = N
        ygT, yg_T = buf([128, DC, NB], BF16)
        if has_sample:
            hsh, hsh_T = buf([128, DC, 128], BF16)
        lora, lora_T = buf([128, 4, NB], BF16)
        w2s, w2s_T = buf([128, 128], BF16); a2s, a2s_T = buf([128, 128], BF16); g2s, g2s_T = buf([128, 2, 128], BF16)
        rT_, rT_T = buf([128, NB], BF16); kT_, kT_T = buf([128, NB], BF16); vT_, vT_T = buf([128, NB], BF16)
        aT_, aT_T = buf([128, NB], BF16); kkn, kkn_T = buf([128, NB], BF16); ynT, ynT_T = buf([128, NB], BF16)
        lwT, lwT_T = buf([128, NB], F32)
        (wa, wa_T), (wb, wb_T) = buf([128, DC, 128], BF16), buf([128, DC, 128], BF16)

        def aux_views(flat_bf16, region_T, shapes):
            outs = []
            o_ = 0
            for shp in shapes:
                n_ = int(np.prod(shp[1:]))
                ap_ = flat_bf16[:, o_:o_ + n_]
                if len(shp) == 3:
                    ap_ = ap_.rearrange("p (a b) -> p a b", b=shp[2])
                t_ = T("aux")
                for rt_ in region_T:
                    for k_, ev_ in rt_.w.items():
                        if k_ not in t_.w or t_.w[k_][1] < ev_[1]:
                            t_.w[k_] = ev_
                    for k_, ev_ in rt_.r.items():
                        if k_ not in t_.r or t_.r[k_][1] < ev_[1]:
                            t_.r[k_] = ev_
                outs.append((ap_, [t_]))
                o_ += n_
            return outs

        def aux_release(region_T, subs):
            for rt_ in region_T:
                for _, tl in subs:
                    for t_ in tl:
                        for k_, ev_ in list(t_.w.items()) + list(t_.r.items()):
                            if k_ not in rt_.r or rt_.r[k_][1] < ev_[1]:
                                rt_.r[k_] = ev_

        d0 = dsl[:, 0:2].rearrange("p a b c -> p (a b c)")
        d1 = dsl[:, 2:4].rearrange("p a b c -> p (a b c)")
        r0 = rstd[:].bitcast(BF16)
        w2f = wslot[2][:].rearrange("p a b -> p (a b)")
        w3f = wslot[3][:].rearrange("p a b -> p (a b)")
        aux0 = aux_views(d0, dslot_T[0:2], [[128, U, 128]] * 5)
        aux1 = aux_views(d1, dslot_T[2:4], [[128, U, 128]] * 3 + [[128, U, 256]])
        aux2 = aux_views(r0, [rstd_T], [[128, U, 128]] * 3)
        aux3 = aux_views(w2f, [wslot_T[2]], [[128, U, 128]] * 4)
        aux4 = aux_views(w3f, [wslot_T[3]], [[128, U, 128]] * 4)
        (Dm, Dm_T), (Em, Em_T), (Mfull, Mfull_T) = aux0[0], aux0[1], aux0[2]
        (bdB, bdB_T), (bdK, bdK_T), (bdV, bdV_T) = aux2
        Qm, Qm_T = buf([128, U, 128], BF16); P1b, P1b_T = buf([128, U, 128], BF16)
        Xb, Xb_T = buf([128, U, 256], BF16)
        Sb, Sb_T = buf([128, 128], BF16)
        e1, e1_T = buf([128, 256], BF16); e2, e2_T = buf([128, 256], BF16); e3, e3_T = buf([128, 256], BF16)
        SETS = []
        s0 = dict(tokV=aux1[0], tokK=aux1[1], tokB=aux1[2], bdR=aux1[3], ynbd=aux0[3])
        s0["ArbT"] = buf([128, U, 128], BF16); s0["ArkT"] = buf([128, U, 128], BF16); s0["Vbar"] = buf([128, U, 128], BF16)
        s0["AbT"] = buf([128, U, 128], BF16); s0["Ub"] = buf([128, U, 128], BF16); s0["gam"] = buf([128, 8], F32)
        s1 = dict(tokV=aux3[0], tokK=aux3[1], tokB=aux3[2], AbT=aux3[3], ArbT=aux4[0], ArkT=aux4[1], Vbar=aux4[2], Ub=aux4[3], ynbd=aux0[4])
        s1["bdR"] = buf([128, U, 256], BF16); s1["gam"] = buf([128, 8], F32)
        SETS = [s0, s1]
        if has_sample:
            Ss, Ss_T = buf([128, U, 128], F32)
        assert st["off"] <= BIGB, st["off"]
        def zero_bd(sets):
            zl = [(bdB, bdB_T), (bdK, bdK_T), (bdV, bdV_T)]
            for S_ in sets:
                zl += [S_["bdR"], S_["ynbd"]]
            for a_, t_ in zl:
                P.op("pool", lambda e: e.memset(a_, 0.0), writes=t_)

        if has_sample:
            di = ctr["tmpf"] % 2; ctr["tmpf"] += 1
            sh32, sh32_T = tmpf[di], tmpf_T[di]
            P.dma("sp", sh32[:, 0:4 * DC], sshift.rearrange("p s c -> p (s c)"), writes=[sh32_T])
            for s_ in range(4):
                P.op("dve", lambda e: e.tensor_copy(out=hsh[:, :, 32 * s_:32 * s_ + 1],
                                                    in_=sh32[:, s_ * DC:(s_ + 1) * DC].unsqueeze(2)),
                     reads=[sh32_T], writes=hsh_T)
                P.op("dve", lambda e: e.tensor_copy(out=hsh[:, :, 32 * s_ + 1:32 * s_ + 32],
                                                    in_=hT[:, :, 1 + npr + 32 * s_:1 + npr + 32 * s_ + 31]),
                     reads=hT_T, writes=hsh_T)

        def rhs_pairs(c0, cn):
            if has_sample and c0 >= npr:
                return (lambda kt: hT[:, kt, 1 + c0:1 + c0 + cn]), (lambda kt: hsh[:, kt, c0 - npr:c0 - npr + cn]), hsh_T
            return (lambda kt: hT[:, kt, 1 + c0:1 + c0 + cn]), (lambda kt: hT[:, kt, c0:c0 + cn]), []

        def mixed_linear(wdram_ap, n, M, evac):
            si = wctr[0] % 2; wctr[0] += 1
            ws, wT = wslot[si], wslot_T[si]
            P.dma("pool", ws[:, :, 0:M], wdram_ap, writes=[wT])
            mu = consts[:, CO["mu"] + n * 16:CO["mu"] + (n + 1) * 16].unsqueeze(2).broadcast_to([128, DC, M])
            P.op("dve", lambda e: e.tensor_tensor(out=wb[:, :, 0:M], in0=ws[:, :, 0:M], in1=mu, op=ALU.mult),
                 reads=[wT, consts_T], writes=wb_T)
            P.op("dve", lambda e: e.tensor_tensor(out=wa[:, :, 0:M], in0=ws[:, :, 0:M], in1=wb[:, :, 0:M], op=ALU.subtract),
                 reads=[wT] + wb_T, writes=wa_T)
            for (c0, cn) in blks:
                cur, shf, extra = rhs_pairs(c0, cn)
                b = bank()
                for kt in range(DC):
                    P.op("pe", lambda e: e.matmul(psb[b][0:M, 0:cn], wa[:, kt, 0:M], cur(kt), start=(kt == 0), stop=False),
                         reads=wa_T + [hT_T[kt]], writes=psT[b], inc=False)
                    P.op("pe", lambda e: e.matmul(psb[b][0:M, 0:cn], wb[:, kt, 0:M], shf(kt), start=False, stop=(kt == DC - 1)),
                         reads=wb_T + [hT_T[kt]] + extra, writes=psT[b], inc=(kt == DC - 1))
                evac(b, c0, cn)

        mixed_linear(w1_d, 1, 96, lambda b, c0, cn: P.op(
            "act", lambda e: e.activation(out=lora[0:96, 0, c0:c0 + cn], in_=psb[b][0:96, 0:cn], func=AF.Tanh),
            reads=psT[b], writes=lora_T))
        mixed_linear(a1_d, 4, 96, lambda b, c0, cn: P.op(
            "act", lambda e: e.copy(out=lora[0:96, 1, c0:c0 + cn], in_=psb[b][0:96, 0:cn]),
            reads=psT[b], writes=lora_T))
        for gc in range(2):
            mixed_linear(g1_d[gc], 5, 128, lambda b, c0, cn: P.op(
                "act", lambda e: e.activation(out=lora[:, 2 + gc, c0:c0 + cn], in_=psb[b][:, 0:cn], func=AF.Sigmoid),
                reads=psT[b], writes=lora_T))

        bcount = [0]

        def v3(ap_, TP, w0_, w1_):
            return ap_[0:TP, :, w0_:w1_]

        def prep_gen(j, cb, L, nu, S_, rezero):
            TP = 2 * L
            ncol = nu * L
            ntl = ncol // 128
            (tokV, tokV_T), (tokK, tokK_T), (tokB, tokB_T), (bdR, bdR_T) = S_["tokV"], S_["tokK"], S_["tokB"], S_["bdR"]
            (ArbT, ArbT_T), (ArkT, ArkT_T), (Vbar, Vbar_T), (AbT, AbT_T) = S_["ArbT"], S_["ArkT"], S_["Vbar"], S_["AbT"]
            gam, gam_T = S_["gam"]
            if rezero:
                zero_bd([S_])
            bcl = bank()
            for tl in range(ntl):
                c_ = cb + 128 * tl
                b_ = bank()
                P.op("pe", lambda e: e.transpose(psb[b_][:, 0:128], lwT[:, c_:c_ + 128], identf[:]), reads=lwT_T + [cT], writes=psT[b_][0:1])
                di = ctr["tmpf"] % 2; ctr["tmpf"] += 1
                P.op("act", lambda e: e.copy(out=tmpf[di][:, 0:128], in_=psb[b_][:, 0:128]), reads=psT[b_][0:1], writes=[tmpf_T[di]])
                P.op("pe", lambda e: e.matmul(psb[bcl][:, 128 * tl:128 * (tl + 1)], tmpf[di][:, 0:128], tri[L][:], start=True, stop=True),
                     reads=[tmpf_T[di], cT], writes=psT[bcl])
            cl = psb[bcl][:, 0:ncol]
            P.op("act", lambda e: e.activation(out=e1[:, 0:ncol], in_=cl, func=AF.Exp), reads=psT[bcl], writes=e1_T)
            P.op("act", lambda e: e.activation(out=e2[:, 0:ncol], in_=cl, func=AF.Exp, scale=-1.0), reads=psT[bcl], writes=e2_T)
            P.op("act", lambda e: e.activation(out=gam[:, 0:nu], in_=psb[bcl][:, 0:ncol].rearrange("p (u l) -> p u l", l=L)[:, :, L - 1],
                                               func=AF.Exp), reads=psT[bcl], writes=gam_T)
            di = ctr["tmpf"] % 2; ctr["tmpf"] += 1
            P.op("dve", lambda e: e.tensor_tensor(out=tmpf[di][:, 0:ncol], in0=cl, in1=lwT[:, cb:cb + ncol], op=ALU.subtract),
                 reads=psT[bcl] + lwT_T, writes=[tmpf_T[di]])
            P.op("act", lambda e: e.activation(out=e3[:, 0:ncol], in_=tmpf[di][:, 0:ncol], func=AF.Exp), reads=[tmpf_T[di]], writes=e3_T)
            yield
            def src(ap_, ps_):
                return ap_[ps_, cb:cb + ncol].rearrange("p (u l) -> p u l", l=L)

            def esrc(ap_, ps_):
                return ap_[ps_, 0:ncol].rearrange("p (u l) -> p u l", l=L)
            for hh in range(2):
                ps_ = slice(64 * hh, 64 * hh + 64)
                cs0, cs1 = L * hh, L * hh + L
                eng = "dve" if hh == 0 else "pool"
                P.op("dve", lambda e: e.scalar_tensor_tensor(out=bdR[ps_, 0:nu, cs0:cs1], in0=src(kkn, ps_), scalar=-1.0, in1=esrc(e3, ps_),
                                                           op0=ALU.mult, op1=ALU.mult),
                     reads=kkn_T + e3_T, writes=bdR_T)
                P.op(eng, lambda e: e.tensor_tensor(out=bdR[ps_, 0:nu, TP + cs0:TP + cs1], in0=src(rT_, ps_), in1=esrc(e1, ps_), op=ALU.mult),
                     reads=rT_T + e1_T, writes=bdR_T)
                P.op(eng, lambda e: e.tensor_tensor(out=bdK[ps_, 0:nu, cs0:cs1], in0=src(kT_, ps_), in1=esrc(e2, ps_), op=ALU.mult),
                     reads=kT_T + e2_T, writes=bdK_T)
                P.op(eng, lambda e: e.tensor_tensor(out=bdB[ps_, 0:nu, cs0:cs1], in0=src(kkn, ps_), in1=src(aT_, ps_), op=ALU.mult),
                     reads=kkn_T + aT_T, writes=bdB_T)
                P.op(eng, lambda e: e.tensor_tensor(out=bdB[ps_, 0:nu, cs0:cs1], in0=bdB[ps_, 0:nu, cs0:cs1], in1=esrc(e2, ps_), op=ALU.mult),
                     reads=bdB_T + e2_T, writes=bdB_T)
                P.op("act", lambda e: e.copy(out=bdV[ps_, 0:nu, cs0:cs1], in_=src(vT_, ps_)), reads=vT_T, writes=bdV_T)
            yield
            def tr_all(dst, dst_T, srcfn, src_T, rows_in, cols_in, eng):
                b_ = bank()
                pv = psb[b_][:, :].bitcast(BF16)
                for u in range(nu):
                    P.op("pe", lambda e: e.transpose(pv[0:cols_in, rows_in * u:rows_in * (u + 1)], srcfn(u), identb[0:rows_in, 0:rows_in]),
                         reads=src_T + [cT], writes=psT[b_], inc=(u == nu - 1))
                copy(eng, dst, pv[0:cols_in, 0:rows_in * nu].rearrange("p (u r) -> p u r", r=rows_in), psT[b_], dst_T)
            tr_all(tokV[0:TP, 0:nu, :], tokV_T, lambda u: bdV[:, u, 0:TP], bdV_T, 128, TP, "act")
            tr_all(tokK[0:TP, 0:nu, :], tokK_T, lambda u: bdK[:, u, 0:TP], bdK_T, 128, TP, "dve")
            tr_all(tokB[0:TP, 0:nu, :], tokB_T, lambda u: bdB[:, u, 0:TP], bdB_T, 128, TP, "act")
            tr_all(Xb[0:TP, 0:nu, 0:128], Xb_T, lambda u: bdR[:, u, 0:TP], bdR_T, 128, TP, "dve")
            yield
            upb = 512 // (2 * TP)
            for lhs, lhs_T, outs in ((bdB, bdB_T, ((Mfull, Mfull_T, None), (ArbT, ArbT_T, mincl[L]))),
                                     (bdK, bdK_T, ((P1b, P1b_T, mstrict[L]), (ArkT, ArkT_T, mincl[L])))):
                for u0 in range(0, nu, upb):
                    b_ = bank()
                    n_ = min(upb, nu - u0)
                    for u in range(u0, u0 + n_):
                        o_ = (u - u0) * 2 * TP
                        P.op("pe", lambda e: e.matmul(psb[b_][0:TP, o_:o_ + 2 * TP], lhs[:, u, 0:TP], bdR[:, u, 0:2 * TP], start=True, stop=True),
                             reads=lhs_T + bdR_T, writes=psT[b_], inc=(u == u0 + n_ - 1))
                    pv = psb[b_][0:TP, 0:n_ * 2 * TP].rearrange("p (u c) -> p u c", c=2 * TP)
                    for part, (dst, dst_T, msk) in enumerate(outs):
                        sv = pv[:, :, part * TP:(part + 1) * TP]
                        dv = dst[0:TP, u0:u0 + n_, 0:TP]
                        if msk is None:
                            copy("act", dv, sv, psT[b_], dst_T)
                        else:
                            P.op("dve", lambda e: e.tensor_tensor(out=dv, in0=sv, in1=msk[0:TP, 0:TP].unsqueeze(1).broadcast_to([TP, n_, TP]), op=ALU.mult),
                                 reads=psT[b_] + [cT], writes=dst_T)
            yield
            b_ = bank()
            for u in range(nu):
                P.op("pe", lambda e: e.matmul(psb[b_][0:TP, 128 * u:128 * (u + 1)], P1b[0:TP, u, 0:TP], tokV[0:TP, u, :], start=True, stop=True),
                     reads=P1b_T + tokV_T, writes=psT[b_], inc=(u == nu - 1))
            copy("act", Xb[0:TP, 0:nu, 128:256], psb[b_][0:TP, 0:128 * nu].rearrange("p (u c) -> p u c", c=128), psT[b_], Xb_T)
            idb = identb[0:TP, 0:TP].unsqueeze(1).broadcast_to([TP, nu, TP])
            yield
            P.op("act", lambda e: e.copy(out=Dm[0:TP, 0:nu, 0:TP], in_=idb), reads=[cT], writes=Dm_T)
            P.op("pool", lambda e: e.tensor_copy(out=Em[0:TP, 0:nu, 0:TP], in_=idb), reads=[cT], writes=Em_T)
            for lv in range(len(lvm[L])):
                P.op("pool", lambda e: e.tensor_tensor(out=Qm[0:TP, 0:nu, 0:TP], in0=Mfull[0:TP, 0:nu, 0:TP],
                                                       in1=lvm[L][lv][0:TP, 0:TP].unsqueeze(1).broadcast_to([TP, nu, TP]), op=ALU.mult),
                     reads=Mfull_T + [cT], writes=Qm_T)
                b1_ = bank()
                for u in range(nu):
                    P.op("pe", lambda e: e.matmul(psb[b1_][0:TP, 128 * u:128 * u + TP], Qm[0:TP, u, 0:TP], Dm[0:TP, u, 0:TP], start=True, stop=True),
                         reads=Qm_T + Dm_T, writes=psT[b1_], inc=(u == nu - 1))
                copy("act", P1b[0:TP, 0:nu, 0:TP], psb[b1_][0:TP, 0:128 * nu].rearrange("p (u c) -> p u c", c=128)[:, :, 0:TP], psT[b1_], P1b_T)
                yield
                b2_ = bank()
                for u in range(nu):
                    P.op("pe", lambda e: e.matmul(psb[b2_][0:TP, 128 * u:128 * u + TP], Em[0:TP, u, 0:TP], P1b[0:TP, u, 0:TP], start=True, stop=True),
                         reads=Em_T + P1b_T, writes=psT[b2_], inc=(u == nu - 1))
                P.op("dve", lambda e: e.tensor_tensor(out=Dm[0:TP, 0:nu, 0:TP], in0=Dm[0:TP, 0:nu, 0:TP],
                                                      in1=psb[b2_][0:TP, 0:128 * nu].rearrange("p (u c) -> p u c", c=128)[:, :, 0:TP], op=ALU.add),
                     reads=Dm_T + psT[b2_], writes=Dm_T)
                yield
                tr_all(Em[0:TP, 0:nu, 0:TP], Em_T, lambda u: Dm[0:TP, u, 0:TP], Dm_T, TP, TP, "act")
                yield
            Ec, Ec_T = Em, Em_T
            for u0 in range(0, nu, 2):
                b_ = bank()
                n_ = min(2, nu - u0)
                for u in range(u0, u0 + n_):
                    P.op("pe", lambda e: e.matmul(psb[b_][0:TP, 256 * (u - u0):256 * (u - u0 + 1)], Ec[0:TP, u, 0:TP], Xb[0:TP, u, :], start=True, stop=True),
                         reads=Ec_T + Xb_T, writes=psT[b_], inc=(u == u0 + n_ - 1))
                pv = psb[b_][0:TP, 0:256 * n_].rearrange("p (u c) -> p u c", c=256)
                copy("act", Vbar[0:TP, u0:u0 + n_, :], pv[:, :, 128:256], psT[b_], Vbar_T)
                copy("dve", Xb[0:TP, u0:u0 + n_, 0:128], pv[:, :, 0:128], psT[b_], Xb_T)
            yield
            tr_all(AbT[:, 0:nu, 0:TP], AbT_T, lambda u: Xb[0:TP, u, 0:128], Xb_T, TP, 128, "act")
            yield

        def scan_gen(j, cb, L, nu, S_, states):
            TP = 2 * L
            ncol = nu * L
            (tokV, tokV_T), (tokK, tokK_T), (tokB, tokB_T), (bdR, bdR_T) = S_["tokV"], S_["tokK"], S_["tokB"], S_["bdR"]
            (ArbT, ArbT_T), (ArkT, ArkT_T), (Vbar, Vbar_T), (AbT, AbT_T) = S_["ArbT"], S_["ArkT"], S_["Vbar"], S_["AbT"]
            (Ub, Ub_T), (gam, gam_T), (ynbd, ynbd_T) = S_["Ub"], S_["gam"], S_["ynbd"]
            bcount[0] += 1
            by = SSB[bcount[0] % 2]
            for u in range(nu):
                Sm_ap, Sm_T = states[u]
                P.op("act", lambda e: e.copy(out=Sb[:, :], in_=Sm_ap), reads=Sm_T, writes=Sb_T)
                bu = bank()
                P.op("pe", lambda e: e.matmul(psb[bu][0:TP, 0:128], AbT[:, u, 0:TP], Sb[:, :], start=True, stop=True),
                     reads=AbT_T + Sb_T, writes=psT[bu])
                P.op("dve", lambda e: e.tensor_tensor(out=Ub[0:TP, u, :], in0=psb[bu][0:TP, 0:128], in1=Vbar[0:TP, u, :], op=ALU.add),
                     reads=psT[bu] + Vbar_T, writes=Ub_T)
                yield
                yo = psb[by][0:TP, 128 * u:128 * (u + 1)]
                P.op("pe", lambda e: e.matmul(yo, bdR[:, u, TP:2 * TP], Sb[:, :], start=True, stop=False),
                     reads=bdR_T + Sb_T, writes=psT[by], inc=False)
                P.op("pe", lambda e: e.matmul(yo, ArkT[0:TP, u, 0:TP], tokV[0:TP, u, :], start=False, stop=False),
                     reads=ArkT_T + tokV_T, writes=psT[by], inc=False)
                P.op("pe", lambda e: e.matmul(yo, ArbT[0:TP, u, 0:TP], Ub[0:TP, u, :], start=False, stop=True),
                     reads=ArbT_T + Ub_T, writes=psT[by])
                bs = bank()
                P.op("pe", lambda e: e.matmul(psb[bs][:, 0:128], tokK[0:TP, u, :], tokV[0:TP, u, :], start=True, stop=False),
                     reads=tokK_T + tokV_T, writes=psT[bs], inc=False)
                P.op("pe", lambda e: e.matmul(psb[bs][:, 0:128], tokB[0:TP, u, :], Ub[0:TP, u, :], start=False, stop=True),
                     reads=tokB_T + Ub_T, writes=psT[bs])
                di = ctr["tmpf"] % 2; ctr["tmpf"] += 1
                P.op("act", lambda e: e.activation(out=tmpf[di][:, 0:128], in_=psb[bs][:, 0:128], func=AF.Copy, scale=gam[:, u:u + 1]),
                     reads=psT[bs] + gam_T, writes=[tmpf_T[di]])
                P.op("dve", lambda e: e.scalar_tensor_tensor(out=Sm_ap, in0=Sm_ap, scalar=gam[:, u:u + 1], in1=tmpf[di][:, 0:128],
                                                             op0=ALU.mult, op1=ALU.add),
                     reads=Sm_T + gam_T + [tmpf_T[di]], writes=Sm_T)
                yield
            g = bcount[0] % 4
            sm = small[:, 16 * g:16 * g + 16]; sT = [small_T[g]]
            Y3 = psb[by][0:TP, 0:128 * nu].rearrange("p (u c) -> p u c", c=128)
            yield
            P.op("dve", lambda e: e.reduce_sum(out=sm[0:TP, 0:nu], in_=Y3, axis=AX.X), reads=psT[by], writes=sT)
            di = ctr["tmpf"] % 2; ctr["tmpf"] += 1
            sqv = tmpf[di][0:TP, 0:128 * nu].rearrange("p (u c) -> p u c", c=128)
            P.op("act", lambda e: e.activation(out=sqv, in_=Y3, func=AF.Square), reads=psT[by], writes=[tmpf_T[di]])
            P.op("dve", lambda e: e.reduce_sum(out=sm[0:TP, 4:4 + nu], in_=sqv, axis=AX.X), reads=[tmpf_T[di]], writes=sT)
            P.op("dve", lambda e: e.tensor_scalar(out=sm[0:TP, 0:nu], in0=sm[0:TP, 0:nu], scalar1=1.0 / 64, scalar2=None, op0=ALU.mult),
                 reads=sT, writes=sT)
            P.op("dve", lambda e: e.tensor_tensor(out=sm[0:TP, 8:8 + nu], in0=sm[0:TP, 0:nu], in1=sm[0:TP, 0:nu], op=ALU.mult),
                 reads=sT, writes=sT)
            P.op("dve", lambda e: e.scalar_tensor_tensor(out=sm[0:TP, 4:4 + nu], in0=sm[0:TP, 4:4 + nu], scalar=1.0 / 64, in1=sm[0:TP, 8:8 + nu],
                                                         op0=ALU.mult, op1=ALU.subtract),
                 reads=sT, writes=sT)
            P.op("dve", lambda e: e.tensor_scalar(out=sm[0:TP, 4:4 + nu], in0=sm[0:TP, 4:4 + nu], scalar1=GN_EPS, scalar2=None, op0=ALU.add),
                 reads=sT, writes=sT)
            P.op("act", lambda e: e.activation(out=sm[0:TP, 4:4 + nu], in_=sm[0:TP, 4:4 + nu], func=AF.Ln), reads=sT, writes=sT)
            P.op("act", lambda e: e.activation(out=sm[0:TP, 4:4 + nu], in_=sm[0:TP, 4:4 + nu], func=AF.Exp, scale=-0.5), reads=sT, writes=sT)
            for hh in range(2):
                rs = slice(L * hh, L * hh + L)
                cs = slice(64 * hh, 64 * hh + 64)
                P.op("dve", lambda e: e.tensor_tensor(out=ynbd[rs, 0:nu, cs], in0=Y3[rs, :, cs],
                                                      in1=sm[rs, 0:nu].unsqueeze(2).broadcast_to([L, nu, 64]), op=ALU.subtract),
                     reads=psT[by] + sT, writes=ynbd_T)
                P.op("dve", lambda e: e.tensor_tensor(out=ynbd[rs, 0:nu, cs], in0=ynbd[rs, 0:nu, cs],
                                                      in1=sm[rs, 4:4 + nu].unsqueeze(2).broadcast_to([L, nu, 64]), op=ALU.mult),
                     reads=ynbd_T + sT, writes=ynbd_T)
            yield
            b_ = bank()
            pv = psb[b_][:, :].bitcast(BF16)
            for u in range(nu):
                P.op("pe", lambda e: e.transpose(pv[:, TP * u:TP * (u + 1)], ynbd[0:TP, u, :], identb[0:TP, 0:TP]),
                     reads=ynbd_T + [cT], writes=psT[b_], inc=(u == nu - 1))
            for hh in range(2):
                ps_ = slice(64 * hh, 64 * hh + 64)
                P.op("act", lambda e: e.copy(out=ynT[ps_, cb:cb + ncol].rearrange("p (u l) -> p u l", l=L),
                                             in_=pv[ps_, 0:TP * nu].rearrange("p (u c) -> p u c", c=TP)[:, :, L * hh:L * hh + L]),
                     reads=psT[b_], writes=ynT_T)

        def state_out(src_ap, src_T, dst):
            b_ = bank()
            P.op("pe", lambda e: e.transpose(psb[b_][:, 0:128], src_ap, identf[:]), reads=src_T + [cT], writes=psT[b_])
            di = ctr["tmpf"] % 2; ctr["tmpf"] += 1
            so, so_T = tmpf[di], [tmpf_T[di]]
            for hh in range(2):
                ps_ = slice(64 * hh, 64 * hh + 64)
                P.op("act", lambda e: e.copy(out=so[ps_, 0:64], in_=psb[b_][ps_, 64 * hh:64 * hh + 64]), reads=psT[b_], writes=so_T)
            P.dma("sp", dst, so[:, 0:64], reads=so_T, is_output=True)

        for j in range(DC):
            P.dma("pool", w2s[0:96, :], w2_d[:, 128 * j:128 * (j + 1)], writes=w2s_T, semt=w2s_T[0])
            P.dma("pool", a2s[0:96, :], a2_d[:, 128 * j:128 * (j + 1)], writes=a2s_T, semt=a2s_T[0])
            P.dma("pool", g2s[:, :, :], g2_d[:, :, 128 * j:128 * (j + 1)], writes=g2s_T, semt=g2s_T[0])
            for (wdram, n, dst, dst_T) in ((wr_d, 0, rT_, rT_T), (wk_d, 2, kT_, kT_T), (wv_d, 3, vT_, vT_T)):
                mixed_linear(wdram[j], n, 128, lambda b, c0, cn, dst=dst, dst_T=dst_T: copy(
                    ev_eng(), dst[:, c0:c0 + cn], psb[b][:, 0:cn], psT[b], dst_T))
            for (c0, cn) in blks:
                b = bank()
                P.op("pe", lambda e: e.matmul(psb[b][:, 0:cn], w2s[0:96, :], lora[0:96, 0, c0:c0 + cn], start=True, stop=True),
                     reads=w2s_T + lora_T, writes=psT[b])
                P.op("act", lambda e: e.activation(out=lwT[:, c0:c0 + cn], in_=psb[b][:, 0:cn], func=AF.Exp, bias=cc("w0", j), scale=1.0),
                     reads=psT[b] + [consts_T], writes=lwT_T)
                ts = ctr["tmpf"] % 2; ctr["tmpf"] += 1
                tf, tf_T = tmpf[ts], [tmpf_T[ts]]
                P.op("dve", lambda e: e.tensor_scalar(out=tf[:, 0:cn], in0=lwT[:, c0:c0 + cn], scalar1=1.0, scalar2=None, op0=ALU.add),
                     reads=lwT_T, writes=tf_T)
                P.op("dve", lambda e: e.reciprocal(out=tf[:, 0:cn], in_=tf[:, 0:cn]), reads=tf_T, writes=tf_T)
                P.op("dve", lambda e: e.scalar_tensor_tensor(out=lwT[:, c0:c0 + cn], in0=lwT[:, c0:c0 + cn], scalar=-math.exp(-0.5), in1=tf[:, 0:cn],
                                                             op0=ALU.mult, op1=ALU.mult),
                     reads=lwT_T + tf_T, writes=lwT_T)
                b = bank()
                P.op("pe", lambda e: e.matmul(psb[b][:, 0:cn], a2s[0:96, :], lora[0:96, 1, c0:c0 + cn], start=True, stop=True),
                     reads=a2s_T + lora_T, writes=psT[b])
                P.op("act", lambda e: e.activation(out=aT_[:, c0:c0 + cn], in_=psb[b][:, 0:cn], func=AF.Sigmoid, bias=cc("a0", j), scale=1.0),
                     reads=psT[b] + [consts_T], writes=aT_T)
            P.op("dve", lambda e: e.tensor_scalar(out=kkn[:, 0:N], in0=kT_[:, 0:N], scalar1=cc("kk", j), scalar2=None, op0=ALU.mult),
                 reads=kT_T + [consts_T], writes=kkn_T)
            s = ctr["sq"] % 2; ctr["sq"] += 1
            P.op("act", lambda e: e.activation(out=sq[s][:, 0:N], in_=kkn[:, 0:N], func=AF.Square), reads=kkn_T, writes=[sq_T[s]])
            for (c0, cn) in blks:
                b = bank()
                P.op("pe", lambda e: e.matmul(psb[b][:, 0:cn], bdones[:], sq[s][:, c0:c0 + cn], start=True, stop=True),
                     reads=[sq_T[s], cT], writes=psT[b])
                ts = ctr["tmpf"] % 2; ctr["tmpf"] += 1
                tf, tf_T = tmpf[ts], [tmpf_T[ts]]
                P.op("dve", lambda e: e.tensor_scalar(out=tf[:, 0:cn], in0=psb[b][:, 0:cn], scalar1=1e-30, scalar2=None, op0=ALU.add),
                     reads=psT[b], writes=tf_T)
                P.op("act", lambda e: e.activation(out=tf[:, 0:cn], in_=tf[:, 0:cn], func=AF.Ln), reads=tf_T, writes=tf_T)
                P.op("act", lambda e: e.activation(out=tf[:, 0:cn], in_=tf[:, 0:cn], func=AF.Exp, scale=-0.5), reads=tf_T, writes=tf_T)
                P.op("dve", lambda e: e.tensor_tensor(out=kkn[:, c0:c0 + cn], in0=kkn[:, c0:c0 + cn], in1=tf[:, 0:cn], op=ALU.mult),
                     reads=kkn_T + tf_T, writes=kkn_T)
                P.op("dve", lambda e: e.tensor_scalar(out=tf[:, 0:cn], in0=aT_[:, c0:c0 + cn], scalar1=-1.0, scalar2=cc("ka", j), op0=ALU.add, op1=ALU.mult),
                     reads=aT_T + [consts_T], writes=tf_T)
                P.op("dve", lambda e: e.tensor_scalar(out=tf[:, 0:cn], in0=tf[:, 0:cn], scalar1=1.0, scalar2=None, op0=ALU.add),
                     reads=tf_T, writes=tf_T)
                P.op("dve", lambda e: e.tensor_tensor(out=kT_[:, c0:c0 + cn], in0=kT_[:, c0:c0 + cn], in1=tf[:, 0:cn], op=ALU.mult),
                     reads=kT_T + tf_T, writes=kT_T)
            zero_bd(SETS)
            blist = [(256 * bi_, 64, [(Smast[:, j, :], [Smast_T[j]])] * 4, False) for bi_ in range(npr // 256)]
            if has_sample:
                for u in range(4):
                    di = ctr["tmpf"] % 2; ctr["tmpf"] += 1
                    si_, si_T = tmpf[di], [tmpf_T[di]]
                    P.dma("sp", si_[:, 256:320], swkv[u, j], writes=si_T)
                    P.op("dve", lambda e: e.memset(si_[:, 0:128], 0.0), writes=si_T)
                    for hh in range(2):
                        ps_ = slice(64 * hh, 64 * hh + 64)
                        P.op("dve", lambda e: e.tensor_copy(out=si_[ps_, 64 * hh:64 * hh + 64], in_=si_[ps_, 256:320]), reads=si_T, writes=si_T)
                    b_ = bank()
                    P.op("pe", lambda e: e.transpose(psb[b_][:, 0:128], si_[:, 0:128], identf[:]), reads=si_T + [cT], writes=psT[b_])
                    P.op("act", lambda e: e.copy(out=Ss[:, u, :], in_=psb[b_][:, 0:128]), reads=psT[b_], writes=Ss_T)
                blist.append((npr, 32, [(Ss[:, u, :], Ss_T) for u in range(4)], True))

            def drive(gens):
                gens = [g_ for g_ in gens if g_ is not None]
                while gens:
                    for g_ in list(gens):
                        try:
                            next(g_)
                        except StopIteration:
                            gens.remove(g_)
            drive([prep_gen(j, blist[0][0], blist[0][1], 4, SETS[0], blist[0][3])])
            for bi_, (cb_, L_, states_, rz_) in enumerate(blist):
                nxt = None
                if bi_ + 1 < len(blist):
                    n_ = blist[bi_ + 1]
                    nxt = prep_gen(j, n_[0], n_[1], 4, SETS[(bi_ + 1) % 2], n_[3])
                drive([scan_gen(j, cb_, L_, 4, SETS[bi_ % 2], states_), nxt])
            if has_sample:
                for u in range(4):
                    state_out(Ss[:, u, :], Ss_T, wkvs_d[u, j])
            if gi == 2:
                state_out(Smast[:, j, :], [Smast_T[j]], wkvp_d[j])
            s = ctr["sq"] % 2; ctr["sq"] += 1
            P.op("dve", lambda e: e.scalar_tensor_tensor(out=sq[s][:, 0:N], in0=rT_[:, 0:N], scalar=cc("rk", j), in1=kT_[:, 0:N],
                                                         op0=ALU.mult, op1=ALU.mult),
                 reads=rT_T + kT_T + [consts_T], writes=[sq_T[s]])
            for (c0, cn) in blks:
                b = bank()
                P.op("pe", lambda e: e.matmul(psb[b][:, 0:cn], bdones[:], sq[s][:, c0:c0 + cn], start=True, stop=True),
                     reads=[sq_T[s], cT], writes=psT[b])
                ts = ctr["tmpf"] % 2; ctr["tmpf"] += 1
                tf, tf_T = tmpf[ts], [tmpf_T[ts]]
                P.op("dve", lambda e: e.tensor_tensor(out=tf[:, 0:cn], in0=psb[b][:, 0:cn], in1=vT_[:, c0:c0 + cn], op=ALU.mult),
                     reads=psT[b] + vT_T, writes=tf_T)
                ts2 = ctr["tmpf"] % 2; ctr["tmpf"] += 1
                tg, tg_T = tmpf[ts2], [tmpf_T[ts2]]
                P.op("dve", lambda e: e.tensor_scalar(out=tg[:, 0:cn], in0=ynT[:, c0:c0 + cn], scalar1=cc("lnw", j), scalar2=cc("lnb", j),
                                                      op0=ALU.mult, op1=ALU.add),
                     reads=ynT_T + [consts_T], writes=tg_T)
                P.op("dve", lambda e: e.tensor_tensor(out=tf[:, 0:cn], in0=tf[:, 0:cn], in1=tg[:, 0:cn], op=ALU.add),
                     reads=tf_T + tg_T, writes=tf_T)
                bg = bank()
                for kt in range(2):
                    P.op("pe", lambda e: e.matmul(psb[bg][:, 0:cn], g2s[:, kt, :], lora[:, 2 + kt, c0:c0 + cn],
                                                  start=(kt == 0), stop=(kt == 1)),
                         reads=g2s_T + lora_T, writes=psT[bg], inc=(kt == 1))
                P.op("dve", lambda e: e.tensor_tensor(out=ygT[:, j, c0:c0 + cn], in0=tf[:, 0:cn], in1=psb[bg][:, 0:cn], op=ALU.mult),
                     reads=tf_T + psT[bg], writes=yg_T)

        aux_release(dslot_T[0:2], aux0); aux_release(dslot_T[2:4], aux1); aux_release([rstd_T], aux2)
        aux_release([wslot_T[2]], aux3); aux_release([wslot_T[3]], aux4)
        return ygT, yg_T

    def rwkv_full(gi, N, npr, has_sample):
        ygT, yg_T = rwkv(gi, N, npr, has_sample)
        if dbg == "yg":
            for c in range(DC):
                P.op("act", lambda e: e.copy(out=xT[:, c, 0:N], in_=ygT[:, c, 0:N]), reads=yg_T, writes=[xT_T[c]])
            return
        P.op("pool", lambda e: e.tensor_copy(out=hT[:, :, 0:1], in_=hT[:, :, npr:npr + 1]), reads=hT_T, writes=hT_T)
        blks = blocks(N)
        set_pool(6)
        ssb = SSB
        pend = None
        for dch in range(DC):
            ws, wT = load_w(wro_d[dch])
            pb = [bank() for _ in blks]
            for bi, (c0, cn) in enumerate(blks):
                for kt in range(DC):
                    P.op("pe", lambda e: e.matmul(psb[pb[bi]][:, 0:cn], ws[:, kt, :], ygT[:, kt, c0:c0 + cn],
                                                  start=(kt == 0), stop=(kt == DC - 1)),
                         reads=[wT] + yg_T, writes=psT[pb[bi]], inc=(kt == DC - 1))
            if pend is not None:
                pend()
            pend = out_evac_ss(dch, N, pb, ssb, dch == 0, dch == DC - 1)
        pend()
        postnorm_add(1, 3, N, ssb, 1.0)

    for gi, (p0, npr, has_s) in enumerate(GROUPS[:ngroups]):
        N = npr + (128 if has_s else 0)
        for c in range(DC):
            P.dma("sp", xT[:, c, 0:npr], xp[:, c, p0:p0 + npr], writes=[xT_T[c]])
            if has_s:
                P.dma("sp", xT[:, c, npr:npr + 128], xs[:, c, :], writes=[xT_T[c]])
        for l in range(nlayers):
            ffn(l, 0, N)
            if dbg == f"ffn{l}0" and gi == 0:
                break
            if l == 0:
                attention(gi, N, npr, has_s)
            else:
                rwkv_full(gi, N, npr, has_s)
            if dbg in (f"mix{l}", "yg") and gi == 0 and (dbg != "yg" or l == 1):
                break
            ffn(l, 1, N)
        for c in range(DC):
            P.dma("sp", yT_d[:, c, p0:p0 + npr], xT[:, c, 0:npr], reads=[xT_T[c]], is_output=True)
            if has_s:
                P.dma("sp", yT_d[:, c, SEQ:SEQ + 128], xT[:, c, npr:npr + 128], reads=[xT_T[c]], is_output=True)
    P.finish()
    stats = dict(ops=P.n_ops, waits=P.n_waits, sems=P.nsem, cnt=dict(P.cnt))
    P.close()
    return nc, stats


def prep_shared(inp):
    f = lambda a: np.ascontiguousarray(np.asarray(a, dtype=np.float32))
    sh = {}
    sh["wg"] = np.stack([np.stack([w_chunks(f(inp["ffn_w_gate"][l, s])) for s in range(2)]) for l in range(2)])
    sh["wu"] = np.stack([np.stack([w_chunks(f(inp["ffn_w_up"][l, s])) for s in range(2)]) for l in range(2)])
    wd = np.stack([np.stack([w_chunks(f(inp["ffn_w_down"][l, s])) for s in range(2)]) for l in range(2)])
    sh["wd"] = np.ascontiguousarray(wd.reshape(2, 2, DC, 128, 4, 11, 128).transpose(0, 1, 2, 4, 3, 5, 6))
    wqkv = f(inp["att_w_qkv"][0])
    bqkv = f(inp["att_b_qkv"][0])
    kcols = [np.concatenate([wqkv[:, 2048 + 64 * h:2048 + 64 * (h + 1)]] * 2, axis=1) for h in range(4)]
    wext = np.concatenate([wqkv[:, :2048]] + kcols, axis=1)
    sh["wqkv"] = w_chunks(wext)
    bext = np.concatenate([bqkv[:2048]] + [np.concatenate([bqkv[2048 + 64 * h:2048 + 64 * (h + 1)]] * 2) for h in range(4)])
    sh["wkvt"] = w_chunks(wqkv[:, 2048:2560])
    sh["bkv"] = bqkv[2048:2560].reshape(1, 512).copy()
    sh["sinks"] = f(inp["att_sinks"]).reshape(1, 32).copy()
    sh["table"] = f(inp["rel_table"])
    sh["wao"] = w_chunks(f(inp["att_w_o"][0]))
    sh["wr"] = w_chunks(f(inp["rwkv_w_r"][0]))
    sh["wk"] = w_chunks(f(inp["rwkv_w_k"][0]))
    sh["wv"] = w_chunks(f(inp["rwkv_w_v"][0]))
    sh["wro"] = w_chunks(f(inp["rwkv_w_o"][0]))
    sh["w1"] = w_chunks(f(inp["rwkv_w1"][0]), 96)[0]
    sh["a1"] = w_chunks(f(inp["rwkv_a1"][0]), 96)[0]
    sh["g1"] = w_chunks(f(inp["rwkv_g1"][0]))
    sh["w2"] = f(inp["rwkv_w2"][0])
    sh["a2"] = f(inp["rwkv_a2"][0])
    sh["g2"] = np.ascontiguousarray(f(inp["rwkv_g2"][0]).reshape(2, 128, D).transpose(1, 0, 2))
    cols = [fcol(f(inp["norm_g"])).reshape(128, 12 * 16),
            bext.reshape(20, 128).T,
            fcol(f(inp["rwkv_mu"][0])).reshape(128, 6 * 16)]
    for nm in ("rwkv_w0", "rwkv_a0", "rwkv_k_k", "rwkv_k_a"):
        cols.append(fcol(f(inp[nm][0])))
    cols.append(fcol(f(inp["rwkv_r_k"][0]).reshape(D)))
    for nm in ("rwkv_ln_w", "rwkv_ln_b"):
        cols.append(fcol(f(inp[nm][0])))
    sh["consts"] = np.ascontiguousarray(np.concatenate(cols, axis=1))
    for k, v in static_consts().items():
        sh["c_" + k] = v
    return sh


def prep_core(inp, c):
    f = lambda a: np.ascontiguousarray(np.asarray(a, dtype=np.float32))
    m = {}
    m["xp"] = fcol(f(inp["x_prompt"][c]).T.copy()) if False else np.ascontiguousarray(
        f(inp["x_prompt"][c]).T.reshape(DC, 128, SEQ).transpose(1, 0, 2))
    xs = f(inp["x_sample"][4 * c:4 * c + 4]).reshape(128, D)
    m["xs"] = np.ascontiguousarray(xs.T.reshape(DC, 128, 128).transpose(1, 0, 2))
    m["ck"] = f(inp["cache_k"][0, 4 * c:4 * c + 4]).reshape(4, 128, 256)
    m["cv"] = f(inp["cache_v"][0, 4 * c:4 * c + 4]).reshape(4, 128, 256)
    ss = f(inp["state_shift"][0, 4 * c:4 * c + 4, 0])
    m["sshift"] = np.ascontiguousarray(ss.reshape(4, DC, 128).transpose(2, 0, 1))
    m["swkv"] = f(inp["state_wkv"][0, 4 * c:4 * c + 4]).reshape(4, 16, 128, 64)
    return m


_CACHE = {}


def kernel(**inputs):
    if "nc" not in _CACHE:
        _CACHE["nc"] = build()[0]
    nc = _CACHE["nc"]
    sh = prep_shared(inputs)
    in_maps = []
    for c in range(NCORE):
        m = dict(sh)
        m.update(prep_core(inputs, c))
        in_maps.append(m)
    res = run_bass_kernel_spmd(nc, in_maps, core_ids=list(range(NCORE)))
    R = res.results
    y_prompt = np.zeros((8, SEQ, D), np.float32)
    y_sample = np.zeros((32, 32, D), np.float32)
    k_prompt = np.zeros((1, 8, 128, 4, 64), np.float32)
    v_prompt = np.zeros((1, 8, 128, 4, 64), np.float32)
    k_sample = np.zeros((1, 32, 32, 4, 64), np.float32)
    v_sample = np.zeros((1, 32, 32, 4, 64), np.float32)
    shift_prompt = np.zeros((1, 8, 1, D), np.float32)
    wkv_prompt = np.zeros((1, 8, 32, 64, 64), np.float32)
    shift_sample = np.zeros((1, 32, 1, D), np.float32)
    wkv_sample = np.zeros((1, 32, 32, 64, 64), np.float32)
    for c in range(NCORE):
        r = R[c]
        yT = np.asarray(r["yT"])
        y = yT.transpose(2, 1, 0).reshape(SEQ + 128, D)
        y_prompt[c] = y[:SEQ]
        y_sample[4 * c:4 * c + 4] = y[SEQ:].reshape(4, 32, D)
        k_prompt[0, c] = np.asarray(r["kp"]).reshape(128, 4, 64)
        v_prompt[0, c] = np.asarray(r["vp"]).reshape(128, 4, 64)
        k_sample[0, 4 * c:4 * c + 4] = np.asarray(r["ks"]).reshape(4, 32, 4, 64)
        v_sample[0, 4 * c:4 * c + 4] = np.asarray(r["vs"]).reshape(4, 32, 4, 64)
        shift_prompt[0, c, 0] = np.asarray(r["shp"]).T.reshape(D)
        shift_sample[0, 4 * c:4 * c + 4, 0] = np.asarray(r["shs"]).transpose(1, 2, 0).reshape(4, D)
        wkv_prompt[0, c] = np.asarray(r["wkvp"]).reshape(32, 64, 64)
        wkv_sample[0, 4 * c:4 * c + 4] = np.asarray(r["wkvs"]).reshape(4, 32, 64, 64)
    return (y_prompt, y_sample, k_prompt, v_prompt, k_sample, v_sample,
            shift_prompt, wkv_prompt, shift_sample, wkv_sample)
```

```python
import contextlib
import math
import numpy as np
import concourse.bass as bass
import concourse.mybir as mybir
from concourse.bass_utils import run_bass_kernel_spmd

F32 = mybir.dt.float32
BF16 = mybir.dt.bfloat16
AF = mybir.ActivationFunctionType
ALU = mybir.AluOpType
AX = mybir.AxisListType

D = 2048
DC = 16
FFD = 5632
FC = 44
NCORE = 8
SEQ = 2048
NH = 32
HD = 64
WINDOW = 128
N_BUCKETS = 32
MAX_DISTANCE = 128
RMS_EPS = 1e-6
GN_EPS = 64 * 1e-5
NEG = -1.0e30
GROUPS = [(0, 768, False), (768, 768, False), (1536, 512, True)]
NMAX = 768


class T:
    __slots__ = ("name", "w", "r", "dsem", "dtot", "bank")

    def __init__(self, name):
        self.name = name
        self.w = {}
        self.r = {}
        self.dsem = None
        self.dtot = 0
        self.bank = None


def TL(name, n):
    return [T(f"{name}{i}") for i in range(n)]


class Prog:
    COMPUTE = ("pe", "act", "dve", "pool")

    def __init__(self, nc, strict_same=True):
        self.nc = nc
        self.es = contextlib.ExitStack()
        self.eng = {"pe": nc.tensor, "act": nc.scalar, "dve": nc.vector,
                    "pool": nc.gpsimd, "sp": nc.sync}
        self.sem = {}
        self.cnt = {}
        for e in self.COMPUTE:
            self.sem[e] = self.es.enter_context(nc.semaphore("s_" + e))
            self.cnt[e] = 0
        self.seen = {e: {} for e in self.eng}
        self.strict_same = strict_same
        self.nsem = 0
        self.n_ops = 0
        self.n_waits = 0
        self.out_events = []
        self.uid = 0

    def sb(self, name, shape, dt):
        return self.es.enter_context(self.nc.sbuf_tensor("sb_" + name, list(shape), dt))

    def ps(self, name, shape, dt=F32):
        return self.es.enter_context(self.nc.psum_tensor("ps_" + name, list(shape), dt))

    def newsem(self, name):
        self.nsem += 1
        self.uid += 1
        return self.es.enter_context(self.nc.semaphore(f"{name}_{self.uid}"))

    def _wait(self, e, ev):
        sem, val = ev
        k = id(sem)
        if self.seen[e].get(k, 0) >= val:
            return
        self.seen[e][k] = val
        self.eng[e].wait_ge(sem, val)
        self.n_waits += 1

    def _deps(self, e, reads, writes):
        own = id(self.sem[e]) if e in self.sem else None
        skip_own = (e == "pe") or (not self.strict_same)
        for t in reads:
            for k, ev in t.w.items():
                if k == own and skip_own:
                    continue
                self._wait(e, ev)
        for t in writes:
            for k, ev in t.w.items():
                if k == own and skip_own:
                    continue
                self._wait(e, ev)
            for k, ev in t.r.items():
                if k == own and skip_own:
                    continue
                self._wait(e, ev)
        for t in list(reads) + list(writes):
            if t.bank is not None:
                for k, ev in t.bank.w.items():
                    if k != own:
                        self._wait(e, ev)

    def _record(self, ev, reads, writes):
        k = id(ev[0])
        for t in reads:
            t.r[k] = ev
            if t.bank is not None:
                t.bank.w = {k: ev}
        for t in writes:
            t.w = {k: ev}
            t.r = {}
            if t.bank is not None:
                t.bank.w = {k: ev}

    def op(self, e, fn, reads=(), writes=(), inc=True):
        self._deps(e, reads, writes)
        ins = fn(self.eng[e])
        self.n_ops += 1
        if inc:
            self.cnt[e] += 1
            ins.then_inc(self.sem[e], 1)
            ev = (self.sem[e], self.cnt[e])
        else:
            ev = (self.sem[e], self.cnt[e] + 1)
        self._record(ev, reads, writes)
        return ins

    def dma(self, q, out_ap, in_ap, reads=(), writes=(), semt=None, is_output=False, concurrent=False, **kw):
        if semt is None:
            semt = writes[0] if writes else reads[0]
        if semt.dsem is None:
            semt.dsem = self.newsem("d")
        if concurrent:
            k = id(semt.dsem)
            saved = [(t, t.w.pop(k)) for t in writes if k in t.w]
            self._deps(q, reads, writes)
            for t, ev in saved:
                t.w[k] = ev
        else:
            self._deps(q, reads, writes)
        semt.dtot += 16
        ins = self.eng[q].dma_start(out=out_ap, in_=in_ap, **kw)
        ins.then_inc(semt.dsem, 16)
        self.n_ops += 1
        ev = (semt.dsem, semt.dtot)
        self._record(ev, reads, writes)
        if is_output:
            self.out_events.append(ev)
        return ins

    def finish(self):
        last = {}
        for sem, val in self.out_events:
            k = id(sem)
            if k not in last or last[k][1] < val:
                last[k] = (sem, val)
        for ev in last.values():
            self._wait("sp", ev)
        for e in self.COMPUTE:
            if self.cnt[e] > 0:
                self._wait("sp", (self.sem[e], self.cnt[e]))

    def close(self):
        self.es.close()


def w_chunks(w, cw=128):
    K, M = w.shape
    return np.ascontiguousarray(w.reshape(K // 128, 128, M // cw, cw).transpose(2, 1, 0, 3))


def fcol(v):
    s = v.shape[:-1]
    a = v.reshape(*s, DC, 128)
    a = np.moveaxis(a, -1, 0)
    return np.ascontiguousarray(a)


def t5_bucket_np(rel):
    nb = N_BUCKETS // 2
    max_exact = nb // 2
    offset = np.where(rel > 0, nb, 0)
    n = np.abs(rel)
    nf = np.maximum(n, 1).astype(np.float32)
    large = max_exact + (np.log(nf / np.float32(max_exact)) / np.float32(math.log(MAX_DISTANCE / max_exact))
                         * np.float32(nb - max_exact)).astype(np.int32)
    large = np.minimum(large, nb - 1)
    return offset + np.where(n < max_exact, n, large)


def static_consts():
    c = {}
    c["ident"] = np.eye(128, dtype=np.float32)
    i = np.arange(128)
    for L in (64, 32):
        same = (i[:, None] // L) == (i[None, :] // L)
        c[f"tri{L}"] = (same & (i[:, None] <= i[None, :])).astype(np.float32)
        ii = i % L
        c[f"mstrict{L}"] = (ii[:, None] < ii[None, :]).astype(np.float32)
        c[f"mincl{L}"] = (ii[:, None] <= ii[None, :]).astype(np.float32)
        b = 1
        lv = 0
        while b < L:
            c[f"lv{L}_{lv}"] = ((ii[:, None] // (2 * b) == ii[None, :] // (2 * b)) & ((ii[None, :] // b) % 2 == 1)
                               & ((ii[:, None] // b) % 2 == 0)).astype(np.float32)
            b *= 2
            lv += 1
    c["bdones"] = ((i[:, None] // 64) == (i[None, :] // 64)).astype(np.float32)
    r = np.arange(255)
    bk = t5_bucket_np((r - 191).astype(np.int32))
    oh = np.zeros((32, 255), np.float32)
    oh[bk, r] = 1.0
    c["onehot"] = oh
    return c


def build(ngroups=3, nlayers=2, dbg=None):
    nc = bass.Bass("TRN2", target_bir_lowering=False)
    import os as _os
    P = Prog(nc, strict_same=(_os.environ.get("K_STRICT", "1") == "1"))

    def din(name, shape):
        return nc.dram_tensor(name, list(shape), F32, kind="ExternalInput").ap()

    def dout(name, shape):
        return nc.dram_tensor(name, list(shape), F32, kind="ExternalOutput").ap()

    xp = din("xp", [128, DC, SEQ])
    xs = din("xs", [128, DC, 128])
    ck = din("ck", [4, 128, 256])
    cv = din("cv", [4, 128, 256])
    sshift = din("sshift", [128, 4, DC])
    swkv = din("swkv", [4, 16, 128, 64])
    NCONST = 12 * 16 + 20 + 6 * 16 + 7 * 16
    consts_d = din("consts", [128, NCONST])
    wg_d = din("wg", [2, 2, FC, 128, DC, 128])
    wu_d = din("wu", [2, 2, FC, 128, DC, 128])
    wd_d = din("wd", [2, 2, DC, 4, 128, 11, 128])
    wqkv_d = din("wqkv", [20, 128, DC, 128])
    wkvt_d = din("wkvt", [4, 128, DC, 128])
    bkv_d = nc.dram_tensor("bkv", [1, 512], F32, kind="ExternalInput")
    sinks_d = nc.dram_tensor("sinks", [1, 32], F32, kind="ExternalInput")
    table_d = din("table", [32, 32])
    wao_d = din("wao", [16, 128, DC, 128])
    wr_d = din("wr", [16, 128, DC, 128])
    wk_d = din("wk", [16, 128, DC, 128])
    wv_d = din("wv", [16, 128, DC, 128])
    wro_d = din("wro", [16, 128, DC, 128])
    w1_d = din("w1", [128, DC, 96])
    a1_d = din("a1", [128, DC, 96])
    g1_d = din("g1", [2, 128, DC, 128])
    w2_d = din("w2", [96, D])
    a2_d = din("a2", [96, D])
    g2_d = din("g2", [128, 2, D])
    cst = {k: din("c_" + k, v.shape) for k, v in static_consts().items()}
    fscr = nc.dram_tensor("fscr", [32, 255], F32, kind="Internal")

    yT_d = dout("yT", [128, DC, SEQ + 128])
    kp_d = dout("kp", [128, 256])
    vp_d = dout("vp", [128, 256])
    ks_d = dout("ks", [128, 256])
    vs_d = dout("vs", [128, 256])
    shp_d = dout("shp", [128, DC])
    shs_d = dout("shs", [128, 4, DC])
    wkvp_d = dout("wkvp", [16, 128, 64])
    wkvs_d = dout("wkvs", [4, 16, 128, 64])
    dbg_d = dout("dbg", [128, DC, NMAX]) if dbg else None

    CO = {}
    o = 0
    CO["g"] = o; o += 12 * 16
    CO["bq"] = o; o += 20
    CO["mu"] = o; o += 6 * 16
    for nm in ("w0", "a0", "kk", "ka", "rk", "lnw", "lnb"):
        CO[nm] = o; o += 16
    assert o == NCONST

    xT = P.sb("xT", [128, DC, NMAX], F32); xT_T = TL("xT", DC)
    hT = P.sb("hT", [128, DC, NMAX + 1], BF16); hT_T = TL("hT", DC)
    BIGB = 66 * 1024
    big = P.sb("big", [128, BIGB // 2], BF16)
    big_T = TL("big", FC)
    SL = 768

    def bigv(off_b, shape, dt):
        n = int(np.prod(shape[1:]))
        esz = 4 if dt == F32 else 2
        assert off_b % 4 == 0 and off_b + n * esz <= BIGB, (off_b, shape)
        if dt == F32:
            ap = big[:, off_b // 2: off_b // 2 + n * 2].bitcast(F32)
        else:
            ap = big[:, off_b // 2: off_b // 2 + n]
        if len(shape) == 3:
            ap = ap.rearrange("p (a b) -> p a b", b=shape[2])
        elif len(shape) == 4:
            ap = ap.rearrange("p (a b c) -> p a b c", b=shape[2], c=shape[3])
        t0 = off_b // (SL * 2)
        t1 = (off_b + n * esz - 1) // (SL * 2)
        return ap, big_T[t0:t1 + 1]

    wslot = [P.sb(f"ws{i}", [128, DC, 128], BF16) for i in range(4)]
    wslot_T = TL("ws", 4)
    dsl = P.sb("wds", [128, 4, 11, 128], BF16)
    dslot = [dsl[:, i] for i in range(4)]
    dslot_T = TL("wds", 4)
    wctr = [0, 0]

    consts = P.sb("consts", [128, NCONST], F32); consts_T = T("consts")
    identf = P.sb("identf", [128, 128], F32)
    identb = P.sb("identb", [128, 128], BF16)
    onesb = P.sb("onesb", [128, 128], BF16)
    bdones = P.sb("bdones", [128, 128], BF16)
    tri = {L: P.sb(f"tri{L}", [128, 128], F32) for L in (64, 32)}
    mstrict = {L: P.sb(f"mstrict{L}", [128, 128], BF16) for L in (64, 32)}
    mincl = {L: P.sb(f"mincl{L}", [128, 128], BF16) for L in (64, 32)}
    lvm = {L: [P.sb(f"lv{L}_{i}", [128, 128], BF16) for i in range(6 if L == 64 else 5)] for L in (64, 32)}
    cT = T("cst")
    bkv = P.sb("bkv", [128, 512], F32)
    sinks = P.sb("sinks", [128, 32], F32)
    bias2 = P.sb("bias2", [128, 32, 192], BF16); bias2_T = T("bias2")
    ktc = P.sb("ktc", [128, 4, 128], BF16); ktc_T = T("ktc")
    vbc = P.sb("vbc", [128, 256], BF16); vbc_T = T("vbc")
    Smast = P.sb("Smast", [128, 16, 128], F32); Smast_T = TL("Sm", 16)
    rstd = P.sb("rstd", [128, NMAX], F32); rstd_T = T("rstd")
    sq = [P.sb(f"sq{i}", [128, NMAX], BF16) for i in range(2)]; sq_T = TL("sq", 2)
    tmpf = [P.sb(f"tmpf{i}", [128, 512], F32) for i in range(2)]; tmpf_T = TL("tmpf", 2)
    small = P.sb("small", [128, 64], F32); small_T = TL("small", 4)
    ctr = {"sq": 0, "tmpf": 0, "bank": 0, "q": 0, "ev": 0, "nb": 2}
    SSB = [6, 7]

    psb = [P.ps(f"psb{i}", [128, 512]) for i in range(8)]
    psT = [TL(f"ps{i}_", 4) for i in range(8)]
    for i in range(8):
        bx = T(f"bank{i}")
        for t_ in psT[i]:
            t_.bank = bx

    def set_pool(nb):
        ctr["nb"] = nb

    def bank():
        b = ctr["bank"] % ctr["nb"]
        ctr["bank"] += 1
        return b

    def quarter():
        nbk = 6 - ctr["nb"]
        q = ctr["q"] % (nbk * 4)
        ctr["q"] += 1
        return ctr["nb"] + q % nbk, q // nbk

    def qf(bq):
        b, q = bq
        return psb[b][:, 128 * q:128 * (q + 1)]

    def qb(bq):
        b, q = bq
        return psb[b][:, 128 * q:128 * (q + 1)].bitcast(BF16)

    def qT(bq):
        return [psT[bq[0]][bq[1]]]

    def ev_eng():
        ctr["ev"] += 1
        return "act" if ctr["ev"] % 2 else "dve"

    def copy(e, out, in_, reads, writes):
        if e == "act":
            P.op("act", lambda x: x.copy(out=out, in_=in_), reads=reads, writes=writes)
        else:
            P.op(e, lambda x: x.tensor_copy(out=out, in_=in_), reads=reads, writes=writes)

    def cc(nm, j=None):
        if j is None:
            return consts[:, CO[nm]:CO[nm] + 16]
        return consts[:, CO[nm] + j:CO[nm] + j + 1]

    def gcol(l, n, c=None):
        o0 = CO["g"] + (l * 6 + n) * 16
        if c is None:
            return consts[:, o0:o0 + 16]
        return consts[:, o0 + c:o0 + c + 1]

    def blocks(N):
        out = []
        c0 = 0
        while c0 < N:
            cn = min(512, N - c0)
            out.append((c0, cn))
            c0 += cn
        return out

    P.dma("sp", consts[:], consts_d, writes=[consts_T])
    P.dma("sp", identf[:], cst["ident"], writes=[cT])
    for L in (64, 32):
        P.dma("sp", tri[L][:], cst[f"tri{L}"], writes=[cT])
        P.dma("pool", mstrict[L][:], cst[f"mstrict{L}"], writes=[cT])
        P.dma("pool", mincl[L][:], cst[f"mincl{L}"], writes=[cT])
        for i_, m_ in enumerate(lvm[L]):
            P.dma("pool", m_[:], cst[f"lv{L}_{i_}"], writes=[cT])
    P.dma("pool", identb[:], cst["ident"], writes=[cT])
    P.dma("pool", bdones[:], cst["bdones"], writes=[cT])
    P.dma("sp", bkv[:], bkv_d.ap().partition_broadcast(128), writes=[cT])
    P.dma("sp", sinks[:], sinks_d.ap().partition_broadcast(128), writes=[cT])
    P.op("dve", lambda e: e.memset(onesb[:], 1.0), writes=[cT])
    P.op("dve", lambda e: e.memset(hT[:, :, 0:1], 0.0), writes=hT_T)
    P.op("dve", lambda e: e.memset(Smast[:], 0.0), writes=Smast_T)
    P.op("dve", lambda e: e.memset(ktc[:], 0.0), writes=[ktc_T])
    P.op("dve", lambda e: e.memset(vbc[:], 0.0), writes=[vbc_T])

    def build_bias():
        tb = tmpf[0]; oh = tmpf[1]
        P.dma("sp", tb[0:32, 0:32], table_d, writes=[tmpf_T[0]])
        P.dma("sp", oh[0:32, 0:255], cst["onehot"], writes=[tmpf_T[1]])
        bq = quarter()
        pso = psb[bq[0]][0:32, 0:255]
        P.op("pe", lambda e: e.matmul(pso, tb[0:32, 0:32], oh[0:32, 0:255], start=True, stop=True),
             reads=[tmpf_T[0], tmpf_T[1]], writes=psT[bq[0]])
        fs, fs_T = bigv(0, [128, 256], F32)
        P.op("act", lambda e: e.copy(out=fs[0:32, 0:255], in_=pso), reads=psT[bq[0]], writes=fs_T)
        fT = T("fscr")
        P.dma("sp", fscr.ap(), fs[0:32, 0:255], reads=fs_T, writes=[fT])
        stg, stg_T = bigv(1024, [128, 32, 192], F32)
        for i in range(64):
            src = bass.AP(fscr, 63 - i, [[0, 1], [255, 32], [1, 192]])
            for half in range(2):
                p = half * 64 + i
                P.dma("sp", stg[p:p + 1, :, :], src, reads=[fT], writes=stg_T, semt=stg_T[0], concurrent=True)
        P.op("act", lambda e: e.copy(out=bias2[:, 0:16, :], in_=stg[:, 0:16, :]), reads=stg_T, writes=[bias2_T])
        P.op("dve", lambda e: e.tensor_copy(out=bias2[:, 16:32, :], in_=stg[:, 16:32, :]), reads=stg_T + [bias2_T], writes=[bias2_T])

    build_bias()

    def sumsq_accumulate(src_ap_fn, src_tiles_fn, N, nch, ssb):
        for c in range(nch):
            s = ctr["sq"] % 2; ctr["sq"] += 1
            P.op("act", lambda e: e.activation(out=sq[s][:, 0:N], in_=src_ap_fn(c), func=AF.Square),
                 reads=src_tiles_fn(c), writes=[sq_T[s]])
            for bi, (c0, cn) in enumerate(blocks(N)):
                P.op("pe", lambda e: e.matmul(psb[ssb[bi]][:, 0:cn], onesb[:], sq[s][:, c0:c0 + cn],
                                              start=(c == 0), stop=(c == nch - 1)),
                     reads=[sq_T[s], cT], writes=psT[ssb[bi]], inc=True)

    def rstd_from_ss(N, ssb, eps):
        for bi, (c0, cn) in enumerate(blocks(N)):
            P.op("dve", lambda e: e.tensor_scalar(out=rstd[:, c0:c0 + cn], in0=psb[ssb[bi]][:, 0:cn],
                                                  scalar1=1.0 / D, scalar2=eps, op0=ALU.mult, op1=ALU.add),
                 reads=psT[ssb[bi]], writes=[rstd_T])
        P.op("act", lambda e: e.activation(out=rstd[:, 0:N], in_=rstd[:, 0:N], func=AF.Ln),
             reads=[rstd_T], writes=[rstd_T])
        P.op("act", lambda e: e.activation(out=rstd[:, 0:N], in_=rstd[:, 0:N], func=AF.Exp, scale=-0.5),
             reads=[rstd_T], writes=[rstd_T])

    def prenorm(l, n, N):
        ssb = SSB
        sumsq_accumulate(lambda c: xT[:, c, 0:N], lambda c: [xT_T[c]], N, DC, ssb)
        rstd_from_ss(N, ssb, RMS_EPS)
        for c in range(DC):
            P.op("dve", lambda e: e.scalar_tensor_tensor(out=hT[:, c, 1:1 + N], in0=xT[:, c, 0:N],
                                                         scalar=gcol(l, n, c), in1=rstd[:, 0:N],
                                                         op0=ALU.mult, op1=ALU.mult),
                 reads=[xT_T[c], rstd_T, consts_T], writes=[hT_T[c]])

    def postnorm_add(l, n, N, ssb, weight):
        rstd_from_ss(N, ssb, RMS_EPS)
        for c in range(DC):
            for (c0, cn) in blocks(N):
                s = ctr["tmpf"] % 2; ctr["tmpf"] += 1
                P.op("dve", lambda e: e.scalar_tensor_tensor(out=tmpf[s][:, 0:cn], in0=hT[:, c, 1 + c0:1 + c0 + cn],
                                                             scalar=gcol(l, n, c), in1=rstd[:, c0:c0 + cn],
                                                             op0=ALU.mult, op1=ALU.mult),
                     reads=[hT_T[c], rstd_T, consts_T], writes=[tmpf_T[s]])
                P.op("dve", lambda e: e.scalar_tensor_tensor(out=xT[:, c, c0:c0 + cn], in0=tmpf[s][:, 0:cn],
                                                             scalar=float(weight), in1=xT[:, c, c0:c0 + cn],
                                                             op0=ALU.mult, op1=ALU.add),
                     reads=[tmpf_T[s]], writes=[xT_T[c]])

    def load_w(dram_ap):
        s = wctr[0] % 4; wctr[0] += 1
        P.dma("pool", wslot[s][:], dram_ap, writes=[wslot_T[s]])
        return wslot[s], wslot_T[s]

    def out_evac_ss(c, N, pbanks, ssb, first, last, bias=None):
        s = ctr["sq"] % 2; ctr["sq"] += 1
        for bi, (c0, cn) in enumerate(blocks(N)):
            b = pbanks[bi]
            P.op("act", lambda e: e.copy(out=hT[:, c, 1 + c0:1 + c0 + cn], in_=psb[b][:, 0:cn]),
                 reads=psT[b], writes=[hT_T[c]])
            P.op("act", lambda e: e.activation(out=sq[s][:, c0:c0 + cn], in_=psb[b][:, 0:cn], func=AF.Square),
                 reads=psT[b], writes=[sq_T[s]])
        def pe_part():
            for bi, (c0, cn) in enumerate(blocks(N)):
                P.op("pe", lambda e: e.matmul(psb[ssb[bi]][:, 0:cn], onesb[:], sq[s][:, c0:c0 + cn],
                                              start=first, stop=last),
                     reads=[sq_T[s], cT], writes=psT[ssb[bi]], inc=True)
        return pe_part

    def ffn(l, s, N):
        n_in, n_out = (0, 1) if s == 0 else (4, 5)
        set_pool(6)
        prenorm(l, n_in, N)
        actT = big[:, 0:FC * SL].rearrange("p (f t) -> p f t", t=SL)
        blks = blocks(N)
        for f in range(FC):
            wgs, wgT = load_w(wg_d[l, s, f])
            wus, wuT = load_w(wu_d[l, s, f])
            for (c0, cn) in blks:
                bg = bank(); bu = bank()
                for kt in range(DC):
                    P.op("pe", lambda e: e.matmul(psb[bg][:, 0:cn], wgs[:, kt, :], hT[:, kt, 1 + c0:1 + c0 + cn],
                                                  start=(kt == 0), stop=(kt == DC - 1)),
                         reads=[wgT, hT_T[kt]], writes=psT[bg], inc=(kt == DC - 1))
                for kt in range(DC):
                    P.op("pe", lambda e: e.matmul(psb[bu][:, 0:cn], wus[:, kt, :], hT[:, kt, 1 + c0:1 + c0 + cn],
                                                  start=(kt == 0), stop=(kt == DC - 1)),
                         reads=[wuT, hT_T[kt]], writes=psT[bu], inc=(kt == DC - 1))
                ts = ctr["tmpf"] % 2; ctr["tmpf"] += 1
                P.op("act", lambda e: e.activation(out=tmpf[ts][:, 0:cn], in_=psb[bg][:, 0:cn], func=AF.Silu),
                     reads=psT[bg], writes=[tmpf_T[ts]])
                P.op("dve", lambda e: e.tensor_tensor(out=actT[:, f, c0:c0 + cn], in0=tmpf[ts][:, 0:cn],
                                                      in1=psb[bu][:, 0:cn], op=ALU.mult),
                     reads=[tmpf_T[ts]] + psT[bu], writes=[big_T[f]])
        ssb = SSB
        pend = None
        for d in range(DC):
            pb = [bank() for _ in blks]
            for qr in range(4):
                sl = wctr[1] % 4; wctr[1] += 1
                P.dma("pool", dslot[sl], wd_d[l, s, d, qr], writes=[dslot_T[sl]])
                for bi, (c0, cn) in enumerate(blks):
                    for k in range(11):
                        f = qr * 11 + k
                        P.op("pe", lambda e: e.matmul(psb[pb[bi]][:, 0:cn], dslot[sl][:, k, :], actT[:, f, c0:c0 + cn],
                                                      start=(f == 0), stop=(f == FC - 1)),
                             reads=[dslot_T[sl], big_T[f]], writes=psT[pb[bi]], inc=(k == 10))
            if pend is not None:
                pend()
            pend = out_evac_ss(d, N, pb, ssb, d == 0, d == DC - 1)
        pend()
        postnorm_add(l, n_out, N, ssb, 0.5)

    def attention(gi, N, npr, has_sample):
        l = 0
        ntile = N // 128
        nptile = npr // 128
        set_pool(2)
        prenorm(l, 2, N)
        off = 0
        qTb, qT_T = bigv(off, [128, DC, NMAX], BF16); off += DC * NMAX * 2
        KT, KT_T = bigv(off, [128, 4, 128 + NMAX], BF16); off += 4 * (128 + NMAX) * 2
        Vb, Vb_T = bigv(off, [128, 7, 256], BF16); off += 7 * 256 * 2
        sbufs = []
        for i in range(4):
            a, t = bigv(off, [128, 256], F32); off += 1024
            sbufs.append((a, t))
        pbufs = []
        for i in range(4):
            a, t = bigv(off, [128, 256], BF16); off += 512
            pbufs.append((a, t))
        ptbufs = []
        for i in range(3):
            a, t = bigv(off, [128, 2, 128], BF16); off += 512
            ptbufs.append((a, t))
        stage, stage_T = bigv(off, [128, 512], F32); off += 2048
        if has_sample:
            KTs, KTs_T = bigv(off, [128, 4, 4, 256], BF16); off += 4 * 4 * 256 * 2
            Vc, Vc_T = bigv(off, [128, 4, 256], BF16); off += 4 * 256 * 2
            ssb_s = []
            for i in range(4):
                a, t = bigv(off, [128, 256], F32); off += 1024
                ssb_s.append((a, t))
            ckf, ckf_T = bigv(off, [128, 256], F32); off += 1024
        assert off <= BIGB, off

        P.op("pool", lambda e: e.tensor_copy(out=KT[:, :, 0:128], in_=ktc[:]), reads=[ktc_T], writes=KT_T)
        P.op("pool", lambda e: e.tensor_copy(out=Vb[:, 0, :], in_=vbc[:]), reads=[vbc_T], writes=Vb_T)

        blks = blocks(N)
        for j in range(20):
            ws, wT = load_w(wqkv_d[j])
            for (c0, cn) in blks:
                b = bank()
                for kt in range(DC):
                    P.op("pe", lambda e: e.matmul(psb[b][:, 0:cn], ws[:, kt, :], hT[:, kt, 1 + c0:1 + c0 + cn],
                                                  start=(kt == 0), stop=(kt == DC - 1)),
                         reads=[wT, hT_T[kt]], writes=psT[b], inc=(kt == DC - 1))
                bcol = consts[:, CO["bq"] + j:CO["bq"] + j + 1]
                if j < 16:
                    P.op("act", lambda e: e.activation(out=qTb[:, j, c0:c0 + cn], in_=psb[b][:, 0:cn],
                                                       func=AF.Identity, bias=bcol, scale=1.0),
                         reads=psT[b] + [consts_T], writes=qT_T)
                else:
                    P.op("act", lambda e: e.activation(out=KT[:, j - 16, 128 + c0:128 + c0 + cn], in_=psb[b][:, 0:cn],
                                                       func=AF.Identity, bias=bcol, scale=1.0),
                         reads=psT[b] + [consts_T], writes=KT_T)
        for cchunk in range(4):
            is_k = cchunk < 2
            ws, wT = load_w(wkvt_d[cchunk])
            for t in range(ntile):
                out_tile = (gi == 2) and (t >= nptile - 1)
                if is_k and not out_tile:
                    continue
                bq = quarter()
                for kt in range(DC):
                    P.op("pe", lambda e: e.matmul(qf(bq), hT[:, kt, 1 + 128 * t:1 + 128 * (t + 1)], ws[:, kt, :],
                                                  start=(kt == 0), stop=(kt == DC - 1)),
                         reads=[wT, hT_T[kt]], writes=qT(bq), inc=(kt == DC - 1))
                bsl = bkv[:, 128 * cchunk:128 * (cchunk + 1)]
                if not is_k:
                    P.op("dve", lambda e: e.tensor_tensor(out=Vb[:, 1 + t, 128 * (cchunk - 2):128 * (cchunk - 1)],
                                                          in0=qf(bq), in1=bsl, op=ALU.add),
                         reads=qT(bq) + [cT], writes=Vb_T)
                if out_tile:
                    P.op("dve", lambda e: e.tensor_tensor(out=stage[:, 128 * cchunk:128 * (cchunk + 1)],
                                                          in0=qf(bq), in1=bsl, op=ALU.add),
                         reads=qT(bq) + [cT], writes=stage_T)
                    is_s = (t == nptile)
                    dst = (ks_d if is_s else kp_d) if is_k else (vs_d if is_s else vp_d)
                    co = 128 * (cchunk % 2)
                    P.dma("sp", dst[:, co:co + 128], stage[:, 128 * cchunk:128 * (cchunk + 1)],
                          reads=stage_T, is_output=True)

        def preset(buf, tiles):
            P.op("pool", lambda e: e.memset(buf[:], NEG), writes=tiles)

        for (a, t_) in sbufs:
            preset(a, t_)

        if has_sample:
            for s in range(4):
                preset(ssb_s[s][0], ssb_s[s][1])
                P.dma("pool", Vc[:, s, :], cv[s], writes=Vc_T, semt=Vc_T[0])
                P.dma("sp", ckf[:], ck[s], writes=ckf_T)
                for kvh in range(4):
                    di = ctr["tmpf"] % 2; ctr["tmpf"] += 1
                    dsrc, dsrc_T = tmpf[di], tmpf_T[di]
                    for dup in range(2):
                        P.op("dve", lambda e: e.tensor_copy(out=dsrc[:, 64 * dup:64 * (dup + 1)],
                                                            in_=ckf[:, 64 * kvh:64 * (kvh + 1)]),
                             reads=ckf_T, writes=[dsrc_T])
                    bq = quarter()
                    P.op("pe", lambda e: e.transpose(qf(bq), dsrc[:, 0:128], identf[:]),
                         reads=[dsrc_T, cT], writes=qT(bq))
                    copy("act", KTs[:, s, kvh, 0:128], qf(bq), qT(bq), KTs_T)
                P.op("pool", lambda e: e.tensor_copy(out=KTs[:, s, :, 128:256], in_=KT[:, :, 128 + npr:128 + npr + 128]),
                     reads=KT_T, writes=KTs_T)

        jobs = []

        def stage_a(jb):
            i = jb["i"]; h = jb["h"]; nq = jb["nq"]
            sb, sb_T = jb["sb"]
            b = bank()
            P.op("pe", lambda e: e.matmul(psb[b][0:nq, 0:256], jb["q"], jb["k"], start=True, stop=True),
                 reads=qT_T + jb["k_T"], writes=psT[b][0:2])
            for (r0, r1, oc0, oc1, bc0) in jb["bops"]:
                P.op("dve", lambda e: e.scalar_tensor_tensor(out=sb[r0:r1, oc0:oc1], in0=psb[b][r0:r1, oc0:oc1], scalar=0.125,
                                                             in1=bias2[r0:r1, h, bc0:bc0 + (oc1 - oc0)],
                                                             op0=ALU.mult, op1=ALU.add),
                     reads=psT[b][0:2] + [bias2_T], writes=sb_T)
            g = i % 4
            sm = small[:, 16 * g:16 * g + 16]; sT = [small_T[g]]
            P.op("dve", lambda e: e.reduce_max(out=sm[0:nq, 0:1], in_=sb[0:nq, :], axis=AX.X), reads=sb_T, writes=sT)
            P.op("dve", lambda e: e.tensor_scalar(out=sm[0:nq, 2:3], in0=sm[0:nq, 0:1], scalar1=sinks[0:nq, h:h + 1], scalar2=-1.0,
                                                  op0=ALU.max, op1=ALU.mult),
                 reads=sT + [cT], writes=sT)
            pb_, pb_T = pbufs[i % 4]
            P.op("act", lambda e: e.activation(out=pb_[0:nq, :], in_=sb[0:nq, :], func=AF.Exp, bias=sm[0:nq, 2:3], scale=1.0,
                                               accum_out=sm[0:nq, 3:4]),
                 reads=sb_T + sT, writes=pb_T + sT)
            P.op("act", lambda e: e.activation(out=sm[0:nq, 4:5], in_=sinks[0:nq, h:h + 1], func=AF.Exp, bias=sm[0:nq, 2:3], scale=1.0),
                 reads=sT + [cT], writes=sT)

        def stage_a2(jb):
            i = jb["i"]; nq = jb["nq"]
            g = i % 4
            sm = small[:, 16 * g:16 * g + 16]; sT = [small_T[g]]
            pb_, pb_T = pbufs[i % 4]
            P.op("dve", lambda e: e.tensor_tensor(out=sm[0:nq, 5:6], in0=sm[0:nq, 3:4], in1=sm[0:nq, 4:5], op=ALU.add),
                 reads=sT, writes=sT)
            P.op("dve", lambda e: e.reciprocal(out=sm[0:nq, 6:7], in_=sm[0:nq, 5:6]), reads=sT, writes=sT)
            P.op("dve", lambda e: e.tensor_scalar(out=pb_[0:nq, :], in0=pb_[0:nq, :], scalar1=sm[0:nq, 6:7], scalar2=None, op0=ALU.mult),
                 reads=pb_T + sT, writes=pb_T)

        def stage_b(jb):
            i = jb["i"]; nq = jb["nq"]
            pb_, pb_T = pbufs[i % 4]
            pt_, pt_T = ptbufs[i % 3]
            for kt in range(2):
                bq = quarter()
                P.op("pe", lambda e: e.transpose(qb(bq)[:, 0:nq], pb_[0:nq, 128 * kt:128 * (kt + 1)], identb[0:nq, 0:nq]),
                     reads=pb_T + [cT], writes=qT(bq))
                copy(ev_eng(), pt_[:, kt, 0:nq], qb(bq)[:, 0:nq], qT(bq), pt_T)

        def stage_c(jb):
            i = jb["i"]; nq = jb["nq"]
            pt_, pt_T = ptbufs[i % 3]
            if jb["hh"] == 0:
                jb["pair"]["bq"] = quarter()
            bq = jb["pair"]["bq"]
            r0 = 64 * jb["hh"]
            for kt in range(2):
                P.op("pe", lambda e: e.matmul(qf(bq)[r0:r0 + 64, 0:nq], jb["v"][kt], pt_[:, kt, 0:nq],
                                              start=(kt == 0), stop=(kt == 1)),
                     reads=pt_T + jb["v_T"], writes=qT(bq), inc=(kt == 1))
            if jb["hh"] == 1:
                copy("act", jb["o_dst"], qf(bq)[:, 0:nq], qT(bq), qT_T)

        def add_pair(j, nq, qcols, kfn, k_T, v, v_T, sbpair, bops):
            pair = {}
            for hh in range(2):
                i = len(jobs)
                jobs.append(dict(i=i, h=2 * j + hh, hh=hh, nq=nq, pair=pair,
                                 q=qTb[64 * hh:64 * (hh + 1), j, qcols[0]:qcols[0] + nq], k=kfn(hh), k_T=k_T,
                                 v=v, v_T=v_T, sb=sbpair[i % 2], bops=bops,
                                 o_dst=qTb[:, j, qcols[0]:qcols[0] + nq]))

        for t in range(nptile):
            first_tile = (gi == 0) and t == 0
            if first_tile:
                bops = [(0, 64, 128, 192, 128), (64, 128, 128, 256, 64)]
                sbp = sbufs[0:2]
            else:
                bops = [(0, 64, 0, 192, 0), (64, 128, 64, 256, 0)]
                sbp = sbufs[2:4]
            for j in range(DC):
                kvh = (2 * j) // 8
                add_pair(j, 128, (128 * t,),
                         lambda hh, kvh=kvh, t=t: KT[64 * hh:64 * (hh + 1), kvh, 128 * t:128 * t + 256], KT_T,
                         [Vb[:, t, 64 * kvh:64 * (kvh + 1)], Vb[:, t + 1, 64 * kvh:64 * (kvh + 1)]], Vb_T, sbp, bops)
        if has_sample:
            for j in range(DC):
                kvh = (2 * j) // 8
                for s in range(4):
                    bops = [(0, 32, 0, 128, 0), (0, 32, 128 + 32 * s, 160 + 32 * s, 128)]
                    add_pair(j, 32, (npr + 32 * s,),
                             lambda hh, kvh=kvh, s=s: KTs[64 * hh:64 * (hh + 1), s, kvh, :], KTs_T,
                             [Vc[:, s, 64 * kvh:64 * (kvh + 1)], Vb[:, 1 + nptile, 64 * kvh:64 * (kvh + 1)]],
                             Vc_T + Vb_T, [ssb_s[s], ssb_s[s]], bops)
        nj = len(jobs)
        for step in range(nj + 3):
            if step < nj:
                stage_a(jobs[step])
            if 0 <= step - 1 < nj:
                stage_a2(jobs[step - 1])
            if 0 <= step - 2 < nj:
                stage_b(jobs[step - 2])
            if 0 <= step - 3 < nj:
                stage_c(jobs[step - 3])
        if gi < 2:
            P.op("pool", lambda e: e.tensor_copy(out=ktc[:], in_=KT[:, :, npr:npr + 128]), reads=KT_T, writes=[ktc_T])
            P.op("pool", lambda e: e.tensor_copy(out=vbc[:], in_=Vb[:, nptile, :]), reads=Vb_T, writes=[vbc_T])
        set_pool(6)
        ssb = SSB
        pend = None
        for dch in range(DC):
            ws, wT = load_w(wao_d[dch])
            pb = [bank() for _ in blks]
            for bi, (c0, cn) in enumerate(blks):
                for kt in range(DC):
                    P.op("pe", lambda e: e.matmul(psb[pb[bi]][:, 0:cn], ws[:, kt, :], qTb[:, kt, c0:c0 + cn],
                                                  start=(kt == 0), stop=(kt == DC - 1)),
                         reads=[wT] + qT_T, writes=psT[pb[bi]], inc=(kt == DC - 1))
            if pend is not None:
                pend()
            pend = out_evac_ss(dch, N, pb, ssb, dch == 0, dch == DC - 1)
        pend()
        postnorm_add(l, 3, N, ssb, 1.0)

    def rwkv(gi, N, npr, has_sample):
        l = 1
        set_pool(6)
        prenorm(l, 2, N)
        blks = blocks(N)
        st = {"off": 0}

        def h_last(col, dst_ap):
            di = ctr["tmpf"] % 2; ctr["tmpf"] += 1
            so, so_T = tmpf[di], [tmpf_T[di]]
            P.op("dve", lambda e: e.scalar_tensor_tensor(out=so[:, 0:DC], in0=xT[:, :, col], scalar=rstd[:, col:col + 1], in1=gcol(l, 2),
                                                         op0=ALU.mult, op1=ALU.mult),
                 reads=xT_T + [rstd_T, consts_T], writes=so_T)
            P.dma("sp", dst_ap, so[:, 0:DC], reads=so_T, is_output=True)
        def buf(shape, dt):
            esz = 4 if dt == F32 else 2
            a_, t_ = bigv(st["off"], list(shape), dt)
            st["off"] = (st["off"] + int(np.prod(shape[1:])) * esz + 3) // 4 * 4
            return a_, t_

        if gi == 2:
            h_last(npr - 1, shp_d)
            for s_ in range(4):
                h_last(npr + 32 * s_ + 31, shs_d[:, s_, :])
        U = 4
        NB = N
        ygT, yg_T = buf([128, DC, NB], BF16)
        if has_sample:
            hsh, hsh_T = buf([128, DC, 128], BF16)
        lora, lora_T = buf([128, 4, NB], BF16)
        w2s, w2s_T = buf([128, 128], BF16); a2s, a2s_T = buf([128, 128], BF16); g2s, g2s_T = buf([128, 2, 128], BF16)
        rT_, rT_T = buf([128, NB], BF16); kT_, kT_T = buf([128, NB], BF16); vT_, vT_T = buf([128, NB], BF16)
        aT_, aT_T = buf([128, NB], BF16); kkn, kkn_T = buf([128, NB], BF16); ynT, ynT_T = buf([128, NB], BF16)
        lwT, lwT_T = buf([128, NB], F32)
        (wa, wa_T), (wb, wb_T) = buf([128, DC, 128], BF16), buf([128, DC, 128], BF16)

        def aux_views(flat_bf16, region_T, shapes):
            outs = []
            o_ = 0
            for shp in shapes:
                n_ = int(np.prod(shp[1:]))
                ap_ = flat_bf16[:, o_:o_ + n_]
                if len(shp) == 3:
                    ap_ = ap_.rearrange("p (a b) -> p a b", b=shp[2])
                t_ = T("aux")
                for rt_ in region_T:
                    for k_, ev_ in rt_.w.items():
                        if k_ not in t_.w or t_.w[k_][1] < ev_[1]:
                            t_.w[k_] = ev_
                    for k_, ev_ in rt_.r.items():
                        if k_ not in t_.r or t_.r[k_][1] < ev_[1]:
                            t_.r[k_] = ev_
                outs.append((ap_, [t_]))
                o_ += n_
            return outs

        def aux_release(region_T, subs):
            for rt_ in region_T:
                for _, tl in subs:
                    for t_ in tl:
                        for k_, ev_ in list(t_.w.items()) + list(t_.r.items()):
                            if k_ not in rt_.r or rt_.r[k_][1] < ev_[1]:
                                rt_.r[k_] = ev_

        d0 = dsl[:, 0:2].rearrange("p a b c -> p (a b c)")
        d1 = dsl[:, 2:4].rearrange("p a b c -> p (a b c)")
        r0 = rstd[:].bitcast(BF16)
        w2f = wslot[2][:].rearrange("p a b -> p (a b)")
        w3f = wslot[3][:].rearrange("p a b -> p (a b)")
        aux0 = aux_views(d0, dslot_T[0:2], [[128, U, 128]] * 5)
        aux1 = aux_views(d1, dslot_T[2:4], [[128, U, 128]] * 3 + [[128, U, 256]])
        aux2 = aux_views(r0, [rstd_T], [[128, U, 128]] * 3)
        aux3 = aux_views(w2f, [wslot_T[2]], [[128, U, 128]] * 4)
        aux4 = aux_views(w3f, [wslot_T[3]], [[128, U, 128]] * 4)
        (Dm, Dm_T), (Em, Em_T), (Mfull, Mfull_T) = aux0[0], aux0[1], aux0[2]
        (bdB, bdB_T), (bdK, bdK_T), (bdV, bdV_T) = aux2
        Qm, Qm_T = buf([128, U, 128], BF16); P1b, P1b_T = buf([128, U, 128], BF16)
        Xb, Xb_T = buf([128, U, 256], BF16)
        Sb, Sb_T = buf([128, 128], BF16)
        e1, e1_T = buf([128, 256], BF16); e2, e2_T = buf([128, 256], BF16); e3, e3_T = buf([128, 256], BF16)
        SETS = []
        s0 = dict(tokV=aux1[0], tokK=aux1[1], tokB=aux1[2], bdR=aux1[3], ynbd=aux0[3])
        s0["ArbT"] = buf([128, U, 128], BF16); s0["ArkT"] = buf([128, U, 128], BF16); s0["Vbar"] = buf([128, U, 128], BF16)
        s0["AbT"] = buf([128, U, 128], BF16); s0["Ub"] = buf([128, U, 128], BF16); s0["gam"] = buf([128, 8], F32)
        s1 = dict(tokV=aux3[0], tokK=aux3[1], tokB=aux3[2], AbT=aux3[3], ArbT=aux4[0], ArkT=aux4[1], Vbar=aux4[2], Ub=aux4[3], ynbd=aux0[4])
        s1["bdR"] = buf([128, U, 256], BF16); s1["gam"] = buf([128, 8], F32)
        SETS = [s0, s1]
        if has_sample:
            Ss, Ss_T = buf([128, U, 128], F32)
        assert st["off"] <= BIGB, st["off"]
        def zero_bd(sets):
            zl = [(bdB, bdB_T), (bdK, bdK_T), (bdV, bdV_T)]
            for S_ in sets:
                zl += [S_["bdR"], S_["ynbd"]]
            for a_, t_ in zl:
                P.op("pool", lambda e: e.memset(a_, 0.0), writes=t_)

        if has_sample:
            di = ctr["tmpf"] % 2; ctr["tmpf"] += 1
            sh32, sh32_T = tmpf[di], tmpf_T[di]
            P.dma("sp", sh32[:, 0:4 * DC], sshift.rearrange("p s c -> p (s c)"), writes=[sh32_T])
            for s_ in range(4):
                P.op("dve", lambda e: e.tensor_copy(out=hsh[:, :, 32 * s_:32 * s_ + 1],
                                                    in_=sh32[:, s_ * DC:(s_ + 1) * DC].unsqueeze(2)),
                     reads=[sh32_T], writes=hsh_T)
                P.op("dve", lambda e: e.tensor_copy(out=hsh[:, :, 32 * s_ + 1:32 * s_ + 32],
                                                    in_=hT[:, :, 1 + npr + 32 * s_:1 + npr + 32 * s_ + 31]),
                     reads=hT_T, writes=hsh_T)

        def rhs_pairs(c0, cn):
            if has_sample and c0 >= npr:
                return (lambda kt: hT[:, kt, 1 + c0:1 + c0 + cn]), (lambda kt: hsh[:, kt, c0 - npr:c0 - npr + cn]), hsh_T
            return (lambda kt: hT[:, kt, 1 + c0:1 + c0 + cn]), (lambda kt: hT[:, kt, c0:c0 + cn]), []

        def mixed_linear(wdram_ap, n, M, evac):
            si = wctr[0] % 2; wctr[0] += 1
            ws, wT = wslot[si], wslot_T[si]
            P.dma("pool", ws[:, :, 0:M], wdram_ap, writes=[wT])
            mu = consts[:, CO["mu"] + n * 16:CO["mu"] + (n + 1) * 16].unsqueeze(2).broadcast_to([128, DC, M])
            P.op("dve", lambda e: e.tensor_tensor(out=wb[:, :, 0:M], in0=ws[:, :, 0:M], in1=mu, op=ALU.mult),
                 reads=[wT, consts_T], writes=wb_T)
            P.op("dve", lambda e: e.tensor_tensor(out=wa[:, :, 0:M], in0=ws[:, :, 0:M], in1=wb[:, :, 0:M], op=ALU.subtract),
                 reads=[wT] + wb_T, writes=wa_T)
            for (c0, cn) in blks:
                cur, shf, extra = rhs_pairs(c0, cn)
                b = bank()
                for kt in range(DC):
                    P.op("pe", lambda e: e.matmul(psb[b][0:M, 0:cn], wa[:, kt, 0:M], cur(kt), start=(kt == 0), stop=False),
                         reads=wa_T + [hT_T[kt]], writes=psT[b], inc=False)
                    P.op("pe", lambda e: e.matmul(psb[b][0:M, 0:cn], wb[:, kt, 0:M], shf(kt), start=False, stop=(kt == DC - 1)),
                         reads=wb_T + [hT_T[kt]] + extra, writes=psT[b], inc=(kt == DC - 1))
                evac(b, c0, cn)

        mixed_linear(w1_d, 1, 96, lambda b, c0, cn: P.op(
            "act", lambda e: e.activation(out=lora[0:96, 0, c0:c0 + cn], in_=psb[b][0:96, 0:cn], func=AF.Tanh),
            reads=psT[b], writes=lora_T))
        mixed_linear(a1_d, 4, 96, lambda b, c0, cn: P.op(
            "act", lambda e: e.copy(out=lora[0:96, 1, c0:c0 + cn], in_=psb[b][0:96, 0:cn]),
            reads=psT[b], writes=lora_T))
        for gc in range(2):
            mixed_linear(g1_d[gc], 5, 128, lambda b, c0, cn: P.op(
                "act", lambda e: e.activation(out=lora[:, 2 + gc, c0:c0 + cn], in_=psb[b][:, 0:cn], func=AF.Sigmoid),
                reads=psT[b], writes=lora_T))

        bcount = [0]

        def v3(ap_, TP, w0_, w1_):
            return ap_[0:TP, :, w0_:w1_]

        def prep_gen(j, cb, L, nu, S_, rezero):
            TP = 2 * L
            ncol = nu * L
            ntl = ncol // 128
            (tokV, tokV_T), (tokK, tokK_T), (tokB, tokB_T), (bdR, bdR_T) = S_["tokV"], S_["tokK"], S_["tokB"], S_["bdR"]
            (ArbT, ArbT_T), (ArkT, ArkT_T), (Vbar, Vbar_T), (AbT, AbT_T) = S_["ArbT"], S_["ArkT"], S_["Vbar"], S_["AbT"]
            gam, gam_T = S_["gam"]
            if rezero:
                zero_bd([S_])
            bcl = bank()
            for tl in range(ntl):
                c_ = cb + 128 * tl
                b_ = bank()
                P.op("pe", lambda e: e.transpose(psb[b_][:, 0:128], lwT[:, c_:c_ + 128], identf[:]), reads=lwT_T + [cT], writes=psT[b_][0:1])
                di = ctr["tmpf"] % 2; ctr["tmpf"] += 1
                P.op("act", lambda e: e.copy(out=tmpf[di][:, 0:128], in_=psb[b_][:, 0:128]), reads=psT[b_][0:1], writes=[tmpf_T[di]])
                P.op("pe", lambda e: e.matmul(psb[bcl][:, 128 * tl:128 * (tl + 1)], tmpf[di][:, 0:128], tri[L][:], start=True, stop=True),
                     reads=[tmpf_T[di], cT], writes=psT[bcl])
            cl = psb[bcl][:, 0:ncol]
            P.op("act", lambda e: e.activation(out=e1[:, 0:ncol], in_=cl, func=AF.Exp), reads=psT[bcl], writes=e1_T)
            P.op("act", lambda e: e.activation(out=e2[:, 0:ncol], in_=cl, func=AF.Exp, scale=-1.0), reads=psT[bcl], writes=e2_T)
            P.op("act", lambda e: e.activation(out=gam[:, 0:nu], in_=psb[bcl][:, 0:ncol].rearrange("p (u l) -> p u l", l=L)[:, :, L - 1],
                                               func=AF.Exp), reads=psT[bcl], writes=gam_T)
            di = ctr["tmpf"] % 2; ctr["tmpf"] += 1
            P.op("dve", lambda e: e.tensor_tensor(out=tmpf[di][:, 0:ncol], in0=cl, in1=lwT[:, cb:cb + ncol], op=ALU.subtract),
                 reads=psT[bcl] + lwT_T, writes=[tmpf_T[di]])
            P.op("act", lambda e: e.activation(out=e3[:, 0:ncol], in_=tmpf[di][:, 0:ncol], func=AF.Exp), reads=[tmpf_T[di]], writes=e3_T)
            yield
            def src(ap_, ps_):
                return ap_[ps_, cb:cb + ncol].rearrange("p (u l) -> p u l", l=L)

            def esrc(ap_, ps_):
                return ap_[ps_, 0:ncol].rearrange("p (u l) -> p u l", l=L)
            for hh in range(2):
                ps_ = slice(64 * hh, 64 * hh + 64)
                cs0, cs1 = L * hh, L * hh + L
                eng = "dve" if hh == 0 else "pool"
                P.op("dve", lambda e: e.scalar_tensor_tensor(out=bdR[ps_, 0:nu, cs0:cs1], in0=src(kkn, ps_), scalar=-1.0, in1=esrc(e3, ps_),
                                                           op0=ALU.mult, op1=ALU.mult),
                     reads=kkn_T + e3_T, writes=bdR_T)
                P.op(eng, lambda e: e.tensor_tensor(out=bdR[ps_, 0:nu, TP + cs0:TP + cs1], in0=src(rT_, ps_), in1=esrc(e1, ps_), op=ALU.mult),
                     reads=rT_T + e1_T, writes=bdR_T)
                P.op(eng, lambda e: e.tensor_tensor(out=bdK[ps_, 0:nu, cs0:cs1], in0=src(kT_, ps_), in1=esrc(e2, ps_), op=ALU.mult),
                     reads=kT_T + e2_T, writes=bdK_T)
                P.op(eng, lambda e: e.tensor_tensor(out=bdB[ps_, 0:nu, cs0:cs1], in0=src(kkn, ps_), in1=src(aT_, ps_), op=ALU.mult),
                     reads=kkn_T + aT_T, writes=bdB_T)
                P.op(eng, lambda e: e.tensor_tensor(out=bdB[ps_, 0:nu, cs0:cs1], in0=bdB[ps_, 0:nu, cs0:cs1], in1=esrc(e2, ps_), op=ALU.mult),
                     reads=bdB_T + e2_T, writes=bdB_T)
                P.op("act", lambda e: e.copy(out=bdV[ps_, 0:nu, cs0:cs1], in_=src(vT_, ps_)), reads=vT_T, writes=bdV_T)
            yield
            def tr_all(dst, dst_T, srcfn, src_T, rows_in, cols_in, eng):
                b_ = bank()
                pv = psb[b_][:, :].bitcast(BF16)
                for u in range(nu):
                    P.op("pe", lambda e: e.transpose(pv[0:cols_in, rows_in * u:rows_in * (u + 1)], srcfn(u), identb[0:rows_in, 0:rows_in]),
                         reads=src_T + [cT], writes=psT[b_], inc=(u == nu - 1))
                copy(eng, dst, pv[0:cols_in, 0:rows_in * nu].rearrange("p (u r) -> p u r", r=rows_in), psT[b_], dst_T)
            tr_all(tokV[0:TP, 0:nu, :], tokV_T, lambda u: bdV[:, u, 0:TP], bdV_T, 128, TP, "act")
            tr_all(tokK[0:TP, 0:nu, :], tokK_T, lambda u: bdK[:, u, 0:TP], bdK_T, 128, TP, "dve")
            tr_all(tokB[0:TP, 0:nu, :], tokB_T, lambda u: bdB[:, u, 0:TP], bdB_T, 128, TP, "act")
            tr_all(Xb[0:TP, 0:nu, 0:128], Xb_T, lambda u: bdR[:, u, 0:TP], bdR_T, 128, TP, "dve")
            yield
            upb = 512 // (2 * TP)
            for lhs, lhs_T, outs in ((bdB, bdB_T, ((Mfull, Mfull_T, None), (ArbT, ArbT_T, mincl[L]))),
                                     (bdK, bdK_T, ((P1b, P1b_T, mstrict[L]), (ArkT, ArkT_T, mincl[L])))):
                for u0 in range(0, nu, upb):
                    b_ = bank()
                    n_ = min(upb, nu - u0)
                    for u in range(u0, u0 + n_):
                        o_ = (u - u0) * 2 * TP
                        P.op("pe", lambda e: e.matmul(psb[b_][0:TP, o_:o_ + 2 * TP], lhs[:, u, 0:TP], bdR[:, u, 0:2 * TP], start=True, stop=True),
                             reads=lhs_T + bdR_T, writes=psT[b_], inc=(u == u0 + n_ - 1))
                    pv = psb[b_][0:TP, 0:n_ * 2 * TP].rearrange("p (u c) -> p u c", c=2 * TP)
                    for part, (dst, dst_T, msk) in enumerate(outs):
                        sv = pv[:, :, part * TP:(part + 1) * TP]
                        dv = dst[0:TP, u0:u0 + n_, 0:TP]
                        if msk is None:
                            copy("act", dv, sv, psT[b_], dst_T)
                        else:
                            P.op("dve", lambda e: e.tensor_tensor(out=dv, in0=sv, in1=msk[0:TP, 0:TP].unsqueeze(1).broadcast_to([TP, n_, TP]), op=ALU.mult),
                                 reads=psT[b_] + [cT], writes=dst_T)
            yield
            b_ = bank()
            for u in range(nu):
                P.op("pe", lambda e: e.matmul(psb[b_][0:TP, 128 * u:128 * (u + 1)], P1b[0:TP, u, 0:TP], tokV[0:TP, u, :], start=True, stop=True),
                     reads=P1b_T + tokV_T, writes=psT[b_], inc=(u == nu - 1))
            copy("act", Xb[0:TP, 0:nu, 128:256], psb[b_][0:TP, 0:128 * nu].rearrange("p (u c) -> p u c", c=128), psT[b_], Xb_T)
            idb = identb[0:TP, 0:TP].unsqueeze(1).broadcast_to([TP, nu, TP])
            yield
            P.op("act", lambda e: e.copy(out=Dm[0:TP, 0:nu, 0:TP], in_=idb), reads=[cT], writes=Dm_T)
            P.op("pool", lambda e: e.tensor_copy(out=Em[0:TP, 0:nu, 0:TP], in_=idb), reads=[cT], writes=Em_T)
            for lv in range(len(lvm[L])):
                P.op("pool", lambda e: e.tensor_tensor(out=Qm[0:TP, 0:nu, 0:TP], in0=Mfull[0:TP, 0:nu, 0:TP],
                                                       in1=lvm[L][lv][0:TP, 0:TP].unsqueeze(1).broadcast_to([TP, nu, TP]), op=ALU.mult),
                     reads=Mfull_T + [cT], writes=Qm_T)
                b1_ = bank()
                for u in range(nu):
                    P.op("pe", lambda e: e.matmul(psb[b1_][0:TP, 128 * u:128 * u + TP], Qm[0:TP, u, 0:TP], Dm[0:TP, u, 0:TP], start=True, stop=True),
                         reads=Qm_T + Dm_T, writes=psT[b1_], inc=(u == nu - 1))
                copy("act", P1b[0:TP, 0:nu, 0:TP], psb[b1_][0:TP, 0:128 * nu].rearrange("p (u c) -> p u c", c=128)[:, :, 0:TP], psT[b1_], P1b_T)
                yield
                b2_ = bank()
                for u in range(nu):
                    P.op("pe", lambda e: e.matmul(psb[b2_][0:TP, 128 * u:128 * u + TP], Em[0:TP, u, 0:TP], P1b[0:TP, u, 0:TP], start=True, stop=True),
                         reads=Em_T + P1b_T, writes=psT[b2_], inc=(u == nu - 1))
                P.op("dve", lambda e: e.tensor_tensor(out=Dm[0:TP, 0:nu, 0:TP], in0=Dm[0:TP, 0:nu, 0:TP],
                                                      in1=psb[b2_][0:TP, 0:128 * nu].rearrange("p (u c) -> p u c", c=128)[:, :, 0:TP], op=ALU.add),
                     reads=Dm_T + psT[b2_], writes=Dm_T)
                yield
                tr_all(Em[0:TP, 0:nu, 0:TP], Em_T, lambda u: Dm[0:TP, u, 0:TP], Dm_T, TP, TP, "act")
                yield
            Ec, Ec_T = Em, Em_T
            for u0 in range(0, nu, 2):
                b_ = bank()
                n_ = min(2, nu - u0)
                for u in range(u0, u0 + n_):
                    P.op("pe", lambda e: e.matmul(psb[b_][0:TP, 256 * (u - u0):256 * (u - u0 + 1)], Ec[0:TP, u, 0:TP], Xb[0:TP, u, :], start=True, stop=True),
                         reads=Ec_T + Xb_T, writes=psT[b_], inc=(u == u0 + n_ - 1))
                pv = psb[b_][0:TP, 0:256 * n_].rearrange("p (u c) -> p u c", c=256)
                copy("act", Vbar[0:TP, u0:u0 + n_, :], pv[:, :, 128:256], psT[b_], Vbar_T)
                copy("dve", Xb[0:TP, u0:u0 + n_, 0:128], pv[:, :, 0:128], psT[b_], Xb_T)
            yield
            tr_all(AbT[:, 0:nu, 0:TP], AbT_T, lambda u: Xb[0:TP, u, 0:128], Xb_T, TP, 128, "act")
            yield

        def scan_gen(j, cb, L, nu, S_, states):
            TP = 2 * L
            ncol = nu * L
            (tokV, tokV_T), (tokK, tokK_T), (tokB, tokB_T), (bdR, bdR_T) = S_["tokV"], S_["tokK"], S_["tokB"], S_["bdR"]
            (ArbT, ArbT_T), (ArkT, ArkT_T), (Vbar, Vbar_T), (AbT, AbT_T) = S_["ArbT"], S_["ArkT"], S_["Vbar"], S_["AbT"]
            (Ub, Ub_T), (gam, gam_T), (ynbd, ynbd_T) = S_["Ub"], S_["gam"], S_["ynbd"]
            bcount[0] += 1
            by = SSB[bcount[0] % 2]
            for u in range(nu):
                Sm_ap, Sm_T = states[u]
                P.op("act", lambda e: e.copy(out=Sb[:, :], in_=Sm_ap), reads=Sm_T, writes=Sb_T)
                bu = bank()
                P.op("pe", lambda e: e.matmul(psb[bu][0:TP, 0:128], AbT[:, u, 0:TP], Sb[:, :], start=True, stop=True),
                     reads=AbT_T + Sb_T, writes=psT[bu])
                P.op("dve", lambda e: e.tensor_tensor(out=Ub[0:TP, u, :], in0=psb[bu][0:TP, 0:128], in1=Vbar[0:TP, u, :], op=ALU.add),
                     reads=psT[bu] + Vbar_T, writes=Ub_T)
                yield
                yo = psb[by][0:TP, 128 * u:128 * (u + 1)]
                P.op("pe", lambda e: e.matmul(yo, bdR[:, u, TP:2 * TP], Sb[:, :], start=True, stop=False),
                     reads=bdR_T + Sb_T, writes=psT[by], inc=False)
                P.op("pe", lambda e: e.matmul(yo, ArkT[0:TP, u, 0:TP], tokV[0:TP, u, :], start=False, stop=False),
                     reads=ArkT_T + tokV_T, writes=psT[by], inc=False)
                P.op("pe", lambda e: e.matmul(yo, ArbT[0:TP, u, 0:TP], Ub[0:TP, u, :], start=False, stop=True),
                     reads=ArbT_T + Ub_T, writes=psT[by])
                bs = bank()
                P.op("pe", lambda e: e.matmul(psb[bs][:, 0:128], tokK[0:TP, u, :], tokV[0:TP, u, :], start=True, stop=False),
                     reads=tokK_T + tokV_T, writes=psT[bs], inc=False)
                P.op("pe", lambda e: e.matmul(psb[bs][:, 0:128], tokB[0:TP, u, :], Ub[0:TP, u, :], start=False, stop=True),
                     reads=tokB_T + Ub_T, writes=psT[bs])
                di = ctr["tmpf"] % 2; ctr["tmpf"] += 1
                P.op("act", lambda e: e.activation(out=tmpf[di][:, 0:128], in_=psb[bs][:, 0:128], func=AF.Copy, scale=gam[:, u:u + 1]),
                     reads=psT[bs] + gam_T, writes=[tmpf_T[di]])
                P.op("dve", lambda e: e.scalar_tensor_tensor(out=Sm_ap, in0=Sm_ap, scalar=gam[:, u:u + 1], in1=tmpf[di][:, 0:128],
                                                             op0=ALU.mult, op1=ALU.add),
                     reads=Sm_T + gam_T + [tmpf_T[di]], writes=Sm_T)
                yield
            g = bcount[0] % 4
            sm = small[:, 16 * g:16 * g + 16]; sT = [small_T[g]]
            Y3 = psb[by][0:TP, 0:128 * nu].rearrange("p (u c) -> p u c", c=128)
            yield
            P.op("dve", lambda e: e.reduce_sum(out=sm[0:TP, 0:nu], in_=Y3, axis=AX.X), reads=psT[by], writes=sT)
            di = ctr["tmpf"] % 2; ctr["tmpf"] += 1
            sqv = tmpf[di][0:TP, 0:128 * nu].rearrange("p (u c) -> p u c", c=128)
            P.op("act", lambda e: e.activation(out=sqv, in_=Y3, func=AF.Square), reads=psT[by], writes=[tmpf_T[di]])
            P.op("dve", lambda e: e.reduce_sum(out=sm[0:TP, 4:4 + nu], in_=sqv, axis=AX.X), reads=[tmpf_T[di]], writes=sT)
            P.op("dve", lambda e: e.tensor_scalar(out=sm[0:TP, 0:nu], in0=sm[0:TP, 0:nu], scalar1=1.0 / 64, scalar2=None, op0=ALU.mult),
                 reads=sT, writes=sT)
            P.op("dve", lambda e: e.tensor_tensor(out=sm[0:TP, 8:8 + nu], in0=sm[0:TP, 0:nu], in1=sm[0:TP, 0:nu], op=ALU.mult),
                 reads=sT, writes=sT)
            P.op("dve", lambda e: e.scalar_tensor_tensor(out=sm[0:TP, 4:4 + nu], in0=sm[0:TP, 4:4 + nu], scalar=1.0 / 64, in1=sm[0:TP, 8:8 + nu],
                                                         op0=ALU.mult, op1=ALU.subtract),
                 reads=sT, writes=sT)
            P.op("dve", lambda e: e.tensor_scalar(out=sm[0:TP, 4:4 + nu], in0=sm[0:TP, 4:4 + nu], scalar1=GN_EPS, scalar2=None, op0=ALU.add),
                 reads=sT, writes=sT)
            P.op("act", lambda e: e.activation(out=sm[0:TP, 4:4 + nu], in_=sm[0:TP, 4:4 + nu], func=AF.Ln), reads=sT, writes=sT)
            P.op("act", lambda e: e.activation(out=sm[0:TP, 4:4 + nu], in_=sm[0:TP, 4:4 + nu], func=AF.Exp, scale=-0.5), reads=sT, writes=sT)
            for hh in range(2):
                rs = slice(L * hh, L * hh + L)
                cs = slice(64 * hh, 64 * hh + 64)
                P.op("dve", lambda e: e.tensor_tensor(out=ynbd[rs, 0:nu, cs], in0=Y3[rs, :, cs],
                                                      in1=sm[rs, 0:nu].unsqueeze(2).broadcast_to([L, nu, 64]), op=ALU.subtract),
                     reads=psT[by] + sT, writes=ynbd_T)
                P.op("dve", lambda e: e.tensor_tensor(out=ynbd[rs, 0:nu, cs], in0=ynbd[rs, 0:nu, cs],
                                                      in1=sm[rs, 4:4 + nu].unsqueeze(2).broadcast_to([L, nu, 64]), op=ALU.mult),
                     reads=ynbd_T + sT, writes=ynbd_T)
            yield
            b_ = bank()
            pv = psb[b_][:, :].bitcast(BF16)
            for u in range(nu):
                P.op("pe", lambda e: e.transpose(pv[:, TP * u:TP * (u + 1)], ynbd[0:TP, u, :], identb[0:TP, 0:TP]),
                     reads=ynbd_T + [cT], writes=psT[b_], inc=(u == nu - 1))
            for hh in range(2):
                ps_ = slice(64 * hh, 64 * hh + 64)
                P.op("act", lambda e: e.copy(out=ynT[ps_, cb:cb + ncol].rearrange("p (u l) -> p u l", l=L),
                                             in_=pv[ps_, 0:TP * nu].rearrange("p (u c) -> p u c", c=TP)[:, :, L * hh:L * hh + L]),
                     reads=psT[b_], writes=ynT_T)

        def state_out(src_ap, src_T, dst):
            b_ = bank()
            P.op("pe", lambda e: e.transpose(psb[b_][:, 0:128], src_ap, identf[:]), reads=src_T + [cT], writes=psT[b_])
            di = ctr["tmpf"] % 2; ctr["tmpf"] += 1
            so, so_T = tmpf[di], [tmpf_T[di]]
            for hh in range(2):
                ps_ = slice(64 * hh, 64 * hh + 64)
                P.op("act", lambda e: e.copy(out=so[ps_, 0:64], in_=psb[b_][ps_, 64 * hh:64 * hh + 64]), reads=psT[b_], writes=so_T)
            P.dma("sp", dst, so[:, 0:64], reads=so_T, is_output=True)

        for j in range(DC):
            P.dma("pool", w2s[0:96, :], w2_d[:, 128 * j:128 * (j + 1)], writes=w2s_T, semt=w2s_T[0])
            P.dma("pool", a2s[0:96, :], a2_d[:, 128 * j:128 * (j + 1)], writes=a2s_T, semt=a2s_T[0])
            P.dma("pool", g2s[:, :, :], g2_d[:, :, 128 * j:128 * (j + 1)], writes=g2s_T, semt=g2s_T[0])
            for (wdram, n, dst, dst_T) in ((wr_d, 0, rT_, rT_T), (wk_d, 2, kT_, kT_T), (wv_d, 3, vT_, vT_T)):
                mixed_linear(wdram[j], n, 128, lambda b, c0, cn, dst=dst, dst_T=dst_T: copy(
                    ev_eng(), dst[:, c0:c0 + cn], psb[b][:, 0:cn], psT[b], dst_T))
            for (c0, cn) in blks:
                b = bank()
                P.op("pe", lambda e: e.matmul(psb[b][:, 0:cn], w2s[0:96, :], lora[0:96, 0, c0:c0 + cn], start=True, stop=True),
                     reads=w2s_T + lora_T, writes=psT[b])
                P.op("act", lambda e: e.activation(out=lwT[:, c0:c0 + cn], in_=psb[b][:, 0:cn], func=AF.Exp, bias=cc("w0", j), scale=1.0),
                     reads=psT[b] + [consts_T], writes=lwT_T)
                ts = ctr["tmpf"] % 2; ctr["tmpf"] += 1
                tf, tf_T = tmpf[ts], [tmpf_T[ts]]
                P.op("dve", lambda e: e.tensor_scalar(out=tf[:, 0:cn], in0=lwT[:, c0:c0 + cn], scalar1=1.0, scalar2=None, op0=ALU.add),
                     reads=lwT_T, writes=tf_T)
                P.op("dve", lambda e: e.reciprocal(out=tf[:, 0:cn], in_=tf[:, 0:cn]), reads=tf_T, writes=tf_T)
                P.op("dve", lambda e: e.scalar_tensor_tensor(out=lwT[:, c0:c0 + cn], in0=lwT[:, c0:c0 + cn], scalar=-math.exp(-0.5), in1=tf[:, 0:cn],
                                                             op0=ALU.mult, op1=ALU.mult),
                     reads=lwT_T + tf_T, writes=lwT_T)
                b = bank()
                P.op("pe", lambda e: e.matmul(psb[b][:, 0:cn], a2s[0:96, :], lora[0:96, 1, c0:c0 + cn], start=True, stop=True),
                     reads=a2s_T + lora_T, writes=psT[b])
                P.op("act", lambda e: e.activation(out=aT_[:, c0:c0 + cn], in_=psb[b][:, 0:cn], func=AF.Sigmoid, bias=cc("a0", j), scale=1.0),
                     reads=psT[b] + [consts_T], writes=aT_T)
            P.op("dve", lambda e: e.tensor_scalar(out=kkn[:, 0:N], in0=kT_[:, 0:N], scalar1=cc("kk", j), scalar2=None, op0=ALU.mult),
                 reads=kT_T + [consts_T], writes=kkn_T)
            s = ctr["sq"] % 2; ctr["sq"] += 1
            P.op("act", lambda e: e.activation(out=sq[s][:, 0:N], in_=kkn[:, 0:N], func=AF.Square), reads=kkn_T, writes=[sq_T[s]])
            for (c0, cn) in blks:
                b = bank()
                P.op("pe", lambda e: e.matmul(psb[b][:, 0:cn], bdones[:], sq[s][:, c0:c0 + cn], start=True, stop=True),
                     reads=[sq_T[s], cT], writes=psT[b])
                ts = ctr["tmpf"] % 2; ctr["tmpf"] += 1
                tf, tf_T = tmpf[ts], [tmpf_T[ts]]
                P.op("dve", lambda e: e.tensor_scalar(out=tf[:, 0:cn], in0=psb[b][:, 0:cn], scalar1=1e-30, scalar2=None, op0=ALU.add),
                     reads=psT[b], writes=tf_T)
                P.op("act", lambda e: e.activation(out=tf[:, 0:cn], in_=tf[:, 0:cn], func=AF.Ln), reads=tf_T, writes=tf_T)
                P.op("act", lambda e: e.activation(out=tf[:, 0:cn], in_=tf[:, 0:cn], func=AF.Exp, scale=-0.5), reads=tf_T, writes=tf_T)
                P.op("dve", lambda e: e.tensor_tensor(out=kkn[:, c0:c0 + cn], in0=kkn[:, c0:c0 + cn], in1=tf[:, 0:cn], op=ALU.mult),
                     reads=kkn_T + tf_T, writes=kkn_T)
                P.op("dve", lambda e: e.tensor_scalar(out=tf[:, 0:cn], in0=aT_[:, c0:c0 + cn], scalar1=-1.0, scalar2=cc("ka", j), op0=ALU.add, op1=ALU.mult),
                     reads=aT_T + [consts_T], writes=tf_T)
                P.op("dve", lambda e: e.tensor_scalar(out=tf[:, 0:cn], in0=tf[:, 0:cn], scalar1=1.0, scalar2=None, op0=ALU.add),
                     reads=tf_T, writes=tf_T)
                P.op("dve", lambda e: e.tensor_tensor(out=kT_[:, c0:c0 + cn], in0=kT_[:, c0:c0 + cn], in1=tf[:, 0:cn], op=ALU.mult),
                     reads=kT_T + tf_T, writes=kT_T)
            zero_bd(SETS)
            blist = [(256 * bi_, 64, [(Smast[:, j, :], [Smast_T[j]])] * 4, False) for bi_ in range(npr // 256)]
            if has_sample:
                for u in range(4):
                    di = ctr["tmpf"] % 2; ctr["tmpf"] += 1
                    si_, si_T = tmpf[di], [tmpf_T[di]]
                    P.dma("sp", si_[:, 256:320], swkv[u, j], writes=si_T)
                    P.op("dve", lambda e: e.memset(si_[:, 0:128], 0.0), writes=si_T)
                    for hh in range(2):
                        ps_ = slice(64 * hh, 64 * hh + 64)
                        P.op("dve", lambda e: e.tensor_copy(out=si_[ps_, 64 * hh:64 * hh + 64], in_=si_[ps_, 256:320]), reads=si_T, writes=si_T)
                    b_ = bank()
                    P.op("pe", lambda e: e.transpose(psb[b_][:, 0:128], si_[:, 0:128], identf[:]), reads=si_T + [cT], writes=psT[b_])
                    P.op("act", lambda e: e.copy(out=Ss[:, u, :], in_=psb[b_][:, 0:128]), reads=psT[b_], writes=Ss_T)
                blist.append((npr, 32, [(Ss[:, u, :], Ss_T) for u in range(4)], True))

            def drive(gens):
                gens = [g_ for g_ in gens if g_ is not None]
                while gens:
                    for g_ in list(gens):
                        try:
                            next(g_)
                        except StopIteration:
                            gens.remove(g_)
            drive([prep_gen(j, blist[0][0], blist[0][1], 4, SETS[0], blist[0][3])])
            for bi_, (cb_, L_, states_, rz_) in enumerate(blist):
                nxt = None
                if bi_ + 1 < len(blist):
                    n_ = blist[bi_ + 1]
                    nxt = prep_gen(j, n_[0], n_[1], 4, SETS[(bi_ + 1) % 2], n_[3])
                drive([scan_gen(j, cb_, L_, 4, SETS[bi_ % 2], states_), nxt])
            if has_sample:
                for u in range(4):
                    state_out(Ss[:, u, :], Ss_T, wkvs_d[u, j])
            if gi == 2:
                state_out(Smast[:, j, :], [Smast_T[j]], wkvp_d[j])
            s = ctr["sq"] % 2; ctr["sq"] += 1
            P.op("dve", lambda e: e.scalar_tensor_tensor(out=sq[s][:, 0:N], in0=rT_[:, 0:N], scalar=cc("rk", j), in1=kT_[:, 0:N],
                                                         op0=ALU.mult, op1=ALU.mult),
                 reads=rT_T + kT_T + [consts_T], writes=[sq_T[s]])
            for (c0, cn) in blks:
                b = bank()
                P.op("pe", lambda e: e.matmul(psb[b][:, 0:cn], bdones[:], sq[s][:, c0:c0 + cn], start=True, stop=True),
                     reads=[sq_T[s], cT], writes=psT[b])
                ts = ctr["tmpf"] % 2; ctr["tmpf"] += 1
                tf, tf_T = tmpf[ts], [tmpf_T[ts]]
                P.op("dve", lambda e: e.tensor_tensor(out=tf[:, 0:cn], in0=psb[b][:, 0:cn], in1=vT_[:, c0:c0 + cn], op=ALU.mult),
                     reads=psT[b] + vT_T, writes=tf_T)
                ts2 = ctr["tmpf"] % 2; ctr["tmpf"] += 1
                tg, tg_T = tmpf[ts2], [tmpf_T[ts2]]
                P.op("dve", lambda e: e.tensor_scalar(out=tg[:, 0:cn], in0=ynT[:, c0:c0 + cn], scalar1=cc("lnw", j), scalar2=cc("lnb", j),
                                                      op0=ALU.mult, op1=ALU.add),
                     reads=ynT_T + [consts_T], writes=tg_T)
                P.op("dve", lambda e: e.tensor_tensor(out=tf[:, 0:cn], in0=tf[:, 0:cn], in1=tg[:, 0:cn], op=ALU.add),
                     reads=tf_T + tg_T, writes=tf_T)
                bg = bank()
                for kt in range(2):
                    P.op("pe", lambda e: e.matmul(psb[bg][:, 0:cn], g2s[:, kt, :], lora[:, 2 + kt, c0:c0 + cn],
                                                  start=(kt == 0), stop=(kt == 1)),
                         reads=g2s_T + lora_T, writes=psT[bg], inc=(kt == 1))
                P.op("dve", lambda e: e.tensor_tensor(out=ygT[:, j, c0:c0 + cn], in0=tf[:, 0:cn], in1=psb[bg][:, 0:cn], op=ALU.mult),
                     reads=tf_T + psT[bg], writes=yg_T)

        aux_release(dslot_T[0:2], aux0); aux_release(dslot_T[2:4], aux1); aux_release([rstd_T], aux2)
        aux_release([wslot_T[2]], aux3); aux_release([wslot_T[3]], aux4)
        return ygT, yg_T

    def rwkv_full(gi, N, npr, has_sample):
        ygT, yg_T = rwkv(gi, N, npr, has_sample)
        if dbg == "yg":
            for c in range(DC):
                P.op("act", lambda e: e.copy(out=xT[:, c, 0:N], in_=ygT[:, c, 0:N]), reads=yg_T, writes=[xT_T[c]])
            return
        P.op("pool", lambda e: e.tensor_copy(out=hT[:, :, 0:1], in_=hT[:, :, npr:npr + 1]), reads=hT_T, writes=hT_T)
        blks = blocks(N)
        set_pool(6)
        ssb = SSB
        pend = None
        for dch in range(DC):
            ws, wT = load_w(wro_d[dch])
            pb = [bank() for _ in blks]
            for bi, (c0, cn) in enumerate(blks):
                for kt in range(DC):
                    P.op("pe", lambda e: e.matmul(psb[pb[bi]][:, 0:cn], ws[:, kt, :], ygT[:, kt, c0:c0 + cn],
                                                  start=(kt == 0), stop=(kt == DC - 1)),
                         reads=[wT] + yg_T, writes=psT[pb[bi]], inc=(kt == DC - 1))
            if pend is not None:
                pend()
            pend = out_evac_ss(dch, N, pb, ssb, dch == 0, dch == DC - 1)
        pend()
        postnorm_add(1, 3, N, ssb, 1.0)

    for gi, (p0, npr, has_s) in enumerate(GROUPS[:ngroups]):
        N = npr + (128 if has_s else 0)
        for c in range(DC):
            P.dma("sp", xT[:, c, 0:npr], xp[:, c, p0:p0 + npr], writes=[xT_T[c]])
            if has_s:
                P.dma("sp", xT[:, c, npr:npr + 128], xs[:, c, :], writes=[xT_T[c]])
        for l in range(nlayers):
            ffn(l, 0, N)
            if dbg == f"ffn{l}0" and gi == 0:
                break
            if l == 0:
                attention(gi, N, npr, has_s)
            else:
                rwkv_full(gi, N, npr, has_s)
            if dbg in (f"mix{l}", "yg") and gi == 0 and (dbg != "yg" or l == 1):
                break
            ffn(l, 1, N)
        for c in range(DC):
            P.dma("sp", yT_d[:, c, p0:p0 + npr], xT[:, c, 0:npr], reads=[xT_T[c]], is_output=True)
            if has_s:
                P.dma("sp", yT_d[:, c, SEQ:SEQ + 128], xT[:, c, npr:npr + 128], reads=[xT_T[c]], is_output=True)
    P.finish()
    stats = dict(ops=P.n_ops, waits=P.n_waits, sems=P.nsem, cnt=dict(P.cnt))
    P.close()
    return nc, stats


def prep_shared(inp):
    f = lambda a: np.ascontiguousarray(np.asarray(a, dtype=np.float32))
    sh = {}
    sh["wg"] = np.stack([np.stack([w_chunks(f(inp["ffn_w_gate"][l, s])) for s in range(2)]) for l in range(2)])
    sh["wu"] = np.stack([np.stack([w_chunks(f(inp["ffn_w_up"][l, s])) for s in range(2)]) for l in range(2)])
    wd = np.stack([np.stack([w_chunks(f(inp["ffn_w_down"][l, s])) for s in range(2)]) for l in range(2)])
    sh["wd"] = np.ascontiguousarray(wd.reshape(2, 2, DC, 128, 4, 11, 128).transpose(0, 1, 2, 4, 3, 5, 6))
    wqkv = f(inp["att_w_qkv"][0])
    bqkv = f(inp["att_b_qkv"][0])
    kcols = [np.concatenate([wqkv[:, 2048 + 64 * h:2048 + 64 * (h + 1)]] * 2, axis=1) for h in range(4)]
    wext = np.concatenate([wqkv[:, :2048]] + kcols, axis=1)
    sh["wqkv"] = w_chunks(wext)
    bext = np.concatenate([bqkv[:2048]] + [np.concatenate([bqkv[2048 + 64 * h:2048 + 64 * (h + 1)]] * 2) for h in range(4)])
    sh["wkvt"] = w_chunks(wqkv[:, 2048:2560])
    sh["bkv"] = bqkv[2048:2560].reshape(1, 512).copy()
    sh["sinks"] = f(inp["att_sinks"]).reshape(1, 32).copy()
    sh["table"] = f(inp["rel_table"])
    sh["wao"] = w_chunks(f(inp["att_w_o"][0]))
    sh["wr"] = w_chunks(f(inp["rwkv_w_r"][0]))
    sh["wk"] = w_chunks(f(inp["rwkv_w_k"][0]))
    sh["wv"] = w_chunks(f(inp["rwkv_w_v"][0]))
    sh["wro"] = w_chunks(f(inp["rwkv_w_o"][0]))
    sh["w1"] = w_chunks(f(inp["rwkv_w1"][0]), 96)[0]
    sh["a1"] = w_chunks(f(inp["rwkv_a1"][0]), 96)[0]
    sh["g1"] = w_chunks(f(inp["rwkv_g1"][0]))
    sh["w2"] = f(inp["rwkv_w2"][0])
    sh["a2"] = f(inp["rwkv_a2"][0])
    sh["g2"] = np.ascontiguousarray(f(inp["rwkv_g2"][0]).reshape(2, 128, D).transpose(1, 0, 2))
    cols = [fcol(f(inp["norm_g"])).reshape(128, 12 * 16),
            bext.reshape(20, 128).T,
            fcol(f(inp["rwkv_mu"][0])).reshape(128, 6 * 16)]
    for nm in ("rwkv_w0", "rwkv_a0", "rwkv_k_k", "rwkv_k_a"):
        cols.append(fcol(f(inp[nm][0])))
    cols.append(fcol(f(inp["rwkv_r_k"][0]).reshape(D)))
    for nm in ("rwkv_ln_w", "rwkv_ln_b"):
        cols.append(fcol(f(inp[nm][0])))
    sh["consts"] = np.ascontiguousarray(np.concatenate(cols, axis=1))
    for k, v in static_consts().items():
        sh["c_" + k] = v
    return sh


def prep_core(inp, c):
    f = lambda a: np.ascontiguousarray(np.asarray(a, dtype=np.float32))
    m = {}
    m["xp"] = fcol(f(inp["x_prompt"][c]).T.copy()) if False else np.ascontiguousarray(
        f(inp["x_prompt"][c]).T.reshape(DC, 128, SEQ).transpose(1, 0, 2))
    xs = f(inp["x_sample"][4 * c:4 * c + 4]).reshape(128, D)
    m["xs"] = np.ascontiguousarray(xs.T.reshape(DC, 128, 128).transpose(1, 0, 2))
    m["ck"] = f(inp["cache_k"][0, 4 * c:4 * c + 4]).reshape(4, 128, 256)
    m["cv"] = f(inp["cache_v"][0, 4 * c:4 * c + 4]).reshape(4, 128, 256)
    ss = f(inp["state_shift"][0, 4 * c:4 * c + 4, 0])
    m["sshift"] = np.ascontiguousarray(ss.reshape(4, DC, 128).transpose(2, 0, 1))
    m["swkv"] = f(inp["state_wkv"][0, 4 * c:4 * c + 4]).reshape(4, 16, 128, 64)
    return m


_CACHE = {}


def kernel(**inputs):
    if "nc" not in _CACHE:
        _CACHE["nc"] = build()[0]
    nc = _CACHE["nc"]
    sh = prep_shared(inputs)
    in_maps = []
    for c in range(NCORE):
        m = dict(sh)
        m.update(prep_core(inputs, c))
        in_maps.append(m)
    res = run_bass_kernel_spmd(nc, in_maps, core_ids=list(range(NCORE)))
    R = res.results
    y_prompt = np.zeros((8, SEQ, D), np.float32)
    y_sample = np.zeros((32, 32, D), np.float32)
    k_prompt = np.zeros((1, 8, 128, 4, 64), np.float32)
    v_prompt = np.zeros((1, 8, 128, 4, 64), np.float32)
    k_sample = np.zeros((1, 32, 32, 4, 64), np.float32)
    v_sample = np.zeros((1, 32, 32, 4, 64), np.float32)
    shift_prompt = np.zeros((1, 8, 1, D), np.float32)
    wkv_prompt = np.zeros((1, 8, 32, 64, 64), np.float32)
    shift_sample = np.zeros((1, 32, 1, D), np.float32)
    wkv_sample = np.zeros((1, 32, 32, 64, 64), np.float32)
    for c in range(NCORE):
        r = R[c]
        yT = np.asarray(r["yT"])
        y = yT.transpose(2, 1, 0).reshape(SEQ + 128, D)
        y_prompt[c] = y[:SEQ]
        y_sample[4 * c:4 * c + 4] = y[SEQ:].reshape(4, 32, D)
        k_prompt[0, c] = np.asarray(r["kp"]).reshape(128, 4, 64)
        v_prompt[0, c] = np.asarray(r["vp"]).reshape(128, 4, 64)
        k_sample[0, 4 * c:4 * c + 4] = np.asarray(r["ks"]).reshape(4, 32, 4, 64)
        v_sample[0, 4 * c:4 * c + 4] = np.asarray(r["vs"]).reshape(4, 32, 4, 64)
        shift_prompt[0, c, 0] = np.asarray(r["shp"]).T.reshape(D)
        shift_sample[0, 4 * c:4 * c + 4, 0] = np.asarray(r["shs"]).transpose(1, 2, 0).reshape(4, D)
        wkv_prompt[0, c] = np.asarray(r["wkvp"]).reshape(32, 64, 64)
        wkv_sample[0, 4 * c:4 * c + 4] = np.asarray(r["wkvs"]).reshape(4, 32, 64, 64)
    return (y_prompt, y_sample, k_prompt, v_prompt, k_sample, v_sample,
            shift_prompt, wkv_prompt, shift_sample, wkv_sample)
```

```python
import contextlib
import math
import numpy as np
import concourse.bass as bass
import concourse.mybir as mybir
from concourse.bass_utils import run_bass_kernel_spmd

F32 = mybir.dt.float32
BF16 = mybir.dt.bfloat16
AF = mybir.ActivationFunctionType
ALU = mybir.AluOpType
AX = mybir.AxisListType

D = 2048
DC = 16
FFD = 5632
FC = 44
NCORE = 8
SEQ = 2048
NH = 32
HD = 64
WINDOW = 128
N_BUCKETS = 32
MAX_DISTANCE = 128
RMS_EPS = 1e-6
GN_EPS = 64 * 1e-5
NEG = -1.0e30
GROUPS = [(0, 768, False), (768, 768, False), (1536, 512, True)]
NMAX = 768


class T:
    __slots__ = ("name", "w", "r", "dsem", "dtot", "bank")

    def __init__(self, name):
        self.name = name
        self.w = {}
        self.r = {}
        self.dsem = None
        self.dtot = 0
        self.bank = None


def TL(name, n):
    return [T(f"{name}{i}") for i in range(n)]


class Prog:
    COMPUTE = ("pe", "act", "dve", "pool")

    def __init__(self, nc, strict_same=True):
        self.nc = nc
        self.es = contextlib.ExitStack()
        self.eng = {"pe": nc.tensor, "act": nc.scalar, "dve": nc.vector,
                    "pool": nc.gpsimd, "sp": nc.sync}
        self.sem = {}
        self.cnt = {}
        for e in self.COMPUTE:
            self.sem[e] = self.es.enter_context(nc.semaphore("s_" + e))
            self.cnt[e] = 0
        self.seen = {e: {} for e in self.eng}
        self.strict_same = strict_same
        self.nsem = 0
        self.n_ops = 0
        self.n_waits = 0
        self.out_events = []
        self.uid = 0

    def sb(self, name, shape, dt):
        return self.es.enter_context(self.nc.sbuf_tensor("sb_" + name, list(shape), dt))

    def ps(self, name, shape, dt=F32):
        return self.es.enter_context(self.nc.psum_tensor("ps_" + name, list(shape), dt))

    def newsem(self, name):
        self.nsem += 1
        self.uid += 1
        return self.es.enter_context(self.nc.semaphore(f"{name}_{self.uid}"))

    def _wait(self, e, ev):
        sem, val = ev
        k = id(sem)
        if self.seen[e].get(k, 0) >= val:
            return
        self.seen[e][k] = val
        self.eng[e].wait_ge(sem, val)
        self.n_waits += 1

    def _deps(self, e, reads, writes):
        own = id(self.sem[e]) if e in self.sem else None
        skip_own = (e == "pe") or (not self.strict_same)
        for t in reads:
            for k, ev in t.w.items():
                if k == own and skip_own:
                    continue
                self._wait(e, ev)
        for t in writes:
            for k, ev in t.w.items():
                if k == own and skip_own:
                    continue
                self._wait(e, ev)
            for k, ev in t.r.items():
                if k == own and skip_own:
                    continue
                self._wait(e, ev)
        for t in list(reads) + list(writes):
            if t.bank is not None:
                for k, ev in t.bank.w.items():
                    if k != own:
                        self._wait(e, ev)

    def _record(self, ev, reads, writes):
        k = id(ev[0])
        for t in reads:
            t.r[k] = ev
            if t.bank is not None:
                t.bank.w = {k: ev}
        for t in writes:
            t.w = {k: ev}
            t.r = {}
            if t.bank is not None:
                t.bank.w = {k: ev}

    def op(self, e, fn, reads=(), writes=(), inc=True):
        self._deps(e, reads, writes)
        ins = fn(self.eng[e])
        self.n_ops += 1
        if inc:
            self.cnt[e] += 1
            ins.then_inc(self.sem[e], 1)
            ev = (self.sem[e], self.cnt[e])
        else:
            ev = (self.sem[e], self.cnt[e] + 1)
        self._record(ev, reads, writes)
        return ins

    def dma(self, q, out_ap, in_ap, reads=(), writes=(), semt=None, is_output=False, concurrent=False, **kw):
        if semt is None:
            semt = writes[0] if writes else reads[0]
        if semt.dsem is None:
            semt.dsem = self.newsem("d")
        if concurrent:
            k = id(semt.dsem)
            saved = [(t, t.w.pop(k)) for t in writes if k in t.w]
            self._deps(q, reads, writes)
            for t, ev in saved:
                t.w[k] = ev
        else:
            self._deps(q, reads, writes)
        semt.dtot += 16
        ins = self.eng[q].dma_start(out=out_ap, in_=in_ap, **kw)
        ins.then_inc(semt.dsem, 16)
        self.n_ops += 1
        ev = (semt.dsem, semt.dtot)
        self._record(ev, reads, writes)
        if is_output:
            self.out_events.append(ev)
        return ins

    def finish(self):
        last = {}
        for sem, val in self.out_events:
            k = id(sem)
            if k not in last or last[k][1] < val:
                last[k] = (sem, val)
        for ev in last.values():
            self._wait("sp", ev)
        for e in self.COMPUTE:
            if self.cnt[e] > 0:
                self._wait("sp", (self.sem[e], self.cnt[e]))

    def close(self):
        self.es.close()


def w_chunks(w, cw=128):
    K, M = w.shape
    return np.ascontiguousarray(w.reshape(K // 128, 128, M // cw, cw).transpose(2, 1, 0, 3))


def fcol(v):
    s = v.shape[:-1]
    a = v.reshape(*s, DC, 128)
    a = np.moveaxis(a, -1, 0)
    return np.ascontiguousarray(a)


def t5_bucket_np(rel):
    nb = N_BUCKETS // 2
    max_exact = nb // 2
    offset = np.where(rel > 0, nb, 0)
    n = np.abs(rel)
    nf = np.maximum(n, 1).astype(np.float32)
    large = max_exact + (np.log(nf / np.float32(max_exact)) / np.float32(math.log(MAX_DISTANCE / max_exact))
                         * np.float32(nb - max_exact)).astype(np.int32)
    large = np.minimum(large, nb - 1)
    return offset + np.where(n < max_exact, n, large)


def static_consts():
    c = {}
    c["ident"] = np.eye(128, dtype=np.float32)
    i = np.arange(128)
    for L in (64, 32):
        same = (i[:, None] // L) == (i[None, :] // L)
        c[f"tri{L}"] = (same & (i[:, None] <= i[None, :])).astype(np.float32)
        ii = i % L
        c[f"mstrict{L}"] = (ii[:, None] < ii[None, :]).astype(np.float32)
        c[f"mincl{L}"] = (ii[:, None] <= ii[None, :]).astype(np.float32)
        b = 1
        lv = 0
        while b < L:
            c[f"lv{L}_{lv}"] = ((ii[:, None] // (2 * b) == ii[None, :] // (2 * b)) & ((ii[None, :] // b) % 2 == 1)
                               & ((ii[:, None] // b) % 2 == 0)).astype(np.float32)
            b *= 2
            lv += 1
    c["bdones"] = ((i[:, None] // 64) == (i[None, :] // 64)).astype(np.float32)
    r = np.arange(255)
    bk = t5_bucket_np((r - 191).astype(np.int32))
    oh = np.zeros((32, 255), np.float32)
    oh[bk, r] = 1.0
    c["onehot"] = oh
    return c


def build(ngroups=3, nlayers=2, dbg=None):
    nc = bass.Bass("TRN2", target_bir_lowering=False)
    import os as _os
    P = Prog(nc, strict_same=(_os.environ.get("K_STRICT", "1") == "1"))

    def din(name, shape):
        return nc.dram_tensor(name, list(shape), F32, kind="ExternalInput").ap()

    def dout(name, shape):
        return nc.dram_tensor(name, list(shape), F32, kind="ExternalOutput").ap()

    xp = din("xp", [128, DC, SEQ])
    xs = din("xs", [128, DC, 128])
    ck = din("ck", [4, 128, 256])
    cv = din("cv", [4, 128, 256])
    sshift = din("sshift", [128, 4, DC])
    swkv = din("swkv", [4, 16, 128, 64])
    NCONST = 12 * 16 + 20 + 6 * 16 + 7 * 16
    consts_d = din("consts", [128, NCONST])
    wg_d = din("wg", [2, 2, FC, 128, DC, 128])
    wu_d = din("wu", [2, 2, FC, 128, DC, 128])
    wd_d = din("wd", [2, 2, DC, 4, 128, 11, 128])
    wqkv_d = din("wqkv", [20, 128, DC, 128])
    wkvt_d = din("wkvt", [4, 128, DC, 128])
    bkv_d = nc.dram_tensor("bkv", [1, 512], F32, kind="ExternalInput")
    sinks_d = nc.dram_tensor("sinks", [1, 32], F32, kind="ExternalInput")
    table_d = din("table", [32, 32])
    wao_d = din("wao", [16, 128, DC, 128])
    wr_d = din("wr", [16, 128, DC, 128])
    wk_d = din("wk", [16, 128, DC, 128])
    wv_d = din("wv", [16, 128, DC, 128])
    wro_d = din("wro", [16, 128, DC, 128])
    w1_d = din("w1", [128, DC, 96])
    a1_d = din("a1", [128, DC, 96])
    g1_d = din("g1", [2, 128, DC, 128])
    w2_d = din("w2", [96, D])
    a2_d = din("a2", [96, D])
    g2_d = din("g2", [128, 2, D])
    cst = {k: din("c_" + k, v.shape) for k, v in static_consts().items()}
    fscr = nc.dram_tensor("fscr", [32, 255], F32, kind="Internal")

    yT_d = dout("yT", [128, DC, SEQ + 128])
    kp_d = dout("kp", [128, 256])
    vp_d = dout("vp", [128, 256])
    ks_d = dout("ks", [128, 256])
    vs_d = dout("vs", [128, 256])
    shp_d = dout("shp", [128, DC])
    shs_d = dout("shs", [128, 4, DC])
    wkvp_d = dout("wkvp", [16, 128, 64])
    wkvs_d = dout("wkvs", [4, 16, 128, 64])
    dbg_d = dout("dbg", [128, DC, NMAX]) if dbg else None

    CO = {}
    o = 0
    CO["g"] = o; o += 12 * 16
    CO["bq"] = o; o += 20
    CO["mu"] = o; o += 6 * 16
    for nm in ("w0", "a0", "kk", "ka", "rk", "lnw", "lnb"):
        CO[nm] = o; o += 16
    assert o == NCONST

    xT = P.sb("xT", [128, DC, NMAX], F32); xT_T = TL("xT", DC)
    hT = P.sb("hT", [128, DC, NMAX + 1], BF16); hT_T = TL("hT", DC)
    BIGB = 66 * 1024
    big = P.sb("big", [128, BIGB // 2], BF16)
    big_T = TL("big", FC)
    SL = 768

    def bigv(off_b, shape, dt):
        n = int(np.prod(shape[1:]))
        esz = 4 if dt == F32 else 2
        assert off_b % 4 == 0 and off_b + n * esz <= BIGB, (off_b, shape)
        if dt == F32:
            ap = big[:, off_b // 2: off_b // 2 + n * 2].bitcast(F32)
        else:
            ap = big[:, off_b // 2: off_b // 2 + n]
        if len(shape) == 3:
            ap = ap.rearrange("p (a b) -> p a b", b=shape[2])
        elif len(shape) == 4:
            ap = ap.rearrange("p (a b c) -> p a b c", b=shape[2], c=shape[3])
        t0 = off_b // (SL * 2)
        t1 = (off_b + n * esz - 1) // (SL * 2)
        return ap, big_T[t0:t1 + 1]

    wslot = [P.sb(f"ws{i}", [128, DC, 128], BF16) for i in range(4)]
    wslot_T = TL("ws", 4)
    dsl = P.sb("wds", [128, 4, 11, 128], BF16)
    dslot = [dsl[:, i] for i in range(4)]
    dslot_T = TL("wds", 4)
    wctr = [0, 0]

    consts = P.sb("consts", [128, NCONST], F32); consts_T = T("consts")
    identf = P.sb("identf", [128, 128], F32)
    identb = P.sb("identb", [128, 128], BF16)
    onesb = P.sb("onesb", [128, 128], BF16)
    bdones = P.sb("bdones", [128, 128], BF16)
    tri = {L: P.sb(f"tri{L}", [128, 128], F32) for L in (64, 32)}
    mstrict = {L: P.sb(f"mstrict{L}", [128, 128], BF16) for L in (64, 32)}
    mincl = {L: P.sb(f"mincl{L}", [128, 128], BF16) for L in (64, 32)}
    lvm = {L: [P.sb(f"lv{L}_{i}", [128, 128], BF16) for i in range(6 if L == 64 else 5)] for L in (64, 32)}
    cT = T("cst")
    bkv = P.sb("bkv", [128, 512], F32)
    sinks = P.sb("sinks", [128, 32], F32)
    bias2 = P.sb("bias2", [128, 32, 192], BF16); bias2_T = T("bias2")
    ktc = P.sb("ktc", [128, 4, 128], BF16); ktc_T = T("ktc")
    vbc = P.sb("vbc", [128, 256], BF16); vbc_T = T("vbc")
    Smast = P.sb("Smast", [128, 16, 128], F32); Smast_T = TL("Sm", 16)
    rstd = P.sb("rstd", [128, NMAX], F32); rstd_T = T("rstd")
    sq = [P.sb(f"sq{i}", [128, NMAX], BF16) for i in range(2)]; sq_T = TL("sq", 2)
    tmpf = [P.sb(f"tmpf{i}", [128, 512], F32) for i in range(2)]; tmpf_T = TL("tmpf", 2)
    small = P.sb("small", [128, 64], F32); small_T = TL("small", 4)
    ctr = {"sq": 0, "tmpf": 0, "bank": 0, "q": 0, "ev": 0, "nb": 2}
    SSB = [6, 7]

    psb = [P.ps(f"psb{i}", [128, 512]) for i in range(8)]
    psT = [TL(f"ps{i}_", 4) for i in range(8)]
    for i in range(8):
        bx = T(f"bank{i}")
        for t_ in psT[i]:
            t_.bank = bx

    def set_pool(nb):
        ctr["nb"] = nb

    def bank():
        b = ctr["bank"] % ctr["nb"]
        ctr["bank"] += 1
        return b

    def quarter():
        nbk = 6 - ctr["nb"]
        q = ctr["q"] % (nbk * 4)
        ctr["q"] += 1
        return ctr["nb"] + q % nbk, q // nbk

    def qf(bq):
        b, q = bq
        return psb[b][:, 128 * q:128 * (q + 1)]

    def qb(bq):
        b, q = bq
        return psb[b][:, 128 * q:128 * (q + 1)].bitcast(BF16)

    def qT(bq):
        return [psT[bq[0]][bq[1]]]

    def ev_eng():
        ctr["ev"] += 1
        return "act" if ctr["ev"] % 2 else "dve"

    def copy(e, out, in_, reads, writes):
        if e == "act":
            P.op("act", lambda x: x.copy(out=out, in_=in_), reads=reads, writes=writes)
        else:
            P.op(e, lambda x: x.tensor_copy(out=out, in_=in_), reads=reads, writes=writes)

    def cc(nm, j=None):
        if j is None:
            return consts[:, CO[nm]:CO[nm] + 16]
        return consts[:, CO[nm] + j:CO[nm] + j + 1]

    def gcol(l, n, c=None):
        o0 = CO["g"] + (l * 6 + n) * 16
        if c is None:
            return consts[:, o0:o0 + 16]
        return consts[:, o0 + c:o0 + c + 1]

    def blocks(N):
        out = []
        c0 = 0
        while c0 < N:
            cn = min(512, N - c0)
            out.append((c0, cn))
            c0 += cn
        return out

    P.dma("sp", consts[:], consts_d, writes=[consts_T])
    P.dma("sp", identf[:], cst["ident"], writes=[cT])
    for L in (64, 32):
        P.dma("sp", tri[L][:], cst[f"tri{L}"], writes=[cT])
        P.dma("pool", mstrict[L][:], cst[f"mstrict{L}"], writes=[cT])
        P.dma("pool", mincl[L][:], cst[f"mincl{L}"], writes=[cT])
        for i_, m_ in enumerate(lvm[L]):
            P.dma("pool", m_[:], cst[f"lv{L}_{i_}"], writes=[cT])
    P.dma("pool", identb[:], cst["ident"], writes=[cT])
    P.dma("pool", bdones[:], cst["bdones"], writes=[cT])
    P.dma("sp", bkv[:], bkv_d.ap().partition_broadcast(128), writes=[cT])
    P.dma("sp", sinks[:], sinks_d.ap().partition_broadcast(128), writes=[cT])
    P.op("dve", lambda e: e.memset(onesb[:], 1.0), writes=[cT])
    P.op("dve", lambda e: e.memset(hT[:, :, 0:1], 0.0), writes=hT_T)
    P.op("dve", lambda e: e.memset(Smast[:], 0.0), writes=Smast_T)
    P.op("dve", lambda e: e.memset(ktc[:], 0.0), writes=[ktc_T])
    P.op("dve", lambda e: e.memset(vbc[:], 0.0), writes=[vbc_T])

    def build_bias():
        tb = tmpf[0]; oh = tmpf[1]
        P.dma("sp", tb[0:32, 0:32], table_d, writes=[tmpf_T[0]])
        P.dma("sp", oh[0:32, 0:255], cst["onehot"], writes=[tmpf_T[1]])
        bq = quarter()
        pso = psb[bq[0]][0:32, 0:255]
        P.op("pe", lambda e: e.matmul(pso, tb[0:32, 0:32], oh[0:32, 0:255], start=True, stop=True),
             reads=[tmpf_T[0], tmpf_T[1]], writes=psT[bq[0]])
        fs, fs_T = bigv(0, [128, 256], F32)
        P.op("act", lambda e: e.copy(out=fs[0:32, 0:255], in_=pso), reads=psT[bq[0]], writes=fs_T)
        fT = T("fscr")
        P.dma("sp", fscr.ap(), fs[0:32, 0:255], reads=fs_T, writes=[fT])
        stg, stg_T = bigv(1024, [128, 32, 192], F32)
        for i in range(64):
            src = bass.AP(fscr, 63 - i, [[0, 1], [255, 32], [1, 192]])
            for half in range(2):
                p = half * 64 + i
                P.dma("sp", stg[p:p + 1, :, :], src, reads=[fT], writes=stg_T, semt=stg_T[0], concurrent=True)
        P.op("act", lambda e: e.copy(out=bias2[:, 0:16, :], in_=stg[:, 0:16, :]), reads=stg_T, writes=[bias2_T])
        P.op("dve", lambda e: e.tensor_copy(out=bias2[:, 16:32, :], in_=stg[:, 16:32, :]), reads=stg_T + [bias2_T], writes=[bias2_T])

    build_bias()

    def sumsq_accumulate(src_ap_fn, src_tiles_fn, N, nch, ssb):
        for c in range(nch):
            s = ctr["sq"] % 2; ctr["sq"] += 1
            P.op("act", lambda e: e.activation(out=sq[s][:, 0:N], in_=src_ap_fn(c), func=AF.Square),
                 reads=src_tiles_fn(c), writes=[sq_T[s]])
            for bi, (c0, cn) in enumerate(blocks(N)):
                P.op("pe", lambda e: e.matmul(psb[ssb[bi]][:, 0:cn], onesb[:], sq[s][:, c0:c0 + cn],
                                              start=(c == 0), stop=(c == nch - 1)),
                     reads=[sq_T[s], cT], writes=psT[ssb[bi]], inc=True)

    def rstd_from_ss(N, ssb, eps):
        for bi, (c0, cn) in enumerate(blocks(N)):
            P.op("dve", lambda e: e.tensor_scalar(out=rstd[:, c0:c0 + cn], in0=psb[ssb[bi]][:, 0:cn],
                                                  scalar1=1.0 / D, scalar2=eps, op0=ALU.mult, op1=ALU.add),
                 reads=psT[ssb[bi]], writes=[rstd_T])
        P.op("act", lambda e: e.activation(out=rstd[:, 0:N], in_=rstd[:, 0:N], func=AF.Ln),
             reads=[rstd_T], writes=[rstd_T])
        P.op("act", lambda e: e.activation(out=rstd[:, 0:N], in_=rstd[:, 0:N], func=AF.Exp, scale=-0.5),
             reads=[rstd_T], writes=[rstd_T])

    def prenorm(l, n, N):
        ssb = SSB
        sumsq_accumulate(lambda c: xT[:, c, 0:N], lambda c: [xT_T[c]], N, DC, ssb)
        rstd_from_ss(N, ssb, RMS_EPS)
        for c in range(DC):
            P.op("dve", lambda e: e.scalar_tensor_tensor(out=hT[:, c, 1:1 + N], in0=xT[:, c, 0:N],
                                                         scalar=gcol(l, n, c), in1=rstd[:, 0:N],
                                                         op0=ALU.mult, op1=ALU.mult),
                 reads=[xT_T[c], rstd_T, consts_T], writes=[hT_T[c]])

    def postnorm_add(l, n, N, ssb, weight):
        rstd_from_ss(N, ssb, RMS_EPS)
        for c in range(DC):
            for (c0, cn) in blocks(N):
                s = ctr["tmpf"] % 2; ctr["tmpf"] += 1
                P.op("dve", lambda e: e.scalar_tensor_tensor(out=tmpf[s][:, 0:cn], in0=hT[:, c, 1 + c0:1 + c0 + cn],
                                                             scalar=gcol(l, n, c), in1=rstd[:, c0:c0 + cn],
                                                             op0=ALU.mult, op1=ALU.mult),
                     reads=[hT_T[c], rstd_T, consts_T], writes=[tmpf_T[s]])
                P.op("dve", lambda e: e.scalar_tensor_tensor(out=xT[:, c, c0:c0 + cn], in0=tmpf[s][:, 0:cn],
                                                             scalar=float(weight), in1=xT[:, c, c0:c0 + cn],
                                                             op0=ALU.mult, op1=ALU.add),
                     reads=[tmpf_T[s]], writes=[xT_T[c]])

    def load_w(dram_ap):
        s = wctr[0] % 4; wctr[0] += 1
        P.dma("pool", wslot[s][:], dram_ap, writes=[wslot_T[s]])
        return wslot[s], wslot_T[s]

    def out_evac_ss(c, N, pbanks, ssb, first, last, bias=None):
        s = ctr["sq"] % 2; ctr["sq"] += 1
        for bi, (c0, cn) in enumerate(blocks(N)):
            b = pbanks[bi]
            P.op("act", lambda e: e.copy(out=hT[:, c, 1 + c0:1 + c0 + cn], in_=psb[b][:, 0:cn]),
                 reads=psT[b], writes=[hT_T[c]])
            P.op("act", lambda e: e.activation(out=sq[s][:, c0:c0 + cn], in_=psb[b][:, 0:cn], func=AF.Square),
                 reads=psT[b], writes=[sq_T[s]])
        def pe_part():
            for bi, (c0, cn) in enumerate(blocks(N)):
                P.op("pe", lambda e: e.matmul(psb[ssb[bi]][:, 0:cn], onesb[:], sq[s][:, c0:c0 + cn],
                                              start=first, stop=last),
                     reads=[sq_T[s], cT], writes=psT[ssb[bi]], inc=True)
        return pe_part

    def ffn(l, s, N):
        n_in, n_out = (0, 1) if s == 0 else (4, 5)
        set_pool(6)
        prenorm(l, n_in, N)
        actT = big[:, 0:FC * SL].rearrange("p (f t) -> p f t", t=SL)
        blks = blocks(N)
        for f in range(FC):
            wgs, wgT = load_w(wg_d[l, s, f])
            wus, wuT = load_w(wu_d[l, s, f])
            for (c0, cn) in blks:
                bg = bank(); bu = bank()
                for kt in range(DC):
                    P.op("pe", lambda e: e.matmul(psb[bg][:, 0:cn], wgs[:, kt, :], hT[:, kt, 1 + c0:1 + c0 + cn],
                                                  start=(kt == 0), stop=(kt == DC - 1)),
                         reads=[wgT, hT_T[kt]], writes=psT[bg], inc=(kt == DC - 1))
                for kt in range(DC):
                    P.op("pe", lambda e: e.matmul(psb[bu][:, 0:cn], wus[:, kt, :], hT[:, kt, 1 + c0:1 + c0 + cn],
                                                  start=(kt == 0), stop=(kt == DC - 1)),
                         reads=[wuT, hT_T[kt]], writes=psT[bu], inc=(kt == DC - 1))
                ts = ctr["tmpf"] % 2; ctr["tmpf"] += 1
                P.op("act", lambda e: e.activation(out=tmpf[ts][:, 0:cn], in_=psb[bg][:, 0:cn], func=AF.Silu),
                     reads=psT[bg], writes=[tmpf_T[ts]])
                P.op("dve", lambda e: e.tensor_tensor(out=actT[:, f, c0:c0 + cn], in0=tmpf[ts][:, 0:cn],
                                                      in1=psb[bu][:, 0:cn], op=ALU.mult),
                     reads=[tmpf_T[ts]] + psT[bu], writes=[big_T[f]])
        ssb = SSB
        pend = None
        for d in range(DC):
            pb = [bank() for _ in blks]
            for qr in range(4):
                sl = wctr[1] % 4; wctr[1] += 1
                P.dma("pool", dslot[sl], wd_d[l, s, d, qr], writes=[dslot_T[sl]])
                for bi, (c0, cn) in enumerate(blks):
                    for k in range(11):
                        f = qr * 11 + k
                        P.op("pe", lambda e: e.matmul(psb[pb[bi]][:, 0:cn], dslot[sl][:, k, :], actT[:, f, c0:c0 + cn],
                                                      start=(f == 0), stop=(f == FC - 1)),
                             reads=[dslot_T[sl], big_T[f]], writes=psT[pb[bi]], inc=(k == 10))
            if pend is not None:
                pend()
            pend = out_evac_ss(d, N, pb, ssb, d == 0, d == DC - 1)
        pend()
        postnorm_add(l, n_out, N, ssb, 0.5)

    def attention(gi, N, npr, has_sample):
        l = 0
        ntile = N // 128
        nptile = npr // 128
        set_pool(2)
        prenorm(l, 2, N)
        off = 0
        qTb, qT_T = bigv(off, [128, DC, NMAX], BF16); off += DC * NMAX * 2
        KT, KT_T = bigv(off, [128, 4, 128 + NMAX], BF16); off += 4 * (128 + NMAX) * 2
        Vb, Vb_T = bigv(off, [128, 7, 256], BF16); off += 7 * 256 * 2
        sbufs = []
        for i in range(4):
            a, t = bigv(off, [128, 256], F32); off += 1024
            sbufs.append((a, t))
        pbufs = []
        for i in range(6):
            a, t = bigv(off, [128, 256], BF16); off += 512
            pbufs.append((a, t))
        ptbufs = []
        for i in range(4):
            a, t = bigv(off, [128, 2, 128], BF16); off += 512
            ptbufs.append((a, t))
        stage, stage_T = bigv(off, [128, 512], F32); off += 2048
        if has_sample:
            KTs, KTs_T = bigv(off, [128, 4, 4, 256], BF16); off += 4 * 4 * 256 * 2
            Vc, Vc_T = bigv(off, [128, 4, 256], BF16); off += 4 * 256 * 2
            ssb_s = []
            for i in range(4):
                a, t = bigv(off, [128, 256], F32); off += 1024
                ssb_s.append((a, t))
            ckf, ckf_T = bigv(off, [128, 256], F32); off += 1024
        assert off <= BIGB, off

        P.op("pool", lambda e: e.tensor_copy(out=KT[:, :, 0:128], in_=ktc[:]), reads=[ktc_T], writes=KT_T)
        P.op("pool", lambda e: e.tensor_copy(out=Vb[:, 0, :], in_=vbc[:]), reads=[vbc_T], writes=Vb_T)

        blks = blocks(N)
        for j in range(20):
            ws, wT = load_w(wqkv_d[j])
            for (c0, cn) in blks:
                b = bank()
                for kt in range(DC):
                    P.op("pe", lambda e: e.matmul(psb[b][:, 0:cn], ws[:, kt, :], hT[:, kt, 1 + c0:1 + c0 + cn],
                                                  start=(kt == 0), stop=(kt == DC - 1)),
                         reads=[wT, hT_T[kt]], writes=psT[b], inc=(kt == DC - 1))
                bcol = consts[:, CO["bq"] + j:CO["bq"] + j + 1]
                if j < 16:
                    P.op("act", lambda e: e.activation(out=qTb[:, j, c0:c0 + cn], in_=psb[b][:, 0:cn],
                                                       func=AF.Identity, bias=bcol, scale=1.0),
                         reads=psT[b] + [consts_T], writes=qT_T)
                else:
                    P.op("act", lambda e: e.activation(out=KT[:, j - 16, 128 + c0:128 + c0 + cn], in_=psb[b][:, 0:cn],
                                                       func=AF.Identity, bias=bcol, scale=1.0),
                         reads=psT[b] + [consts_T], writes=KT_T)
        for cchunk in range(4):
            is_k = cchunk < 2
            ws, wT = load_w(wkvt_d[cchunk])
            for t in range(ntile):
                out_tile = (gi == 2) and (t >= nptile - 1)
                if is_k and not out_tile:
                    continue
                bq = quarter()
                for kt in range(DC):
                    P.op("pe", lambda e: e.matmul(qf(bq), hT[:, kt, 1 + 128 * t:1 + 128 * (t + 1)], ws[:, kt, :],
                                                  start=(kt == 0), stop=(kt == DC - 1)),
                         reads=[wT, hT_T[kt]], writes=qT(bq), inc=(kt == DC - 1))
                bsl = bkv[:, 128 * cchunk:128 * (cchunk + 1)]
                if not is_k:
                    P.op("dve", lambda e: e.tensor_tensor(out=Vb[:, 1 + t, 128 * (cchunk - 2):128 * (cchunk - 1)],
                                                          in0=qf(bq), in1=bsl, op=ALU.add),
                         reads=qT(bq) + [cT], writes=Vb_T)
                if out_tile:
                    P.op("dve", lambda e: e.tensor_tensor(out=stage[:, 128 * cchunk:128 * (cchunk + 1)],
                                                          in0=qf(bq), in1=bsl, op=ALU.add),
                         reads=qT(bq) + [cT], writes=stage_T)
                    is_s = (t == nptile)
                    dst = (ks_d if is_s else kp_d) if is_k else (vs_d if is_s else vp_d)
                    co = 128 * (cchunk % 2)
                    P.dma("sp", dst[:, co:co + 128], stage[:, 128 * cchunk:128 * (cchunk + 1)],
                          reads=stage_T, is_output=True)

        def preset(buf, tiles):
            P.op("pool", lambda e: e.memset(buf[:], NEG), writes=tiles)

        for (a, t_) in sbufs:
            preset(a, t_)

        if has_sample:
            for s in range(4):
                preset(ssb_s[s][0], ssb_s[s][1])
                P.dma("pool", Vc[:, s, :], cv[s], writes=Vc_T, semt=Vc_T[0])
                P.dma("sp", ckf[:], ck[s], writes=ckf_T)
                for kvh in range(4):
                    di = ctr["tmpf"] % 2; ctr["tmpf"] += 1
                    dsrc, dsrc_T = tmpf[di], tmpf_T[di]
                    for dup in range(2):
                        P.op("dve", lambda e: e.tensor_copy(out=dsrc[:, 64 * dup:64 * (dup + 1)],
                                                            in_=ckf[:, 64 * kvh:64 * (kvh + 1)]),
                             reads=ckf_T, writes=[dsrc_T])
                    bq = quarter()
                    P.op("pe", lambda e: e.transpose(qf(bq), dsrc[:, 0:128], identf[:]),
                         reads=[dsrc_T, cT], writes=qT(bq))
                    copy("act", KTs[:, s, kvh, 0:128], qf(bq), qT(bq), KTs_T)
                P.op("pool", lambda e: e.tensor_copy(out=KTs[:, s, :, 128:256], in_=KT[:, :, 128 + npr:128 + npr + 128]),
                     reads=KT_T, writes=KTs_T)

        jobs = []

        def stage_a(jbs):
            bks = []
            for jb in jbs:
                b_ = bank(); bks.append(b_)
                P.op("pe", lambda e: e.matmul(psb[b_][0:jb["nq"], 0:256], jb["q"], jb["k"], start=True, stop=True),
                     reads=qT_T + jb["k_T"], writes=psT[b_][0:2])
            for bi_ in range(2):
                for jb, b_ in zip(jbs, bks):
                    (r0, r1, oc0, oc1, bc0) = jb["bops"][bi_]
                    sb, sb_T = jb["sb"]
                    P.op("dve", lambda e: e.scalar_tensor_tensor(out=sb[r0:r1, oc0:oc1], in0=psb[b_][r0:r1, oc0:oc1], scalar=0.125,
                                                                 in1=bias2[r0:r1, jb["h"], bc0:bc0 + (oc1 - oc0)],
                                                                 op0=ALU.mult, op1=ALU.add),
                         reads=psT[b_][0:2] + [bias2_T], writes=sb_T)
            for jb in jbs:
                nq = jb["nq"]; sb, sb_T = jb["sb"]
                sm = small[:, 16 * (jb["i"] % 4):16 * (jb["i"] % 4) + 16]; sT = [small_T[jb["i"] % 4]]
                P.op("dve", lambda e: e.reduce_max(out=sm[0:nq, 0:1], in_=sb[0:nq, :], axis=AX.X), reads=sb_T, writes=sT)
            for jb in jbs:
                nq = jb["nq"]; h = jb["h"]
                sm = small[:, 16 * (jb["i"] % 4):16 * (jb["i"] % 4) + 16]; sT = [small_T[jb["i"] % 4]]
                P.op("dve", lambda e: e.tensor_scalar(out=sm[0:nq, 2:3], in0=sm[0:nq, 0:1], scalar1=sinks[0:nq, h:h + 1], scalar2=-1.0,
                                                      op0=ALU.max, op1=ALU.mult),
                     reads=sT + [cT], writes=sT)
            for jb in jbs:
                nq = jb["nq"]; sb, sb_T = jb["sb"]
                sm = small[:, 16 * (jb["i"] % 4):16 * (jb["i"] % 4) + 16]; sT = [small_T[jb["i"] % 4]]
                pb_, pb_T = pbufs[jb["i"] % 6]
                P.op("act", lambda e: e.activation(out=pb_[0:nq, :], in_=sb[0:nq, :], func=AF.Exp, bias=sm[0:nq, 2:3], scale=1.0,
                                                   accum_out=sm[0:nq, 3:4]),
                     reads=sb_T + sT, writes=pb_T + sT)
            for jb in jbs:
                nq = jb["nq"]; h = jb["h"]
                sm = small[:, 16 * (jb["i"] % 4):16 * (jb["i"] % 4) + 16]; sT = [small_T[jb["i"] % 4]]
                P.op("act", lambda e: e.activation(out=sm[0:nq, 4:5], in_=sinks[0:nq, h:h + 1], func=AF.Exp, bias=sm[0:nq, 2:3], scale=1.0),
                     reads=sT + [cT], writes=sT)

        def stage_a2(jbs):
            for jb in jbs:
                nq = jb["nq"]
                sm = small[:, 16 * (jb["i"] % 4):16 * (jb["i"] % 4) + 16]; sT = [small_T[jb["i"] % 4]]
                P.op("dve", lambda e: e.tensor_tensor(out=sm[0:nq, 5:6], in0=sm[0:nq, 3:4], in1=sm[0:nq, 4:5], op=ALU.add),
                     reads=sT, writes=sT)
            for jb in jbs:
                nq = jb["nq"]
                sm = small[:, 16 * (jb["i"] % 4):16 * (jb["i"] % 4) + 16]; sT = [small_T[jb["i"] % 4]]
                P.op("dve", lambda e: e.reciprocal(out=sm[0:nq, 6:7], in_=sm[0:nq, 5:6]), reads=sT, writes=sT)
            for jb in jbs:
                nq = jb["nq"]
                sm = small[:, 16 * (jb["i"] % 4):16 * (jb["i"] % 4) + 16]; sT = [small_T[jb["i"] % 4]]
                pb_, pb_T = pbufs[jb["i"] % 6]
                P.op("dve", lambda e: e.tensor_scalar(out=pb_[0:nq, :], in0=pb_[0:nq, :], scalar1=sm[0:nq, 6:7], scalar2=None, op0=ALU.mult),
                     reads=pb_T + sT, writes=pb_T)

        def stage_b(jbs):
            qs = []
            for jb in jbs:
                nq = jb["nq"]
                pb_, pb_T = pbufs[jb["i"] % 6]
                for kt in range(2):
                    bq = quarter(); qs.append((jb, kt, bq))
                    P.op("pe", lambda e: e.transpose(qb(bq)[:, 0:nq], pb_[0:nq, 128 * kt:128 * (kt + 1)], identb[0:nq, 0:nq]),
                         reads=pb_T + [cT], writes=qT(bq))
            for (jb, kt, bq) in qs:
                nq = jb["nq"]
                pt_, pt_T = ptbufs[jb["i"] % 4]
                copy(ev_eng(), pt_[:, kt, 0:nq], qb(bq)[:, 0:nq], qT(bq), pt_T)

        def stage_c(jbs):
            for jb in jbs:
                nq = jb["nq"]
                pt_, pt_T = ptbufs[jb["i"] % 4]
                if jb["hh"] == 0:
                    jb["pair"]["bq"] = quarter()
                bq = jb["pair"]["bq"]
                r0 = 64 * jb["hh"]
                for kt in range(2):
                    P.op("pe", lambda e: e.matmul(qf(bq)[r0:r0 + 64, 0:nq], jb["v"][kt], pt_[:, kt, 0:nq],
                                                  start=(kt == 0), stop=(kt == 1)),
                         reads=pt_T + jb["v_T"], writes=qT(bq), inc=(kt == 1))
                if jb["hh"] == 1:
                    copy("act", jb["o_dst"], qf(bq)[:, 0:nq], qT(bq), qT_T)

        def mk_jobs(j, nq, qcols, kfn, k_T, v, v_T, sbpair, bops):
            pair = {}
            return [dict(h=2 * j + hh, hh=hh, nq=nq, pair=pair,
                         q=qTb[64 * hh:64 * (hh + 1), j, qcols[0]:qcols[0] + nq], k=kfn(hh), k_T=k_T,
                         v=v, v_T=v_T, sb=sbpair[hh], bops=bops,
                         o_dst=qTb[:, j, qcols[0]:qcols[0] + nq]) for hh in range(2)]

        def push(jb):
            jb["i"] = len(jobs)
            jobs.append(jb)

        def add_pair(*a_):
            for jb in mk_jobs(*a_):
                push(jb)

        for t in range(nptile):
            first_tile = (gi == 0) and t == 0
            if first_tile:
                bops = [(0, 64, 128, 192, 128), (64, 128, 128, 256, 64)]
                sbp = sbufs[0:2]
            else:
                bops = [(0, 64, 0, 192, 0), (64, 128, 64, 256, 0)]
                sbp = sbufs[2:4]
            for j in range(DC):
                kvh = (2 * j) // 8
                add_pair(j, 128, (128 * t,),
                         lambda hh, kvh=kvh, t=t: KT[64 * hh:64 * (hh + 1), kvh, 128 * t:128 * t + 256], KT_T,
                         [Vb[:, t, 64 * kvh:64 * (kvh + 1)], Vb[:, t + 1, 64 * kvh:64 * (kvh + 1)]], Vb_T, sbp, bops)
        if has_sample:
            for j in range(DC):
                kvh = (2 * j) // 8
                for s0_ in (0, 2):
                    two = []
                    for s in (s0_, s0_ + 1):
                        bops = [(0, 32, 0, 128, 0), (0, 32, 128 + 32 * s, 160 + 32 * s, 128)]
                        two.append(mk_jobs(j, 32, (npr + 32 * s,),
                                           lambda hh, kvh=kvh, s=s: KTs[64 * hh:64 * (hh + 1), s, kvh, :], KTs_T,
                                           [Vc[:, s, 64 * kvh:64 * (kvh + 1)], Vb[:, 1 + nptile, 64 * kvh:64 * (kvh + 1)]],
                                           Vc_T + Vb_T, [ssb_s[s], ssb_s[s]], bops))
                    for hh in range(2):
                        push(two[0][hh]); push(two[1][hh])
        prs = [jobs[i_:i_ + 2] for i_ in range(0, len(jobs), 2)]
        npz = len(prs)
        for step in range(npz + 3):
            if step < npz:
                stage_a(prs[step])
            if 0 <= step - 1 < npz:
                stage_a2(prs[step - 1])
            if 0 <= step - 2 < npz:
                stage_b(prs[step - 2])
            if 0 <= step - 3 < npz:
                stage_c(prs[step - 3])
        if gi < 2:
            P.op("pool", lambda e: e.tensor_copy(out=ktc[:], in_=KT[:, :, npr:npr + 128]), reads=KT_T, writes=[ktc_T])
            P.op("pool", lambda e: e.tensor_copy(out=vbc[:], in_=Vb[:, nptile, :]), reads=Vb_T, writes=[vbc_T])
        set_pool(6)
        ssb = SSB
        pend = None
        for dch in range(DC):
            ws, wT = load_w(wao_d[dch])
            pb = [bank() for _ in blks]
            for bi, (c0, cn) in enumerate(blks):
                for kt in range(DC):
                    P.op("pe", lambda e: e.matmul(psb[pb[bi]][:, 0:cn], ws[:, kt, :], qTb[:, kt, c0:c0 + cn],
                                                  start=(kt == 0), stop=(kt == DC - 1)),
                         reads=[wT] + qT_T, writes=psT[pb[bi]], inc=(kt == DC - 1))
            if pend is not None:
                pend()
            pend = out_evac_ss(dch, N, pb, ssb, dch == 0, dch == DC - 1)
        pend()
        postnorm_add(l, 3, N, ssb, 1.0)

    def rwkv(gi, N, npr, has_sample):
        l = 1
        set_pool(6)
        prenorm(l, 2, N)
        blks = blocks(N)
        st = {"off": 0}

        def h_last(col, dst_ap):
            di = ctr["tmpf"] % 2; ctr["tmpf"] += 1
            so, so_T = tmpf[di], [tmpf_T[di]]
            P.op("dve", lambda e: e.scalar_tensor_tensor(out=so[:, 0:DC], in0=xT[:, :, col], scalar=rstd[:, col:col + 1], in1=gcol(l, 2),
                                                         op0=ALU.mult, op1=ALU.mult),
                 reads=xT_T + [rstd_T, consts_T], writes=so_T)
            P.dma("sp", dst_ap, so[:, 0:DC], reads=so_T, is_output=True)
        def buf(shape, dt):
            esz = 4 if dt == F32 else 2
            a_, t_ = bigv(st["off"], list(shape), dt)
            st["off"] = (st["off"] + int(np.prod(shape[1:])) * esz + 3) // 4 * 4
            return a_, t_

        if gi == 2:
            h_last(npr - 1, shp_d)
            for s_ in range(4):
                h_last(npr + 32 * s_ + 31, shs_d[:, s_, :])
        U = 4
        NB = N
        ygT, yg_T = buf([128, DC, NB], BF16)
        if has_sample:
            hsh, hsh_T = buf([128, DC, 128], BF16)
        lora, lora_T = buf([128, 4, NB], BF16)
        w2s, w2s_T = buf([128, 128], BF16); a2s, a2s_T = buf([128, 128], BF16); g2s, g2s_T = buf([128, 2, 128], BF16)
        rT_, rT_T = buf([128, NB], BF16); kT_, kT_T = buf([128, NB], BF16); vT_, vT_T = buf([128, NB], BF16)
        aT_, aT_T = buf([128, NB], BF16); kkn, kkn_T = buf([128, NB], BF16); ynT, ynT_T = buf([128, NB], BF16)
        lwT, lwT_T = buf([128, NB], F32)
        (wa, wa_T), (wb, wb_T) = buf([128, DC, 128], BF16), buf([128, DC, 128], BF16)

        def aux_views(flat_bf16, region_T, shapes):
            outs = []
            o_ = 0
            for shp in shapes:
                n_ = int(np.prod(shp[1:]))
                ap_ = flat_bf16[:, o_:o_ + n_]
                if len(shp) == 3:
                    ap_ = ap_.rearrange("p (a b) -> p a b", b=shp[2])
                t_ = T("aux")
                for rt_ in region_T:
                    for k_, ev_ in rt_.w.items():
                        if k_ not in t_.w or t_.w[k_][1] < ev_[1]:
                            t_.w[k_] = ev_
                    for k_, ev_ in rt_.r.items():
                        if k_ not in t_.r or t_.r[k_][1] < ev_[1]:
                            t_.r[k_] = ev_
                outs.append((ap_, [t_]))
                o_ += n_
            return outs

        def aux_release(region_T, subs):
            for rt_ in region_T:
                for _, tl in subs:
                    for t_ in tl:
                        for k_, ev_ in list(t_.w.items()) + list(t_.r.items()):
                            if k_ not in rt_.r or rt_.r[k_][1] < ev_[1]:
                                rt_.r[k_] = ev_

        d0 = dsl[:, 0:2].rearrange("p a b c -> p (a b c)")
        d1 = dsl[:, 2:4].rearrange("p a b c -> p (a b c)")
        r0 = rstd[:].bitcast(BF16)
        w2f = wslot[2][:].rearrange("p a b -> p (a b)")
        w3f = wslot[3][:].rearrange("p a b -> p (a b)")
        aux0 = aux_views(d0, dslot_T[0:2], [[128, U, 128]] * 5)
        aux1 = aux_views(d1, dslot_T[2:4], [[128, U, 128]] * 3 + [[128, U, 256]])
        aux2 = aux_views(r0, [rstd_T], [[128, U, 128]] * 3)
        aux3 = aux_views(w2f, [wslot_T[2]], [[128, U, 128]] * 4)
        aux4 = aux_views(w3f, [wslot_T[3]], [[128, U, 128]] * 4)
        (Dm, Dm_T), (Em, Em_T), (Mfull, Mfull_T) = aux0[0], aux0[1], aux0[2]
        (bdB, bdB_T), (bdK, bdK_T), (bdV, bdV_T) = aux2
        Qm2 = [buf([128, U, 128], BF16), buf([128, U, 128], BF16)]; P1b, P1b_T = buf([128, U, 128], BF16)
        Xb, Xb_T = buf([128, U, 256], BF16)
        Sb, Sb_T = buf([128, 128], BF16)
        e1, e1_T = buf([128, 256], BF16); e2, e2_T = buf([128, 256], BF16); e3, e3_T = buf([128, 256], BF16)
        SETS = []
        s0 = dict(tokV=aux1[0], tokK=aux1[1], tokB=aux1[2], bdR=aux1[3], ynbd=aux0[3])
        s0["ArbT"] = buf([128, U, 128], BF16); s0["ArkT"] = buf([128, U, 128], BF16); s0["Vbar"] = buf([128, U, 128], BF16)
        s0["AbT"] = buf([128, U, 128], BF16); s0["Ub"] = buf([128, U, 128], BF16); s0["gam"] = buf([128, 8], F32)
        s1 = dict(tokV=aux3[0], tokK=aux3[1], tokB=aux3[2], AbT=aux3[3], ArbT=aux4[0], ArkT=aux4[1], Vbar=aux4[2], Ub=aux4[3], ynbd=aux0[4])
        s1["bdR"] = buf([128, U, 256], BF16); s1["gam"] = buf([128, 8], F32)
        SETS = [s0, s1]
        if has_sample:
            Ss, Ss_T = buf([128, U, 128], F32)
        assert st["off"] <= BIGB, st["off"]
        def zero_bd(sets):
            zl = [(bdB, bdB_T), (bdK, bdK_T), (bdV, bdV_T)]
            for S_ in sets:
                zl += [S_["bdR"], S_["ynbd"]]
            for a_, t_ in zl:
                P.op("pool", lambda e: e.memset(a_, 0.0), writes=t_)

        if has_sample:
            di = ctr["tmpf"] % 2; ctr["tmpf"] += 1
            sh32, sh32_T = tmpf[di], tmpf_T[di]
            P.dma("sp", sh32[:, 0:4 * DC], sshift.rearrange("p s c -> p (s c)"), writes=[sh32_T])
            for s_ in range(4):
                P.op("dve", lambda e: e.tensor_copy(out=hsh[:, :, 32 * s_:32 * s_ + 1],
                                                    in_=sh32[:, s_ * DC:(s_ + 1) * DC].unsqueeze(2)),
                     reads=[sh32_T], writes=hsh_T)
                P.op("dve", lambda e: e.tensor_copy(out=hsh[:, :, 32 * s_ + 1:32 * s_ + 32],
                                                    in_=hT[:, :, 1 + npr + 32 * s_:1 + npr + 32 * s_ + 31]),
                     reads=hT_T, writes=hsh_T)

        def rhs_pairs(c0, cn):
            if has_sample and c0 >= npr:
                return (lambda kt: hT[:, kt, 1 + c0:1 + c0 + cn]), (lambda kt: hsh[:, kt, c0 - npr:c0 - npr + cn]), hsh_T
            return (lambda kt: hT[:, kt, 1 + c0:1 + c0 + cn]), (lambda kt: hT[:, kt, c0:c0 + cn]), []

        def mixed_linear(wdram_ap, n, M, evac):
            si = wctr[0] % 2; wctr[0] += 1
            ws, wT = wslot[si], wslot_T[si]
            P.dma("pool", ws[:, :, 0:M], wdram_ap, writes=[wT])
            mu = consts[:, CO["mu"] + n * 16:CO["mu"] + (n + 1) * 16].unsqueeze(2).broadcast_to([128, DC, M])
            P.op("dve", lambda e: e.tensor_tensor(out=wb[:, :, 0:M], in0=ws[:, :, 0:M], in1=mu, op=ALU.mult),
                 reads=[wT, consts_T], writes=wb_T)
            P.op("dve", lambda e: e.tensor_tensor(out=wa[:, :, 0:M], in0=ws[:, :, 0:M], in1=wb[:, :, 0:M], op=ALU.subtract),
                 reads=[wT] + wb_T, writes=wa_T)
            for (c0, cn) in blks:
                cur, shf, extra = rhs_pairs(c0, cn)
                b = bank()
                for kt in range(DC):
                    P.op("pe", lambda e: e.matmul(psb[b][0:M, 0:cn], wa[:, kt, 0:M], cur(kt), start=(kt == 0), stop=False),
                         reads=wa_T + [hT_T[kt]], writes=psT[b], inc=False)
                    P.op("pe", lambda e: e.matmul(psb[b][0:M, 0:cn], wb[:, kt, 0:M], shf(kt), start=False, stop=(kt == DC - 1)),
                         reads=wb_T + [hT_T[kt]] + extra, writes=psT[b], inc=(kt == DC - 1))
                evac(b, c0, cn)

        mixed_linear(w1_d, 1, 96, lambda b, c0, cn: P.op(
            "act", lambda e: e.activation(out=lora[0:96, 0, c0:c0 + cn], in_=psb[b][0:96, 0:cn], func=AF.Tanh),
            reads=psT[b], writes=lora_T))
        mixed_linear(a1_d, 4, 96, lambda b, c0, cn: P.op(
            "act", lambda e: e.copy(out=lora[0:96, 1, c0:c0 + cn], in_=psb[b][0:96, 0:cn]),
            reads=psT[b], writes=lora_T))
        for gc in range(2):
            mixed_linear(g1_d[gc], 5, 128, lambda b, c0, cn: P.op(
                "act", lambda e: e.activation(out=lora[:, 2 + gc, c0:c0 + cn], in_=psb[b][:, 0:cn], func=AF.Sigmoid),
                reads=psT[b], writes=lora_T))

        bcount = [0]

        def v3(ap_, TP, w0_, w1_):
            return ap_[0:TP, :, w0_:w1_]

        def prep_gen(j, cb, L, nu, S_, rezero):
            TP = 2 * L
            ncol = nu * L
            ntl = ncol // 128
            (tokV, tokV_T), (tokK, tokK_T), (tokB, tokB_T), (bdR, bdR_T) = S_["tokV"], S_["tokK"], S_["tokB"], S_["bdR"]
            (ArbT, ArbT_T), (ArkT, ArkT_T), (Vbar, Vbar_T), (AbT, AbT_T) = S_["ArbT"], S_["ArkT"], S_["Vbar"], S_["AbT"]
            gam, gam_T = S_["gam"]
            if rezero:
                zero_bd([S_])
            bcl = bank()
            for tl in range(ntl):
                c_ = cb + 128 * tl
                b_ = bank()
                P.op("pe", lambda e: e.transpose(psb[b_][:, 0:128], lwT[:, c_:c_ + 128], identf[:]), reads=lwT_T + [cT], writes=psT[b_][0:1])
                di = ctr["tmpf"] % 2; ctr["tmpf"] += 1
                P.op("act", lambda e: e.copy(out=tmpf[di][:, 0:128], in_=psb[b_][:, 0:128]), reads=psT[b_][0:1], writes=[tmpf_T[di]])
                P.op("pe", lambda e: e.matmul(psb[bcl][:, 128 * tl:128 * (tl + 1)], tmpf[di][:, 0:128], tri[L][:], start=True, stop=True),
                     reads=[tmpf_T[di], cT], writes=psT[bcl])
            cl = psb[bcl][:, 0:ncol]
            P.op("act", lambda e: e.activation(out=e1[:, 0:ncol], in_=cl, func=AF.Exp), reads=psT[bcl], writes=e1_T)
            P.op("act", lambda e: e.activation(out=e2[:, 0:ncol], in_=cl, func=AF.Exp, scale=-1.0), reads=psT[bcl], writes=e2_T)
            P.op("act", lambda e: e.activation(out=gam[:, 0:nu], in_=psb[bcl][:, 0:ncol].rearrange("p (u l) -> p u l", l=L)[:, :, L - 1],
                                               func=AF.Exp), reads=psT[bcl], writes=gam_T)
            di = ctr["tmpf"] % 2; ctr["tmpf"] += 1
            P.op("dve", lambda e: e.tensor_tensor(out=tmpf[di][:, 0:ncol], in0=cl, in1=lwT[:, cb:cb + ncol], op=ALU.subtract),
                 reads=psT[bcl] + lwT_T, writes=[tmpf_T[di]])
            P.op("act", lambda e: e.activation(out=e3[:, 0:ncol], in_=tmpf[di][:, 0:ncol], func=AF.Exp), reads=[tmpf_T[di]], writes=e3_T)
            yield
            def src(ap_, ps_):
                return ap_[ps_, cb:cb + ncol].rearrange("p (u l) -> p u l", l=L)

            def esrc(ap_, ps_):
                return ap_[ps_, 0:ncol].rearrange("p (u l) -> p u l", l=L)
            for hh in range(2):
                ps_ = slice(64 * hh, 64 * hh + 64)
                cs0, cs1 = L * hh, L * hh + L
                eng = "dve" if hh == 0 else "pool"
                P.op("dve", lambda e: e.scalar_tensor_tensor(out=bdR[ps_, 0:nu, cs0:cs1], in0=src(kkn, ps_), scalar=-1.0, in1=esrc(e3, ps_),
                                                           op0=ALU.mult, op1=ALU.mult),
                     reads=kkn_T + e3_T, writes=bdR_T)
                P.op(eng, lambda e: e.tensor_tensor(out=bdR[ps_, 0:nu, TP + cs0:TP + cs1], in0=src(rT_, ps_), in1=esrc(e1, ps_), op=ALU.mult),
                     reads=rT_T + e1_T, writes=bdR_T)
                P.op(eng, lambda e: e.tensor_tensor(out=bdK[ps_, 0:nu, cs0:cs1], in0=src(kT_, ps_), in1=esrc(e2, ps_), op=ALU.mult),
                     reads=kT_T + e2_T, writes=bdK_T)
                P.op(eng, lambda e: e.tensor_tensor(out=bdB[ps_, 0:nu, cs0:cs1], in0=src(kkn, ps_), in1=src(aT_, ps_), op=ALU.mult),
                     reads=kkn_T + aT_T, writes=bdB_T)
                P.op(eng, lambda e: e.tensor_tensor(out=bdB[ps_, 0:nu, cs0:cs1], in0=bdB[ps_, 0:nu, cs0:cs1], in1=esrc(e2, ps_), op=ALU.mult),
                     reads=bdB_T + e2_T, writes=bdB_T)
                P.op("act", lambda e: e.copy(out=bdV[ps_, 0:nu, cs0:cs1], in_=src(vT_, ps_)), reads=vT_T, writes=bdV_T)
            yield
            def tr_all(dst, dst_T, srcfn, src_T, rows_in, cols_in, eng):
                b_ = bank()
                pv = psb[b_][:, :].bitcast(BF16)
                for u in range(nu):
                    P.op("pe", lambda e: e.transpose(pv[0:cols_in, rows_in * u:rows_in * (u + 1)], srcfn(u), identb[0:rows_in, 0:rows_in]),
                         reads=src_T + [cT], writes=psT[b_], inc=(u == nu - 1))
                copy(eng, dst, pv[0:cols_in, 0:rows_in * nu].rearrange("p (u r) -> p u r", r=rows_in), psT[b_], dst_T)
            tr_all(tokV[0:TP, 0:nu, :], tokV_T, lambda u: bdV[:, u, 0:TP], bdV_T, 128, TP, "act")
            tr_all(tokK[0:TP, 0:nu, :], tokK_T, lambda u: bdK[:, u, 0:TP], bdK_T, 128, TP, "dve")
            tr_all(tokB[0:TP, 0:nu, :], tokB_T, lambda u: bdB[:, u, 0:TP], bdB_T, 128, TP, "act")
            tr_all(Xb[0:TP, 0:nu, 0:128], Xb_T, lambda u: bdR[:, u, 0:TP], bdR_T, 128, TP, "dve")
            yield
            upb = 512 // (2 * TP)
            for lhs, lhs_T, outs in ((bdB, bdB_T, ((Mfull, Mfull_T, None), (ArbT, ArbT_T, mincl[L]))),
                                     (bdK, bdK_T, ((P1b, P1b_T, mstrict[L]), (ArkT, ArkT_T, mincl[L])))):
                for u0 in range(0, nu, upb):
                    b_ = bank()
                    n_ = min(upb, nu - u0)
                    for u in range(u0, u0 + n_):
                        o_ = (u - u0) * 2 * TP
                        P.op("pe", lambda e: e.matmul(psb[b_][0:TP, o_:o_ + 2 * TP], lhs[:, u, 0:TP], bdR[:, u, 0:2 * TP], start=True, stop=True),
                             reads=lhs_T + bdR_T, writes=psT[b_], inc=(u == u0 + n_ - 1))
                    pv = psb[b_][0:TP, 0:n_ * 2 * TP].rearrange("p (u c) -> p u c", c=2 * TP)
                    for part, (dst, dst_T, msk) in enumerate(outs):
                        sv = pv[:, :, part * TP:(part + 1) * TP]
                        dv = dst[0:TP, u0:u0 + n_, 0:TP]
                        if msk is None:
                            copy("act", dv, sv, psT[b_], dst_T)
                        else:
                            P.op("dve", lambda e: e.tensor_tensor(out=dv, in0=sv, in1=msk[0:TP, 0:TP].unsqueeze(1).broadcast_to([TP, n_, TP]), op=ALU.mult),
                                 reads=psT[b_] + [cT], writes=dst_T)
            yield
            b_ = bank()
            for u in range(nu):
                P.op("pe", lambda e: e.matmul(psb[b_][0:TP, 128 * u:128 * (u + 1)], P1b[0:TP, u, 0:TP], tokV[0:TP, u, :], start=True, stop=True),
                     reads=P1b_T + tokV_T, writes=psT[b_], inc=(u == nu - 1))
            copy("act", Xb[0:TP, 0:nu, 128:256], psb[b_][0:TP, 0:128 * nu].rearrange("p (u c) -> p u c", c=128), psT[b_], Xb_T)
            idb = identb[0:TP, 0:TP].unsqueeze(1).broadcast_to([TP, nu, TP])
            yield
            P.op("act", lambda e: e.copy(out=Dm[0:TP, 0:nu, 0:TP], in_=idb), reads=[cT], writes=Dm_T)
            P.op("pool", lambda e: e.tensor_copy(out=Em[0:TP, 0:nu, 0:TP], in_=idb), reads=[cT], writes=Em_T)
            for lv in range(len(lvm[L])):
                Qm, Qm_T = Qm2[lv % 2]
                P.op("pool", lambda e: e.tensor_tensor(out=Qm[0:TP, 0:nu, 0:TP], in0=Mfull[0:TP, 0:nu, 0:TP],
                                                       in1=lvm[L][lv][0:TP, 0:TP].unsqueeze(1).broadcast_to([TP, nu, TP]), op=ALU.mult),
                     reads=Mfull_T + [cT], writes=Qm_T)
                b1_ = bank()
                for u in range(nu):
                    P.op("pe", lambda e: e.matmul(psb[b1_][0:TP, 128 * u:128 * u + TP], Qm[0:TP, u, 0:TP], Dm[0:TP, u, 0:TP], start=True, stop=True),
                         reads=Qm_T + Dm_T, writes=psT[b1_], inc=(u == nu - 1))
                copy("act", P1b[0:TP, 0:nu, 0:TP], psb[b1_][0:TP, 0:128 * nu].rearrange("p (u c) -> p u c", c=128)[:, :, 0:TP], psT[b1_], P1b_T)
                yield
                b2_ = bank()
                for u in range(nu):
                    P.op("pe", lambda e: e.matmul(psb[b2_][0:TP, 128 * u:128 * u + TP], Em[0:TP, u, 0:TP], P1b[0:TP, u, 0:TP], start=True, stop=True),
                         reads=Em_T + P1b_T, writes=psT[b2_], inc=(u == nu - 1))
                P.op("dve", lambda e: e.tensor_tensor(out=Dm[0:TP, 0:nu, 0:TP], in0=Dm[0:TP, 0:nu, 0:TP],
                                                      in1=psb[b2_][0:TP, 0:128 * nu].rearrange("p (u c) -> p u c", c=128)[:, :, 0:TP], op=ALU.add),
                     reads=Dm_T + psT[b2_], writes=Dm_T)
                yield
                tr_all(Em[0:TP, 0:nu, 0:TP], Em_T, lambda u: Dm[0:TP, u, 0:TP], Dm_T, TP, TP, "act")
                yield
            Ec, Ec_T = Em, Em_T
            for u0 in range(0, nu, 2):
                b_ = bank()
                n_ = min(2, nu - u0)
                for u in range(u0, u0 + n_):
                    P.op("pe", lambda e: e.matmul(psb[b_][0:TP, 256 * (u - u0):256 * (u - u0 + 1)], Ec[0:TP, u, 0:TP], Xb[0:TP, u, :], start=True, stop=True),
                         reads=Ec_T + Xb_T, writes=psT[b_], inc=(u == u0 + n_ - 1))
                pv = psb[b_][0:TP, 0:256 * n_].rearrange("p (u c) -> p u c", c=256)
                copy("act", Vbar[0:TP, u0:u0 + n_, :], pv[:, :, 128:256], psT[b_], Vbar_T)
                copy("dve", Xb[0:TP, u0:u0 + n_, 0:128], pv[:, :, 0:128], psT[b_], Xb_T)
            yield
            tr_all(AbT[:, 0:nu, 0:TP], AbT_T, lambda u: Xb[0:TP, u, 0:128], Xb_T, TP, 128, "act")
            yield

        def scan_gen(j, cb, L, nu, S_, states):
            TP = 2 * L
            ncol = nu * L
            (tokV, tokV_T), (tokK, tokK_T), (tokB, tokB_T), (bdR, bdR_T) = S_["tokV"], S_["tokK"], S_["tokB"], S_["bdR"]
            (ArbT, ArbT_T), (ArkT, ArkT_T), (Vbar, Vbar_T), (AbT, AbT_T) = S_["ArbT"], S_["ArkT"], S_["Vbar"], S_["AbT"]
            (Ub, Ub_T), (gam, gam_T), (ynbd, ynbd_T) = S_["Ub"], S_["gam"], S_["ynbd"]
            bcount[0] += 1
            by = SSB[bcount[0] % 2]
            for u in range(nu):
                Sm_ap, Sm_T = states[u]
                P.op("act", lambda e: e.copy(out=Sb[:, :], in_=Sm_ap), reads=Sm_T, writes=Sb_T)
                bu = bank()
                P.op("pe", lambda e: e.matmul(psb[bu][0:TP, 0:128], AbT[:, u, 0:TP], Sb[:, :], start=True, stop=True),
                     reads=AbT_T + Sb_T, writes=psT[bu])
                P.op("dve", lambda e: e.tensor_tensor(out=Ub[0:TP, u, :], in0=psb[bu][0:TP, 0:128], in1=Vbar[0:TP, u, :], op=ALU.add),
                     reads=psT[bu] + Vbar_T, writes=Ub_T)
                yield
                yo = psb[by][0:TP, 128 * u:128 * (u + 1)]
                P.op("pe", lambda e: e.matmul(yo, bdR[:, u, TP:2 * TP], Sb[:, :], start=True, stop=False),
                     reads=bdR_T + Sb_T, writes=psT[by], inc=False)
                P.op("pe", lambda e: e.matmul(yo, ArkT[0:TP, u, 0:TP], tokV[0:TP, u, :], start=False, stop=False),
                     reads=ArkT_T + tokV_T, writes=psT[by], inc=False)
                P.op("pe", lambda e: e.matmul(yo, ArbT[0:TP, u, 0:TP], Ub[0:TP, u, :], start=False, stop=True),
                     reads=ArbT_T + Ub_T, writes=psT[by])
                bs = bank()
                P.op("pe", lambda e: e.matmul(psb[bs][:, 0:128], tokK[0:TP, u, :], tokV[0:TP, u, :], start=True, stop=False),
                     reads=tokK_T + tokV_T, writes=psT[bs], inc=False)
                P.op("pe", lambda e: e.matmul(psb[bs][:, 0:128], tokB[0:TP, u, :], Ub[0:TP, u, :], start=False, stop=True),
                     reads=tokB_T + Ub_T, writes=psT[bs])
                di = ctr["tmpf"] % 2; ctr["tmpf"] += 1
                P.op("act", lambda e: e.activation(out=tmpf[di][:, 0:128], in_=psb[bs][:, 0:128], func=AF.Copy, scale=gam[:, u:u + 1]),
                     reads=psT[bs] + gam_T, writes=[tmpf_T[di]])
                P.op("dve", lambda e: e.scalar_tensor_tensor(out=Sm_ap, in0=Sm_ap, scalar=gam[:, u:u + 1], in1=tmpf[di][:, 0:128],
                                                             op0=ALU.mult, op1=ALU.add),
                     reads=Sm_T + gam_T + [tmpf_T[di]], writes=Sm_T)
                yield
            g = bcount[0] % 4
            sm = small[:, 16 * g:16 * g + 16]; sT = [small_T[g]]
            Y3 = psb[by][0:TP, 0:128 * nu].rearrange("p (u c) -> p u c", c=128)
            yield
            P.op("dve", lambda e: e.reduce_sum(out=sm[0:TP, 0:nu], in_=Y3, axis=AX.X), reads=psT[by], writes=sT)
            di = ctr["tmpf"] % 2; ctr["tmpf"] += 1
            sqv = tmpf[di][0:TP, 0:128 * nu].rearrange("p (u c) -> p u c", c=128)
            P.op("act", lambda e: e.activation(out=sqv, in_=Y3, func=AF.Square), reads=psT[by], writes=[tmpf_T[di]])
            P.op("dve", lambda e: e.reduce_sum(out=sm[0:TP, 4:4 + nu], in_=sqv, axis=AX.X), reads=[tmpf_T[di]], writes=sT)
            P.op("dve", lambda e: e.tensor_scalar(out=sm[0:TP, 0:nu], in0=sm[0:TP, 0:nu], scalar1=1.0 / 64, scalar2=None, op0=ALU.mult),
                 reads=sT, writes=sT)
            P.op("dve", lambda e: e.tensor_tensor(out=sm[0:TP, 8:8 + nu], in0=sm[0:TP, 0:nu], in1=sm[0:TP, 0:nu], op=ALU.mult),
                 reads=sT, writes=sT)
            P.op("dve", lambda e: e.scalar_tensor_tensor(out=sm[0:TP, 4:4 + nu], in0=sm[0:TP, 4:4 + nu], scalar=1.0 / 64, in1=sm[0:TP, 8:8 + nu],
                                                         op0=ALU.mult, op1=ALU.subtract),
                 reads=sT, writes=sT)
            P.op("dve", lambda e: e.tensor_scalar(out=sm[0:TP, 4:4 + nu], in0=sm[0:TP, 4:4 + nu], scalar1=GN_EPS, scalar2=None, op0=ALU.add),
                 reads=sT, writes=sT)
            P.op("act", lambda e: e.activation(out=sm[0:TP, 4:4 + nu], in_=sm[0:TP, 4:4 + nu], func=AF.Ln), reads=sT, writes=sT)
            P.op("act", lambda e: e.activation(out=sm[0:TP, 4:4 + nu], in_=sm[0:TP, 4:4 + nu], func=AF.Exp, scale=-0.5), reads=sT, writes=sT)
            for hh in range(2):
                rs = slice(L * hh, L * hh + L)
                cs = slice(64 * hh, 64 * hh + 64)
                P.op("dve", lambda e: e.tensor_tensor(out=ynbd[rs, 0:nu, cs], in0=Y3[rs, :, cs],
                                                      in1=sm[rs, 0:nu].unsqueeze(2).broadcast_to([L, nu, 64]), op=ALU.subtract),
                     reads=psT[by] + sT, writes=ynbd_T)
                P.op("dve", lambda e: e.tensor_tensor(out=ynbd[rs, 0:nu, cs], in0=ynbd[rs, 0:nu, cs],
                                                      in1=sm[rs, 4:4 + nu].unsqueeze(2).broadcast_to([L, nu, 64]), op=ALU.mult),
                     reads=ynbd_T + sT, writes=ynbd_T)
            yield
            b_ = bank()
            pv = psb[b_][:, :].bitcast(BF16)
            for u in range(nu):
                P.op("pe", lambda e: e.transpose(pv[:, TP * u:TP * (u + 1)], ynbd[0:TP, u, :], identb[0:TP, 0:TP]),
                     reads=ynbd_T + [cT], writes=psT[b_], inc=(u == nu - 1))
            for hh in range(2):
                ps_ = slice(64 * hh, 64 * hh + 64)
                P.op("act", lambda e: e.copy(out=ynT[ps_, cb:cb + ncol].rearrange("p (u l) -> p u l", l=L),
                                             in_=pv[ps_, 0:TP * nu].rearrange("p (u c) -> p u c", c=TP)[:, :, L * hh:L * hh + L]),
                     reads=psT[b_], writes=ynT_T)

        def state_out(src_ap, src_T, dst):
            b_ = bank()
            P.op("pe", lambda e: e.transpose(psb[b_][:, 0:128], src_ap, identf[:]), reads=src_T + [cT], writes=psT[b_])
            di = ctr["tmpf"] % 2; ctr["tmpf"] += 1
            so, so_T = tmpf[di], [tmpf_T[di]]
            for hh in range(2):
                ps_ = slice(64 * hh, 64 * hh + 64)
                P.op("act", lambda e: e.copy(out=so[ps_, 0:64], in_=psb[b_][ps_, 64 * hh:64 * hh + 64]), reads=psT[b_], writes=so_T)
            P.dma("sp", dst, so[:, 0:64], reads=so_T, is_output=True)

        for j in range(DC):
            P.dma("pool", w2s[0:96, :], w2_d[:, 128 * j:128 * (j + 1)], writes=w2s_T, semt=w2s_T[0])
            P.dma("pool", a2s[0:96, :], a2_d[:, 128 * j:128 * (j + 1)], writes=a2s_T, semt=a2s_T[0])
            P.dma("pool", g2s[:, :, :], g2_d[:, :, 128 * j:128 * (j + 1)], writes=g2s_T, semt=g2s_T[0])
            for (wdram, n, dst, dst_T) in ((wr_d, 0, rT_, rT_T), (wk_d, 2, kT_, kT_T), (wv_d, 3, vT_, vT_T)):
                mixed_linear(wdram[j], n, 128, lambda b, c0, cn, dst=dst, dst_T=dst_T: copy(
                    ev_eng(), dst[:, c0:c0 + cn], psb[b][:, 0:cn], psT[b], dst_T))
            for (c0, cn) in blks:
                b = bank()
                P.op("pe", lambda e: e.matmul(psb[b][:, 0:cn], w2s[0:96, :], lora[0:96, 0, c0:c0 + cn], start=True, stop=True),
                     reads=w2s_T + lora_T, writes=psT[b])
                P.op("act", lambda e: e.activation(out=lwT[:, c0:c0 + cn], in_=psb[b][:, 0:cn], func=AF.Exp, bias=cc("w0", j), scale=1.0),
                     reads=psT[b] + [consts_T], writes=lwT_T)
                ts = ctr["tmpf"] % 2; ctr["tmpf"] += 1
                tf, tf_T = tmpf[ts], [tmpf_T[ts]]
                P.op("dve", lambda e: e.tensor_scalar(out=tf[:, 0:cn], in0=lwT[:, c0:c0 + cn], scalar1=1.0, scalar2=None, op0=ALU.add),
                     reads=lwT_T, writes=tf_T)
                P.op("dve", lambda e: e.reciprocal(out=tf[:, 0:cn], in_=tf[:, 0:cn]), reads=tf_T, writes=tf_T)
                P.op("dve", lambda e: e.scalar_tensor_tensor(out=lwT[:, c0:c0 + cn], in0=lwT[:, c0:c0 + cn], scalar=-math.exp(-0.5), in1=tf[:, 0:cn],
                                                             op0=ALU.mult, op1=ALU.mult),
                     reads=lwT_T + tf_T, writes=lwT_T)
                b = bank()
                P.op("pe", lambda e: e.matmul(psb[b][:, 0:cn], a2s[0:96, :], lora[0:96, 1, c0:c0 + cn], start=True, stop=True),
                     reads=a2s_T + lora_T, writes=psT[b])
                P.op("act", lambda e: e.activation(out=aT_[:, c0:c0 + cn], in_=psb[b][:, 0:cn], func=AF.Sigmoid, bias=cc("a0", j), scale=1.0),
                     reads=psT[b] + [consts_T], writes=aT_T)
            P.op("dve", lambda e: e.tensor_scalar(out=kkn[:, 0:N], in0=kT_[:, 0:N], scalar1=cc("kk", j), scalar2=None, op0=ALU.mult),
                 reads=kT_T + [consts_T], writes=kkn_T)
            s = ctr["sq"] % 2; ctr["sq"] += 1
            P.op("act", lambda e: e.activation(out=sq[s][:, 0:N], in_=kkn[:, 0:N], func=AF.Square), reads=kkn_T, writes=[sq_T[s]])
            for (c0, cn) in blks:
                b = bank()
                P.op("pe", lambda e: e.matmul(psb[b][:, 0:cn], bdones[:], sq[s][:, c0:c0 + cn], start=True, stop=True),
                     reads=[sq_T[s], cT], writes=psT[b])
                ts = ctr["tmpf"] % 2; ctr["tmpf"] += 1
                tf, tf_T = tmpf[ts], [tmpf_T[ts]]
                P.op("dve", lambda e: e.tensor_scalar(out=tf[:, 0:cn], in0=psb[b][:, 0:cn], scalar1=1e-30, scalar2=None, op0=ALU.add),
                     reads=psT[b], writes=tf_T)
                P.op("act", lambda e: e.activation(out=tf[:, 0:cn], in_=tf[:, 0:cn], func=AF.Ln), reads=tf_T, writes=tf_T)
                P.op("act", lambda e: e.activation(out=tf[:, 0:cn], in_=tf[:, 0:cn], func=AF.Exp, scale=-0.5), reads=tf_T, writes=tf_T)
                P.op("dve", lambda e: e.tensor_tensor(out=kkn[:, c0:c0 + cn], in0=kkn[:, c0:c0 + cn], in1=tf[:, 0:cn], op=ALU.mult),
                     reads=kkn_T + tf_T, writes=kkn_T)
                P.op("dve", lambda e: e.tensor_scalar(out=tf[:, 0:cn], in0=aT_[:, c0:c0 + cn], scalar1=-1.0, scalar2=cc("ka", j), op0=ALU.add, op1=ALU.mult),
                     reads=aT_T + [consts_T], writes=tf_T)
                P.op("dve", lambda e: e.tensor_scalar(out=tf[:, 0:cn], in0=tf[:, 0:cn], scalar1=1.0, scalar2=None, op0=ALU.add),
                     reads=tf_T, writes=tf_T)
                P.op("dve", lambda e: e.tensor_tensor(out=kT_[:, c0:c0 + cn], in0=kT_[:, c0:c0 + cn], in1=tf[:, 0:cn], op=ALU.mult),
                     reads=kT_T + tf_T, writes=kT_T)
            zero_bd(SETS)
            blist = [(256 * bi_, 64, [(Smast[:, j, :], [Smast_T[j]])] * 4, False) for bi_ in range(npr // 256)]
            if has_sample:
                for u in range(4):
                    di = ctr["tmpf"] % 2; ctr["tmpf"] += 1
                    si_, si_T = tmpf[di], [tmpf_T[di]]
                    P.dma("sp", si_[:, 256:320], swkv[u, j], writes=si_T)
                    P.op("dve", lambda e: e.memset(si_[:, 0:128], 0.0), writes=si_T)
                    for hh in range(2):
                        ps_ = slice(64 * hh, 64 * hh + 64)
                        P.op("dve", lambda e: e.tensor_copy(out=si_[ps_, 64 * hh:64 * hh + 64], in_=si_[ps_, 256:320]), reads=si_T, writes=si_T)
                    b_ = bank()
                    P.op("pe", lambda e: e.transpose(psb[b_][:, 0:128], si_[:, 0:128], identf[:]), reads=si_T + [cT], writes=psT[b_])
                    P.op("act", lambda e: e.copy(out=Ss[:, u, :], in_=psb[b_][:, 0:128]), reads=psT[b_], writes=Ss_T)
                blist.append((npr, 32, [(Ss[:, u, :], Ss_T) for u in range(4)], True))

            def drive(gens):
                gens = [g_ for g_ in gens if g_ is not None]
                while gens:
                    for g_ in list(gens):
                        try:
                            next(g_)
                        except StopIteration:
                            gens.remove(g_)
            drive([prep_gen(j, blist[0][0], blist[0][1], 4, SETS[0], blist[0][3])])
            for bi_, (cb_, L_, states_, rz_) in enumerate(blist):
                nxt = None
                if bi_ + 1 < len(blist):
                    n_ = blist[bi_ + 1]
                    nxt = prep_gen(j, n_[0], n_[1], 4, SETS[(bi_ + 1) % 2], n_[3])
                drive([scan_gen(j, cb_, L_, 4, SETS[bi_ % 2], states_), nxt])
            if has_sample:
                for u in range(4):
                    state_out(Ss[:, u, :], Ss_T, wkvs_d[u, j])
            if gi == 2:
                state_out(Smast[:, j, :], [Smast_T[j]], wkvp_d[j])
            s = ctr["sq"] % 2; ctr["sq"] += 1
            P.op("dve", lambda e: e.scalar_tensor_tensor(out=sq[s][:, 0:N], in0=rT_[:, 0:N], scalar=cc("rk", j), in1=kT_[:, 0:N],
                                                         op0=ALU.mult, op1=ALU.mult),
                 reads=rT_T + kT_T + [consts_T], writes=[sq_T[s]])
            for (c0, cn) in blks:
                b = bank()
                P.op("pe", lambda e: e.matmul(psb[b][:, 0:cn], bdones[:], sq[s][:, c0:c0 + cn], start=True, stop=True),
                     reads=[sq_T[s], cT], writes=psT[b])
                ts = ctr["tmpf"] % 2; ctr["tmpf"] += 1
                tf, tf_T = tmpf[ts], [tmpf_T[ts]]
                P.op("dve", lambda e: e.tensor_tensor(out=tf[:, 0:cn], in0=psb[b][:, 0:cn], in1=vT_[:, c0:c0 + cn], op=ALU.mult),
                     reads=psT[b] + vT_T, writes=tf_T)
                ts2 = ctr["tmpf"] % 2; ctr["tmpf"] += 1
                tg, tg_T = tmpf[ts2], [tmpf_T[ts2]]
                P.op("dve", lambda e: e.tensor_scalar(out=tg[:, 0:cn], in0=ynT[:, c0:c0 + cn], scalar1=cc("lnw", j), scalar2=cc("lnb", j),
                                                      op0=ALU.mult, op1=ALU.add),
                     reads=ynT_T + [consts_T], writes=tg_T)
                P.op("dve", lambda e: e.tensor_tensor(out=tf[:, 0:cn], in0=tf[:, 0:cn], in1=tg[:, 0:cn], op=ALU.add),
                     reads=tf_T + tg_T, writes=tf_T)
                bg = bank()
                for kt in range(2):
                    P.op("pe", lambda e: e.matmul(psb[bg][:, 0:cn], g2s[:, kt, :], lora[:, 2 + kt, c0:c0 + cn],
                                                  start=(kt == 0), stop=(kt == 1)),
                         reads=g2s_T + lora_T, writes=psT[bg], inc=(kt == 1))
                P.op("dve", lambda e: e.tensor_tensor(out=ygT[:, j, c0:c0 + cn], in0=tf[:, 0:cn], in1=psb[bg][:, 0:cn], op=ALU.mult),
                     reads=tf_T + psT[bg], writes=yg_T)

        aux_release(dslot_T[0:2], aux0); aux_release(dslot_T[2:4], aux1); aux_release([rstd_T], aux2)
        aux_release([wslot_T[2]], aux3); aux_release([wslot_T[3]], aux4)
        return ygT, yg_T

    def rwkv_full(gi, N, npr, has_sample):
        ygT, yg_T = rwkv(gi, N, npr, has_sample)
        if dbg == "yg":
            for c in range(DC):
                P.op("act", lambda e: e.copy(out=xT[:, c, 0:N], in_=ygT[:, c, 0:N]), reads=yg_T, writes=[xT_T[c]])
            return
        P.op("pool", lambda e: e.tensor_copy(out=hT[:, :, 0:1], in_=hT[:, :, npr:npr + 1]), reads=hT_T, writes=hT_T)
        blks = blocks(N)
        set_pool(6)
        ssb = SSB
        pend = None
        for dch in range(DC):
            ws, wT = load_w(wro_d[dch])
            pb = [bank() for _ in blks]
            for bi, (c0, cn) in enumerate(blks):
                for kt in range(DC):
                    P.op("pe", lambda e: e.matmul(psb[pb[bi]][:, 0:cn], ws[:, kt, :], ygT[:, kt, c0:c0 + cn],
                                                  start=(kt == 0), stop=(kt == DC - 1)),
                         reads=[wT] + yg_T, writes=psT[pb[bi]], inc=(kt == DC - 1))
            if pend is not None:
                pend()
            pend = out_evac_ss(dch, N, pb, ssb, dch == 0, dch == DC - 1)
        pend()
        postnorm_add(1, 3, N, ssb, 1.0)

    for gi, (p0, npr, has_s) in enumerate(GROUPS[:ngroups]):
        N = npr + (128 if has_s else 0)
        for c in range(DC):
            P.dma("sp", xT[:, c, 0:npr], xp[:, c, p0:p0 + npr], writes=[xT_T[c]])
            if has_s:
                P.dma("sp", xT[:, c, npr:npr + 128], xs[:, c, :], writes=[xT_T[c]])
        for l in range(nlayers):
            ffn(l, 0, N)
            if dbg == f"ffn{l}0" and gi == 0:
                break
            if l == 0:
                attention(gi, N, npr, has_s)
            else:
                rwkv_full(gi, N, npr, has_s)
            if dbg in (f"mix{l}", "yg") and gi == 0 and (dbg != "yg" or l == 1):
                break
            ffn(l, 1, N)
        for c in range(DC):
            P.dma("sp", yT_d[:, c, p0:p0 + npr], xT[:, c, 0:npr], reads=[xT_T[c]], is_output=True)
            if has_s:
                P.dma("sp", yT_d[:, c, SEQ:SEQ + 128], xT[:, c, npr:npr + 128], reads=[xT_T[c]], is_output=True)
    P.finish()
    stats = dict(ops=P.n_ops, waits=P.n_waits, sems=P.nsem, cnt=dict(P.cnt))
    P.close()
    return nc, stats


def prep_shared(inp):
    f = lambda a: np.ascontiguousarray(np.asarray(a, dtype=np.float32))
    sh = {}
    sh["wg"] = np.stack([np.stack([w_chunks(f(inp["ffn_w_gate"][l, s])) for s in range(2)]) for l in range(2)])
    sh["wu"] = np.stack([np.stack([w_chunks(f(inp["ffn_w_up"][l, s])) for s in range(2)]) for l in range(2)])
    wd = np.stack([np.stack([w_chunks(f(inp["ffn_w_down"][l, s])) for s in range(2)]) for l in range(2)])
    sh["wd"] = np.ascontiguousarray(wd.reshape(2, 2, DC, 128, 4, 11, 128).transpose(0, 1, 2, 4, 3, 5, 6))
    wqkv = f(inp["att_w_qkv"][0])
    bqkv = f(inp["att_b_qkv"][0])
    kcols = [np.concatenate([wqkv[:, 2048 + 64 * h:2048 + 64 * (h + 1)]] * 2, axis=1) for h in range(4)]
    wext = np.concatenate([wqkv[:, :2048]] + kcols, axis=1)
    sh["wqkv"] = w_chunks(wext)
    bext = np.concatenate([bqkv[:2048]] + [np.concatenate([bqkv[2048 + 64 * h:2048 + 64 * (h + 1)]] * 2) for h in range(4)])
    sh["wkvt"] = w_chunks(wqkv[:, 2048:2560])
    sh["bkv"] = bqkv[2048:2560].reshape(1, 512).copy()
    sh["sinks"] = f(inp["att_sinks"]).reshape(1, 32).copy()
    sh["table"] = f(inp["rel_table"])
    sh["wao"] = w_chunks(f(inp["att_w_o"][0]))
    sh["wr"] = w_chunks(f(inp["rwkv_w_r"][0]))
    sh["wk"] = w_chunks(f(inp["rwkv_w_k"][0]))
    sh["wv"] = w_chunks(f(inp["rwkv_w_v"][0]))
    sh["wro"] = w_chunks(f(inp["rwkv_w_o"][0]))
    sh["w1"] = w_chunks(f(inp["rwkv_w1"][0]), 96)[0]
    sh["a1"] = w_chunks(f(inp["rwkv_a1"][0]), 96)[0]
    sh["g1"] = w_chunks(f(inp["rwkv_g1"][0]))
    sh["w2"] = f(inp["rwkv_w2"][0])
    sh["a2"] = f(inp["rwkv_a2"][0])
    sh["g2"] = np.ascontiguousarray(f(inp["rwkv_g2"][0]).reshape(2, 128, D).transpose(1, 0, 2))
    cols = [fcol(f(inp["norm_g"])).reshape(128, 12 * 16),
            bext.reshape(20, 128).T,
            fcol(f(inp["rwkv_mu"][0])).reshape(128, 6 * 16)]
    for nm in ("rwkv_w0", "rwkv_a0", "rwkv_k_k", "rwkv_k_a"):
        cols.append(fcol(f(inp[nm][0])))
    cols.append(fcol(f(inp["rwkv_r_k"][0]).reshape(D)))
    for nm in ("rwkv_ln_w", "rwkv_ln_b"):
        cols.append(fcol(f(inp[nm][0])))
    sh["consts"] = np.ascontiguousarray(np.concatenate(cols, axis=1))
    for k, v in static_consts().items():
        sh["c_" + k] = v
    return sh


def prep_core(inp, c):
    f = lambda a: np.ascontiguousarray(np.asarray(a, dtype=np.float32))
    m = {}
    m["xp"] = fcol(f(inp["x_prompt"][c]).T.copy()) if False else np.ascontiguousarray(
        f(inp["x_prompt"][c]).T.reshape(DC, 128, SEQ).transpose(1, 0, 2))
    xs = f(inp["x_sample"][4 * c:4 * c + 4]).reshape(128, D)
    m["xs"] = np.ascontiguousarray(xs.T.reshape(DC, 128, 128).transpose(1, 0, 2))
    m["ck"] = f(inp["cache_k"][0, 4 * c:4 * c + 4]).reshape(4, 128, 256)
    m["cv"] = f(inp["cache_v"][0, 4 * c:4 * c + 4]).reshape(4, 128, 256)
    ss = f(inp["state_shift"][0, 4 * c:4 * c + 4, 0])
    m["sshift"] = np.ascontiguousarray(ss.reshape(4, DC, 128).transpose(2, 0, 1))
    m["swkv"] = f(inp["state_wkv"][0, 4 * c:4 * c + 4]).reshape(4, 16, 128, 64)
    return m


_CACHE = {}


def kernel(**inputs):
    if "nc" not in _CACHE:
        _CACHE["nc"] = build()[0]
    nc = _CACHE["nc"]
    sh = prep_shared(inputs)
    in_maps = []
    for c in range(NCORE):
        m = dict(sh)
        m.update(prep_core(inputs, c))
        in_maps.append(m)
    res = run_bass_kernel_spmd(nc, in_maps, core_ids=list(range(NCORE)))
    R = res.results
    y_prompt = np.zeros((8, SEQ, D), np.float32)
    y_sample = np.zeros((32, 32, D), np.float32)
    k_prompt = np.zeros((1, 8, 128, 4, 64), np.float32)
    v_prompt = np.zeros((1, 8, 128, 4, 64), np.float32)
    k_sample = np.zeros((1, 32, 32, 4, 64), np.float32)
    v_sample = np.zeros((1, 32, 32, 4, 64), np.float32)
    shift_prompt = np.zeros((1, 8, 1, D), np.float32)
    wkv_prompt = np.zeros((1, 8, 32, 64, 64), np.float32)
    shift_sample = np.zeros((1, 32, 1, D), np.float32)
    wkv_sample = np.zeros((1, 32, 32, 64, 64), np.float32)
    for c in range(NCORE):
        r = R[c]
        yT = np.asarray(r["yT"])
        y = yT.transpose(2, 1, 0).reshape(SEQ + 128, D)
        y_prompt[c] = y[:SEQ]
        y_sample[4 * c:4 * c + 4] = y[SEQ:].reshape(4, 32, D)
        k_prompt[0, c] = np.asarray(r["kp"]).reshape(128, 4, 64)
        v_prompt[0, c] = np.asarray(r["vp"]).reshape(128, 4, 64)
        k_sample[0, 4 * c:4 * c + 4] = np.asarray(r["ks"]).reshape(4, 32, 4, 64)
        v_sample[0, 4 * c:4 * c + 4] = np.asarray(r["vs"]).reshape(4, 32, 4, 64)
        shift_prompt[0, c, 0] = np.asarray(r["shp"]).T.reshape(D)
        shift_sample[0, 4 * c:4 * c + 4, 0] = np.asarray(r["shs"]).transpose(1, 2, 0).reshape(4, D)
        wkv_prompt[0, c] = np.asarray(r["wkvp"]).reshape(32, 64, 64)
        wkv_sample[0, 4 * c:4 * c + 4] = np.asarray(r["wkvs"]).reshape(4, 32, 64, 64)
    return (y_prompt, y_sample, k_prompt, v_prompt, k_sample, v_sample,
            shift_prompt, wkv_prompt, shift_sample, wkv_sample)
```

```python
import contextlib
import math
import numpy as np
import concourse.bass as bass
import concourse.mybir as mybir
from concourse.bass_utils import run_bass_kernel_spmd

F32 = mybir.dt.float32
BF16 = mybir.dt.bfloat16
AF = mybir.ActivationFunctionType
ALU = mybir.AluOpType
AX = mybir.AxisListType

D = 2048
DC = 16
FFD = 5632
FC = 44
NCORE = 8
SEQ = 2048
NH = 32
HD = 64
WINDOW = 128
N_BUCKETS = 32
MAX_DISTANCE = 128
RMS_EPS = 1e-6
GN_EPS = 64 * 1e-5
NEG = -1.0e30
GROUPS = [(0, 768, False), (768, 768, False), (1536, 512, True)]
NMAX = 768


class T:
    __slots__ = ("name", "w", "r", "dsem", "dtot", "bank")

    def __init__(self, name):
        self.name = name
        self.w = {}
        self.r = {}
        self.dsem = None
        self.dtot = 0
        self.bank = None


def TL(name, n):
    return [T(f"{name}{i}") for i in range(n)]


class Prog:
    COMPUTE = ("pe", "act", "dve", "pool")

    def __init__(self, nc, strict_same=True):
        self.nc = nc
        self.es = contextlib.ExitStack()
        self.eng = {"pe": nc.tensor, "act": nc.scalar, "dve": nc.vector,
                    "pool": nc.gpsimd, "sp": nc.sync}
        self.sem = {}
        self.cnt = {}
        for e in self.COMPUTE:
            self.sem[e] = self.es.enter_context(nc.semaphore("s_" + e))
            self.cnt[e] = 0
        self.seen = {e: {} for e in self.eng}
        self.strict_same = strict_same
        self.nsem = 0
        self.n_ops = 0
        self.n_waits = 0
        self.out_events = []
        self.uid = 0

    def sb(self, name, shape, dt):
        return self.es.enter_context(self.nc.sbuf_tensor("sb_" + name, list(shape), dt))

    def ps(self, name, shape, dt=F32):
        return self.es.enter_context(self.nc.psum_tensor("ps_" + name, list(shape), dt))

    def newsem(self, name):
        self.nsem += 1
        self.uid += 1
        return self.es.enter_context(self.nc.semaphore(f"{name}_{self.uid}"))

    def _wait(self, e, ev):
        sem, val = ev
        k = id(sem)
        if self.seen[e].get(k, 0) >= val:
            return
        self.seen[e][k] = val
        self.eng[e].wait_ge(sem, val)
        self.n_waits += 1

    def _deps(self, e, reads, writes):
        own = id(self.sem[e]) if e in self.sem else None
        skip_own = (e == "pe") or (not self.strict_same)
        for t in reads:
            for k, ev in t.w.items():
                if k == own and skip_own:
                    continue
                self._wait(e, ev)
        for t in writes:
            for k, ev in t.w.items():
                if k == own and skip_own:
                    continue
                self._wait(e, ev)
            for k, ev in t.r.items():
                if k == own and skip_own:
                    continue
                self._wait(e, ev)
        for t in list(reads) + list(writes):
            if t.bank is not None:
                for k, ev in t.bank.w.items():
                    if k != own:
                        self._wait(e, ev)

    def _record(self, ev, reads, writes):
        k = id(ev[0])
        for t in reads:
            t.r[k] = ev
            if t.bank is not None:
                t.bank.w = {k: ev}
        for t in writes:
            t.w = {k: ev}
            t.r = {}
            if t.bank is not None:
                t.bank.w = {k: ev}

    def op(self, e, fn, reads=(), writes=(), inc=True):
        self._deps(e, reads, writes)
        ins = fn(self.eng[e])
        self.n_ops += 1
        if inc:
            self.cnt[e] += 1
            ins.then_inc(self.sem[e], 1)
            ev = (self.sem[e], self.cnt[e])
        else:
            ev = (self.sem[e], self.cnt[e] + 1)
        self._record(ev, reads, writes)
        return ins

    def dma(self, q, out_ap, in_ap, reads=(), writes=(), semt=None, is_output=False, concurrent=False, **kw):
        if semt is None:
            semt = writes[0] if writes else reads[0]
        if semt.dsem is None:
            semt.dsem = self.newsem("d")
        if concurrent:
            k = id(semt.dsem)
            saved = [(t, t.w.pop(k)) for t in writes if k in t.w]
            self._deps(q, reads, writes)
            for t, ev in saved:
                t.w[k] = ev
        else:
            self._deps(q, reads, writes)
        semt.dtot += 16
        ins = self.eng[q].dma_start(out=out_ap, in_=in_ap, **kw)
        ins.then_inc(semt.dsem, 16)
        self.n_ops += 1
        ev = (semt.dsem, semt.dtot)
        self._record(ev, reads, writes)
        if is_output:
            self.out_events.append(ev)
        return ins

    def finish(self):
        last = {}
        for sem, val in self.out_events:
            k = id(sem)
            if k not in last or last[k][1] < val:
                last[k] = (sem, val)
        for ev in last.values():
            self._wait("sp", ev)
        for e in self.COMPUTE:
            if self.cnt[e] > 0:
                self._wait("sp", (self.sem[e], self.cnt[e]))

    def close(self):
        self.es.close()


def w_chunks(w, cw=128):
    K, M = w.shape
    return np.ascontiguousarray(w.reshape(K // 128, 128, M // cw, cw).transpose(2, 1, 0, 3))


def fcol(v):
    s = v.shape[:-1]
    a = v.reshape(*s, DC, 128)
    a = np.moveaxis(a, -1, 0)
    return np.ascontiguousarray(a)


def t5_bucket_np(rel):
    nb = N_BUCKETS // 2
    max_exact = nb // 2
    offset = np.where(rel > 0, nb, 0)
    n = np.abs(rel)
    nf = np.maximum(n, 1).astype(np.float32)
    large = max_exact + (np.log(nf / np.float32(max_exact)) / np.float32(math.log(MAX_DISTANCE / max_exact))
                         * np.float32(nb - max_exact)).astype(np.int32)
    large = np.minimum(large, nb - 1)
    return offset + np.where(n < max_exact, n, large)


def static_consts():
    c = {}
    c["ident"] = np.eye(128, dtype=np.float32)
    i = np.arange(128)
    for L in (64, 32):
        same = (i[:, None] // L) == (i[None, :] // L)
        c[f"tri{L}"] = (same & (i[:, None] <= i[None, :])).astype(np.float32)
        ii = i % L
        c[f"mstrict{L}"] = (ii[:, None] < ii[None, :]).astype(np.float32)
        c[f"mincl{L}"] = (ii[:, None] <= ii[None, :]).astype(np.float32)
        b = 1
        lv = 0
        while b < L:
            c[f"lv{L}_{lv}"] = ((ii[:, None] // (2 * b) == ii[None, :] // (2 * b)) & ((ii[None, :] // b) % 2 == 1)
                               & ((ii[:, None] // b) % 2 == 0)).astype(np.float32)
            b *= 2
            lv += 1
    c["bdones"] = ((i[:, None] // 64) == (i[None, :] // 64)).astype(np.float32)
    r = np.arange(255)
    bk = t5_bucket_np((r - 191).astype(np.int32))
    oh = np.zeros((32, 255), np.float32)
    oh[bk, r] = 1.0
    c["onehot"] = oh
    return c


def build(ngroups=3, nlayers=2, dbg=None):
    nc = bass.Bass("TRN2", target_bir_lowering=False)
    import os as _os
    P = Prog(nc, strict_same=(_os.environ.get("K_STRICT", "1") == "1"))

    def din(name, shape):
        return nc.dram_tensor(name, list(shape), F32, kind="ExternalInput").ap()

    def dout(name, shape):
        return nc.dram_tensor(name, list(shape), F32, kind="ExternalOutput").ap()

    xp = din("xp", [128, DC, SEQ])
    xs = din("xs", [128, DC, 128])
    ck = din("ck", [4, 128, 256])
    cv = din("cv", [4, 128, 256])
    sshift = din("sshift", [128, 4, DC])
    swkv = din("swkv", [4, 16, 128, 64])
    NCONST = 12 * 16 + 20 + 6 * 16 + 7 * 16
    consts_d = din("consts", [128, NCONST])
    wg_d = din("wg", [2, 2, FC, 128, DC, 128])
    wu_d = din("wu", [2, 2, FC, 128, DC, 128])
    wd_d = din("wd", [2, 2, DC, 4, 128, 11, 128])
    wqkv_d = din("wqkv", [20, 128, DC, 128])
    wkvt_d = din("wkvt", [4, 128, DC, 128])
    bkv_d = nc.dram_tensor("bkv", [1, 512], F32, kind="ExternalInput")
    sinks_d = nc.dram_tensor("sinks", [1, 32], F32, kind="ExternalInput")
    table_d = din("table", [32, 32])
    wao_d = din("wao", [16, 128, DC, 128])
    wr_d = din("wr", [16, 128, DC, 128])
    wk_d = din("wk", [16, 128, DC, 128])
    wv_d = din("wv", [16, 128, DC, 128])
    wro_d = din("wro", [16, 128, DC, 128])
    w1_d = din("w1", [128, DC, 96])
    a1_d = din("a1", [128, DC, 96])
    g1_d = din("g1", [2, 128, DC, 128])
    w2_d = din("w2", [96, D])
    a2_d = din("a2", [96, D])
    g2_d = din("g2", [128, 2, D])
    cst = {k: din("c_" + k, v.shape) for k, v in static_consts().items()}
    fscr = nc.dram_tensor("fscr", [32, 255], F32, kind="Internal")

    yT_d = dout("yT", [128, DC, SEQ + 128])
    kp_d = dout("kp", [128, 256])
    vp_d = dout("vp", [128, 256])
    ks_d = dout("ks", [128, 256])
    vs_d = dout("vs", [128, 256])
    shp_d = dout("shp", [128, DC])
    shs_d = dout("shs", [128, 4, DC])
    wkvp_d = dout("wkvp", [16, 128, 64])
    wkvs_d = dout("wkvs", [4, 16, 128, 64])
    dbg_d = dout("dbg", [128, DC, NMAX]) if dbg else None

    CO = {}
    o = 0
    CO["g"] = o; o += 12 * 16
    CO["bq"] = o; o += 20
    CO["mu"] = o; o += 6 * 16
    for nm in ("w0", "a0", "kk", "ka", "rk", "lnw", "lnb"):
        CO[nm] = o; o += 16
    assert o == NCONST

    xT = P.sb("xT", [128, DC, NMAX], F32); xT_T = TL("xT", DC)
    hT = P.sb("hT", [128, DC, NMAX + 1], BF16); hT_T = TL("hT", DC)
    BIGB = 66 * 1024
    big = P.sb("big", [128, BIGB // 2], BF16)
    big_T = TL("big", FC)
    SL = 768

    def bigv(off_b, shape, dt):
        n = int(np.prod(shape[1:]))
        esz = 4 if dt == F32 else 2
        assert off_b % 4 == 0 and off_b + n * esz <= BIGB, (off_b, shape)
        if dt == F32:
            ap = big[:, off_b // 2: off_b // 2 + n * 2].bitcast(F32)
        else:
            ap = big[:, off_b // 2: off_b // 2 + n]
        if len(shape) == 3:
            ap = ap.rearrange("p (a b) -> p a b", b=shape[2])
        elif len(shape) == 4:
            ap = ap.rearrange("p (a b c) -> p a b c", b=shape[2], c=shape[3])
        t0 = off_b // (SL * 2)
        t1 = (off_b + n * esz - 1) // (SL * 2)
        return ap, big_T[t0:t1 + 1]

    wslot = [P.sb(f"ws{i}", [128, DC, 128], BF16) for i in range(4)]
    wslot_T = TL("ws", 4)
    dsl = P.sb("wds", [128, 4, 11, 128], BF16)
    dslot = [dsl[:, i] for i in range(4)]
    dslot_T = TL("wds", 4)
    wctr = [0, 0]

    consts = P.sb("consts", [128, NCONST], F32); consts_T = T("consts")
    identf = P.sb("identf", [128, 128], F32)
    identb = P.sb("identb", [128, 128], BF16)
    onesb = P.sb("onesb", [128, 128], BF16)
    bdones = P.sb("bdones", [128, 128], BF16)
    tri = {L: P.sb(f"tri{L}", [128, 128], F32) for L in (64, 32)}
    mstrict = {L: P.sb(f"mstrict{L}", [128, 128], BF16) for L in (64, 32)}
    mincl = {L: P.sb(f"mincl{L}", [128, 128], BF16) for L in (64, 32)}
    lvm = {L: [P.sb(f"lv{L}_{i}", [128, 128], BF16) for i in range(6 if L == 64 else 5)] for L in (64, 32)}
    cT = T("cst")
    bkv = P.sb("bkv", [128, 512], F32)
    sinks = P.sb("sinks", [128, 32], F32)
    bias2 = P.sb("bias2", [128, 32, 192], BF16); bias2_T = T("bias2")
    ktc = P.sb("ktc", [128, 4, 128], BF16); ktc_T = T("ktc")
    vbc = P.sb("vbc", [128, 256], BF16); vbc_T = T("vbc")
    Smast = P.sb("Smast", [128, 16, 128], F32); Smast_T = TL("Sm", 16)
    rstd = P.sb("rstd", [128, NMAX], F32); rstd_T = T("rstd")
    sq = [P.sb(f"sq{i}", [128, NMAX], BF16) for i in range(2)]; sq_T = TL("sq", 2)
    tmpf = [P.sb(f"tmpf{i}", [128, 512], F32) for i in range(2)]; tmpf_T = TL("tmpf", 2)
    small = P.sb("small", [128, 64], F32); small_T = TL("small", 4)
    ctr = {"sq": 0, "tmpf": 0, "bank": 0, "q": 0, "ev": 0, "nb": 2}
    SSB = [6, 7]

    psb = [P.ps(f"psb{i}", [128, 512]) for i in range(8)]
    psT = [TL(f"ps{i}_", 4) for i in range(8)]
    for i in range(8):
        bx = T(f"bank{i}")
        for t_ in psT[i]:
            t_.bank = bx

    def set_pool(nb):
        ctr["nb"] = nb

    def bank():
        b = ctr["bank"] % ctr["nb"]
        ctr["bank"] += 1
        return b

    def quarter():
        nbk = 6 - ctr["nb"]
        q = ctr["q"] % (nbk * 4)
        ctr["q"] += 1
        return ctr["nb"] + q % nbk, q // nbk

    def qf(bq):
        b, q = bq
        return psb[b][:, 128 * q:128 * (q + 1)]

    def qb(bq):
        b, q = bq
        return psb[b][:, 128 * q:128 * (q + 1)].bitcast(BF16)

    def qT(bq):
        return [psT[bq[0]][bq[1]]]

    def ev_eng():
        ctr["ev"] += 1
        return "act" if ctr["ev"] % 2 else "dve"

    def copy(e, out, in_, reads, writes):
        if e == "act":
            P.op("act", lambda x: x.copy(out=out, in_=in_), reads=reads, writes=writes)
        else:
            P.op(e, lambda x: x.tensor_copy(out=out, in_=in_), reads=reads, writes=writes)

    def cc(nm, j=None):
        if j is None:
            return consts[:, CO[nm]:CO[nm] + 16]
        return consts[:, CO[nm] + j:CO[nm] + j + 1]

    def gcol(l, n, c=None):
        o0 = CO["g"] + (l * 6 + n) * 16
        if c is None:
            return consts[:, o0:o0 + 16]
        return consts[:, o0 + c:o0 + c + 1]

    def blocks(N):
        out = []
        c0 = 0
        while c0 < N:
            cn = min(512, N - c0)
            out.append((c0, cn))
            c0 += cn
        return out

    P.dma("sp", consts[:], consts_d, writes=[consts_T])
    P.dma("sp", identf[:], cst["ident"], writes=[cT])
    for L in (64, 32):
        P.dma("sp", tri[L][:], cst[f"tri{L}"], writes=[cT])
        P.dma("pool", mstrict[L][:], cst[f"mstrict{L}"], writes=[cT])
        P.dma("pool", mincl[L][:], cst[f"mincl{L}"], writes=[cT])
        for i_, m_ in enumerate(lvm[L]):
            P.dma("pool", m_[:], cst[f"lv{L}_{i_}"], writes=[cT])
    P.dma("pool", identb[:], cst["ident"], writes=[cT])
    P.dma("pool", bdones[:], cst["bdones"], writes=[cT])
    P.dma("sp", bkv[:], bkv_d.ap().partition_broadcast(128), writes=[cT])
    P.dma("sp", sinks[:], sinks_d.ap().partition_broadcast(128), writes=[cT])
    P.op("dve", lambda e: e.memset(onesb[:], 1.0), writes=[cT])
    P.op("dve", lambda e: e.memset(hT[:, :, 0:1], 0.0), writes=hT_T)
    P.op("dve", lambda e: e.memset(Smast[:], 0.0), writes=Smast_T)
    P.op("dve", lambda e: e.memset(ktc[:], 0.0), writes=[ktc_T])
    P.op("dve", lambda e: e.memset(vbc[:], 0.0), writes=[vbc_T])

    def build_bias():
        tb = tmpf[0]; oh = tmpf[1]
        P.dma("sp", tb[0:32, 0:32], table_d, writes=[tmpf_T[0]])
        P.dma("sp", oh[0:32, 0:255], cst["onehot"], writes=[tmpf_T[1]])
        bq = quarter()
        pso = psb[bq[0]][0:32, 0:255]
        P.op("pe", lambda e: e.matmul(pso, tb[0:32, 0:32], oh[0:32, 0:255], start=True, stop=True),
             reads=[tmpf_T[0], tmpf_T[1]], writes=psT[bq[0]])
        fs, fs_T = bigv(0, [128, 256], F32)
        P.op("act", lambda e: e.copy(out=fs[0:32, 0:255], in_=pso), reads=psT[bq[0]], writes=fs_T)
        fT = T("fscr")
        P.dma("sp", fscr.ap(), fs[0:32, 0:255], reads=fs_T, writes=[fT])
        stg, stg_T = bigv(1024, [128, 32, 192], F32)
        for i in range(64):
            src = bass.AP(fscr, 63 - i, [[0, 1], [255, 32], [1, 192]])
            for half in range(2):
                p = half * 64 + i
                P.dma("sp", stg[p:p + 1, :, :], src, reads=[fT], writes=stg_T, semt=stg_T[0], concurrent=True)
        P.op("act", lambda e: e.copy(out=bias2[:, 0:16, :], in_=stg[:, 0:16, :]), reads=stg_T, writes=[bias2_T])
        P.op("dve", lambda e: e.tensor_copy(out=bias2[:, 16:32, :], in_=stg[:, 16:32, :]), reads=stg_T + [bias2_T], writes=[bias2_T])

    build_bias()

    def sumsq_accumulate(src_ap_fn, src_tiles_fn, N, nch, ssb):
        for c in range(nch):
            s = ctr["sq"] % 2; ctr["sq"] += 1
            P.op("act", lambda e: e.activation(out=sq[s][:, 0:N], in_=src_ap_fn(c), func=AF.Square),
                 reads=src_tiles_fn(c), writes=[sq_T[s]])
            for bi, (c0, cn) in enumerate(blocks(N)):
                P.op("pe", lambda e: e.matmul(psb[ssb[bi]][:, 0:cn], onesb[:], sq[s][:, c0:c0 + cn],
                                              start=(c == 0), stop=(c == nch - 1)),
                     reads=[sq_T[s], cT], writes=psT[ssb[bi]], inc=True)

    def rstd_from_ss(N, ssb, eps):
        for bi, (c0, cn) in enumerate(blocks(N)):
            P.op("dve", lambda e: e.tensor_scalar(out=rstd[:, c0:c0 + cn], in0=psb[ssb[bi]][:, 0:cn],
                                                  scalar1=1.0 / D, scalar2=eps, op0=ALU.mult, op1=ALU.add),
                 reads=psT[ssb[bi]], writes=[rstd_T])
        P.op("act", lambda e: e.activation(out=rstd[:, 0:N], in_=rstd[:, 0:N], func=AF.Ln),
             reads=[rstd_T], writes=[rstd_T])
        P.op("act", lambda e: e.activation(out=rstd[:, 0:N], in_=rstd[:, 0:N], func=AF.Exp, scale=-0.5),
             reads=[rstd_T], writes=[rstd_T])

    def prenorm(l, n, N):
        ssb = SSB
        sumsq_accumulate(lambda c: xT[:, c, 0:N], lambda c: [xT_T[c]], N, DC, ssb)
        rstd_from_ss(N, ssb, RMS_EPS)
        for c in range(DC):
            P.op("dve", lambda e: e.scalar_tensor_tensor(out=hT[:, c, 1:1 + N], in0=xT[:, c, 0:N],
                                                         scalar=gcol(l, n, c), in1=rstd[:, 0:N],
                                                         op0=ALU.mult, op1=ALU.mult),
                 reads=[xT_T[c], rstd_T, consts_T], writes=[hT_T[c]])

    def postnorm_add(l, n, N, ssb, weight):
        rstd_from_ss(N, ssb, RMS_EPS)
        for c in range(DC):
            for (c0, cn) in blocks(N):
                s = ctr["tmpf"] % 2; ctr["tmpf"] += 1
                P.op("dve", lambda e: e.scalar_tensor_tensor(out=tmpf[s][:, 0:cn], in0=hT[:, c, 1 + c0:1 + c0 + cn],
                                                             scalar=gcol(l, n, c), in1=rstd[:, c0:c0 + cn],
                                                             op0=ALU.mult, op1=ALU.mult),
                     reads=[hT_T[c], rstd_T, consts_T], writes=[tmpf_T[s]])
                P.op("dve", lambda e: e.scalar_tensor_tensor(out=xT[:, c, c0:c0 + cn], in0=tmpf[s][:, 0:cn],
                                                             scalar=float(weight), in1=xT[:, c, c0:c0 + cn],
                                                             op0=ALU.mult, op1=ALU.add),
                     reads=[tmpf_T[s]], writes=[xT_T[c]])

    def load_w(dram_ap):
        s = wctr[0] % 4; wctr[0] += 1
        P.dma("pool", wslot[s][:], dram_ap, writes=[wslot_T[s]])
        return wslot[s], wslot_T[s]

    def out_evac_ss(c, N, pbanks, ssb, first, last, bias=None):
        s = ctr["sq"] % 2; ctr["sq"] += 1
        for bi, (c0, cn) in enumerate(blocks(N)):
            b = pbanks[bi]
            P.op("act", lambda e: e.copy(out=hT[:, c, 1 + c0:1 + c0 + cn], in_=psb[b][:, 0:cn]),
                 reads=psT[b], writes=[hT_T[c]])
            P.op("act", lambda e: e.activation(out=sq[s][:, c0:c0 + cn], in_=psb[b][:, 0:cn], func=AF.Square),
                 reads=psT[b], writes=[sq_T[s]])
        def pe_part():
            for bi, (c0, cn) in enumerate(blocks(N)):
                P.op("pe", lambda e: e.matmul(psb[ssb[bi]][:, 0:cn], onesb[:], sq[s][:, c0:c0 + cn],
                                              start=first, stop=last),
                     reads=[sq_T[s], cT], writes=psT[ssb[bi]], inc=True)
        return pe_part

    def ffn(l, s, N):
        n_in, n_out = (0, 1) if s == 0 else (4, 5)
        set_pool(6)
        prenorm(l, n_in, N)
        actT = big[:, 0:FC * SL].rearrange("p (f t) -> p f t", t=SL)
        blks = blocks(N)
        for f in range(FC):
            wgs, wgT = load_w(wg_d[l, s, f])
            wus, wuT = load_w(wu_d[l, s, f])
            for (c0, cn) in blks:
                bg = bank(); bu = bank()
                for kt in range(DC):
                    P.op("pe", lambda e: e.matmul(psb[bg][:, 0:cn], wgs[:, kt, :], hT[:, kt, 1 + c0:1 + c0 + cn],
                                                  start=(kt == 0), stop=(kt == DC - 1)),
                         reads=[wgT, hT_T[kt]], writes=psT[bg], inc=(kt == DC - 1))
                for kt in range(DC):
                    P.op("pe", lambda e: e.matmul(psb[bu][:, 0:cn], wus[:, kt, :], hT[:, kt, 1 + c0:1 + c0 + cn],
                                                  start=(kt == 0), stop=(kt == DC - 1)),
                         reads=[wuT, hT_T[kt]], writes=psT[bu], inc=(kt == DC - 1))
                ts = ctr["tmpf"] % 2; ctr["tmpf"] += 1
                P.op("act", lambda e: e.activation(out=tmpf[ts][:, 0:cn], in_=psb[bg][:, 0:cn], func=AF.Silu),
                     reads=psT[bg], writes=[tmpf_T[ts]])
                P.op("dve", lambda e: e.tensor_tensor(out=actT[:, f, c0:c0 + cn], in0=tmpf[ts][:, 0:cn],
                                                      in1=psb[bu][:, 0:cn], op=ALU.mult),
                     reads=[tmpf_T[ts]] + psT[bu], writes=[big_T[f]])
        ssb = SSB
        pend = None
        for d in range(DC):
            pb = [bank() for _ in blks]
            for qr in range(4):
                sl = wctr[1] % 4; wctr[1] += 1
                P.dma("pool", dslot[sl], wd_d[l, s, d, qr], writes=[dslot_T[sl]])
                for bi, (c0, cn) in enumerate(blks):
                    for k in range(11):
                        f = qr * 11 + k
                        P.op("pe", lambda e: e.matmul(psb[pb[bi]][:, 0:cn], dslot[sl][:, k, :], actT[:, f, c0:c0 + cn],
                                                      start=(f == 0), stop=(f == FC - 1)),
                             reads=[dslot_T[sl], big_T[f]], writes=psT[pb[bi]], inc=(k == 10))
            if pend is not None:
                pend()
            pend = out_evac_ss(d, N, pb, ssb, d == 0, d == DC - 1)
        pend()
        postnorm_add(l, n_out, N, ssb, 0.5)

    def attention(gi, N, npr, has_sample):
        l = 0
        ntile = N // 128
        nptile = npr // 128
        set_pool(2)
        prenorm(l, 2, N)
        off = 0
        qTb, qT_T = bigv(off, [128, DC, NMAX], BF16); off += DC * NMAX * 2
        KT, KT_T = bigv(off, [128, 4, 128 + NMAX], BF16); off += 4 * (128 + NMAX) * 2
        Vb, Vb_T = bigv(off, [128, 7, 256], BF16); off += 7 * 256 * 2
        sbufs = []
        for i in range(4):
            a, t = bigv(off, [128, 256], F32); off += 1024
            sbufs.append((a, t))
        pbufs = []
        for i in range(6):
            a, t = bigv(off, [128, 256], BF16); off += 512
            pbufs.append((a, t))
        ptbufs = []
        for i in range(4):
            a, t = bigv(off, [128, 2, 128], BF16); off += 512
            ptbufs.append((a, t))
        stage, stage_T = bigv(off, [128, 512], F32); off += 2048
        if has_sample:
            KTs, KTs_T = bigv(off, [128, 4, 4, 256], BF16); off += 4 * 4 * 256 * 2
            Vc, Vc_T = bigv(off, [128, 4, 256], BF16); off += 4 * 256 * 2
            ssb_s = []
            for i in range(4):
                a, t = bigv(off, [128, 256], F32); off += 1024
                ssb_s.append((a, t))
            ckf, ckf_T = bigv(off, [128, 256], F32); off += 1024
        assert off <= BIGB, off

        P.op("pool", lambda e: e.tensor_copy(out=KT[:, :, 0:128], in_=ktc[:]), reads=[ktc_T], writes=KT_T)
        P.op("pool", lambda e: e.tensor_copy(out=Vb[:, 0, :], in_=vbc[:]), reads=[vbc_T], writes=Vb_T)

        blks = blocks(N)
        for j in range(20):
            ws, wT = load_w(wqkv_d[j])
            for (c0, cn) in blks:
                b = bank()
                for kt in range(DC):
                    P.op("pe", lambda e: e.matmul(psb[b][:, 0:cn], ws[:, kt, :], hT[:, kt, 1 + c0:1 + c0 + cn],
                                                  start=(kt == 0), stop=(kt == DC - 1)),
                         reads=[wT, hT_T[kt]], writes=psT[b], inc=(kt == DC - 1))
                bcol = consts[:, CO["bq"] + j:CO["bq"] + j + 1]
                if j < 16:
                    P.op("act", lambda e: e.activation(out=qTb[:, j, c0:c0 + cn], in_=psb[b][:, 0:cn],
                                                       func=AF.Identity, bias=bcol, scale=1.0),
                         reads=psT[b] + [consts_T], writes=qT_T)
                else:
                    P.op("act", lambda e: e.activation(out=KT[:, j - 16, 128 + c0:128 + c0 + cn], in_=psb[b][:, 0:cn],
                                                       func=AF.Identity, bias=bcol, scale=1.0),
                         reads=psT[b] + [consts_T], writes=KT_T)
        for cchunk in range(4):
            is_k = cchunk < 2
            ws, wT = load_w(wkvt_d[cchunk])
            for t in range(ntile):
                out_tile = (gi == 2) and (t >= nptile - 1)
                if is_k and not out_tile:
                    continue
                bq = quarter()
                for kt in range(DC):
                    P.op("pe", lambda e: e.matmul(qf(bq), hT[:, kt, 1 + 128 * t:1 + 128 * (t + 1)], ws[:, kt, :],
                                                  start=(kt == 0), stop=(kt == DC - 1)),
                         reads=[wT, hT_T[kt]], writes=qT(bq), inc=(kt == DC - 1))
                bsl = bkv[:, 128 * cchunk:128 * (cchunk + 1)]
                if not is_k:
                    P.op("dve", lambda e: e.tensor_tensor(out=Vb[:, 1 + t, 128 * (cchunk - 2):128 * (cchunk - 1)],
                                                          in0=qf(bq), in1=bsl, op=ALU.add),
                         reads=qT(bq) + [cT], writes=Vb_T)
                if out_tile:
                    P.op("dve", lambda e: e.tensor_tensor(out=stage[:, 128 * cchunk:128 * (cchunk + 1)],
                                                          in0=qf(bq), in1=bsl, op=ALU.add),
                         reads=qT(bq) + [cT], writes=stage_T)
                    is_s = (t == nptile)
                    dst = (ks_d if is_s else kp_d) if is_k else (vs_d if is_s else vp_d)
                    co = 128 * (cchunk % 2)
                    P.dma("sp", dst[:, co:co + 128], stage[:, 128 * cchunk:128 * (cchunk + 1)],
                          reads=stage_T, is_output=True)

        def preset(buf, tiles):
            P.op("pool", lambda e: e.memset(buf[:], NEG), writes=tiles)

        for (a, t_) in sbufs:
            preset(a, t_)

        if has_sample:
            for s in range(4):
                preset(ssb_s[s][0], ssb_s[s][1])
                P.dma("pool", Vc[:, s, :], cv[s], writes=Vc_T, semt=Vc_T[0])
                P.dma("sp", ckf[:], ck[s], writes=ckf_T)
                for kvh in range(4):
                    di = ctr["tmpf"] % 2; ctr["tmpf"] += 1
                    dsrc, dsrc_T = tmpf[di], tmpf_T[di]
                    for dup in range(2):
                        P.op("dve", lambda e: e.tensor_copy(out=dsrc[:, 64 * dup:64 * (dup + 1)],
                                                            in_=ckf[:, 64 * kvh:64 * (kvh + 1)]),
                             reads=ckf_T, writes=[dsrc_T])
                    bq = quarter()
                    P.op("pe", lambda e: e.transpose(qf(bq), dsrc[:, 0:128], identf[:]),
                         reads=[dsrc_T, cT], writes=qT(bq))
                    copy("act", KTs[:, s, kvh, 0:128], qf(bq), qT(bq), KTs_T)
                P.op("pool", lambda e: e.tensor_copy(out=KTs[:, s, :, 128:256], in_=KT[:, :, 128 + npr:128 + npr + 128]),
                     reads=KT_T, writes=KTs_T)

        jobs = []

        def stage_a(jbs):
            bks = []
            for jb in jbs:
                b_ = bank(); bks.append(b_)
                P.op("pe", lambda e: e.matmul(psb[b_][0:jb["nq"], 0:256], jb["q"], jb["k"], start=True, stop=True),
                     reads=qT_T + jb["k_T"], writes=psT[b_][0:2])
            for bi_ in range(2):
                for jb, b_ in zip(jbs, bks):
                    (r0, r1, oc0, oc1, bc0) = jb["bops"][bi_]
                    sb, sb_T = jb["sb"]
                    P.op("dve", lambda e: e.scalar_tensor_tensor(out=sb[r0:r1, oc0:oc1], in0=psb[b_][r0:r1, oc0:oc1], scalar=0.125,
                                                                 in1=bias2[r0:r1, jb["h"], bc0:bc0 + (oc1 - oc0)],
                                                                 op0=ALU.mult, op1=ALU.add),
                         reads=psT[b_][0:2] + [bias2_T], writes=sb_T)
            for jb in jbs:
                nq = jb["nq"]; sb, sb_T = jb["sb"]
                sm = small[:, 16 * (jb["i"] % 4):16 * (jb["i"] % 4) + 16]; sT = [small_T[jb["i"] % 4]]
                P.op("dve", lambda e: e.reduce_max(out=sm[0:nq, 0:1], in_=sb[0:nq, :], axis=AX.X), reads=sb_T, writes=sT)
            for jb in jbs:
                nq = jb["nq"]; h = jb["h"]
                sm = small[:, 16 * (jb["i"] % 4):16 * (jb["i"] % 4) + 16]; sT = [small_T[jb["i"] % 4]]
                P.op("dve", lambda e: e.tensor_scalar(out=sm[0:nq, 2:3], in0=sm[0:nq, 0:1], scalar1=sinks[0:nq, h:h + 1], scalar2=-1.0,
                                                      op0=ALU.max, op1=ALU.mult),
                     reads=sT + [cT], writes=sT)
            for jb in jbs:
                nq = jb["nq"]; sb, sb_T = jb["sb"]
                sm = small[:, 16 * (jb["i"] % 4):16 * (jb["i"] % 4) + 16]; sT = [small_T[jb["i"] % 4]]
                pb_, pb_T = pbufs[jb["i"] % 6]
                P.op("act", lambda e: e.activation(out=pb_[0:nq, :], in_=sb[0:nq, :], func=AF.Exp, bias=sm[0:nq, 2:3], scale=1.0,
                                                   accum_out=sm[0:nq, 3:4]),
                     reads=sb_T + sT, writes=pb_T + sT)
            for jb in jbs:
                nq = jb["nq"]; h = jb["h"]
                sm = small[:, 16 * (jb["i"] % 4):16 * (jb["i"] % 4) + 16]; sT = [small_T[jb["i"] % 4]]
                P.op("act", lambda e: e.activation(out=sm[0:nq, 4:5], in_=sinks[0:nq, h:h + 1], func=AF.Exp, bias=sm[0:nq, 2:3], scale=1.0),
                     reads=sT + [cT], writes=sT)

        def stage_a2(jbs):
            for jb in jbs:
                nq = jb["nq"]
                sm = small[:, 16 * (jb["i"] % 4):16 * (jb["i"] % 4) + 16]; sT = [small_T[jb["i"] % 4]]
                P.op("dve", lambda e: e.tensor_tensor(out=sm[0:nq, 5:6], in0=sm[0:nq, 3:4], in1=sm[0:nq, 4:5], op=ALU.add),
                     reads=sT, writes=sT)
            for jb in jbs:
                nq = jb["nq"]
                sm = small[:, 16 * (jb["i"] % 4):16 * (jb["i"] % 4) + 16]; sT = [small_T[jb["i"] % 4]]
                P.op("dve", lambda e: e.reciprocal(out=sm[0:nq, 6:7], in_=sm[0:nq, 5:6]), reads=sT, writes=sT)
            for jb in jbs:
                nq = jb["nq"]
                sm = small[:, 16 * (jb["i"] % 4):16 * (jb["i"] % 4) + 16]; sT = [small_T[jb["i"] % 4]]
                pb_, pb_T = pbufs[jb["i"] % 6]
                P.op("dve", lambda e: e.tensor_scalar(out=pb_[0:nq, :], in0=pb_[0:nq, :], scalar1=sm[0:nq, 6:7], scalar2=None, op0=ALU.mult),
                     reads=pb_T + sT, writes=pb_T)

        def stage_b(jbs):
            qs = []
            for jb in jbs:
                nq = jb["nq"]
                pb_, pb_T = pbufs[jb["i"] % 6]
                for kt in range(2):
                    bq = quarter(); qs.append((jb, kt, bq))
                    P.op("pe", lambda e: e.transpose(qb(bq)[:, 0:nq], pb_[0:nq, 128 * kt:128 * (kt + 1)], identb[0:nq, 0:nq]),
                         reads=pb_T + [cT], writes=qT(bq))
            for (jb, kt, bq) in qs:
                nq = jb["nq"]
                pt_, pt_T = ptbufs[jb["i"] % 4]
                copy(ev_eng(), pt_[:, kt, 0:nq], qb(bq)[:, 0:nq], qT(bq), pt_T)

        def stage_c(jbs):
            for jb in jbs:
                nq = jb["nq"]
                pt_, pt_T = ptbufs[jb["i"] % 4]
                if jb["hh"] == 0:
                    jb["pair"]["bq"] = quarter()
                bq = jb["pair"]["bq"]
                r0 = 64 * jb["hh"]
                for kt in range(2):
                    P.op("pe", lambda e: e.matmul(qf(bq)[r0:r0 + 64, 0:nq], jb["v"][kt], pt_[:, kt, 0:nq],
                                                  start=(kt == 0), stop=(kt == 1)),
                         reads=pt_T + jb["v_T"], writes=qT(bq), inc=(kt == 1))
                if jb["hh"] == 1:
                    copy("act", jb["o_dst"], qf(bq)[:, 0:nq], qT(bq), qT_T)

        def mk_jobs(j, nq, qcols, kfn, k_T, v, v_T, sbpair, bops):
            pair = {}
            return [dict(h=2 * j + hh, hh=hh, nq=nq, pair=pair,
                         q=qTb[64 * hh:64 * (hh + 1), j, qcols[0]:qcols[0] + nq], k=kfn(hh), k_T=k_T,
                         v=v, v_T=v_T, sb=sbpair[hh], bops=bops,
                         o_dst=qTb[:, j, qcols[0]:qcols[0] + nq]) for hh in range(2)]

        def push(jb):
            jb["i"] = len(jobs)
            jobs.append(jb)

        def add_pair(*a_):
            for jb in mk_jobs(*a_):
                push(jb)

        for t in range(nptile):
            first_tile = (gi == 0) and t == 0
            if first_tile:
                bops = [(0, 64, 128, 192, 128), (64, 128, 128, 256, 64)]
                sbp = sbufs[0:2]
            else:
                bops = [(0, 64, 0, 192, 0), (64, 128, 64, 256, 0)]
                sbp = sbufs[2:4]
            for j in range(DC):
                kvh = (2 * j) // 8
                add_pair(j, 128, (128 * t,),
                         lambda hh, kvh=kvh, t=t: KT[64 * hh:64 * (hh + 1), kvh, 128 * t:128 * t + 256], KT_T,
                         [Vb[:, t, 64 * kvh:64 * (kvh + 1)], Vb[:, t + 1, 64 * kvh:64 * (kvh + 1)]], Vb_T, sbp, bops)
        if has_sample:
            for j in range(DC):
                kvh = (2 * j) // 8
                for s0_ in (0, 2):
                    two = []
                    for s in (s0_, s0_ + 1):
                        bops = [(0, 32, 0, 128, 0), (0, 32, 128 + 32 * s, 160 + 32 * s, 128)]
                        two.append(mk_jobs(j, 32, (npr + 32 * s,),
                                           lambda hh, kvh=kvh, s=s: KTs[64 * hh:64 * (hh + 1), s, kvh, :], KTs_T,
                                           [Vc[:, s, 64 * kvh:64 * (kvh + 1)], Vb[:, 1 + nptile, 64 * kvh:64 * (kvh + 1)]],
                                           Vc_T + Vb_T, [ssb_s[s], ssb_s[s]], bops))
                    for hh in range(2):
                        push(two[0][hh]); push(two[1][hh])
        prs = [jobs[i_:i_ + 2] for i_ in range(0, len(jobs), 2)]
        npz = len(prs)
        for step in range(npz + 3):
            if step < npz:
                stage_a(prs[step])
            if 0 <= step - 1 < npz:
                stage_a2(prs[step - 1])
            if 0 <= step - 2 < npz:
                stage_b(prs[step - 2])
            if 0 <= step - 3 < npz:
                stage_c(prs[step - 3])
        if gi < 2:
            P.op("pool", lambda e: e.tensor_copy(out=ktc[:], in_=KT[:, :, npr:npr + 128]), reads=KT_T, writes=[ktc_T])
            P.op("pool", lambda e: e.tensor_copy(out=vbc[:], in_=Vb[:, nptile, :]), reads=Vb_T, writes=[vbc_T])
        set_pool(6)
        ssb = SSB
        pend = None
        for dch in range(DC):
            ws, wT = load_w(wao_d[dch])
            pb = [bank() for _ in blks]
            for bi, (c0, cn) in enumerate(blks):
                for kt in range(DC):
                    P.op("pe", lambda e: e.matmul(psb[pb[bi]][:, 0:cn], ws[:, kt, :], qTb[:, kt, c0:c0 + cn],
                                                  start=(kt == 0), stop=(kt == DC - 1)),
                         reads=[wT] + qT_T, writes=psT[pb[bi]], inc=(kt == DC - 1))
            if pend is not None:
                pend()
            pend = out_evac_ss(dch, N, pb, ssb, dch == 0, dch == DC - 1)
        pend()
        postnorm_add(l, 3, N, ssb, 1.0)

    def rwkv(gi, N, npr, has_sample):
        l = 1
        set_pool(6)
        prenorm(l, 2, N)
        blks = blocks(N)
        st = {"off": 0}

        def h_last(col, dst_ap):
            di = ctr["tmpf"] % 2; ctr["tmpf"] += 1
            so, so_T = tmpf[di], [tmpf_T[di]]
            P.op("dve", lambda e: e.scalar_tensor_tensor(out=so[:, 0:DC], in0=xT[:, :, col], scalar=rstd[:, col:col + 1], in1=gcol(l, 2),
                                                         op0=ALU.mult, op1=ALU.mult),
                 reads=xT_T + [rstd_T, consts_T], writes=so_T)
            P.dma("sp", dst_ap, so[:, 0:DC], reads=so_T, is_output=True)
        def buf(shape, dt):
            esz = 4 if dt == F32 else 2
            a_, t_ = bigv(st["off"], list(shape), dt)
            st["off"] = (st["off"] + int(np.prod(shape[1:])) * esz + 3) // 4 * 4
            return a_, t_

        if gi == 2:
            h_last(npr - 1, shp_d)
            for s_ in range(4):
                h_last(npr + 32 * s_ + 31, shs_d[:, s_, :])
        U = 4
        NB = N
        ygT, yg_T = buf([128, DC, NB], BF16)
        if has_sample:
            hsh, hsh_T = buf([128, DC, 128], BF16)
        lora, lora_T = buf([128, 4, NB], BF16)
        w2s, w2s_T = buf([128, 128], BF16); a2s, a2s_T = buf([128, 128], BF16); g2s, g2s_T = buf([128, 2, 128], BF16)
        rT_, rT_T = buf([128, NB], BF16); kT_, kT_T = buf([128, NB], BF16); vT_, vT_T = buf([128, NB], BF16)
        aT_, aT_T = buf([128, NB], BF16); kkn, kkn_T = buf([128, NB], BF16); ynT, ynT_T = buf([128, NB], BF16)
        lwT, lwT_T = buf([128, NB], F32)
        (wa, wa_T), (wb, wb_T) = buf([128, DC, 128], BF16), buf([128, DC, 128], BF16)

        def aux_views(flat_bf16, region_T, shapes):
            outs = []
            o_ = 0
            for shp in shapes:
                n_ = int(np.prod(shp[1:]))
                ap_ = flat_bf16[:, o_:o_ + n_]
                if len(shp) == 3:
                    ap_ = ap_.rearrange("p (a b) -> p a b", b=shp[2])
                t_ = T("aux")
                for rt_ in region_T:
                    for k_, ev_ in rt_.w.items():
                        if k_ not in t_.w or t_.w[k_][1] < ev_[1]:
                            t_.w[k_] = ev_
                    for k_, ev_ in rt_.r.items():
                        if k_ not in t_.r or t_.r[k_][1] < ev_[1]:
                            t_.r[k_] = ev_
                outs.append((ap_, [t_]))
                o_ += n_
            return outs

        def aux_release(region_T, subs):
            for rt_ in region_T:
                for _, tl in subs:
                    for t_ in tl:
                        for k_, ev_ in list(t_.w.items()) + list(t_.r.items()):
                            if k_ not in rt_.r or rt_.r[k_][1] < ev_[1]:
                                rt_.r[k_] = ev_

        d0 = dsl[:, 0:2].rearrange("p a b c -> p (a b c)")
        d1 = dsl[:, 2:4].rearrange("p a b c -> p (a b c)")
        r0 = rstd[:].bitcast(BF16)
        w2f = wslot[2][:].rearrange("p a b -> p (a b)")
        w3f = wslot[3][:].rearrange("p a b -> p (a b)")
        aux0 = aux_views(d0, dslot_T[0:2], [[128, U, 128]] * 5)
        aux1 = aux_views(d1, dslot_T[2:4], [[128, U, 128]] * 3 + [[128, U, 256]])
        aux2 = aux_views(r0, [rstd_T], [[128, U, 128]] * 3)
        aux3 = aux_views(w2f, [wslot_T[2]], [[128, U, 128]] * 4)
        aux4 = aux_views(w3f, [wslot_T[3]], [[128, U, 128]] * 4)
        (Dm, Dm_T), (Em, Em_T), (Mfull, Mfull_T) = aux0[0], aux0[1], aux0[2]
        (bdB, bdB_T), (bdK, bdK_T), (bdV, bdV_T) = aux2
        Qm2 = [buf([128, U, 128], BF16), buf([128, U, 128], BF16)]; P1b, P1b_T = buf([128, U, 128], BF16)
        Xb, Xb_T = buf([128, U, 256], BF16)
        Sb, Sb_T = buf([128, 128], BF16)
        e1, e1_T = buf([128, 256], BF16); e2, e2_T = buf([128, 256], BF16); e3, e3_T = buf([128, 256], BF16)
        SETS = []
        s0 = dict(tokV=aux1[0], tokK=aux1[1], tokB=aux1[2], bdR=aux1[3], ynbd=aux0[3])
        s0["ArbT"] = buf([128, U, 128], BF16); s0["ArkT"] = buf([128, U, 128], BF16); s0["Vbar"] = buf([128, U, 128], BF16)
        s0["AbT"] = buf([128, U, 128], BF16); s0["Ub"] = buf([128, U, 128], BF16); s0["gam"] = buf([128, 8], F32)
        s1 = dict(tokV=aux3[0], tokK=aux3[1], tokB=aux3[2], AbT=aux3[3], ArbT=aux4[0], ArkT=aux4[1], Vbar=aux4[2], Ub=aux4[3], ynbd=aux0[4])
        s1["bdR"] = buf([128, U, 256], BF16); s1["gam"] = buf([128, 8], F32)
        SETS = [s0, s1]
        if has_sample:
            Ss, Ss_T = buf([128, U, 128], F32)
        assert st["off"] <= BIGB, st["off"]
        def zero_bd(sets):
            zl = [(bdB, bdB_T), (bdK, bdK_T), (bdV, bdV_T)]
            for S_ in sets:
                zl += [S_["bdR"], S_["ynbd"]]
            for a_, t_ in zl:
                P.op("pool", lambda e: e.memset(a_, 0.0), writes=t_)

        if has_sample:
            di = ctr["tmpf"] % 2; ctr["tmpf"] += 1
            sh32, sh32_T = tmpf[di], tmpf_T[di]
            P.dma("sp", sh32[:, 0:4 * DC], sshift.rearrange("p s c -> p (s c)"), writes=[sh32_T])
            for s_ in range(4):
                P.op("dve", lambda e: e.tensor_copy(out=hsh[:, :, 32 * s_:32 * s_ + 1],
                                                    in_=sh32[:, s_ * DC:(s_ + 1) * DC].unsqueeze(2)),
                     reads=[sh32_T], writes=hsh_T)
                P.op("dve", lambda e: e.tensor_copy(out=hsh[:, :, 32 * s_ + 1:32 * s_ + 32],
                                                    in_=hT[:, :, 1 + npr + 32 * s_:1 + npr + 32 * s_ + 31]),
                     reads=hT_T, writes=hsh_T)

        def rhs_pairs(c0, cn):
            if has_sample and c0 >= npr:
                return (lambda kt: hT[:, kt, 1 + c0:1 + c0 + cn]), (lambda kt: hsh[:, kt, c0 - npr:c0 - npr + cn]), hsh_T
            return (lambda kt: hT[:, kt, 1 + c0:1 + c0 + cn]), (lambda kt: hT[:, kt, c0:c0 + cn]), []

        def priv(tiles):
            t_ = T("priv")
            for rt_ in tiles:
                for k_, ev_ in rt_.w.items():
                    if k_ not in t_.w or t_.w[k_][1] < ev_[1]:
                        t_.w[k_] = ev_
                for k_, ev_ in rt_.r.items():
                    if k_ not in t_.r or t_.r[k_][1] < ev_[1]:
                        t_.r[k_] = ev_
            return t_
        wa_h = [priv(wa_T), priv(wa_T)]
        wb_h = [priv(wb_T), priv(wb_T)]

        def mixed_linear(wdram_ap, n, M, evac):
            si = wctr[0] % 2; wctr[0] += 1
            ws, wT = wslot[si], wslot_T[si]
            P.dma("pool", ws[:, :, 0:M], wdram_ap, writes=[wT])
            for hf in range(2):
                k0, k1 = 8 * hf, 8 * hf + 8
                mu = consts[:, CO["mu"] + n * 16 + k0:CO["mu"] + n * 16 + k1].unsqueeze(2).broadcast_to([128, 8, M])
                P.op("dve", lambda e: e.tensor_tensor(out=wb[:, k0:k1, 0:M], in0=ws[:, k0:k1, 0:M], in1=mu, op=ALU.mult),
                     reads=[wT, consts_T], writes=[wb_h[hf]])
                P.op("dve", lambda e: e.tensor_tensor(out=wa[:, k0:k1, 0:M], in0=ws[:, k0:k1, 0:M], in1=wb[:, k0:k1, 0:M], op=ALU.subtract),
                     reads=[wT, wb_h[hf]], writes=[wa_h[hf]])
            for (c0, cn) in blks:
                cur, shf, extra = rhs_pairs(c0, cn)
                b = bank()
                for kt in range(DC):
                    P.op("pe", lambda e: e.matmul(psb[b][0:M, 0:cn], wa[:, kt, 0:M], cur(kt), start=(kt == 0), stop=False),
                         reads=[wa_h[kt // 8], hT_T[kt]], writes=psT[b], inc=False)
                    P.op("pe", lambda e: e.matmul(psb[b][0:M, 0:cn], wb[:, kt, 0:M], shf(kt), start=False, stop=(kt == DC - 1)),
                         reads=[wb_h[kt // 8], hT_T[kt]] + extra, writes=psT[b], inc=(kt % 8 == 7))
                evac(b, c0, cn)

        mixed_linear(w1_d, 1, 96, lambda b, c0, cn: P.op(
            "act", lambda e: e.activation(out=lora[0:96, 0, c0:c0 + cn], in_=psb[b][0:96, 0:cn], func=AF.Tanh),
            reads=psT[b], writes=lora_T))
        mixed_linear(a1_d, 4, 96, lambda b, c0, cn: P.op(
            "act", lambda e: e.copy(out=lora[0:96, 1, c0:c0 + cn], in_=psb[b][0:96, 0:cn]),
            reads=psT[b], writes=lora_T))
        for gc in range(2):
            mixed_linear(g1_d[gc], 5, 128, lambda b, c0, cn: P.op(
                "act", lambda e: e.activation(out=lora[:, 2 + gc, c0:c0 + cn], in_=psb[b][:, 0:cn], func=AF.Sigmoid),
                reads=psT[b], writes=lora_T))

        bcount = [0]

        def v3(ap_, TP, w0_, w1_):
            return ap_[0:TP, :, w0_:w1_]

        def prep_gen(j, cb, L, nu, S_, rezero):
            TP = 2 * L
            ncol = nu * L
            ntl = ncol // 128
            (tokV, tokV_T), (tokK, tokK_T), (tokB, tokB_T), (bdR, bdR_T) = S_["tokV"], S_["tokK"], S_["tokB"], S_["bdR"]
            (ArbT, ArbT_T), (ArkT, ArkT_T), (Vbar, Vbar_T), (AbT, AbT_T) = S_["ArbT"], S_["ArkT"], S_["Vbar"], S_["AbT"]
            gam, gam_T = S_["gam"]
            if rezero:
                zero_bd([S_])
            bcl = bank()
            for tl in range(ntl):
                c_ = cb + 128 * tl
                b_ = bank()
                P.op("pe", lambda e: e.transpose(psb[b_][:, 0:128], lwT[:, c_:c_ + 128], identf[:]), reads=lwT_T + [cT], writes=psT[b_][0:1])
                di = ctr["tmpf"] % 2; ctr["tmpf"] += 1
                P.op("act", lambda e: e.copy(out=tmpf[di][:, 0:128], in_=psb[b_][:, 0:128]), reads=psT[b_][0:1], writes=[tmpf_T[di]])
                P.op("pe", lambda e: e.matmul(psb[bcl][:, 128 * tl:128 * (tl + 1)], tmpf[di][:, 0:128], tri[L][:], start=True, stop=True),
                     reads=[tmpf_T[di], cT], writes=psT[bcl])
            cl = psb[bcl][:, 0:ncol]
            P.op("act", lambda e: e.activation(out=e1[:, 0:ncol], in_=cl, func=AF.Exp), reads=psT[bcl], writes=e1_T)
            P.op("act", lambda e: e.activation(out=e2[:, 0:ncol], in_=cl, func=AF.Exp, scale=-1.0), reads=psT[bcl], writes=e2_T)
            P.op("act", lambda e: e.activation(out=gam[:, 0:nu], in_=psb[bcl][:, 0:ncol].rearrange("p (u l) -> p u l", l=L)[:, :, L - 1],
                                               func=AF.Exp), reads=psT[bcl], writes=gam_T)
            di = ctr["tmpf"] % 2; ctr["tmpf"] += 1
            P.op("dve", lambda e: e.tensor_tensor(out=tmpf[di][:, 0:ncol], in0=cl, in1=lwT[:, cb:cb + ncol], op=ALU.subtract),
                 reads=psT[bcl] + lwT_T, writes=[tmpf_T[di]])
            P.op("act", lambda e: e.activation(out=e3[:, 0:ncol], in_=tmpf[di][:, 0:ncol], func=AF.Exp), reads=[tmpf_T[di]], writes=e3_T)
            yield
            def src(ap_, ps_):
                return ap_[ps_, cb:cb + ncol].rearrange("p (u l) -> p u l", l=L)

            def esrc(ap_, ps_):
                return ap_[ps_, 0:ncol].rearrange("p (u l) -> p u l", l=L)
            for hh in range(2):
                ps_ = slice(64 * hh, 64 * hh + 64)
                cs0, cs1 = L * hh, L * hh + L
                eng = "dve" if hh == 0 else "pool"
                P.op("dve", lambda e: e.scalar_tensor_tensor(out=bdR[ps_, 0:nu, cs0:cs1], in0=src(kkn, ps_), scalar=-1.0, in1=esrc(e3, ps_),
                                                           op0=ALU.mult, op1=ALU.mult),
                     reads=kkn_T + e3_T, writes=bdR_T)
                P.op(eng, lambda e: e.tensor_tensor(out=bdR[ps_, 0:nu, TP + cs0:TP + cs1], in0=src(rT_, ps_), in1=esrc(e1, ps_), op=ALU.mult),
                     reads=rT_T + e1_T, writes=bdR_T)
                P.op(eng, lambda e: e.tensor_tensor(out=bdK[ps_, 0:nu, cs0:cs1], in0=src(kT_, ps_), in1=esrc(e2, ps_), op=ALU.mult),
                     reads=kT_T + e2_T, writes=bdK_T)
                P.op(eng, lambda e: e.tensor_tensor(out=bdB[ps_, 0:nu, cs0:cs1], in0=src(kkn, ps_), in1=src(aT_, ps_), op=ALU.mult),
                     reads=kkn_T + aT_T, writes=bdB_T)
                P.op(eng, lambda e: e.tensor_tensor(out=bdB[ps_, 0:nu, cs0:cs1], in0=bdB[ps_, 0:nu, cs0:cs1], in1=esrc(e2, ps_), op=ALU.mult),
                     reads=bdB_T + e2_T, writes=bdB_T)
                P.op("act", lambda e: e.copy(out=bdV[ps_, 0:nu, cs0:cs1], in_=src(vT_, ps_)), reads=vT_T, writes=bdV_T)
            yield
            def tr_all(dst, dst_T, srcfn, src_T, rows_in, cols_in, eng):
                b_ = bank()
                pv = psb[b_][:, :].bitcast(BF16)
                for u in range(nu):
                    P.op("pe", lambda e: e.transpose(pv[0:cols_in, rows_in * u:rows_in * (u + 1)], srcfn(u), identb[0:rows_in, 0:rows_in]),
                         reads=src_T + [cT], writes=psT[b_], inc=(u == nu - 1))
                copy(eng, dst, pv[0:cols_in, 0:rows_in * nu].rearrange("p (u r) -> p u r", r=rows_in), psT[b_], dst_T)
            tr_all(tokV[0:TP, 0:nu, :], tokV_T, lambda u: bdV[:, u, 0:TP], bdV_T, 128, TP, "act")
            tr_all(tokK[0:TP, 0:nu, :], tokK_T, lambda u: bdK[:, u, 0:TP], bdK_T, 128, TP, "dve")
            tr_all(tokB[0:TP, 0:nu, :], tokB_T, lambda u: bdB[:, u, 0:TP], bdB_T, 128, TP, "act")
            tr_all(Xb[0:TP, 0:nu, 0:128], Xb_T, lambda u: bdR[:, u, 0:TP], bdR_T, 128, TP, "dve")
            yield
            upb = 512 // (2 * TP)
            for lhs, lhs_T, outs in ((bdB, bdB_T, ((Mfull, Mfull_T, None), (ArbT, ArbT_T, mincl[L]))),
                                     (bdK, bdK_T, ((P1b, P1b_T, mstrict[L]), (ArkT, ArkT_T, mincl[L])))):
                for u0 in range(0, nu, upb):
                    b_ = bank()
                    n_ = min(upb, nu - u0)
                    for u in range(u0, u0 + n_):
                        o_ = (u - u0) * 2 * TP
                        P.op("pe", lambda e: e.matmul(psb[b_][0:TP, o_:o_ + 2 * TP], lhs[:, u, 0:TP], bdR[:, u, 0:2 * TP], start=True, stop=True),
                             reads=lhs_T + bdR_T, writes=psT[b_], inc=(u == u0 + n_ - 1))
                    pv = psb[b_][0:TP, 0:n_ * 2 * TP].rearrange("p (u c) -> p u c", c=2 * TP)
                    for part, (dst, dst_T, msk) in enumerate(outs):
                        sv = pv[:, :, part * TP:(part + 1) * TP]
                        dv = dst[0:TP, u0:u0 + n_, 0:TP]
                        if msk is None:
                            copy("act", dv, sv, psT[b_], dst_T)
                        else:
                            P.op("dve", lambda e: e.tensor_tensor(out=dv, in0=sv, in1=msk[0:TP, 0:TP].unsqueeze(1).broadcast_to([TP, n_, TP]), op=ALU.mult),
                                 reads=psT[b_] + [cT], writes=dst_T)
            yield
            b_ = bank()
            for u in range(nu):
                P.op("pe", lambda e: e.matmul(psb[b_][0:TP, 128 * u:128 * (u + 1)], P1b[0:TP, u, 0:TP], tokV[0:TP, u, :], start=True, stop=True),
                     reads=P1b_T + tokV_T, writes=psT[b_], inc=(u == nu - 1))
            copy("act", Xb[0:TP, 0:nu, 128:256], psb[b_][0:TP, 0:128 * nu].rearrange("p (u c) -> p u c", c=128), psT[b_], Xb_T)
            idb = identb[0:TP, 0:TP].unsqueeze(1).broadcast_to([TP, nu, TP])
            yield
            P.op("act", lambda e: e.copy(out=Dm[0:TP, 0:nu, 0:TP], in_=idb), reads=[cT], writes=Dm_T)
            P.op("pool", lambda e: e.tensor_copy(out=Em[0:TP, 0:nu, 0:TP], in_=idb), reads=[cT], writes=Em_T)
            for lv in range(len(lvm[L])):
                Qm, Qm_T = Qm2[lv % 2]
                P.op("pool", lambda e: e.tensor_tensor(out=Qm[0:TP, 0:nu, 0:TP], in0=Mfull[0:TP, 0:nu, 0:TP],
                                                       in1=lvm[L][lv][0:TP, 0:TP].unsqueeze(1).broadcast_to([TP, nu, TP]), op=ALU.mult),
                     reads=Mfull_T + [cT], writes=Qm_T)
                b1_ = bank()
                for u in range(nu):
                    P.op("pe", lambda e: e.matmul(psb[b1_][0:TP, 128 * u:128 * u + TP], Qm[0:TP, u, 0:TP], Dm[0:TP, u, 0:TP], start=True, stop=True),
                         reads=Qm_T + Dm_T, writes=psT[b1_], inc=(u == nu - 1))
                copy("act", P1b[0:TP, 0:nu, 0:TP], psb[b1_][0:TP, 0:128 * nu].rearrange("p (u c) -> p u c", c=128)[:, :, 0:TP], psT[b1_], P1b_T)
                yield
                b2_ = bank()
                for u in range(nu):
                    P.op("pe", lambda e: e.matmul(psb[b2_][0:TP, 128 * u:128 * u + TP], Em[0:TP, u, 0:TP], P1b[0:TP, u, 0:TP], start=True, stop=True),
                         reads=Em_T + P1b_T, writes=psT[b2_], inc=(u == nu - 1))
                P.op("dve", lambda e: e.tensor_tensor(out=Dm[0:TP, 0:nu, 0:TP], in0=Dm[0:TP, 0:nu, 0:TP],
                                                      in1=psb[b2_][0:TP, 0:128 * nu].rearrange("p (u c) -> p u c", c=128)[:, :, 0:TP], op=ALU.add),
                     reads=Dm_T + psT[b2_], writes=Dm_T)
                yield
                tr_all(Em[0:TP, 0:nu, 0:TP], Em_T, lambda u: Dm[0:TP, u, 0:TP], Dm_T, TP, TP, "act")
                yield
            Ec, Ec_T = Em, Em_T
            for u0 in range(0, nu, 2):
                b_ = bank()
                n_ = min(2, nu - u0)
                for u in range(u0, u0 + n_):
                    P.op("pe", lambda e: e.matmul(psb[b_][0:TP, 256 * (u - u0):256 * (u - u0 + 1)], Ec[0:TP, u, 0:TP], Xb[0:TP, u, :], start=True, stop=True),
                         reads=Ec_T + Xb_T, writes=psT[b_], inc=(u == u0 + n_ - 1))
                pv = psb[b_][0:TP, 0:256 * n_].rearrange("p (u c) -> p u c", c=256)
                copy("act", Vbar[0:TP, u0:u0 + n_, :], pv[:, :, 128:256], psT[b_], Vbar_T)
                copy("dve", Xb[0:TP, u0:u0 + n_, 0:128], pv[:, :, 0:128], psT[b_], Xb_T)
            yield
            tr_all(AbT[:, 0:nu, 0:TP], AbT_T, lambda u: Xb[0:TP, u, 0:128], Xb_T, TP, 128, "act")
            yield

        def scan_gen(j, cb, L, nu, S_, states):
            TP = 2 * L
            ncol = nu * L
            (tokV, tokV_T), (tokK, tokK_T), (tokB, tokB_T), (bdR, bdR_T) = S_["tokV"], S_["tokK"], S_["tokB"], S_["bdR"]
            (ArbT, ArbT_T), (ArkT, ArkT_T), (Vbar, Vbar_T), (AbT, AbT_T) = S_["ArbT"], S_["ArkT"], S_["Vbar"], S_["AbT"]
            (Ub, Ub_T), (gam, gam_T), (ynbd, ynbd_T) = S_["Ub"], S_["gam"], S_["ynbd"]
            bcount[0] += 1
            by = SSB[bcount[0] % 2]
            for u in range(nu):
                Sm_ap, Sm_T = states[u]
                P.op("act", lambda e: e.copy(out=Sb[:, :], in_=Sm_ap), reads=Sm_T, writes=Sb_T)
                bu = bank()
                P.op("pe", lambda e: e.matmul(psb[bu][0:TP, 0:128], AbT[:, u, 0:TP], Sb[:, :], start=True, stop=True),
                     reads=AbT_T + Sb_T, writes=psT[bu])
                P.op("dve", lambda e: e.tensor_tensor(out=Ub[0:TP, u, :], in0=psb[bu][0:TP, 0:128], in1=Vbar[0:TP, u, :], op=ALU.add),
                     reads=psT[bu] + Vbar_T, writes=Ub_T)
                yield
                yo = psb[by][0:TP, 128 * u:128 * (u + 1)]
                P.op("pe", lambda e: e.matmul(yo, bdR[:, u, TP:2 * TP], Sb[:, :], start=True, stop=False),
                     reads=bdR_T + Sb_T, writes=psT[by], inc=False)
                P.op("pe", lambda e: e.matmul(yo, ArkT[0:TP, u, 0:TP], tokV[0:TP, u, :], start=False, stop=False),
                     reads=ArkT_T + tokV_T, writes=psT[by], inc=False)
                P.op("pe", lambda e: e.matmul(yo, ArbT[0:TP, u, 0:TP], Ub[0:TP, u, :], start=False, stop=True),
                     reads=ArbT_T + Ub_T, writes=psT[by])
                bs = bank()
                P.op("pe", lambda e: e.matmul(psb[bs][:, 0:128], tokK[0:TP, u, :], tokV[0:TP, u, :], start=True, stop=False),
                     reads=tokK_T + tokV_T, writes=psT[bs], inc=False)
                P.op("pe", lambda e: e.matmul(psb[bs][:, 0:128], tokB[0:TP, u, :], Ub[0:TP, u, :], start=False, stop=True),
                     reads=tokB_T + Ub_T, writes=psT[bs])
                di = ctr["tmpf"] % 2; ctr["tmpf"] += 1
                P.op("act", lambda e: e.activation(out=tmpf[di][:, 0:128], in_=psb[bs][:, 0:128], func=AF.Copy, scale=gam[:, u:u + 1]),
                     reads=psT[bs] + gam_T, writes=[tmpf_T[di]])
                P.op("dve", lambda e: e.scalar_tensor_tensor(out=Sm_ap, in0=Sm_ap, scalar=gam[:, u:u + 1], in1=tmpf[di][:, 0:128],
                                                             op0=ALU.mult, op1=ALU.add),
                     reads=Sm_T + gam_T + [tmpf_T[di]], writes=Sm_T)
                yield
            g = bcount[0] % 4
            sm = small[:, 16 * g:16 * g + 16]; sT = [small_T[g]]
            Y3 = psb[by][0:TP, 0:128 * nu].rearrange("p (u c) -> p u c", c=128)
            yield
            P.op("dve", lambda e: e.reduce_sum(out=sm[0:TP, 0:nu], in_=Y3, axis=AX.X), reads=psT[by], writes=sT)
            di = ctr["tmpf"] % 2; ctr["tmpf"] += 1
            sqv = tmpf[di][0:TP, 0:128 * nu].rearrange("p (u c) -> p u c", c=128)
            P.op("act", lambda e: e.activation(out=sqv, in_=Y3, func=AF.Square), reads=psT[by], writes=[tmpf_T[di]])
            P.op("dve", lambda e: e.reduce_sum(out=sm[0:TP, 4:4 + nu], in_=sqv, axis=AX.X), reads=[tmpf_T[di]], writes=sT)
            P.op("dve", lambda e: e.tensor_scalar(out=sm[0:TP, 0:nu], in0=sm[0:TP, 0:nu], scalar1=1.0 / 64, scalar2=None, op0=ALU.mult),
                 reads=sT, writes=sT)
            P.op("dve", lambda e: e.tensor_tensor(out=sm[0:TP, 8:8 + nu], in0=sm[0:TP, 0:nu], in1=sm[0:TP, 0:nu], op=ALU.mult),
                 reads=sT, writes=sT)
            P.op("dve", lambda e: e.scalar_tensor_tensor(out=sm[0:TP, 4:4 + nu], in0=sm[0:TP, 4:4 + nu], scalar=1.0 / 64, in1=sm[0:TP, 8:8 + nu],
                                                         op0=ALU.mult, op1=ALU.subtract),
                 reads=sT, writes=sT)
            P.op("dve", lambda e: e.tensor_scalar(out=sm[0:TP, 4:4 + nu], in0=sm[0:TP, 4:4 + nu], scalar1=GN_EPS, scalar2=None, op0=ALU.add),
                 reads=sT, writes=sT)
            P.op("act", lambda e: e.activation(out=sm[0:TP, 4:4 + nu], in_=sm[0:TP, 4:4 + nu], func=AF.Ln), reads=sT, writes=sT)
            P.op("act", lambda e: e.activation(out=sm[0:TP, 4:4 + nu], in_=sm[0:TP, 4:4 + nu], func=AF.Exp, scale=-0.5), reads=sT, writes=sT)
            for hh in range(2):
                rs = slice(L * hh, L * hh + L)
                cs = slice(64 * hh, 64 * hh + 64)
                P.op("dve", lambda e: e.tensor_tensor(out=ynbd[rs, 0:nu, cs], in0=Y3[rs, :, cs],
                                                      in1=sm[rs, 0:nu].unsqueeze(2).broadcast_to([L, nu, 64]), op=ALU.subtract),
                     reads=psT[by] + sT, writes=ynbd_T)
                P.op("dve", lambda e: e.tensor_tensor(out=ynbd[rs, 0:nu, cs], in0=ynbd[rs, 0:nu, cs],
                                                      in1=sm[rs, 4:4 + nu].unsqueeze(2).broadcast_to([L, nu, 64]), op=ALU.mult),
                     reads=ynbd_T + sT, writes=ynbd_T)
            yield
            b_ = bank()
            pv = psb[b_][:, :].bitcast(BF16)
            for u in range(nu):
                P.op("pe", lambda e: e.transpose(pv[:, TP * u:TP * (u + 1)], ynbd[0:TP, u, :], identb[0:TP, 0:TP]),
                     reads=ynbd_T + [cT], writes=psT[b_], inc=(u == nu - 1))
            for hh in range(2):
                ps_ = slice(64 * hh, 64 * hh + 64)
                P.op("act", lambda e: e.copy(out=ynT[ps_, cb:cb + ncol].rearrange("p (u l) -> p u l", l=L),
                                             in_=pv[ps_, 0:TP * nu].rearrange("p (u c) -> p u c", c=TP)[:, :, L * hh:L * hh + L]),
                     reads=psT[b_], writes=ynT_T)

        def state_out(src_ap, src_T, dst):
            b_ = bank()
            P.op("pe", lambda e: e.transpose(psb[b_][:, 0:128], src_ap, identf[:]), reads=src_T + [cT], writes=psT[b_])
            di = ctr["tmpf"] % 2; ctr["tmpf"] += 1
            so, so_T = tmpf[di], [tmpf_T[di]]
            for hh in range(2):
                ps_ = slice(64 * hh, 64 * hh + 64)
                P.op("act", lambda e: e.copy(out=so[ps_, 0:64], in_=psb[b_][ps_, 64 * hh:64 * hh + 64]), reads=psT[b_], writes=so_T)
            P.dma("sp", dst, so[:, 0:64], reads=so_T, is_output=True)

        for j in range(DC):
            P.dma("pool", w2s[0:96, :], w2_d[:, 128 * j:128 * (j + 1)], writes=w2s_T, semt=w2s_T[0])
            P.dma("pool", a2s[0:96, :], a2_d[:, 128 * j:128 * (j + 1)], writes=a2s_T, semt=a2s_T[0])
            P.dma("pool", g2s[:, :, :], g2_d[:, :, 128 * j:128 * (j + 1)], writes=g2s_T, semt=g2s_T[0])
            for (wdram, n, dst, dst_T) in ((wr_d, 0, rT_, rT_T), (wk_d, 2, kT_, kT_T), (wv_d, 3, vT_, vT_T)):
                mixed_linear(wdram[j], n, 128, lambda b, c0, cn, dst=dst, dst_T=dst_T: copy(
                    ev_eng(), dst[:, c0:c0 + cn], psb[b][:, 0:cn], psT[b], dst_T))
            for (c0, cn) in blks:
                b = bank()
                P.op("pe", lambda e: e.matmul(psb[b][:, 0:cn], w2s[0:96, :], lora[0:96, 0, c0:c0 + cn], start=True, stop=True),
                     reads=w2s_T + lora_T, writes=psT[b])
                P.op("act", lambda e: e.activation(out=lwT[:, c0:c0 + cn], in_=psb[b][:, 0:cn], func=AF.Exp, bias=cc("w0", j), scale=1.0),
                     reads=psT[b] + [consts_T], writes=lwT_T)
                ts = ctr["tmpf"] % 2; ctr["tmpf"] += 1
                tf, tf_T = tmpf[ts], [tmpf_T[ts]]
                P.op("dve", lambda e: e.tensor_scalar(out=tf[:, 0:cn], in0=lwT[:, c0:c0 + cn], scalar1=1.0, scalar2=None, op0=ALU.add),
                     reads=lwT_T, writes=tf_T)
                P.op("dve", lambda e: e.reciprocal(out=tf[:, 0:cn], in_=tf[:, 0:cn]), reads=tf_T, writes=tf_T)
                P.op("dve", lambda e: e.scalar_tensor_tensor(out=lwT[:, c0:c0 + cn], in0=lwT[:, c0:c0 + cn], scalar=-math.exp(-0.5), in1=tf[:, 0:cn],
                                                             op0=ALU.mult, op1=ALU.mult),
                     reads=lwT_T + tf_T, writes=lwT_T)
                b = bank()
                P.op("pe", lambda e: e.matmul(psb[b][:, 0:cn], a2s[0:96, :], lora[0:96, 1, c0:c0 + cn], start=True, stop=True),
                     reads=a2s_T + lora_T, writes=psT[b])
                P.op("act", lambda e: e.activation(out=aT_[:, c0:c0 + cn], in_=psb[b][:, 0:cn], func=AF.Sigmoid, bias=cc("a0", j), scale=1.0),
                     reads=psT[b] + [consts_T], writes=aT_T)
            P.op("dve", lambda e: e.tensor_scalar(out=kkn[:, 0:N], in0=kT_[:, 0:N], scalar1=cc("kk", j), scalar2=None, op0=ALU.mult),
                 reads=kT_T + [consts_T], writes=kkn_T)
            s = ctr["sq"] % 2; ctr["sq"] += 1
            P.op("act", lambda e: e.activation(out=sq[s][:, 0:N], in_=kkn[:, 0:N], func=AF.Square), reads=kkn_T, writes=[sq_T[s]])
            for (c0, cn) in blks:
                b = bank()
                P.op("pe", lambda e: e.matmul(psb[b][:, 0:cn], bdones[:], sq[s][:, c0:c0 + cn], start=True, stop=True),
                     reads=[sq_T[s], cT], writes=psT[b])
                ts = ctr["tmpf"] % 2; ctr["tmpf"] += 1
                tf, tf_T = tmpf[ts], [tmpf_T[ts]]
                P.op("dve", lambda e: e.tensor_scalar(out=tf[:, 0:cn], in0=psb[b][:, 0:cn], scalar1=1e-30, scalar2=None, op0=ALU.add),
                     reads=psT[b], writes=tf_T)
                P.op("act", lambda e: e.activation(out=tf[:, 0:cn], in_=tf[:, 0:cn], func=AF.Ln), reads=tf_T, writes=tf_T)
                P.op("act", lambda e: e.activation(out=tf[:, 0:cn], in_=tf[:, 0:cn], func=AF.Exp, scale=-0.5), reads=tf_T, writes=tf_T)
                P.op("dve", lambda e: e.tensor_tensor(out=kkn[:, c0:c0 + cn], in0=kkn[:, c0:c0 + cn], in1=tf[:, 0:cn], op=ALU.mult),
                     reads=kkn_T + tf_T, writes=kkn_T)
                P.op("dve", lambda e: e.tensor_scalar(out=tf[:, 0:cn], in0=aT_[:, c0:c0 + cn], scalar1=-1.0, scalar2=cc("ka", j), op0=ALU.add, op1=ALU.mult),
                     reads=aT_T + [consts_T], writes=tf_T)
                P.op("dve", lambda e: e.tensor_scalar(out=tf[:, 0:cn], in0=tf[:, 0:cn], scalar1=1.0, scalar2=None, op0=ALU.add),
                     reads=tf_T, writes=tf_T)
                P.op("dve", lambda e: e.tensor_tensor(out=kT_[:, c0:c0 + cn], in0=kT_[:, c0:c0 + cn], in1=tf[:, 0:cn], op=ALU.mult),
                     reads=kT_T + tf_T, writes=kT_T)
            zero_bd(SETS)
            blist = [(256 * bi_, 64, [(Smast[:, j, :], [Smast_T[j]])] * 4, False) for bi_ in range(npr // 256)]
            if has_sample:
                for u in range(4):
                    di = ctr["tmpf"] % 2; ctr["tmpf"] += 1
                    si_, si_T = tmpf[di], [tmpf_T[di]]
                    P.dma("sp", si_[:, 256:320], swkv[u, j], writes=si_T)
                    P.op("dve", lambda e: e.memset(si_[:, 0:128], 0.0), writes=si_T)
                    for hh in range(2):
                        ps_ = slice(64 * hh, 64 * hh + 64)
                        P.op("dve", lambda e: e.tensor_copy(out=si_[ps_, 64 * hh:64 * hh + 64], in_=si_[ps_, 256:320]), reads=si_T, writes=si_T)
                    b_ = bank()
                    P.op("pe", lambda e: e.transpose(psb[b_][:, 0:128], si_[:, 0:128], identf[:]), reads=si_T + [cT], writes=psT[b_])
                    P.op("act", lambda e: e.copy(out=Ss[:, u, :], in_=psb[b_][:, 0:128]), reads=psT[b_], writes=Ss_T)
                blist.append((npr, 32, [(Ss[:, u, :], Ss_T) for u in range(4)], True))

            def drive(gens):
                gens = [g_ for g_ in gens if g_ is not None]
                while gens:
                    for g_ in list(gens):
                        try:
                            next(g_)
                        except StopIteration:
                            gens.remove(g_)
            drive([prep_gen(j, blist[0][0], blist[0][1], 4, SETS[0], blist[0][3])])
            for bi_, (cb_, L_, states_, rz_) in enumerate(blist):
                nxt = None
                if bi_ + 1 < len(blist):
                    n_ = blist[bi_ + 1]
                    nxt = prep_gen(j, n_[0], n_[1], 4, SETS[(bi_ + 1) % 2], n_[3])
                drive([scan_gen(j, cb_, L_, 4, SETS[bi_ % 2], states_), nxt])
            if has_sample:
                for u in range(4):
                    state_out(Ss[:, u, :], Ss_T, wkvs_d[u, j])
            if gi == 2:
                state_out(Smast[:, j, :], [Smast_T[j]], wkvp_d[j])
            s = ctr["sq"] % 2; ctr["sq"] += 1
            P.op("dve", lambda e: e.scalar_tensor_tensor(out=sq[s][:, 0:N], in0=rT_[:, 0:N], scalar=cc("rk", j), in1=kT_[:, 0:N],
                                                         op0=ALU.mult, op1=ALU.mult),
                 reads=rT_T + kT_T + [consts_T], writes=[sq_T[s]])
            for (c0, cn) in blks:
                b = bank()
                P.op("pe", lambda e: e.matmul(psb[b][:, 0:cn], bdones[:], sq[s][:, c0:c0 + cn], start=True, stop=True),
                     reads=[sq_T[s], cT], writes=psT[b])
                ts = ctr["tmpf"] % 2; ctr["tmpf"] += 1
                tf, tf_T = tmpf[ts], [tmpf_T[ts]]
                P.op("dve", lambda e: e.tensor_tensor(out=tf[:, 0:cn], in0=psb[b][:, 0:cn], in1=vT_[:, c0:c0 + cn], op=ALU.mult),
                     reads=psT[b] + vT_T, writes=tf_T)
                ts2 = ctr["tmpf"] % 2; ctr["tmpf"] += 1
                tg, tg_T = tmpf[ts2], [tmpf_T[ts2]]
                P.op("dve", lambda e: e.tensor_scalar(out=tg[:, 0:cn], in0=ynT[:, c0:c0 + cn], scalar1=cc("lnw", j), scalar2=cc("lnb", j),
                                                      op0=ALU.mult, op1=ALU.add),
                     reads=ynT_T + [consts_T], writes=tg_T)
                P.op("dve", lambda e: e.tensor_tensor(out=tf[:, 0:cn], in0=tf[:, 0:cn], in1=tg[:, 0:cn], op=ALU.add),
                     reads=tf_T + tg_T, writes=tf_T)
                bg = bank()
                for kt in range(2):
                    P.op("pe", lambda e: e.matmul(psb[bg][:, 0:cn], g2s[:, kt, :], lora[:, 2 + kt, c0:c0 + cn],
                                                  start=(kt == 0), stop=(kt == 1)),
                         reads=g2s_T + lora_T, writes=psT[bg], inc=(kt == 1))
                P.op("dve", lambda e: e.tensor_tensor(out=ygT[:, j, c0:c0 + cn], in0=tf[:, 0:cn], in1=psb[bg][:, 0:cn], op=ALU.mult),
                     reads=tf_T + psT[bg], writes=yg_T)

        aux_release(dslot_T[0:2], aux0); aux_release(dslot_T[2:4], aux1); aux_release([rstd_T], aux2)
        aux_release([wslot_T[2]], aux3); aux_release([wslot_T[3]], aux4)
        aux_release(wa_T, [(None, wa_h)]); aux_release(wb_T, [(None, wb_h)])
        return ygT, yg_T

    def rwkv_full(gi, N, npr, has_sample):
        ygT, yg_T = rwkv(gi, N, npr, has_sample)
        if dbg == "yg":
            for c in range(DC):
                P.op("act", lambda e: e.copy(out=xT[:, c, 0:N], in_=ygT[:, c, 0:N]), reads=yg_T, writes=[xT_T[c]])
            return
        P.op("pool", lambda e: e.tensor_copy(out=hT[:, :, 0:1], in_=hT[:, :, npr:npr + 1]), reads=hT_T, writes=hT_T)
        blks = blocks(N)
        set_pool(6)
        ssb = SSB
        pend = None
        for dch in range(DC):
            ws, wT = load_w(wro_d[dch])
            pb = [bank() for _ in blks]
            for bi, (c0, cn) in enumerate(blks):
                for kt in range(DC):
                    P.op("pe", lambda e: e.matmul(psb[pb[bi]][:, 0:cn], ws[:, kt, :], ygT[:, kt, c0:c0 + cn],
                                                  start=(kt == 0), stop=(kt == DC - 1)),
                         reads=[wT] + yg_T, writes=psT[pb[bi]], inc=(kt == DC - 1))
            if pend is not None:
                pend()
            pend = out_evac_ss(dch, N, pb, ssb, dch == 0, dch == DC - 1)
        pend()
        postnorm_add(1, 3, N, ssb, 1.0)

    for gi, (p0, npr, has_s) in enumerate(GROUPS[:ngroups]):
        N = npr + (128 if has_s else 0)
        for c in range(DC):
            P.dma("sp", xT[:, c, 0:npr], xp[:, c, p0:p0 + npr], writes=[xT_T[c]])
            if has_s:
                P.dma("sp", xT[:, c, npr:npr + 128], xs[:, c, :], writes=[xT_T[c]])
        for l in range(nlayers):
            ffn(l, 0, N)
            if dbg == f"ffn{l}0" and gi == 0:
                break
            if l == 0:
                attention(gi, N, npr, has_s)
            else:
                rwkv_full(gi, N, npr, has_s)
            if dbg in (f"mix{l}", "yg") and gi == 0 and (dbg != "yg" or l == 1):
                break
            ffn(l, 1, N)
        for c in range(DC):
            P.dma("sp", yT_d[:, c, p0:p0 + npr], xT[:, c, 0:npr], reads=[xT_T[c]], is_output=True)
            if has_s:
                P.dma("sp", yT_d[:, c, SEQ:SEQ + 128], xT[:, c, npr:npr + 128], reads=[xT_T[c]], is_output=True)
    P.finish()
    stats = dict(ops=P.n_ops, waits=P.n_waits, sems=P.nsem, cnt=dict(P.cnt))
    P.close()
    return nc, stats


def prep_shared(inp):
    f = lambda a: np.ascontiguousarray(np.asarray(a, dtype=np.float32))
    sh = {}
    sh["wg"] = np.stack([np.stack([w_chunks(f(inp["ffn_w_gate"][l, s])) for s in range(2)]) for l in range(2)])
    sh["wu"] = np.stack([np.stack([w_chunks(f(inp["ffn_w_up"][l, s])) for s in range(2)]) for l in range(2)])
    wd = np.stack([np.stack([w_chunks(f(inp["ffn_w_down"][l, s])) for s in range(2)]) for l in range(2)])
    sh["wd"] = np.ascontiguousarray(wd.reshape(2, 2, DC, 128, 4, 11, 128).transpose(0, 1, 2, 4, 3, 5, 6))
    wqkv = f(inp["att_w_qkv"][0])
    bqkv = f(inp["att_b_qkv"][0])
    kcols = [np.concatenate([wqkv[:, 2048 + 64 * h:2048 + 64 * (h + 1)]] * 2, axis=1) for h in range(4)]
    wext = np.concatenate([wqkv[:, :2048]] + kcols, axis=1)
    sh["wqkv"] = w_chunks(wext)
    bext = np.concatenate([bqkv[:2048]] + [np.concatenate([bqkv[2048 + 64 * h:2048 + 64 * (h + 1)]] * 2) for h in range(4)])
    sh["wkvt"] = w_chunks(wqkv[:, 2048:2560])
    sh["bkv"] = bqkv[2048:2560].reshape(1, 512).copy()
    sh["sinks"] = f(inp["att_sinks"]).reshape(1, 32).copy()
    sh["table"] = f(inp["rel_table"])
    sh["wao"] = w_chunks(f(inp["att_w_o"][0]))
    sh["wr"] = w_chunks(f(inp["rwkv_w_r"][0]))
    sh["wk"] = w_chunks(f(inp["rwkv_w_k"][0]))
    sh["wv"] = w_chunks(f(inp["rwkv_w_v"][0]))
    sh["wro"] = w_chunks(f(inp["rwkv_w_o"][0]))
    sh["w1"] = w_chunks(f(inp["rwkv_w1"][0]), 96)[0]
    sh["a1"] = w_chunks(f(inp["rwkv_a1"][0]), 96)[0]
    sh["g1"] = w_chunks(f(inp["rwkv_g1"][0]))
    sh["w2"] = f(inp["rwkv_w2"][0])
    sh["a2"] = f(inp["rwkv_a2"][0])
    sh["g2"] = np.ascontiguousarray(f(inp["rwkv_g2"][0]).reshape(2, 128, D).transpose(1, 0, 2))
    cols = [fcol(f(inp["norm_g"])).reshape(128, 12 * 16),
            bext.reshape(20, 128).T,
            fcol(f(inp["rwkv_mu"][0])).reshape(128, 6 * 16)]
    for nm in ("rwkv_w0", "rwkv_a0", "rwkv_k_k", "rwkv_k_a"):
        cols.append(fcol(f(inp[nm][0])))
    cols.append(fcol(f(inp["rwkv_r_k"][0]).reshape(D)))
    for nm in ("rwkv_ln_w", "rwkv_ln_b"):
        cols.append(fcol(f(inp[nm][0])))
    sh["consts"] = np.ascontiguousarray(np.concatenate(cols, axis=1))
    for k, v in static_consts().items():
        sh["c_" + k] = v
    return sh


def prep_core(inp, c):
    f = lambda a: np.ascontiguousarray(np.asarray(a, dtype=np.float32))
    m = {}
    m["xp"] = fcol(f(inp["x_prompt"][c]).T.copy()) if False else np.ascontiguousarray(
        f(inp["x_prompt"][c]).T.reshape(DC, 128, SEQ).transpose(1, 0, 2))
    xs = f(inp["x_sample"][4 * c:4 * c + 4]).reshape(128, D)
    m["xs"] = np.ascontiguousarray(xs.T.reshape(DC, 128, 128).transpose(1, 0, 2))
    m["ck"] = f(inp["cache_k"][0, 4 * c:4 * c + 4]).reshape(4, 128, 256)
    m["cv"] = f(inp["cache_v"][0, 4 * c:4 * c + 4]).reshape(4, 128, 256)
    ss = f(inp["state_shift"][0, 4 * c:4 * c + 4, 0])
    m["sshift"] = np.ascontiguousarray(ss.reshape(4, DC, 128).transpose(2, 0, 1))
    m["swkv"] = f(inp["state_wkv"][0, 4 * c:4 * c + 4]).reshape(4, 16, 128, 64)
    return m


_CACHE = {}


def kernel(**inputs):
    if "nc" not in _CACHE:
        _CACHE["nc"] = build()[0]
    nc = _CACHE["nc"]
    sh = prep_shared(inputs)
    in_maps = []
    for c in range(NCORE):
        m = dict(sh)
        m.update(prep_core(inputs, c))
        in_maps.append(m)
    res = run_bass_kernel_spmd(nc, in_maps, core_ids=list(range(NCORE)))
    R = res.results
    y_prompt = np.zeros((8, SEQ, D), np.float32)
    y_sample = np.zeros((32, 32, D), np.float32)
    k_prompt = np.zeros((1, 8, 128, 4, 64), np.float32)
    v_prompt = np.zeros((1, 8, 128, 4, 64), np.float32)
    k_sample = np.zeros((1, 32, 32, 4, 64), np.float32)
    v_sample = np.zeros((1, 32, 32, 4, 64), np.float32)
    shift_prompt = np.zeros((1, 8, 1, D), np.float32)
    wkv_prompt = np.zeros((1, 8, 32, 64, 64), np.float32)
    shift_sample = np.zeros((1, 32, 1, D), np.float32)
    wkv_sample = np.zeros((1, 32, 32, 64, 64), np.float32)
    for c in range(NCORE):
        r = R[c]
        yT = np.asarray(r["yT"])
        y = yT.transpose(2, 1, 0).reshape(SEQ + 128, D)
        y_prompt[c] = y[:SEQ]
        y_sample[4 * c:4 * c + 4] = y[SEQ:].reshape(4, 32, D)
        k_prompt[0, c] = np.asarray(r["kp"]).reshape(128, 4, 64)
        v_prompt[0, c] = np.asarray(r["vp"]).reshape(128, 4, 64)
        k_sample[0, 4 * c:4 * c + 4] = np.asarray(r["ks"]).reshape(4, 32, 4, 64)
        v_sample[0, 4 * c:4 * c + 4] = np.asarray(r["vs"]).reshape(4, 32, 4, 64)
        shift_prompt[0, c, 0] = np.asarray(r["shp"]).T.reshape(D)
        shift_sample[0, 4 * c:4 * c + 4, 0] = np.asarray(r["shs"]).transpose(1, 2, 0).reshape(4, D)
        wkv_prompt[0, c] = np.asarray(r["wkvp"]).reshape(32, 64, 64)
        wkv_sample[0, 4 * c:4 * c + 4] = np.asarray(r["wkvs"]).reshape(4, 32, 64, 64)
    return (y_prompt, y_sample, k_prompt, v_prompt, k_sample, v_sample,
            shift_prompt, wkv_prompt, shift_sample, wkv_sample)
```

```python
import contextlib
import math
import numpy as np
import concourse.bass as bass
import concourse.mybir as mybir
from concourse.bass_utils import run_bass_kernel_spmd

F32 = mybir.dt.float32
BF16 = mybir.dt.bfloat16
AF = mybir.ActivationFunctionType
ALU = mybir.AluOpType
AX = mybir.AxisListType

D = 2048
DC = 16
FFD = 5632
FC = 44
NCORE = 8
SEQ = 2048
NH = 32
HD = 64
WINDOW = 128
N_BUCKETS = 32
MAX_DISTANCE = 128
RMS_EPS = 1e-6
GN_EPS = 64 * 1e-5
NEG = -1.0e30
GROUPS = [(0, 768, False), (768, 768, False), (1536, 512, True)]
NMAX = 768


class T:
    __slots__ = ("name", "w", "r", "dsem", "dtot", "bank")

    def __init__(self, name):
        self.name = name
        self.w = {}
        self.r = {}
        self.dsem = None
        self.dtot = 0
        self.bank = None


def TL(name, n):
    return [T(f"{name}{i}") for i in range(n)]


class Prog:
    COMPUTE = ("pe", "act", "dve", "pool")

    def __init__(self, nc, strict_same=True):
        self.nc = nc
        self.es = contextlib.ExitStack()
        self.eng = {"pe": nc.tensor, "act": nc.scalar, "dve": nc.vector,
                    "pool": nc.gpsimd, "sp": nc.sync}
        self.sem = {}
        self.cnt = {}
        for e in self.COMPUTE:
            self.sem[e] = self.es.enter_context(nc.semaphore("s_" + e))
            self.cnt[e] = 0
        self.seen = {e: {} for e in self.eng}
        self.strict_same = strict_same
        self.nsem = 0
        self.n_ops = 0
        self.n_waits = 0
        self.out_events = []
        self.uid = 0

    def sb(self, name, shape, dt):
        return self.es.enter_context(self.nc.sbuf_tensor("sb_" + name, list(shape), dt))

    def ps(self, name, shape, dt=F32):
        return self.es.enter_context(self.nc.psum_tensor("ps_" + name, list(shape), dt))

    def newsem(self, name):
        self.nsem += 1
        self.uid += 1
        return self.es.enter_context(self.nc.semaphore(f"{name}_{self.uid}"))

    def _wait(self, e, ev):
        sem, val = ev
        k = id(sem)
        if self.seen[e].get(k, 0) >= val:
            return
        self.seen[e][k] = val
        self.eng[e].wait_ge(sem, val)
        self.n_waits += 1

    def _deps(self, e, reads, writes):
        own = id(self.sem[e]) if e in self.sem else None
        skip_own = (e == "pe") or (not self.strict_same)
        for t in reads:
            for k, ev in t.w.items():
                if k == own and skip_own:
                    continue
                self._wait(e, ev)
        for t in writes:
            for k, ev in t.w.items():
                if k == own and skip_own:
                    continue
                self._wait(e, ev)
            for k, ev in t.r.items():
                if k == own and skip_own:
                    continue
                self._wait(e, ev)
        for t in list(reads) + list(writes):
            if t.bank is not None:
                for k, ev in t.bank.w.items():
                    if k != own:
                        self._wait(e, ev)

    def _record(self, ev, reads, writes):
        k = id(ev[0])
        for t in reads:
            t.r[k] = ev
            if t.bank is not None:
                t.bank.w = {k: ev}
        for t in writes:
            t.w = {k: ev}
            t.r = {}
            if t.bank is not None:
                t.bank.w = {k: ev}

    def op(self, e, fn, reads=(), writes=(), inc=True):
        self._deps(e, reads, writes)
        ins = fn(self.eng[e])
        self.n_ops += 1
        if inc:
            self.cnt[e] += 1
            ins.then_inc(self.sem[e], 1)
            ev = (self.sem[e], self.cnt[e])
        else:
            ev = (self.sem[e], self.cnt[e] + 1)
        self._record(ev, reads, writes)
        return ins

    def dma(self, q, out_ap, in_ap, reads=(), writes=(), semt=None, is_output=False, concurrent=False, **kw):
        if semt is None:
            semt = writes[0] if writes else reads[0]
        if semt.dsem is None:
            semt.dsem = self.newsem("d")
        if concurrent:
            k = id(semt.dsem)
            saved = [(t, t.w.pop(k)) for t in writes if k in t.w]
            self._deps(q, reads, writes)
            for t, ev in saved:
                t.w[k] = ev
        else:
            self._deps(q, reads, writes)
        semt.dtot += 16
        ins = self.eng[q].dma_start(out=out_ap, in_=in_ap, **kw)
        ins.then_inc(semt.dsem, 16)
        self.n_ops += 1
        ev = (semt.dsem, semt.dtot)
        self._record(ev, reads, writes)
        if is_output:
            self.out_events.append(ev)
        return ins

    def finish(self):
        last = {}
        for sem, val in self.out_events:
            k = id(sem)
            if k not in last or last[k][1] < val:
                last[k] = (sem, val)
        for ev in last.values():
            self._wait("sp", ev)
        for e in self.COMPUTE:
            if self.cnt[e] > 0:
                self._wait("sp", (self.sem[e], self.cnt[e]))

    def close(self):
        self.es.close()


def w_chunks(w, cw=128):
    K, M = w.shape
    return np.ascontiguousarray(w.reshape(K // 128, 128, M // cw, cw).transpose(2, 1, 0, 3))


def fcol(v):
    s = v.shape[:-1]
    a = v.reshape(*s, DC, 128)
    a = np.moveaxis(a, -1, 0)
    return np.ascontiguousarray(a)


def t5_bucket_np(rel):
    nb = N_BUCKETS // 2
    max_exact = nb // 2
    offset = np.where(rel > 0, nb, 0)
    n = np.abs(rel)
    nf = np.maximum(n, 1).astype(np.float32)
    large = max_exact + (np.log(nf / np.float32(max_exact)) / np.float32(math.log(MAX_DISTANCE / max_exact))
                         * np.float32(nb - max_exact)).astype(np.int32)
    large = np.minimum(large, nb - 1)
    return offset + np.where(n < max_exact, n, large)


def static_consts():
    c = {}
    c["ident"] = np.eye(128, dtype=np.float32)
    i = np.arange(128)
    for L in (64, 32):
        same = (i[:, None] // L) == (i[None, :] // L)
        c[f"tri{L}"] = (same & (i[:, None] <= i[None, :])).astype(np.float32)
        ii = i % L
        c[f"mstrict{L}"] = (ii[:, None] < ii[None, :]).astype(np.float32)
        c[f"mincl{L}"] = (ii[:, None] <= ii[None, :]).astype(np.float32)
        b = 1
        lv = 0
        while b < L:
            c[f"lv{L}_{lv}"] = ((ii[:, None] // (2 * b) == ii[None, :] // (2 * b)) & ((ii[None, :] // b) % 2 == 1)
                               & ((ii[:, None] // b) % 2 == 0)).astype(np.float32)
            b *= 2
            lv += 1
    c["bdones"] = ((i[:, None] // 64) == (i[None, :] // 64)).astype(np.float32)
    r = np.arange(255)
    bk = t5_bucket_np((r - 191).astype(np.int32))
    oh = np.zeros((32, 255), np.float32)
    oh[bk, r] = 1.0
    c["onehot"] = oh
    return c


def build(ngroups=3, nlayers=2, dbg=None):
    nc = bass.Bass("TRN2", target_bir_lowering=False)
    import os as _os
    P = Prog(nc, strict_same=(_os.environ.get("K_STRICT", "1") == "1"))

    def din(name, shape):
        return nc.dram_tensor(name, list(shape), F32, kind="ExternalInput").ap()

    def dout(name, shape):
        return nc.dram_tensor(name, list(shape), F32, kind="ExternalOutput").ap()

    xp = din("xp", [128, DC, SEQ])
    xs = din("xs", [128, DC, 128])
    ck = din("ck", [4, 128, 256])
    cv = din("cv", [4, 128, 256])
    sshift = din("sshift", [128, 4, DC])
    swkv = din("swkv", [4, 16, 128, 64])
    NCONST = 12 * 16 + 20 + 6 * 16 + 7 * 16
    consts_d = din("consts", [128, NCONST])
    wg_d = din("wg", [2, 2, FC, 128, DC, 128])
    wu_d = din("wu", [2, 2, FC, 128, DC, 128])
    wd_d = din("wd", [2, 2, DC, 4, 128, 11, 128])
    wqkv_d = din("wqkv", [20, 128, DC, 128])
    wkvt_d = din("wkvt", [4, 128, DC, 128])
    bkv_d = nc.dram_tensor("bkv", [1, 512], F32, kind="ExternalInput")
    sinks_d = nc.dram_tensor("sinks", [1, 32], F32, kind="ExternalInput")
    table_d = din("table", [32, 32])
    wao_d = din("wao", [16, 128, DC, 128])
    wr_d = din("wr", [16, 128, DC, 128])
    wk_d = din("wk", [16, 128, DC, 128])
    wv_d = din("wv", [16, 128, DC, 128])
    wro_d = din("wro", [16, 128, DC, 128])
    w1_d = din("w1", [128, DC, 96])
    a1_d = din("a1", [128, DC, 96])
    g1_d = din("g1", [2, 128, DC, 128])
    w2_d = din("w2", [96, D])
    a2_d = din("a2", [96, D])
    g2_d = din("g2", [128, 2, D])
    cst = {k: din("c_" + k, v.shape) for k, v in static_consts().items()}
    fscr = nc.dram_tensor("fscr", [32, 255], F32, kind="Internal")

    yT_d = dout("yT", [128, DC, SEQ + 128])
    kp_d = dout("kp", [128, 256])
    vp_d = dout("vp", [128, 256])
    ks_d = dout("ks", [128, 256])
    vs_d = dout("vs", [128, 256])
    shp_d = dout("shp", [128, DC])
    shs_d = dout("shs", [128, 4, DC])
    wkvp_d = dout("wkvp", [16, 128, 64])
    wkvs_d = dout("wkvs", [4, 16, 128, 64])
    dbg_d = dout("dbg", [128, DC, NMAX]) if dbg else None

    CO = {}
    o = 0
    CO["g"] = o; o += 12 * 16
    CO["bq"] = o; o += 20
    CO["mu"] = o; o += 6 * 16
    for nm in ("w0", "a0", "kk", "ka", "rk", "lnw", "lnb"):
        CO[nm] = o; o += 16
    assert o == NCONST

    xT = P.sb("xT", [128, DC, NMAX], F32); xT_T = TL("xT", DC)
    hT = P.sb("hT", [128, DC, NMAX + 1], BF16); hT_T = TL("hT", DC)
    BIGB = 66 * 1024
    big = P.sb("big", [128, BIGB // 2], BF16)
    big_T = TL("big", FC)
    SL = 768

    def bigv(off_b, shape, dt):
        n = int(np.prod(shape[1:]))
        esz = 4 if dt == F32 else 2
        assert off_b % 4 == 0 and off_b + n * esz <= BIGB, (off_b, shape)
        if dt == F32:
            ap = big[:, off_b // 2: off_b // 2 + n * 2].bitcast(F32)
        else:
            ap = big[:, off_b // 2: off_b // 2 + n]
        if len(shape) == 3:
            ap = ap.rearrange("p (a b) -> p a b", b=shape[2])
        elif len(shape) == 4:
            ap = ap.rearrange("p (a b c) -> p a b c", b=shape[2], c=shape[3])
        t0 = off_b // (SL * 2)
        t1 = (off_b + n * esz - 1) // (SL * 2)
        return ap, big_T[t0:t1 + 1]

    wslot = [P.sb(f"ws{i}", [128, DC, 128], BF16) for i in range(4)]
    wslot_T = TL("ws", 4)
    dsl = P.sb("wds", [128, 4, 11, 128], BF16)
    dslot = [dsl[:, i] for i in range(4)]
    dslot_T = TL("wds", 4)
    wctr = [0, 0]

    consts = P.sb("consts", [128, NCONST], F32); consts_T = T("consts")
    identf = P.sb("identf", [128, 128], F32)
    identb = P.sb("identb", [128, 128], BF16)
    onesb = P.sb("onesb", [128, 128], BF16)
    bdones = P.sb("bdones", [128, 128], BF16)
    tri = {L: P.sb(f"tri{L}", [128, 128], F32) for L in (64, 32)}
    mstrict = {L: P.sb(f"mstrict{L}", [128, 128], BF16) for L in (64, 32)}
    mincl = {L: P.sb(f"mincl{L}", [128, 128], BF16) for L in (64, 32)}
    lvm = {L: [P.sb(f"lv{L}_{i}", [128, 128], BF16) for i in range(6 if L == 64 else 5)] for L in (64, 32)}
    cT = T("cst")
    bkv = P.sb("bkv", [128, 512], F32)
    sinks = P.sb("sinks", [128, 32], F32)
    bias2 = P.sb("bias2", [128, 32, 192], BF16); bias2_T = T("bias2")
    ktc = P.sb("ktc", [128, 4, 128], BF16); ktc_T = T("ktc")
    vbc = P.sb("vbc", [128, 256], BF16); vbc_T = T("vbc")
    Smast = P.sb("Smast", [128, 16, 128], F32); Smast_T = TL("Sm", 16)
    rstd = P.sb("rstd", [128, NMAX], F32); rstd_T = T("rstd")
    sq = [P.sb(f"sq{i}", [128, NMAX], BF16) for i in range(2)]; sq_T = TL("sq", 2)
    tmpf = [P.sb(f"tmpf{i}", [128, 512], F32) for i in range(2)]; tmpf_T = TL("tmpf", 2)
    small = P.sb("small", [128, 64], F32); small_T = TL("small", 4)
    ctr = {"sq": 0, "tmpf": 0, "bank": 0, "q": 0, "ev": 0, "nb": 2}
    SSB = [6, 7]

    psb = [P.ps(f"psb{i}", [128, 512]) for i in range(8)]
    psT = [TL(f"ps{i}_", 4) for i in range(8)]
    for i in range(8):
        bx = T(f"bank{i}")
        for t_ in psT[i]:
            t_.bank = bx

    def set_pool(nb):
        ctr["nb"] = nb

    def bank():
        b = ctr["bank"] % ctr["nb"]
        ctr["bank"] += 1
        return b

    def quarter():
        nbk = 6 - ctr["nb"]
        q = ctr["q"] % (nbk * 4)
        ctr["q"] += 1
        return ctr["nb"] + q % nbk, q // nbk

    def qf(bq):
        b, q = bq
        return psb[b][:, 128 * q:128 * (q + 1)]

    def qb(bq):
        b, q = bq
        return psb[b][:, 128 * q:128 * (q + 1)].bitcast(BF16)

    def qT(bq):
        return [psT[bq[0]][bq[1]]]

    def ev_eng():
        ctr["ev"] += 1
        return "act" if ctr["ev"] % 2 else "dve"

    def copy(e, out, in_, reads, writes):
        if e == "act":
            P.op("act", lambda x: x.copy(out=out, in_=in_), reads=reads, writes=writes)
        else:
            P.op(e, lambda x: x.tensor_copy(out=out, in_=in_), reads=reads, writes=writes)

    def cc(nm, j=None):
        if j is None:
            return consts[:, CO[nm]:CO[nm] + 16]
        return consts[:, CO[nm] + j:CO[nm] + j + 1]

    def gcol(l, n, c=None):
        o0 = CO["g"] + (l * 6 + n) * 16
        if c is None:
            return consts[:, o0:o0 + 16]
        return consts[:, o0 + c:o0 + c + 1]

    def blocks(N):
        out = []
        c0 = 0
        while c0 < N:
            cn = min(512, N - c0)
            out.append((c0, cn))
            c0 += cn
        return out

    P.dma("sp", consts[:], consts_d, writes=[consts_T])
    P.dma("sp", identf[:], cst["ident"], writes=[cT])
    for L in (64, 32):
        P.dma("sp", tri[L][:], cst[f"tri{L}"], writes=[cT])
        P.dma("pool", mstrict[L][:], cst[f"mstrict{L}"], writes=[cT])
        P.dma("pool", mincl[L][:], cst[f"mincl{L}"], writes=[cT])
        for i_, m_ in enumerate(lvm[L]):
            P.dma("pool", m_[:], cst[f"lv{L}_{i_}"], writes=[cT])
    P.dma("pool", identb[:], cst["ident"], writes=[cT])
    P.dma("pool", bdones[:], cst["bdones"], writes=[cT])
    P.dma("sp", bkv[:], bkv_d.ap().partition_broadcast(128), writes=[cT])
    P.dma("sp", sinks[:], sinks_d.ap().partition_broadcast(128), writes=[cT])
    P.op("dve", lambda e: e.memset(onesb[:], 1.0), writes=[cT])
    P.op("dve", lambda e: e.memset(hT[:, :, 0:1], 0.0), writes=hT_T)
    P.op("dve", lambda e: e.memset(Smast[:], 0.0), writes=Smast_T)
    P.op("dve", lambda e: e.memset(ktc[:], 0.0), writes=[ktc_T])
    P.op("dve", lambda e: e.memset(vbc[:], 0.0), writes=[vbc_T])

    def build_bias():
        tb = tmpf[0]; oh = tmpf[1]
        P.dma("sp", tb[0:32, 0:32], table_d, writes=[tmpf_T[0]])
        P.dma("sp", oh[0:32, 0:255], cst["onehot"], writes=[tmpf_T[1]])
        bq = quarter()
        pso = psb[bq[0]][0:32, 0:255]
        P.op("pe", lambda e: e.matmul(pso, tb[0:32, 0:32], oh[0:32, 0:255], start=True, stop=True),
             reads=[tmpf_T[0], tmpf_T[1]], writes=psT[bq[0]])
        fs, fs_T = bigv(0, [128, 256], F32)
        P.op("act", lambda e: e.copy(out=fs[0:32, 0:255], in_=pso), reads=psT[bq[0]], writes=fs_T)
        fT = T("fscr")
        P.dma("sp", fscr.ap(), fs[0:32, 0:255], reads=fs_T, writes=[fT])
        stg, stg_T = bigv(1024, [128, 32, 192], F32)
        for i in range(64):
            src = bass.AP(fscr, 63 - i, [[0, 1], [255, 32], [1, 192]])
            for half in range(2):
                p = half * 64 + i
                P.dma("sp", stg[p:p + 1, :, :], src, reads=[fT], writes=stg_T, semt=stg_T[0], concurrent=True)
        P.op("act", lambda e: e.copy(out=bias2[:, 0:16, :], in_=stg[:, 0:16, :]), reads=stg_T, writes=[bias2_T])
        P.op("dve", lambda e: e.tensor_copy(out=bias2[:, 16:32, :], in_=stg[:, 16:32, :]), reads=stg_T + [bias2_T], writes=[bias2_T])

    build_bias()

    def sumsq_accumulate(src_ap_fn, src_tiles_fn, N, nch, ssb):
        for c in range(nch):
            s = ctr["sq"] % 2; ctr["sq"] += 1
            P.op("act", lambda e: e.activation(out=sq[s][:, 0:N], in_=src_ap_fn(c), func=AF.Square),
                 reads=src_tiles_fn(c), writes=[sq_T[s]])
            for bi, (c0, cn) in enumerate(blocks(N)):
                P.op("pe", lambda e: e.matmul(psb[ssb[bi]][:, 0:cn], onesb[:], sq[s][:, c0:c0 + cn],
                                              start=(c == 0), stop=(c == nch - 1)),
                     reads=[sq_T[s], cT], writes=psT[ssb[bi]], inc=True)

    def rstd_from_ss(N, ssb, eps):
        for bi, (c0, cn) in enumerate(blocks(N)):
            P.op("dve", lambda e: e.tensor_scalar(out=rstd[:, c0:c0 + cn], in0=psb[ssb[bi]][:, 0:cn],
                                                  scalar1=1.0 / D, scalar2=eps, op0=ALU.mult, op1=ALU.add),
                 reads=psT[ssb[bi]], writes=[rstd_T])
        P.op("act", lambda e: e.activation(out=rstd[:, 0:N], in_=rstd[:, 0:N], func=AF.Ln),
             reads=[rstd_T], writes=[rstd_T])
        P.op("act", lambda e: e.activation(out=rstd[:, 0:N], in_=rstd[:, 0:N], func=AF.Exp, scale=-0.5),
             reads=[rstd_T], writes=[rstd_T])

    def prenorm(l, n, N):
        ssb = SSB
        sumsq_accumulate(lambda c: xT[:, c, 0:N], lambda c: [xT_T[c]], N, DC, ssb)
        rstd_from_ss(N, ssb, RMS_EPS)
        for c in range(DC):
            P.op("dve", lambda e: e.scalar_tensor_tensor(out=hT[:, c, 1:1 + N], in0=xT[:, c, 0:N],
                                                         scalar=gcol(l, n, c), in1=rstd[:, 0:N],
                                                         op0=ALU.mult, op1=ALU.mult),
                 reads=[xT_T[c], rstd_T, consts_T], writes=[hT_T[c]])

    def postnorm_add(l, n, N, ssb, weight):
        rstd_from_ss(N, ssb, RMS_EPS)
        for c in range(DC):
            for (c0, cn) in blocks(N):
                s = ctr["tmpf"] % 2; ctr["tmpf"] += 1
                P.op("dve", lambda e: e.scalar_tensor_tensor(out=tmpf[s][:, 0:cn], in0=hT[:, c, 1 + c0:1 + c0 + cn],
                                                             scalar=gcol(l, n, c), in1=rstd[:, c0:c0 + cn],
                                                             op0=ALU.mult, op1=ALU.mult),
                     reads=[hT_T[c], rstd_T, consts_T], writes=[tmpf_T[s]])
                P.op("dve", lambda e: e.scalar_tensor_tensor(out=xT[:, c, c0:c0 + cn], in0=tmpf[s][:, 0:cn],
                                                             scalar=float(weight), in1=xT[:, c, c0:c0 + cn],
                                                             op0=ALU.mult, op1=ALU.add),
                     reads=[tmpf_T[s]], writes=[xT_T[c]])

    def load_w(dram_ap):
        s = wctr[0] % 4; wctr[0] += 1
        P.dma("pool", wslot[s][:], dram_ap, writes=[wslot_T[s]])
        return wslot[s], wslot_T[s]

    def out_evac_ss(c, N, pbanks, ssb, first, last, bias=None):
        s = ctr["sq"] % 2; ctr["sq"] += 1
        for bi, (c0, cn) in enumerate(blocks(N)):
            b = pbanks[bi]
            P.op("act", lambda e: e.copy(out=hT[:, c, 1 + c0:1 + c0 + cn], in_=psb[b][:, 0:cn]),
                 reads=psT[b], writes=[hT_T[c]])
            P.op("act", lambda e: e.activation(out=sq[s][:, c0:c0 + cn], in_=psb[b][:, 0:cn], func=AF.Square),
                 reads=psT[b], writes=[sq_T[s]])
        def pe_part():
            for bi, (c0, cn) in enumerate(blocks(N)):
                P.op("pe", lambda e: e.matmul(psb[ssb[bi]][:, 0:cn], onesb[:], sq[s][:, c0:c0 + cn],
                                              start=first, stop=last),
                     reads=[sq_T[s], cT], writes=psT[ssb[bi]], inc=True)
        return pe_part

    def ffn(l, s, N):
        n_in, n_out = (0, 1) if s == 0 else (4, 5)
        set_pool(6)
        prenorm(l, n_in, N)
        actT = big[:, 0:FC * SL].rearrange("p (f t) -> p f t", t=SL)
        blks = blocks(N)
        for f in range(FC):
            wgs, wgT = load_w(wg_d[l, s, f])
            wus, wuT = load_w(wu_d[l, s, f])
            for (c0, cn) in blks:
                bg = bank(); bu = bank()
                for kt in range(DC):
                    P.op("pe", lambda e: e.matmul(psb[bg][:, 0:cn], wgs[:, kt, :], hT[:, kt, 1 + c0:1 + c0 + cn],
                                                  start=(kt == 0), stop=(kt == DC - 1)),
                         reads=[wgT, hT_T[kt]], writes=psT[bg], inc=(kt == DC - 1))
                for kt in range(DC):
                    P.op("pe", lambda e: e.matmul(psb[bu][:, 0:cn], wus[:, kt, :], hT[:, kt, 1 + c0:1 + c0 + cn],
                                                  start=(kt == 0), stop=(kt == DC - 1)),
                         reads=[wuT, hT_T[kt]], writes=psT[bu], inc=(kt == DC - 1))
                ts = ctr["tmpf"] % 2; ctr["tmpf"] += 1
                P.op("act", lambda e: e.activation(out=tmpf[ts][:, 0:cn], in_=psb[bg][:, 0:cn], func=AF.Silu),
                     reads=psT[bg], writes=[tmpf_T[ts]])
                P.op("dve", lambda e: e.tensor_tensor(out=actT[:, f, c0:c0 + cn], in0=tmpf[ts][:, 0:cn],
                                                      in1=psb[bu][:, 0:cn], op=ALU.mult),
                     reads=[tmpf_T[ts]] + psT[bu], writes=[big_T[f]])
        ssb = SSB
        pend = None
        for d in range(DC):
            pb = [bank() for _ in blks]
            for qr in range(4):
                sl = wctr[1] % 4; wctr[1] += 1
                P.dma("pool", dslot[sl], wd_d[l, s, d, qr], writes=[dslot_T[sl]])
                for bi, (c0, cn) in enumerate(blks):
                    for k in range(11):
                        f = qr * 11 + k
                        P.op("pe", lambda e: e.matmul(psb[pb[bi]][:, 0:cn], dslot[sl][:, k, :], actT[:, f, c0:c0 + cn],
                                                      start=(f == 0), stop=(f == FC - 1)),
                             reads=[dslot_T[sl], big_T[f]], writes=psT[pb[bi]], inc=(k == 10))
            if pend is not None:
                pend()
            pend = out_evac_ss(d, N, pb, ssb, d == 0, d == DC - 1)
        pend()
        postnorm_add(l, n_out, N, ssb, 0.5)

    def attention(gi, N, npr, has_sample):
        l = 0
        ntile = N // 128
        nptile = npr // 128
        set_pool(2)
        prenorm(l, 2, N)
        off = 0
        qTb, qT_T = bigv(off, [128, DC, NMAX], BF16); off += DC * NMAX * 2
        KT, KT_T = bigv(off, [128, 4, 128 + NMAX], BF16); off += 4 * (128 + NMAX) * 2
        Vb, Vb_T = bigv(off, [128, 7, 256], BF16); off += 7 * 256 * 2
        sbufs = []
        for i in range(4):
            a, t = bigv(off, [128, 256], F32); off += 1024
            sbufs.append((a, t))
        pbufs = []
        for i in range(6):
            a, t = bigv(off, [128, 256], BF16); off += 512
            pbufs.append((a, t))
        ptbufs = []
        for i in range(4):
            a, t = bigv(off, [128, 2, 128], BF16); off += 512
            ptbufs.append((a, t))
        stage, stage_T = bigv(off, [128, 512], F32); off += 2048
        if has_sample:
            KTs, KTs_T = bigv(off, [128, 4, 4, 256], BF16); off += 4 * 4 * 256 * 2
            Vc, Vc_T = bigv(off, [128, 4, 256], BF16); off += 4 * 256 * 2
            ssb_s = []
            for i in range(4):
                a, t = bigv(off, [128, 256], F32); off += 1024
                ssb_s.append((a, t))
            ckf, ckf_T = bigv(off, [128, 256], F32); off += 1024
        assert off <= BIGB, off

        P.op("pool", lambda e: e.tensor_copy(out=KT[:, :, 0:128], in_=ktc[:]), reads=[ktc_T], writes=KT_T)
        P.op("pool", lambda e: e.tensor_copy(out=Vb[:, 0, :], in_=vbc[:]), reads=[vbc_T], writes=Vb_T)

        blks = blocks(N)
        for j in range(20):
            ws, wT = load_w(wqkv_d[j])
            for (c0, cn) in blks:
                b = bank()
                for kt in range(DC):
                    P.op("pe", lambda e: e.matmul(psb[b][:, 0:cn], ws[:, kt, :], hT[:, kt, 1 + c0:1 + c0 + cn],
                                                  start=(kt == 0), stop=(kt == DC - 1)),
                         reads=[wT, hT_T[kt]], writes=psT[b], inc=(kt == DC - 1))
                bcol = consts[:, CO["bq"] + j:CO["bq"] + j + 1]
                if j < 16:
                    P.op("act", lambda e: e.activation(out=qTb[:, j, c0:c0 + cn], in_=psb[b][:, 0:cn],
                                                       func=AF.Identity, bias=bcol, scale=1.0),
                         reads=psT[b] + [consts_T], writes=qT_T)
                else:
                    P.op("act", lambda e: e.activation(out=KT[:, j - 16, 128 + c0:128 + c0 + cn], in_=psb[b][:, 0:cn],
                                                       func=AF.Identity, bias=bcol, scale=1.0),
                         reads=psT[b] + [consts_T], writes=KT_T)
        for cchunk in range(4):
            is_k = cchunk < 2
            ws, wT = load_w(wkvt_d[cchunk])
            for t in range(ntile):
                out_tile = (gi == 2) and (t >= nptile - 1)
                if is_k and not out_tile:
                    continue
                bq = quarter()
                for kt in range(DC):
                    P.op("pe", lambda e: e.matmul(qf(bq), hT[:, kt, 1 + 128 * t:1 + 128 * (t + 1)], ws[:, kt, :],
                                                  start=(kt == 0), stop=(kt == DC - 1)),
                         reads=[wT, hT_T[kt]], writes=qT(bq), inc=(kt == DC - 1))
                bsl = bkv[:, 128 * cchunk:128 * (cchunk + 1)]
                if not is_k:
                    P.op("dve", lambda e: e.tensor_tensor(out=Vb[:, 1 + t, 128 * (cchunk - 2):128 * (cchunk - 1)],
                                                          in0=qf(bq), in1=bsl, op=ALU.add),
                         reads=qT(bq) + [cT], writes=Vb_T)
                if out_tile:
                    P.op("dve", lambda e: e.tensor_tensor(out=stage[:, 128 * cchunk:128 * (cchunk + 1)],
                                                          in0=qf(bq), in1=bsl, op=ALU.add),
                         reads=qT(bq) + [cT], writes=stage_T)
                    is_s = (t == nptile)
                    dst = (ks_d if is_s else kp_d) if is_k else (vs_d if is_s else vp_d)
                    co = 128 * (cchunk % 2)
                    P.dma("sp", dst[:, co:co + 128], stage[:, 128 * cchunk:128 * (cchunk + 1)],
                          reads=stage_T, is_output=True)

        def preset(buf, tiles):
            P.op("pool", lambda e: e.memset(buf[:], NEG), writes=tiles)

        for (a, t_) in sbufs:
            preset(a, t_)

        if has_sample:
            for s in range(4):
                preset(ssb_s[s][0], ssb_s[s][1])
                P.dma("pool", Vc[:, s, :], cv[s], writes=Vc_T, semt=Vc_T[0])
                P.dma("sp", ckf[:], ck[s], writes=ckf_T)
                for kvh in range(4):
                    di = ctr["tmpf"] % 2; ctr["tmpf"] += 1
                    dsrc, dsrc_T = tmpf[di], tmpf_T[di]
                    for dup in range(2):
                        P.op("dve", lambda e: e.tensor_copy(out=dsrc[:, 64 * dup:64 * (dup + 1)],
                                                            in_=ckf[:, 64 * kvh:64 * (kvh + 1)]),
                             reads=ckf_T, writes=[dsrc_T])
                    bq = quarter()
                    P.op("pe", lambda e: e.transpose(qf(bq), dsrc[:, 0:128], identf[:]),
                         reads=[dsrc_T, cT], writes=qT(bq))
                    copy("act", KTs[:, s, kvh, 0:128], qf(bq), qT(bq), KTs_T)
                P.op("pool", lambda e: e.tensor_copy(out=KTs[:, s, :, 128:256], in_=KT[:, :, 128 + npr:128 + npr + 128]),
                     reads=KT_T, writes=KTs_T)

        jobs = []

        def stage_a(jbs):
            bks = []
            for jb in jbs:
                b_ = bank(); bks.append(b_)
                P.op("pe", lambda e: e.matmul(psb[b_][0:jb["nq"], 0:256], jb["q"], jb["k"], start=True, stop=True),
                     reads=qT_T + jb["k_T"], writes=psT[b_][0:2])
            for bi_ in range(2):
                for jb, b_ in zip(jbs, bks):
                    (r0, r1, oc0, oc1, bc0) = jb["bops"][bi_]
                    sb, sb_T = jb["sb"]
                    P.op("dve", lambda e: e.scalar_tensor_tensor(out=sb[r0:r1, oc0:oc1], in0=psb[b_][r0:r1, oc0:oc1], scalar=0.125,
                                                                 in1=bias2[r0:r1, jb["h"], bc0:bc0 + (oc1 - oc0)],
                                                                 op0=ALU.mult, op1=ALU.add),
                         reads=psT[b_][0:2] + [bias2_T], writes=sb_T)
            for jb in jbs:
                nq = jb["nq"]; sb, sb_T = jb["sb"]
                sm = small[:, 16 * (jb["i"] % 4):16 * (jb["i"] % 4) + 16]; sT = [small_T[jb["i"] % 4]]
                P.op("dve", lambda e: e.reduce_max(out=sm[0:nq, 0:1], in_=sb[0:nq, :], axis=AX.X), reads=sb_T, writes=sT)
            for jb in jbs:
                nq = jb["nq"]; h = jb["h"]
                sm = small[:, 16 * (jb["i"] % 4):16 * (jb["i"] % 4) + 16]; sT = [small_T[jb["i"] % 4]]
                P.op("dve", lambda e: e.tensor_scalar(out=sm[0:nq, 2:3], in0=sm[0:nq, 0:1], scalar1=sinks[0:nq, h:h + 1], scalar2=-1.0,
                                                      op0=ALU.max, op1=ALU.mult),
                     reads=sT + [cT], writes=sT)
            for jb in jbs:
                nq = jb["nq"]; sb, sb_T = jb["sb"]
                sm = small[:, 16 * (jb["i"] % 4):16 * (jb["i"] % 4) + 16]; sT = [small_T[jb["i"] % 4]]
                pb_, pb_T = pbufs[jb["i"] % 6]
                P.op("act", lambda e: e.activation(out=pb_[0:nq, :], in_=sb[0:nq, :], func=AF.Exp, bias=sm[0:nq, 2:3], scale=1.0,
                                                   accum_out=sm[0:nq, 3:4]),
                     reads=sb_T + sT, writes=pb_T + sT)
            for jb in jbs:
                nq = jb["nq"]; h = jb["h"]
                sm = small[:, 16 * (jb["i"] % 4):16 * (jb["i"] % 4) + 16]; sT = [small_T[jb["i"] % 4]]
                P.op("act", lambda e: e.activation(out=sm[0:nq, 4:5], in_=sinks[0:nq, h:h + 1], func=AF.Exp, bias=sm[0:nq, 2:3], scale=1.0),
                     reads=sT + [cT], writes=sT)

        def stage_a2(jbs):
            for jb in jbs:
                nq = jb["nq"]
                sm = small[:, 16 * (jb["i"] % 4):16 * (jb["i"] % 4) + 16]; sT = [small_T[jb["i"] % 4]]
                P.op("dve", lambda e: e.tensor_tensor(out=sm[0:nq, 5:6], in0=sm[0:nq, 3:4], in1=sm[0:nq, 4:5], op=ALU.add),
                     reads=sT, writes=sT)
            for jb in jbs:
                nq = jb["nq"]
                sm = small[:, 16 * (jb["i"] % 4):16 * (jb["i"] % 4) + 16]; sT = [small_T[jb["i"] % 4]]
                P.op("dve", lambda e: e.reciprocal(out=sm[0:nq, 6:7], in_=sm[0:nq, 5:6]), reads=sT, writes=sT)
            for jb in jbs:
                nq = jb["nq"]
                sm = small[:, 16 * (jb["i"] % 4):16 * (jb["i"] % 4) + 16]; sT = [small_T[jb["i"] % 4]]
                pb_, pb_T = pbufs[jb["i"] % 6]
                P.op("dve", lambda e: e.tensor_scalar(out=pb_[0:nq, :], in0=pb_[0:nq, :], scalar1=sm[0:nq, 6:7], scalar2=None, op0=ALU.mult),
                     reads=pb_T + sT, writes=pb_T)

        def stage_b(jbs):
            qs = []
            for jb in jbs:
                nq = jb["nq"]
                pb_, pb_T = pbufs[jb["i"] % 6]
                for kt in range(2):
                    bq = quarter(); qs.append((jb, kt, bq))
                    P.op("pe", lambda e: e.transpose(qb(bq)[:, 0:nq], pb_[0:nq, 128 * kt:128 * (kt + 1)], identb[0:nq, 0:nq]),
                         reads=pb_T + [cT], writes=qT(bq))
            for (jb, kt, bq) in qs:
                nq = jb["nq"]
                pt_, pt_T = ptbufs[jb["i"] % 4]
                copy(ev_eng(), pt_[:, kt, 0:nq], qb(bq)[:, 0:nq], qT(bq), pt_T)

        def stage_c(jbs):
            for jb in jbs:
                nq = jb["nq"]
                pt_, pt_T = ptbufs[jb["i"] % 4]
                if jb["hh"] == 0:
                    jb["pair"]["bq"] = quarter()
                bq = jb["pair"]["bq"]
                r0 = 64 * jb["hh"]
                for kt in range(2):
                    P.op("pe", lambda e: e.matmul(qf(bq)[r0:r0 + 64, 0:nq], jb["v"][kt], pt_[:, kt, 0:nq],
                                                  start=(kt == 0), stop=(kt == 1)),
                         reads=pt_T + jb["v_T"], writes=qT(bq), inc=(kt == 1))
                if jb["hh"] == 1:
                    copy("act", jb["o_dst"], qf(bq)[:, 0:nq], qT(bq), qT_T)

        def mk_jobs(j, nq, qcols, kfn, k_T, v, v_T, sbpair, bops):
            pair = {}
            return [dict(h=2 * j + hh, hh=hh, nq=nq, pair=pair,
                         q=qTb[64 * hh:64 * (hh + 1), j, qcols[0]:qcols[0] + nq], k=kfn(hh), k_T=k_T,
                         v=v, v_T=v_T, sb=sbpair[hh], bops=bops,
                         o_dst=qTb[:, j, qcols[0]:qcols[0] + nq]) for hh in range(2)]

        def push(jb):
            jb["i"] = len(jobs)
            jobs.append(jb)

        def add_pair(*a_):
            for jb in mk_jobs(*a_):
                push(jb)

        for t in range(nptile):
            first_tile = (gi == 0) and t == 0
            if first_tile:
                bops = [(0, 64, 128, 192, 128), (64, 128, 128, 256, 64)]
                sbp = sbufs[0:2]
            else:
                bops = [(0, 64, 0, 192, 0), (64, 128, 64, 256, 0)]
                sbp = sbufs[2:4]
            for j in range(DC):
                kvh = (2 * j) // 8
                add_pair(j, 128, (128 * t,),
                         lambda hh, kvh=kvh, t=t: KT[64 * hh:64 * (hh + 1), kvh, 128 * t:128 * t + 256], KT_T,
                         [Vb[:, t, 64 * kvh:64 * (kvh + 1)], Vb[:, t + 1, 64 * kvh:64 * (kvh + 1)]], Vb_T, sbp, bops)
        if has_sample:
            for j in range(DC):
                kvh = (2 * j) // 8
                for s0_ in (0, 2):
                    two = []
                    for s in (s0_, s0_ + 1):
                        bops = [(0, 32, 0, 128, 0), (0, 32, 128 + 32 * s, 160 + 32 * s, 128)]
                        two.append(mk_jobs(j, 32, (npr + 32 * s,),
                                           lambda hh, kvh=kvh, s=s: KTs[64 * hh:64 * (hh + 1), s, kvh, :], KTs_T,
                                           [Vc[:, s, 64 * kvh:64 * (kvh + 1)], Vb[:, 1 + nptile, 64 * kvh:64 * (kvh + 1)]],
                                           Vc_T + Vb_T, [ssb_s[s], ssb_s[s]], bops))
                    for hh in range(2):
                        push(two[0][hh]); push(two[1][hh])
        prs = [jobs[i_:i_ + 2] for i_ in range(0, len(jobs), 2)]
        npz = len(prs)
        for step in range(npz + 3):
            if step < npz:
                stage_a(prs[step])
            if 0 <= step - 1 < npz:
                stage_a2(prs[step - 1])
            if 0 <= step - 2 < npz:
                stage_b(prs[step - 2])
            if 0 <= step - 3 < npz:
                stage_c(prs[step - 3])
        if gi < 2:
            P.op("pool", lambda e: e.tensor_copy(out=ktc[:], in_=KT[:, :, npr:npr + 128]), reads=KT_T, writes=[ktc_T])
            P.op("pool", lambda e: e.tensor_copy(out=vbc[:], in_=Vb[:, nptile, :]), reads=Vb_T, writes=[vbc_T])
        set_pool(6)
        ssb = SSB
        pend = None
        for dch in range(DC):
            ws, wT = load_w(wao_d[dch])
            pb = [bank() for _ in blks]
            for bi, (c0, cn) in enumerate(blks):
                for kt in range(DC):
                    P.op("pe", lambda e: e.matmul(psb[pb[bi]][:, 0:cn], ws[:, kt, :], qTb[:, kt, c0:c0 + cn],
                                                  start=(kt == 0), stop=(kt == DC - 1)),
                         reads=[wT] + qT_T, writes=psT[pb[bi]], inc=(kt == DC - 1))
            if pend is not None:
                pend()
            pend = out_evac_ss(dch, N, pb, ssb, dch == 0, dch == DC - 1)
        pend()
        postnorm_add(l, 3, N, ssb, 1.0)

    def rwkv(gi, N, npr, has_sample):
        l = 1
        set_pool(6)
        prenorm(l, 2, N)
        blks = blocks(N)
        st = {"off": 0}

        def h_last(col, dst_ap):
            di = ctr["tmpf"] % 2; ctr["tmpf"] += 1
            so, so_T = tmpf[di], [tmpf_T[di]]
            P.op("dve", lambda e: e.scalar_tensor_tensor(out=so[:, 0:DC], in0=xT[:, :, col], scalar=rstd[:, col:col + 1], in1=gcol(l, 2),
                                                         op0=ALU.mult, op1=ALU.mult),
                 reads=xT_T + [rstd_T, consts_T], writes=so_T)
            P.dma("sp", dst_ap, so[:, 0:DC], reads=so_T, is_output=True)
        def buf(shape, dt):
            esz = 4 if dt == F32 else 2
            a_, t_ = bigv(st["off"], list(shape), dt)
            st["off"] = (st["off"] + int(np.prod(shape[1:])) * esz + 3) // 4 * 4
            return a_, t_

        if gi == 2:
            h_last(npr - 1, shp_d)
            for s_ in range(4):
                h_last(npr + 32 * s_ + 31, shs_d[:, s_, :])
        U = 4
        NB = N
        ygT, yg_T = buf([128, DC, NB], BF16)
        if has_sample:
            hsh, hsh_T = buf([128, DC, 128], BF16)
        lora, lora_T = buf([128, 4, NB], BF16)
        w2s, w2s_T = buf([128, 128], BF16); a2s, a2s_T = buf([128, 128], BF16); g2s, g2s_T = buf([128, 2, 128], BF16)
        rT_, rT_T = buf([128, NB], BF16); kT_, kT_T = buf([128, NB], BF16); vT_, vT_T = buf([128, NB], BF16)
        aT_, aT_T = buf([128, NB], BF16); kkn, kkn_T = buf([128, NB], BF16); ynT, ynT_T = buf([128, NB], BF16)
        lwT, lwT_T = buf([128, NB], F32)
        (wa, wa_T), (wb, wb_T) = buf([128, DC, 128], BF16), buf([128, DC, 128], BF16)

        def aux_views(flat_bf16, region_T, shapes):
            outs = []
            o_ = 0
            for shp in shapes:
                n_ = int(np.prod(shp[1:]))
                ap_ = flat_bf16[:, o_:o_ + n_]
                if len(shp) == 3:
                    ap_ = ap_.rearrange("p (a b) -> p a b", b=shp[2])
                t_ = T("aux")
                for rt_ in region_T:
                    for k_, ev_ in rt_.w.items():
                        if k_ not in t_.w or t_.w[k_][1] < ev_[1]:
                            t_.w[k_] = ev_
                    for k_, ev_ in rt_.r.items():
                        if k_ not in t_.r or t_.r[k_][1] < ev_[1]:
                            t_.r[k_] = ev_
                outs.append((ap_, [t_]))
                o_ += n_
            return outs

        def aux_release(region_T, subs):
            for rt_ in region_T:
                for _, tl in subs:
                    for t_ in tl:
                        for k_, ev_ in list(t_.w.items()) + list(t_.r.items()):
                            if k_ not in rt_.r or rt_.r[k_][1] < ev_[1]:
                                rt_.r[k_] = ev_

        d0 = dsl[:, 0:2].rearrange("p a b c -> p (a b c)")
        d1 = dsl[:, 2:4].rearrange("p a b c -> p (a b c)")
        r0 = rstd[:].bitcast(BF16)
        w2f = wslot[2][:].rearrange("p a b -> p (a b)")
        w3f = wslot[3][:].rearrange("p a b -> p (a b)")
        aux0 = aux_views(d0, dslot_T[0:2], [[128, U, 128]] * 5)
        aux1 = aux_views(d1, dslot_T[2:4], [[128, U, 128]] * 3 + [[128, U, 256]])
        aux2 = aux_views(r0, [rstd_T], [[128, U, 128]] * 3)
        aux3 = aux_views(w2f, [wslot_T[2]], [[128, U, 128]] * 4)
        aux4 = aux_views(w3f, [wslot_T[3]], [[128, U, 128]] * 4)
        (Dm, Dm_T), (Em, Em_T), (Mfull, Mfull_T) = aux0[0], aux0[1], aux0[2]
        (bdB, bdB_T), (bdK, bdK_T), (bdV, bdV_T) = aux2
        Qm2 = [buf([128, U, 128], BF16), buf([128, U, 128], BF16)]; P1b, P1b_T = buf([128, U, 128], BF16)
        Xb, Xb_T = buf([128, U, 256], BF16)
        Sb, Sb_T = buf([128, 128], BF16)
        e1, e1_T = buf([128, 256], BF16); e2, e2_T = buf([128, 256], BF16); e3, e3_T = buf([128, 256], BF16)
        SETS = []
        s0 = dict(tokV=aux1[0], tokK=aux1[1], tokB=aux1[2], bdR=aux1[3], ynbd=aux0[3])
        s0["ArbT"] = buf([128, U, 128], BF16); s0["ArkT"] = buf([128, U, 128], BF16); s0["Vbar"] = buf([128, U, 128], BF16)
        s0["AbT"] = buf([128, U, 128], BF16); s0["Ub"] = buf([128, U, 128], BF16); s0["gam"] = buf([128, 8], F32)
        s1 = dict(tokV=aux3[0], tokK=aux3[1], tokB=aux3[2], AbT=aux3[3], ArbT=aux4[0], ArkT=aux4[1], Vbar=aux4[2], Ub=aux4[3], ynbd=aux0[4])
        s1["bdR"] = buf([128, U, 256], BF16); s1["gam"] = buf([128, 8], F32)
        SETS = [s0, s1]
        if has_sample:
            Ss, Ss_T = buf([128, U, 128], F32)
        assert st["off"] <= BIGB, st["off"]
        def zero_bd(sets):
            zl = [(bdB, bdB_T), (bdK, bdK_T), (bdV, bdV_T)]
            for S_ in sets:
                zl += [S_["bdR"], S_["ynbd"]]
            for a_, t_ in zl:
                P.op("pool", lambda e: e.memset(a_, 0.0), writes=t_)

        if has_sample:
            di = ctr["tmpf"] % 2; ctr["tmpf"] += 1
            sh32, sh32_T = tmpf[di], tmpf_T[di]
            P.dma("sp", sh32[:, 0:4 * DC], sshift.rearrange("p s c -> p (s c)"), writes=[sh32_T])
            for s_ in range(4):
                P.op("dve", lambda e: e.tensor_copy(out=hsh[:, :, 32 * s_:32 * s_ + 1],
                                                    in_=sh32[:, s_ * DC:(s_ + 1) * DC].unsqueeze(2)),
                     reads=[sh32_T], writes=hsh_T)
                P.op("dve", lambda e: e.tensor_copy(out=hsh[:, :, 32 * s_ + 1:32 * s_ + 32],
                                                    in_=hT[:, :, 1 + npr + 32 * s_:1 + npr + 32 * s_ + 31]),
                     reads=hT_T, writes=hsh_T)

        def rhs_pairs(c0, cn):
            if has_sample and c0 >= npr:
                return (lambda kt: hT[:, kt, 1 + c0:1 + c0 + cn]), (lambda kt: hsh[:, kt, c0 - npr:c0 - npr + cn]), hsh_T
            return (lambda kt: hT[:, kt, 1 + c0:1 + c0 + cn]), (lambda kt: hT[:, kt, c0:c0 + cn]), []

        def priv(tiles):
            t_ = T("priv")
            for rt_ in tiles:
                for k_, ev_ in rt_.w.items():
                    if k_ not in t_.w or t_.w[k_][1] < ev_[1]:
                        t_.w[k_] = ev_
                for k_, ev_ in rt_.r.items():
                    if k_ not in t_.r or t_.r[k_][1] < ev_[1]:
                        t_.r[k_] = ev_
            return t_
        wa_h = [priv(wa_T), priv(wa_T)]
        wb_h = [priv(wb_T), priv(wb_T)]

        def mixed_linear(wdram_ap, n, M, evac):
            si = wctr[0] % 2; wctr[0] += 1
            ws, wT = wslot[si], wslot_T[si]
            P.dma("pool", ws[:, :, 0:M], wdram_ap, writes=[wT])
            for hf in range(2):
                k0, k1 = 8 * hf, 8 * hf + 8
                mu = consts[:, CO["mu"] + n * 16 + k0:CO["mu"] + n * 16 + k1].unsqueeze(2).broadcast_to([128, 8, M])
                P.op("dve", lambda e: e.tensor_tensor(out=wb[:, k0:k1, 0:M], in0=ws[:, k0:k1, 0:M], in1=mu, op=ALU.mult),
                     reads=[wT, consts_T], writes=[wb_h[hf]])
                P.op("dve", lambda e: e.tensor_tensor(out=wa[:, k0:k1, 0:M], in0=ws[:, k0:k1, 0:M], in1=wb[:, k0:k1, 0:M], op=ALU.subtract),
                     reads=[wT, wb_h[hf]], writes=[wa_h[hf]])
            for (c0, cn) in blks:
                cur, shf, extra = rhs_pairs(c0, cn)
                b = bank()
                for kt in range(DC):
                    P.op("pe", lambda e: e.matmul(psb[b][0:M, 0:cn], wa[:, kt, 0:M], cur(kt), start=(kt == 0), stop=False),
                         reads=[wa_h[kt // 8], hT_T[kt]], writes=psT[b], inc=False)
                    P.op("pe", lambda e: e.matmul(psb[b][0:M, 0:cn], wb[:, kt, 0:M], shf(kt), start=False, stop=(kt == DC - 1)),
                         reads=[wb_h[kt // 8], hT_T[kt]] + extra, writes=psT[b], inc=(kt % 8 == 7))
                evac(b, c0, cn)

        mixed_linear(w1_d, 1, 96, lambda b, c0, cn: P.op(
            "act", lambda e: e.activation(out=lora[0:96, 0, c0:c0 + cn], in_=psb[b][0:96, 0:cn], func=AF.Tanh),
            reads=psT[b], writes=lora_T))
        mixed_linear(a1_d, 4, 96, lambda b, c0, cn: P.op(
            "act", lambda e: e.copy(out=lora[0:96, 1, c0:c0 + cn], in_=psb[b][0:96, 0:cn]),
            reads=psT[b], writes=lora_T))
        for gc in range(2):
            mixed_linear(g1_d[gc], 5, 128, lambda b, c0, cn: P.op(
                "act", lambda e: e.activation(out=lora[:, 2 + gc, c0:c0 + cn], in_=psb[b][:, 0:cn], func=AF.Sigmoid),
                reads=psT[b], writes=lora_T))

        bcount = [0]

        def v3(ap_, TP, w0_, w1_):
            return ap_[0:TP, :, w0_:w1_]

        def prep_gen(j, cb, L, nu, S_, rezero):
            TP = 2 * L
            ncol = nu * L
            ntl = ncol // 128
            (tokV, tokV_T), (tokK, tokK_T), (tokB, tokB_T), (bdR, bdR_T) = S_["tokV"], S_["tokK"], S_["tokB"], S_["bdR"]
            (ArbT, ArbT_T), (ArkT, ArkT_T), (Vbar, Vbar_T), (AbT, AbT_T) = S_["ArbT"], S_["ArkT"], S_["Vbar"], S_["AbT"]
            gam, gam_T = S_["gam"]
            if rezero:
                zero_bd([S_])
            bcl = bank()
            for tl in range(ntl):
                c_ = cb + 128 * tl
                b_ = bank()
                P.op("pe", lambda e: e.transpose(psb[b_][:, 0:128], lwT[:, c_:c_ + 128], identf[:]), reads=lwT_T + [cT], writes=psT[b_][0:1])
                di = ctr["tmpf"] % 2; ctr["tmpf"] += 1
                P.op("act", lambda e: e.copy(out=tmpf[di][:, 0:128], in_=psb[b_][:, 0:128]), reads=psT[b_][0:1], writes=[tmpf_T[di]])
                P.op("pe", lambda e: e.matmul(psb[bcl][:, 128 * tl:128 * (tl + 1)], tmpf[di][:, 0:128], tri[L][:], start=True, stop=True),
                     reads=[tmpf_T[di], cT], writes=psT[bcl])
            cl = psb[bcl][:, 0:ncol]
            P.op("act", lambda e: e.activation(out=e1[:, 0:ncol], in_=cl, func=AF.Exp), reads=psT[bcl], writes=e1_T)
            P.op("act", lambda e: e.activation(out=e2[:, 0:ncol], in_=cl, func=AF.Exp, scale=-1.0), reads=psT[bcl], writes=e2_T)
            P.op("act", lambda e: e.activation(out=gam[:, 0:nu], in_=psb[bcl][:, 0:ncol].rearrange("p (u l) -> p u l", l=L)[:, :, L - 1],
                                               func=AF.Exp), reads=psT[bcl], writes=gam_T)
            di = ctr["tmpf"] % 2; ctr["tmpf"] += 1
            P.op("dve", lambda e: e.tensor_tensor(out=tmpf[di][:, 0:ncol], in0=cl, in1=lwT[:, cb:cb + ncol], op=ALU.subtract),
                 reads=psT[bcl] + lwT_T, writes=[tmpf_T[di]])
            P.op("act", lambda e: e.activation(out=e3[:, 0:ncol], in_=tmpf[di][:, 0:ncol], func=AF.Exp), reads=[tmpf_T[di]], writes=e3_T)
            yield
            def src(ap_, ps_):
                return ap_[ps_, cb:cb + ncol].rearrange("p (u l) -> p u l", l=L)

            def esrc(ap_, ps_):
                return ap_[ps_, 0:ncol].rearrange("p (u l) -> p u l", l=L)
            for hh in range(2):
                ps_ = slice(64 * hh, 64 * hh + 64)
                cs0, cs1 = L * hh, L * hh + L
                eng = "dve" if hh == 0 else "pool"
                P.op("dve", lambda e: e.scalar_tensor_tensor(out=bdR[ps_, 0:nu, cs0:cs1], in0=src(kkn, ps_), scalar=-1.0, in1=esrc(e3, ps_),
                                                           op0=ALU.mult, op1=ALU.mult),
                     reads=kkn_T + e3_T, writes=bdR_T)
                P.op(eng, lambda e: e.tensor_tensor(out=bdR[ps_, 0:nu, TP + cs0:TP + cs1], in0=src(rT_, ps_), in1=esrc(e1, ps_), op=ALU.mult),
                     reads=rT_T + e1_T, writes=bdR_T)
                P.op(eng, lambda e: e.tensor_tensor(out=bdK[ps_, 0:nu, cs0:cs1], in0=src(kT_, ps_), in1=esrc(e2, ps_), op=ALU.mult),
                     reads=kT_T + e2_T, writes=bdK_T)
                P.op(eng, lambda e: e.tensor_tensor(out=bdB[ps_, 0:nu, cs0:cs1], in0=src(kkn, ps_), in1=src(aT_, ps_), op=ALU.mult),
                     reads=kkn_T + aT_T, writes=bdB_T)
                P.op(eng, lambda e: e.tensor_tensor(out=bdB[ps_, 0:nu, cs0:cs1], in0=bdB[ps_, 0:nu, cs0:cs1], in1=esrc(e2, ps_), op=ALU.mult),
                     reads=bdB_T + e2_T, writes=bdB_T)
                P.op("act", lambda e: e.copy(out=bdV[ps_, 0:nu, cs0:cs1], in_=src(vT_, ps_)), reads=vT_T, writes=bdV_T)
            yield
            def tr_all(dst, dst_T, srcfn, src_T, rows_in, cols_in, eng):
                b_ = bank()
                pv = psb[b_][:, :].bitcast(BF16)
                for u in range(nu):
                    P.op("pe", lambda e: e.transpose(pv[0:cols_in, rows_in * u:rows_in * (u + 1)], srcfn(u), identb[0:rows_in, 0:rows_in]),
                         reads=src_T + [cT], writes=psT[b_], inc=(u == nu - 1))
                copy(eng, dst, pv[0:cols_in, 0:rows_in * nu].rearrange("p (u r) -> p u r", r=rows_in), psT[b_], dst_T)
            tr_all(tokV[0:TP, 0:nu, :], tokV_T, lambda u: bdV[:, u, 0:TP], bdV_T, 128, TP, "act")
            tr_all(tokK[0:TP, 0:nu, :], tokK_T, lambda u: bdK[:, u, 0:TP], bdK_T, 128, TP, "dve")
            tr_all(tokB[0:TP, 0:nu, :], tokB_T, lambda u: bdB[:, u, 0:TP], bdB_T, 128, TP, "act")
            tr_all(Xb[0:TP, 0:nu, 0:128], Xb_T, lambda u: bdR[:, u, 0:TP], bdR_T, 128, TP, "dve")
            yield
            upb = 512 // (2 * TP)
            for lhs, lhs_T, outs in ((bdB, bdB_T, ((Mfull, Mfull_T, None), (ArbT, ArbT_T, mincl[L]))),
                                     (bdK, bdK_T, ((P1b, P1b_T, mstrict[L]), (ArkT, ArkT_T, mincl[L])))):
                for u0 in range(0, nu, upb):
                    b_ = bank()
                    n_ = min(upb, nu - u0)
                    for u in range(u0, u0 + n_):
                        o_ = (u - u0) * 2 * TP
                        P.op("pe", lambda e: e.matmul(psb[b_][0:TP, o_:o_ + 2 * TP], lhs[:, u, 0:TP], bdR[:, u, 0:2 * TP], start=True, stop=True),
                             reads=lhs_T + bdR_T, writes=psT[b_], inc=(u == u0 + n_ - 1))
                    pv = psb[b_][0:TP, 0:n_ * 2 * TP].rearrange("p (u c) -> p u c", c=2 * TP)
                    for part, (dst, dst_T, msk) in enumerate(outs):
                        sv = pv[:, :, part * TP:(part + 1) * TP]
                        dv = dst[0:TP, u0:u0 + n_, 0:TP]
                        if msk is None:
                            copy("act", dv, sv, psT[b_], dst_T)
                        else:
                            P.op("dve", lambda e: e.tensor_tensor(out=dv, in0=sv, in1=msk[0:TP, 0:TP].unsqueeze(1).broadcast_to([TP, n_, TP]), op=ALU.mult),
                                 reads=psT[b_] + [cT], writes=dst_T)
            yield
            b_ = bank()
            for u in range(nu):
                P.op("pe", lambda e: e.matmul(psb[b_][0:TP, 128 * u:128 * (u + 1)], P1b[0:TP, u, 0:TP], tokV[0:TP, u, :], start=True, stop=True),
                     reads=P1b_T + tokV_T, writes=psT[b_], inc=(u == nu - 1))
            copy("act", Xb[0:TP, 0:nu, 128:256], psb[b_][0:TP, 0:128 * nu].rearrange("p (u c) -> p u c", c=128), psT[b_], Xb_T)
            idb = identb[0:TP, 0:TP].unsqueeze(1).broadcast_to([TP, nu, TP])
            yield
            P.op("act", lambda e: e.copy(out=Dm[0:TP, 0:nu, 0:TP], in_=idb), reads=[cT], writes=Dm_T)
            P.op("pool", lambda e: e.tensor_copy(out=Em[0:TP, 0:nu, 0:TP], in_=idb), reads=[cT], writes=Em_T)
            for lv in range(len(lvm[L])):
                Qm, Qm_T = Qm2[lv % 2]
                P.op("pool", lambda e: e.tensor_tensor(out=Qm[0:TP, 0:nu, 0:TP], in0=Mfull[0:TP, 0:nu, 0:TP],
                                                       in1=lvm[L][lv][0:TP, 0:TP].unsqueeze(1).broadcast_to([TP, nu, TP]), op=ALU.mult),
                     reads=Mfull_T + [cT], writes=Qm_T)
                b1_ = bank()
                for u in range(nu):
                    P.op("pe", lambda e: e.matmul(psb[b1_][0:TP, 128 * u:128 * u + TP], Qm[0:TP, u, 0:TP], Dm[0:TP, u, 0:TP], start=True, stop=True),
                         reads=Qm_T + Dm_T, writes=psT[b1_], inc=(u == nu - 1))
                copy("act", P1b[0:TP, 0:nu, 0:TP], psb[b1_][0:TP, 0:128 * nu].rearrange("p (u c) -> p u c", c=128)[:, :, 0:TP], psT[b1_], P1b_T)
                yield
                b2_ = bank()
                for u in range(nu):
                    P.op("pe", lambda e: e.matmul(psb[b2_][0:TP, 128 * u:128 * u + TP], Em[0:TP, u, 0:TP], P1b[0:TP, u, 0:TP], start=True, stop=True),
                         reads=Em_T + P1b_T, writes=psT[b2_], inc=(u == nu - 1))
                P.op("dve", lambda e: e.tensor_tensor(out=Dm[0:TP, 0:nu, 0:TP], in0=Dm[0:TP, 0:nu, 0:TP],
                                                      in1=psb[b2_][0:TP, 0:128 * nu].rearrange("p (u c) -> p u c", c=128)[:, :, 0:TP], op=ALU.add),
                     reads=Dm_T + psT[b2_], writes=Dm_T)
                yield
                tr_all(Em[0:TP, 0:nu, 0:TP], Em_T, lambda u: Dm[0:TP, u, 0:TP], Dm_T, TP, TP, "act")
                yield
            Ec, Ec_T = Em, Em_T
            for u0 in range(0, nu, 2):
                b_ = bank()
                n_ = min(2, nu - u0)
                for u in range(u0, u0 + n_):
                    P.op("pe", lambda e: e.matmul(psb[b_][0:TP, 256 * (u - u0):256 * (u - u0 + 1)], Ec[0:TP, u, 0:TP], Xb[0:TP, u, :], start=True, stop=True),
                         reads=Ec_T + Xb_T, writes=psT[b_], inc=(u == u0 + n_ - 1))
                pv = psb[b_][0:TP, 0:256 * n_].rearrange("p (u c) -> p u c", c=256)
                copy("act", Vbar[0:TP, u0:u0 + n_, :], pv[:, :, 128:256], psT[b_], Vbar_T)
                copy("dve", Xb[0:TP, u0:u0 + n_, 0:128], pv[:, :, 0:128], psT[b_], Xb_T)
            yield
            tr_all(AbT[:, 0:nu, 0:TP], AbT_T, lambda u: Xb[0:TP, u, 0:128], Xb_T, TP, 128, "act")
            yield

        def scan_gen(j, cb, L, nu, S_, states):
            TP = 2 * L
            ncol = nu * L
            (tokV, tokV_T), (tokK, tokK_T), (tokB, tokB_T), (bdR, bdR_T) = S_["tokV"], S_["tokK"], S_["tokB"], S_["bdR"]
            (ArbT, ArbT_T), (ArkT, ArkT_T), (Vbar, Vbar_T), (AbT, AbT_T) = S_["ArbT"], S_["ArkT"], S_["Vbar"], S_["AbT"]
            (Ub, Ub_T), (gam, gam_T), (ynbd, ynbd_T) = S_["Ub"], S_["gam"], S_["ynbd"]
            bcount[0] += 1
            by = SSB[bcount[0] % 2]
            for u in range(nu):
                Sm_ap, Sm_T = states[u]
                P.op("act", lambda e: e.copy(out=Sb[:, :], in_=Sm_ap), reads=Sm_T, writes=Sb_T)
                bu = bank()
                P.op("pe", lambda e: e.matmul(psb[bu][0:TP, 0:128], AbT[:, u, 0:TP], Sb[:, :], start=True, stop=True),
                     reads=AbT_T + Sb_T, writes=psT[bu])
                P.op("dve", lambda e: e.tensor_tensor(out=Ub[0:TP, u, :], in0=psb[bu][0:TP, 0:128], in1=Vbar[0:TP, u, :], op=ALU.add),
                     reads=psT[bu] + Vbar_T, writes=Ub_T)
                yield
                yo = psb[by][0:TP, 128 * u:128 * (u + 1)]
                P.op("pe", lambda e: e.matmul(yo, bdR[:, u, TP:2 * TP], Sb[:, :], start=True, stop=False),
                     reads=bdR_T + Sb_T, writes=psT[by], inc=False)
                P.op("pe", lambda e: e.matmul(yo, ArkT[0:TP, u, 0:TP], tokV[0:TP, u, :], start=False, stop=False),
                     reads=ArkT_T + tokV_T, writes=psT[by], inc=False)
                P.op("pe", lambda e: e.matmul(yo, ArbT[0:TP, u, 0:TP], Ub[0:TP, u, :], start=False, stop=True),
                     reads=ArbT_T + Ub_T, writes=psT[by])
                bs = bank()
                P.op("pe", lambda e: e.matmul(psb[bs][:, 0:128], tokK[0:TP, u, :], tokV[0:TP, u, :], start=True, stop=False),
                     reads=tokK_T + tokV_T, writes=psT[bs], inc=False)
                P.op("pe", lambda e: e.matmul(psb[bs][:, 0:128], tokB[0:TP, u, :], Ub[0:TP, u, :], start=False, stop=True),
                     reads=tokB_T + Ub_T, writes=psT[bs])
                di = ctr["tmpf"] % 2; ctr["tmpf"] += 1
                P.op("act", lambda e: e.activation(out=tmpf[di][:, 0:128], in_=psb[bs][:, 0:128], func=AF.Copy, scale=gam[:, u:u + 1]),
                     reads=psT[bs] + gam_T, writes=[tmpf_T[di]])
                P.op("dve", lambda e: e.scalar_tensor_tensor(out=Sm_ap, in0=Sm_ap, scalar=gam[:, u:u + 1], in1=tmpf[di][:, 0:128],
                                                             op0=ALU.mult, op1=ALU.add),
                     reads=Sm_T + gam_T + [tmpf_T[di]], writes=Sm_T)
                yield
            g = bcount[0] % 4
            sm = small[:, 16 * g:16 * g + 16]; sT = [small_T[g]]
            Y3 = psb[by][0:TP, 0:128 * nu].rearrange("p (u c) -> p u c", c=128)
            yield
            P.op("dve", lambda e: e.reduce_sum(out=sm[0:TP, 0:nu], in_=Y3, axis=AX.X), reads=psT[by], writes=sT)
            di = ctr["tmpf"] % 2; ctr["tmpf"] += 1
            sqv = tmpf[di][0:TP, 0:128 * nu].rearrange("p (u c) -> p u c", c=128)
            P.op("act", lambda e: e.activation(out=sqv, in_=Y3, func=AF.Square), reads=psT[by], writes=[tmpf_T[di]])
            P.op("dve", lambda e: e.reduce_sum(out=sm[0:TP, 4:4 + nu], in_=sqv, axis=AX.X), reads=[tmpf_T[di]], writes=sT)
            P.op("dve", lambda e: e.tensor_scalar(out=sm[0:TP, 0:nu], in0=sm[0:TP, 0:nu], scalar1=1.0 / 64, scalar2=None, op0=ALU.mult),
                 reads=sT, writes=sT)
            P.op("dve", lambda e: e.tensor_tensor(out=sm[0:TP, 8:8 + nu], in0=sm[0:TP, 0:nu], in1=sm[0:TP, 0:nu], op=ALU.mult),
                 reads=sT, writes=sT)
            P.op("dve", lambda e: e.scalar_tensor_tensor(out=sm[0:TP, 4:4 + nu], in0=sm[0:TP, 4:4 + nu], scalar=1.0 / 64, in1=sm[0:TP, 8:8 + nu],
                                                         op0=ALU.mult, op1=ALU.subtract),
                 reads=sT, writes=sT)
            P.op("dve", lambda e: e.tensor_scalar(out=sm[0:TP, 4:4 + nu], in0=sm[0:TP, 4:4 + nu], scalar1=GN_EPS, scalar2=None, op0=ALU.add),
                 reads=sT, writes=sT)
            P.op("act", lambda e: e.activation(out=sm[0:TP, 4:4 + nu], in_=sm[0:TP, 4:4 + nu], func=AF.Ln), reads=sT, writes=sT)
            P.op("act", lambda e: e.activation(out=sm[0:TP, 4:4 + nu], in_=sm[0:TP, 4:4 + nu], func=AF.Exp, scale=-0.5), reads=sT, writes=sT)
            for hh in range(2):
                rs = slice(L * hh, L * hh + L)
                cs = slice(64 * hh, 64 * hh + 64)
                P.op("dve", lambda e: e.tensor_tensor(out=ynbd[rs, 0:nu, cs], in0=Y3[rs, :, cs],
                                                      in1=sm[rs, 0:nu].unsqueeze(2).broadcast_to([L, nu, 64]), op=ALU.subtract),
                     reads=psT[by] + sT, writes=ynbd_T)
                P.op("dve", lambda e: e.tensor_tensor(out=ynbd[rs, 0:nu, cs], in0=ynbd[rs, 0:nu, cs],
                                                      in1=sm[rs, 4:4 + nu].unsqueeze(2).broadcast_to([L, nu, 64]), op=ALU.mult),
                     reads=ynbd_T + sT, writes=ynbd_T)
            yield
            b_ = bank()
            pv = psb[b_][:, :].bitcast(BF16)
            for u in range(nu):
                P.op("pe", lambda e: e.transpose(pv[:, TP * u:TP * (u + 1)], ynbd[0:TP, u, :], identb[0:TP, 0:TP]),
                     reads=ynbd_T + [cT], writes=psT[b_], inc=(u == nu - 1))
            for hh in range(2):
                ps_ = slice(64 * hh, 64 * hh + 64)
                P.op("act", lambda e: e.copy(out=ynT[ps_, cb:cb + ncol].rearrange("p (u l) -> p u l", l=L),
                                             in_=pv[ps_, 0:TP * nu].rearrange("p (u c) -> p u c", c=TP)[:, :, L * hh:L * hh + L]),
                     reads=psT[b_], writes=ynT_T)

        def state_out(src_ap, src_T, dst):
            b_ = bank()
            P.op("pe", lambda e: e.transpose(psb[b_][:, 0:128], src_ap, identf[:]), reads=src_T + [cT], writes=psT[b_])
            di = ctr["tmpf"] % 2; ctr["tmpf"] += 1
            so, so_T = tmpf[di], [tmpf_T[di]]
            for hh in range(2):
                ps_ = slice(64 * hh, 64 * hh + 64)
                P.op("act", lambda e: e.copy(out=so[ps_, 0:64], in_=psb[b_][ps_, 64 * hh:64 * hh + 64]), reads=psT[b_], writes=so_T)
            P.dma("sp", dst, so[:, 0:64], reads=so_T, is_output=True)

        for j in range(DC):
            P.dma("pool", w2s[0:96, :], w2_d[:, 128 * j:128 * (j + 1)], writes=w2s_T, semt=w2s_T[0])
            P.dma("pool", a2s[0:96, :], a2_d[:, 128 * j:128 * (j + 1)], writes=a2s_T, semt=a2s_T[0])
            P.dma("pool", g2s[:, :, :], g2_d[:, :, 128 * j:128 * (j + 1)], writes=g2s_T, semt=g2s_T[0])
            for (wdram, n, dst, dst_T) in ((wr_d, 0, rT_, rT_T), (wk_d, 2, kT_, kT_T), (wv_d, 3, vT_, vT_T)):
                mixed_linear(wdram[j], n, 128, lambda b, c0, cn, dst=dst, dst_T=dst_T: copy(
                    ev_eng(), dst[:, c0:c0 + cn], psb[b][:, 0:cn], psT[b], dst_T))
            for (c0, cn) in blks:
                b = bank()
                P.op("pe", lambda e: e.matmul(psb[b][:, 0:cn], w2s[0:96, :], lora[0:96, 0, c0:c0 + cn], start=True, stop=True),
                     reads=w2s_T + lora_T, writes=psT[b])
                P.op("act", lambda e: e.activation(out=lwT[:, c0:c0 + cn], in_=psb[b][:, 0:cn], func=AF.Exp, bias=cc("w0", j), scale=1.0),
                     reads=psT[b] + [consts_T], writes=lwT_T)
                ts = ctr["tmpf"] % 2; ctr["tmpf"] += 1
                tf, tf_T = tmpf[ts], [tmpf_T[ts]]
                P.op("dve", lambda e: e.tensor_scalar(out=tf[:, 0:cn], in0=lwT[:, c0:c0 + cn], scalar1=1.0, scalar2=None, op0=ALU.add),
                     reads=lwT_T, writes=tf_T)
                P.op("dve", lambda e: e.reciprocal(out=tf[:, 0:cn], in_=tf[:, 0:cn]), reads=tf_T, writes=tf_T)
                P.op("dve", lambda e: e.scalar_tensor_tensor(out=lwT[:, c0:c0 + cn], in0=lwT[:, c0:c0 + cn], scalar=-math.exp(-0.5), in1=tf[:, 0:cn],
                                                             op0=ALU.mult, op1=ALU.mult),
                     reads=lwT_T + tf_T, writes=lwT_T)
                b = bank()
                P.op("pe", lambda e: e.matmul(psb[b][:, 0:cn], a2s[0:96, :], lora[0:96, 1, c0:c0 + cn], start=True, stop=True),
                     reads=a2s_T + lora_T, writes=psT[b])
                P.op("act", lambda e: e.activation(out=aT_[:, c0:c0 + cn], in_=psb[b][:, 0:cn], func=AF.Sigmoid, bias=cc("a0", j), scale=1.0),
                     reads=psT[b] + [consts_T], writes=aT_T)
            P.op("dve", lambda e: e.tensor_scalar(out=kkn[:, 0:N], in0=kT_[:, 0:N], scalar1=cc("kk", j), scalar2=None, op0=ALU.mult),
                 reads=kT_T + [consts_T], writes=kkn_T)
            s = ctr["sq"] % 2; ctr["sq"] += 1
            P.op("act", lambda e: e.activation(out=sq[s][:, 0:N], in_=kkn[:, 0:N], func=AF.Square), reads=kkn_T, writes=[sq_T[s]])
            for (c0, cn) in blks:
                b = bank()
                P.op("pe", lambda e: e.matmul(psb[b][:, 0:cn], bdones[:], sq[s][:, c0:c0 + cn], start=True, stop=True),
                     reads=[sq_T[s], cT], writes=psT[b])
                ts = ctr["tmpf"] % 2; ctr["tmpf"] += 1
                tf, tf_T = tmpf[ts], [tmpf_T[ts]]
                P.op("dve", lambda e: e.tensor_scalar(out=tf[:, 0:cn], in0=psb[b][:, 0:cn], scalar1=1e-30, scalar2=None, op0=ALU.add),
                     reads=psT[b], writes=tf_T)
                P.op("act", lambda e: e.activation(out=tf[:, 0:cn], in_=tf[:, 0:cn], func=AF.Ln), reads=tf_T, writes=tf_T)
                P.op("act", lambda e: e.activation(out=tf[:, 0:cn], in_=tf[:, 0:cn], func=AF.Exp, scale=-0.5), reads=tf_T, writes=tf_T)
                P.op("dve", lambda e: e.tensor_tensor(out=kkn[:, c0:c0 + cn], in0=kkn[:, c0:c0 + cn], in1=tf[:, 0:cn], op=ALU.mult),
                     reads=kkn_T + tf_T, writes=kkn_T)
                P.op("dve", lambda e: e.tensor_scalar(out=tf[:, 0:cn], in0=aT_[:, c0:c0 + cn], scalar1=-1.0, scalar2=cc("ka", j), op0=ALU.add, op1=ALU.mult),
                     reads=aT_T + [consts_T], writes=tf_T)
                P.op("dve", lambda e: e.tensor_scalar(out=tf[:, 0:cn], in0=tf[:, 0:cn], scalar1=1.0, scalar2=None, op0=ALU.add),
                     reads=tf_T, writes=tf_T)
                P.op("dve", lambda e: e.tensor_tensor(out=kT_[:, c0:c0 + cn], in0=kT_[:, c0:c0 + cn], in1=tf[:, 0:cn], op=ALU.mult),
                     reads=kT_T + tf_T, writes=kT_T)
            if j == 0 or has_sample:
                zero_bd(SETS)
            blist = [(256 * bi_, 64, [(Smast[:, j, :], [Smast_T[j]])] * 4, False) for bi_ in range(npr // 256)]
            if has_sample:
                for u in range(4):
                    di = ctr["tmpf"] % 2; ctr["tmpf"] += 1
                    si_, si_T = tmpf[di], [tmpf_T[di]]
                    P.dma("sp", si_[:, 256:320], swkv[u, j], writes=si_T)
                    P.op("dve", lambda e: e.memset(si_[:, 0:128], 0.0), writes=si_T)
                    for hh in range(2):
                        ps_ = slice(64 * hh, 64 * hh + 64)
                        P.op("dve", lambda e: e.tensor_copy(out=si_[ps_, 64 * hh:64 * hh + 64], in_=si_[ps_, 256:320]), reads=si_T, writes=si_T)
                    b_ = bank()
                    P.op("pe", lambda e: e.transpose(psb[b_][:, 0:128], si_[:, 0:128], identf[:]), reads=si_T + [cT], writes=psT[b_])
                    P.op("act", lambda e: e.copy(out=Ss[:, u, :], in_=psb[b_][:, 0:128]), reads=psT[b_], writes=Ss_T)
                blist.append((npr, 32, [(Ss[:, u, :], Ss_T) for u in range(4)], True))

            def drive(gens):
                gens = [g_ for g_ in gens if g_ is not None]
                while gens:
                    for g_ in list(gens):
                        try:
                            next(g_)
                        except StopIteration:
                            gens.remove(g_)
            drive([prep_gen(j, blist[0][0], blist[0][1], 4, SETS[0], blist[0][3])])
            for bi_, (cb_, L_, states_, rz_) in enumerate(blist):
                nxt = None
                if bi_ + 1 < len(blist):
                    n_ = blist[bi_ + 1]
                    nxt = prep_gen(j, n_[0], n_[1], 4, SETS[(bi_ + 1) % 2], n_[3])
                drive([scan_gen(j, cb_, L_, 4, SETS[bi_ % 2], states_), nxt])
            if has_sample:
                for u in range(4):
                    state_out(Ss[:, u, :], Ss_T, wkvs_d[u, j])
            if gi == 2:
                state_out(Smast[:, j, :], [Smast_T[j]], wkvp_d[j])
            s = ctr["sq"] % 2; ctr["sq"] += 1
            P.op("dve", lambda e: e.scalar_tensor_tensor(out=sq[s][:, 0:N], in0=rT_[:, 0:N], scalar=cc("rk", j), in1=kT_[:, 0:N],
                                                         op0=ALU.mult, op1=ALU.mult),
                 reads=rT_T + kT_T + [consts_T], writes=[sq_T[s]])
            for (c0, cn) in blks:
                b = bank()
                P.op("pe", lambda e: e.matmul(psb[b][:, 0:cn], bdones[:], sq[s][:, c0:c0 + cn], start=True, stop=True),
                     reads=[sq_T[s], cT], writes=psT[b])
                ts = ctr["tmpf"] % 2; ctr["tmpf"] += 1
                tf, tf_T = tmpf[ts], [tmpf_T[ts]]
                P.op("dve", lambda e: e.tensor_tensor(out=tf[:, 0:cn], in0=psb[b][:, 0:cn], in1=vT_[:, c0:c0 + cn], op=ALU.mult),
                     reads=psT[b] + vT_T, writes=tf_T)
                ts2 = ctr["tmpf"] % 2; ctr["tmpf"] += 1
                tg, tg_T = tmpf[ts2], [tmpf_T[ts2]]
                P.op("dve", lambda e: e.tensor_scalar(out=tg[:, 0:cn], in0=ynT[:, c0:c0 + cn], scalar1=cc("lnw", j), scalar2=cc("lnb", j),
                                                      op0=ALU.mult, op1=ALU.add),
                     reads=ynT_T + [consts_T], writes=tg_T)
                P.op("dve", lambda e: e.tensor_tensor(out=tf[:, 0:cn], in0=tf[:, 0:cn], in1=tg[:, 0:cn], op=ALU.add),
                     reads=tf_T + tg_T, writes=tf_T)
                bg = bank()
                for kt in range(2):
                    P.op("pe", lambda e: e.matmul(psb[bg][:, 0:cn], g2s[:, kt, :], lora[:, 2 + kt, c0:c0 + cn],
                                                  start=(kt == 0), stop=(kt == 1)),
                         reads=g2s_T + lora_T, writes=psT[bg], inc=(kt == 1))
                P.op("dve", lambda e: e.tensor_tensor(out=ygT[:, j, c0:c0 + cn], in0=tf[:, 0:cn], in1=psb[bg][:, 0:cn], op=ALU.mult),
                     reads=tf_T + psT[bg], writes=yg_T)

        aux_release(dslot_T[0:2], aux0); aux_release(dslot_T[2:4], aux1); aux_release([rstd_T], aux2)
        aux_release([wslot_T[2]], aux3); aux_release([wslot_T[3]], aux4)
        aux_release(wa_T, [(None, wa_h)]); aux_release(wb_T, [(None, wb_h)])
        return ygT, yg_T

    def rwkv_full(gi, N, npr, has_sample):
        ygT, yg_T = rwkv(gi, N, npr, has_sample)
        if dbg == "yg":
            for c in range(DC):
                P.op("act", lambda e: e.copy(out=xT[:, c, 0:N], in_=ygT[:, c, 0:N]), reads=yg_T, writes=[xT_T[c]])
            return
        P.op("pool", lambda e: e.tensor_copy(out=hT[:, :, 0:1], in_=hT[:, :, npr:npr + 1]), reads=hT_T, writes=hT_T)
        blks = blocks(N)
        set_pool(6)
        ssb = SSB
        pend = None
        for dch in range(DC):
            ws, wT = load_w(wro_d[dch])
            pb = [bank() for _ in blks]
            for bi, (c0, cn) in enumerate(blks):
                for kt in range(DC):
                    P.op("pe", lambda e: e.matmul(psb[pb[bi]][:, 0:cn], ws[:, kt, :], ygT[:, kt, c0:c0 + cn],
                                                  start=(kt == 0), stop=(kt == DC - 1)),
                         reads=[wT] + yg_T, writes=psT[pb[bi]], inc=(kt == DC - 1))
            if pend is not None:
                pend()
            pend = out_evac_ss(dch, N, pb, ssb, dch == 0, dch == DC - 1)
        pend()
        postnorm_add(1, 3, N, ssb, 1.0)

    for gi, (p0, npr, has_s) in enumerate(GROUPS[:ngroups]):
        N = npr + (128 if has_s else 0)
        for c in range(DC):
            P.dma("sp", xT[:, c, 0:npr], xp[:, c, p0:p0 + npr], writes=[xT_T[c]])
            if has_s:
                P.dma("sp", xT[:, c, npr:npr + 128], xs[:, c, :], writes=[xT_T[c]])
        for l in range(nlayers):
            ffn(l, 0, N)
            if dbg == f"ffn{l}0" and gi == 0:
                break
            if l == 0:
                attention(gi, N, npr, has_s)
            else:
                rwkv_full(gi, N, npr, has_s)
            if dbg in (f"mix{l}", "yg") and gi == 0 and (dbg != "yg" or l == 1):
                break
            ffn(l, 1, N)
        for c in range(DC):
            P.dma("sp", yT_d[:, c, p0:p0 + npr], xT[:, c, 0:npr], reads=[xT_T[c]], is_output=True)
            if has_s:
                P.dma("sp", yT_d[:, c, SEQ:SEQ + 128], xT[:, c, npr:npr + 128], reads=[xT_T[c]], is_output=True)
    P.finish()
    stats = dict(ops=P.n_ops, waits=P.n_waits, sems=P.nsem, cnt=dict(P.cnt))
    P.close()
    return nc, stats


def prep_shared(inp):
    f = lambda a: np.ascontiguousarray(np.asarray(a, dtype=np.float32))
    sh = {}
    sh["wg"] = np.stack([np.stack([w_chunks(f(inp["ffn_w_gate"][l, s])) for s in range(2)]) for l in range(2)])
    sh["wu"] = np.stack([np.stack([w_chunks(f(inp["ffn_w_up"][l, s])) for s in range(2)]) for l in range(2)])
    wd = np.stack([np.stack([w_chunks(f(inp["ffn_w_down"][l, s])) for s in range(2)]) for l in range(2)])
    sh["wd"] = np.ascontiguousarray(wd.reshape(2, 2, DC, 128, 4, 11, 128).transpose(0, 1, 2, 4, 3, 5, 6))
    wqkv = f(inp["att_w_qkv"][0])
    bqkv = f(inp["att_b_qkv"][0])
    kcols = [np.concatenate([wqkv[:, 2048 + 64 * h:2048 + 64 * (h + 1)]] * 2, axis=1) for h in range(4)]
    wext = np.concatenate([wqkv[:, :2048]] + kcols, axis=1)
    sh["wqkv"] = w_chunks(wext)
    bext = np.concatenate([bqkv[:2048]] + [np.concatenate([bqkv[2048 + 64 * h:2048 + 64 * (h + 1)]] * 2) for h in range(4)])
    sh["wkvt"] = w_chunks(wqkv[:, 2048:2560])
    sh["bkv"] = bqkv[2048:2560].reshape(1, 512).copy()
    sh["sinks"] = f(inp["att_sinks"]).reshape(1, 32).copy()
    sh["table"] = f(inp["rel_table"])
    sh["wao"] = w_chunks(f(inp["att_w_o"][0]))
    sh["wr"] = w_chunks(f(inp["rwkv_w_r"][0]))
    sh["wk"] = w_chunks(f(inp["rwkv_w_k"][0]))
    sh["wv"] = w_chunks(f(inp["rwkv_w_v"][0]))
    sh["wro"] = w_chunks(f(inp["rwkv_w_o"][0]))
    sh["w1"] = w_chunks(f(inp["rwkv_w1"][0]), 96)[0]
    sh["a1"] = w_chunks(f(inp["rwkv_a1"][0]), 96)[0]
    sh["g1"] = w_chunks(f(inp["rwkv_g1"][0]))
    sh["w2"] = f(inp["rwkv_w2"][0])
    sh["a2"] = f(inp["rwkv_a2"][0])
    sh["g2"] = np.ascontiguousarray(f(inp["rwkv_g2"][0]).reshape(2, 128, D).transpose(1, 0, 2))
    cols = [fcol(f(inp["norm_g"])).reshape(128, 12 * 16),
            bext.reshape(20, 128).T,
            fcol(f(inp["rwkv_mu"][0])).reshape(128, 6 * 16)]
    for nm in ("rwkv_w0", "rwkv_a0", "rwkv_k_k", "rwkv_k_a"):
        cols.append(fcol(f(inp[nm][0])))
    cols.append(fcol(f(inp["rwkv_r_k"][0]).reshape(D)))
    for nm in ("rwkv_ln_w", "rwkv_ln_b"):
        cols.append(fcol(f(inp[nm][0])))
    sh["consts"] = np.ascontiguousarray(np.concatenate(cols, axis=1))
    for k, v in static_consts().items():
        sh["c_" + k] = v
    return sh


def prep_core(inp, c):
    f = lambda a: np.ascontiguousarray(np.asarray(a, dtype=np.float32))
    m = {}
    m["xp"] = fcol(f(inp["x_prompt"][c]).T.copy()) if False else np.ascontiguousarray(
        f(inp["x_prompt"][c]).T.reshape(DC, 128, SEQ).transpose(1, 0, 2))
    xs = f(inp["x_sample"][4 * c:4 * c + 4]).reshape(128, D)
    m["xs"] = np.ascontiguousarray(xs.T.reshape(DC, 128, 128).transpose(1, 0, 2))
    m["ck"] = f(inp["cache_k"][0, 4 * c:4 * c + 4]).reshape(4, 128, 256)
    m["cv"] = f(inp["cache_v"][0, 4 * c:4 * c + 4]).reshape(4, 128, 256)
    ss = f(inp["state_shift"][0, 4 * c:4 * c + 4, 0])
    m["sshift"] = np.ascontiguousarray(ss.reshape(4, DC, 128).transpose(2, 0, 1))
    m["swkv"] = f(inp["state_wkv"][0, 4 * c:4 * c + 4]).reshape(4, 16, 128, 64)
    return m


_CACHE = {}


def kernel(**inputs):
    if "nc" not in _CACHE:
        _CACHE["nc"] = build()[0]
    nc = _CACHE["nc"]
    sh = prep_shared(inputs)
    in_maps = []
    for c in range(NCORE):
        m = dict(sh)
        m.update(prep_core(inputs, c))
        in_maps.append(m)
    res = run_bass_kernel_spmd(nc, in_maps, core_ids=list(range(NCORE)))
    R = res.results
    y_prompt = np.zeros((8, SEQ, D), np.float32)
    y_sample = np.zeros((32, 32, D), np.float32)
    k_prompt = np.zeros((1, 8, 128, 4, 64), np.float32)
    v_prompt = np.zeros((1, 8, 128, 4, 64), np.float32)
    k_sample = np.zeros((1, 32, 32, 4, 64), np.float32)
    v_sample = np.zeros((1, 32, 32, 4, 64), np.float32)
    shift_prompt = np.zeros((1, 8, 1, D), np.float32)
    wkv_prompt = np.zeros((1, 8, 32, 64, 64), np.float32)
    shift_sample = np.zeros((1, 32, 1, D), np.float32)
    wkv_sample = np.zeros((1, 32, 32, 64, 64), np.float32)
    for c in range(NCORE):
        r = R[c]
        yT = np.asarray(r["yT"])
        y = yT.transpose(2, 1, 0).reshape(SEQ + 128, D)
        y_prompt[c] = y[:SEQ]
        y_sample[4 * c:4 * c + 4] = y[SEQ:].reshape(4, 32, D)
        k_prompt[0, c] = np.asarray(r["kp"]).reshape(128, 4, 64)
        v_prompt[0, c] = np.asarray(r["vp"]).reshape(128, 4, 64)
        k_sample[0, 4 * c:4 * c + 4] = np.asarray(r["ks"]).reshape(4, 32, 4, 64)
        v_sample[0, 4 * c:4 * c + 4] = np.asarray(r["vs"]).reshape(4, 32, 4, 64)
        shift_prompt[0, c, 0] = np.asarray(r["shp"]).T.reshape(D)
        shift_sample[0, 4 * c:4 * c + 4, 0] = np.asarray(r["shs"]).transpose(1, 2, 0).reshape(4, D)
        wkv_prompt[0, c] = np.asarray(r["wkvp"]).reshape(32, 64, 64)
        wkv_sample[0, 4 * c:4 * c + 4] = np.asarray(r["wkvs"]).reshape(4, 32, 64, 64)
    return (y_prompt, y_sample, k_prompt, v_prompt, k_sample, v_sample,
            shift_prompt, wkv_prompt, shift_sample, wkv_sample)
```
